# Optimizing a Trainium2 kernel written in Bass

```python
import jax, jax.numpy as jnp
from jax import lax
import numpy as np

D_MODEL = 1024
BATCH = 4
SEQ = 4096
DEPTH = 1
DEC_BATCH = 128
DEC_SEQ = 8
PAST_LEN = 16384
PAGE_SIZE = 128

HEAD_DIM = 64
ATTN_WIDTH = D_MODEL // 2
N_HEADS = ATTN_WIDTH // HEAD_DIM
N_KV_HEADS = N_HEADS // 4
GROUP = N_HEADS // N_KV_HEADS
KV_WIDTH = N_KV_HEADS * HEAD_DIM
WINDOW = 128
BLOCK = WINDOW
POOL_WIDTH = D_MODEL - ATTN_WIDTH
POOL_WINDOWS = (2, 4, 8, 16)
N_POOL_GROUPS = len(POOL_WINDOWS)
POOL_GROUP_WIDTH = POOL_WIDTH // N_POOL_GROUPS
POOL_HIST = max(POOL_WINDOWS) - 1
IN_WIDTH = ATTN_WIDTH + 2 * KV_WIDTH + POOL_WIDTH
N_MEM = 256
N_CROSS_HEADS = 4
CROSS_HEAD_DIM = D_MODEL // N_CROSS_HEADS
D_FF = 4 * D_MODEL
RMS_EPS = 1e-5
NEG_INF = -1e30

kernel_name = 'hymba_swa_sink_pool_memxattn_step'


def _rmsnorm(x, g):
    xf = x.astype(jnp.float32)
    xf = xf * lax.rsqrt(jnp.mean(xf * xf, axis=-1, keepdims=True) + RMS_EPS)
    return (xf * g.astype(jnp.float32)).astype(x.dtype)


def _alibi_slopes():
    return 2.0 ** (-8.0 * jnp.arange(1, N_HEADS + 1, dtype=jnp.float32) / N_HEADS)


def _mixer_in(x, g_mix, w_in):
    B, L, _ = x.shape
    h = _rmsnorm(x, g_mix)
    proj = h @ w_in
    q = proj[..., :ATTN_WIDTH].reshape(B, L, N_KV_HEADS, GROUP, HEAD_DIM)
    k = proj[..., ATTN_WIDTH:ATTN_WIDTH + KV_WIDTH].reshape(B, L, N_KV_HEADS, HEAD_DIM)
    v = proj[..., ATTN_WIDTH + KV_WIDTH:ATTN_WIDTH + 2 * KV_WIDTH].reshape(B, L, N_KV_HEADS, HEAD_DIM)
    u = proj[..., ATTN_WIDTH + 2 * KV_WIDTH:]
    return q, k, v, u


def _window_probs(scores, qpos, kpos, sinks):
    dist = qpos[..., :, None] - kpos[..., None, :]
    valid = (dist >= 0) & (dist <= WINDOW) & (kpos[..., None, :] >= 0)
    dist = dist[..., None, None, :, :].astype(jnp.float32)
    valid = valid[..., None, None, :, :]
    slopes = _alibi_slopes().reshape(N_KV_HEADS, GROUP, 1, 1)
    logits = jnp.where(valid, scores - slopes * dist, NEG_INF)
    sink = jnp.broadcast_to(sinks.astype(jnp.float32).reshape(N_KV_HEADS, GROUP, 1, 1),
                            logits.shape[:-1] + (1,))
    p = jax.nn.softmax(jnp.concatenate([logits, sink], axis=-1), axis=-1)
    return p[..., :-1]


def _window_attn_prompt(q, k, v, sinks):
    B, S = q.shape[:2]
    nb = S // BLOCK
    qb = q.reshape(B, nb, BLOCK, N_KV_HEADS, GROUP, HEAD_DIM)
    kb = k.reshape(B, nb, BLOCK, N_KV_HEADS, HEAD_DIM)
    vb = v.reshape(B, nb, BLOCK, N_KV_HEADS, HEAD_DIM)
    pad = jnp.zeros_like(kb[:, :1])
    k2 = jnp.concatenate([jnp.concatenate([pad, kb[:, :-1]], axis=1), kb], axis=2)
    v2 = jnp.concatenate([jnp.concatenate([pad, vb[:, :-1]], axis=1), vb], axis=2)
    scores = jnp.einsum('bnqkgd,bnskd->bnkgqs', qb, k2).astype(jnp.float32) * (HEAD_DIM ** -0.5)
    blk = jnp.arange(nb, dtype=jnp.int32)[:, None] * BLOCK
    qpos = blk + jnp.arange(BLOCK, dtype=jnp.int32)[None, :]
    kpos = blk - BLOCK + jnp.arange(2 * BLOCK, dtype=jnp.int32)[None, :]
    p = _window_probs(scores, qpos, kpos, sinks)
    o = jnp.einsum('bnkgqs,bnskd->bnqkgd', p.astype(v.dtype), v2)
    return o.reshape(B, S, ATTN_WIDTH)


def _window_attn_sample(q, k, v, cache_k, cache_v, sinks):
    B, T = q.shape[:2]
    kc = jnp.concatenate([cache_k, k], axis=1)
    vc = jnp.concatenate([cache_v, v], axis=1)
    n_keys = kc.shape[1]
    scores = jnp.einsum('btkgd,bskd->bkgts', q, kc).astype(jnp.float32) * (HEAD_DIM ** -0.5)
    qpos = PAST_LEN + jnp.arange(T, dtype=jnp.int32)
    kpos = PAST_LEN - cache_k.shape[1] + jnp.arange(n_keys, dtype=jnp.int32)
    p = _window_probs(scores, qpos, kpos, sinks)
    o = jnp.einsum('bkgts,bskd->btkgd', p.astype(v.dtype), vc)
    return o.reshape(B, T, ATTN_WIDTH), kc[:, -WINDOW:], vc[:, -WINDOW:]


def _pool_mix(u, u_prev, start_pos, w_pool, pool_scale):
    B, L, _ = u.shape
    full = jnp.concatenate([u_prev, u], axis=1)
    cs = jnp.cumsum(full.astype(jnp.float32), axis=1)
    cs = jnp.pad(cs, ((0, 0), (1, 0), (0, 0)))
    hi = cs[:, POOL_HIST + 1:]
    pos = start_pos + jnp.arange(L, dtype=jnp.int32)
    means = []
    for g, w in enumerate(POOL_WINDOWS):
        ch = slice(g * POOL_GROUP_WIDTH, (g + 1) * POOL_GROUP_WIDTH)
        lo = cs[:, POOL_HIST + 1 - w:POOL_HIST + 1 - w + L, ch]
        cnt = jnp.minimum(pos + 1, w).astype(jnp.float32)[None, :, None]
        means.append((hi[..., ch] - lo) / cnt)
    d = (jnp.concatenate(means, axis=-1) - u.astype(jnp.float32)).astype(u.dtype)
    d = d.reshape(B, L, N_POOL_GROUPS, POOL_GROUP_WIDTH)
    y = jnp.einsum('blgc,gce->blge', d, w_pool).reshape(B, L, POOL_WIDTH)
    return y * pool_scale, full[:, -POOL_HIST:]


def _mem_kv(mem, g_mem, w_ck, w_cv):
    B = mem.shape[0]
    hm = _rmsnorm(mem, g_mem)
    k = (hm @ w_ck).reshape(B, N_MEM, N_CROSS_HEADS, CROSS_HEAD_DIM)
    v = (hm @ w_cv).reshape(B, N_MEM, N_CROSS_HEADS, CROSS_HEAD_DIM)
    return k, v


def _layer_tail(x, attn_o, pool_o, mem_k, mem_v, w_out, g_cross, w_cq, w_co, g_ffn, w_up, w_down):
    B, L, _ = x.shape
    x = x + jnp.concatenate([attn_o, pool_o], axis=-1) @ w_out
    h = _rmsnorm(x, g_cross)
    q = (h @ w_cq).reshape(B, L, N_CROSS_HEADS, CROSS_HEAD_DIM)
    s = jnp.einsum('blhd,bmhd->bhlm', q, mem_k).astype(jnp.float32) * (CROSS_HEAD_DIM ** -0.5)
    p = jax.nn.softmax(s, axis=-1)
    o = jnp.einsum('bhlm,bmhd->blhd', p.astype(mem_v.dtype), mem_v).reshape(B, L, D_MODEL)
    x = x + o @ w_co
    h = _rmsnorm(x, g_ffn)
    x = x + jnp.square(jax.nn.relu(h @ w_up)) @ w_down
    return x


def setup_inputs(seed: int = 0) -> dict:
    key = jax.random.key(seed)
    ks = jax.random.split(key, 26)
    f32 = jnp.float32

    def nrm(k, shape, scale=1.0):
        return jax.random.normal(k, shape, f32) * scale

    def gain(k, shape):
        return 1.0 + 0.05 * jax.random.normal(k, shape, f32)

    return {
        'x_prompt': nrm(ks[0], (BATCH, SEQ, D_MODEL)),
        'x_sample': nrm(ks[1], (DEC_BATCH, DEC_SEQ, D_MODEL)),
        'cache_win_k': nrm(ks[2], (DEPTH, DEC_BATCH, WINDOW, N_KV_HEADS, HEAD_DIM)),
        'cache_win_v': nrm(ks[3], (DEPTH, DEC_BATCH, WINDOW, N_KV_HEADS, HEAD_DIM)),
        'state_pool': nrm(ks[4], (DEPTH, DEC_BATCH, POOL_HIST, POOL_WIDTH)),
        'cache_mem_k': nrm(ks[5], (DEPTH, DEC_BATCH, N_MEM, N_CROSS_HEADS, CROSS_HEAD_DIM)),
        'cache_mem_v': nrm(ks[6], (DEPTH, DEC_BATCH, N_MEM, N_CROSS_HEADS, CROSS_HEAD_DIM)),
        'mem_prompt': nrm(ks[7], (BATCH, N_MEM, D_MODEL)),
        'g_mix': gain(ks[8], (DEPTH, D_MODEL)),
        'w_in': nrm(ks[9], (DEPTH, D_MODEL, IN_WIDTH), D_MODEL ** -0.5),
        'attn_sinks': nrm(ks[10], (DEPTH, N_HEADS)),
        'w_pool': nrm(ks[11], (DEPTH, N_POOL_GROUPS, POOL_GROUP_WIDTH, POOL_GROUP_WIDTH), POOL_GROUP_WIDTH ** -0.5),
        'pool_scale': gain(ks[12], (DEPTH, POOL_WIDTH)),
        'w_out': nrm(ks[13], (DEPTH, D_MODEL, D_MODEL), D_MODEL ** -0.5),
        'g_cross': gain(ks[14], (DEPTH, D_MODEL)),
        'g_mem': gain(ks[15], (DEPTH, D_MODEL)),
        'w_cq': nrm(ks[16], (DEPTH, D_MODEL, D_MODEL), D_MODEL ** -0.5),
        'w_ck': nrm(ks[17], (DEPTH, D_MODEL, D_MODEL), D_MODEL ** -0.5),
        'w_cv': nrm(ks[18], (DEPTH, D_MODEL, D_MODEL), D_MODEL ** -0.5),
        'w_co': nrm(ks[19], (DEPTH, D_MODEL, D_MODEL), D_MODEL ** -0.5),
        'g_ffn': gain(ks[20], (DEPTH, D_MODEL)),
        'w_up': nrm(ks[21], (DEPTH, D_MODEL, D_FF), D_MODEL ** -0.5),
        'w_down': nrm(ks[22], (DEPTH, D_FF, D_MODEL), D_FF ** -0.5),
        'g_final': gain(ks[23], (D_MODEL,)),
    }


def reference(x_prompt, x_sample, cache_win_k, cache_win_v, state_pool, cache_mem_k, cache_mem_v,
              mem_prompt, g_mix, w_in, attn_sinks, w_pool, pool_scale, w_out, g_cross, g_mem,
              w_cq, w_ck, w_cv, w_co, g_ffn, w_up, w_down, g_final):
    xp, xs = x_prompt, x_sample
    wk_p, wv_p, pool_p, mk_p, mv_p = [], [], [], [], []
    wk_s, wv_s, pool_s = [], [], []
    for l in range(DEPTH):
        q, k, v, u = _mixer_in(xp, g_mix[l], w_in[l])
        attn_o = _window_attn_prompt(q, k, v, attn_sinks[l])
        u_prev = jnp.zeros((u.shape[0], POOL_HIST, POOL_WIDTH), u.dtype)
        pool_o, pool_new = _pool_mix(u, u_prev, 0, w_pool[l], pool_scale[l])
        mem_k, mem_v = _mem_kv(mem_prompt, g_mem[l], w_ck[l], w_cv[l])
        xp = _layer_tail(xp, attn_o, pool_o, mem_k, mem_v, w_out[l], g_cross[l], w_cq[l], w_co[l],
                         g_ffn[l], w_up[l], w_down[l])
        wk_p.append(k[:, -WINDOW:])
        wv_p.append(v[:, -WINDOW:])
        pool_p.append(pool_new)
        mk_p.append(mem_k)
        mv_p.append(mem_v)
        q, k, v, u = _mixer_in(xs, g_mix[l], w_in[l])
        attn_o, k_buf, v_buf = _window_attn_sample(q, k, v, cache_win_k[l], cache_win_v[l], attn_sinks[l])
        pool_o, pool_new = _pool_mix(u, state_pool[l], PAST_LEN, w_pool[l], pool_scale[l])
        xs = _layer_tail(xs, attn_o, pool_o, cache_mem_k[l], cache_mem_v[l], w_out[l], g_cross[l],
                         w_cq[l], w_co[l], g_ffn[l], w_up[l], w_down[l])
        wk_s.append(k_buf)
        wv_s.append(v_buf)
        pool_s.append(pool_new)
    y_prompt = _rmsnorm(xp, g_final)
    y_sample = _rmsnorm(xs, g_final)
    return (y_prompt, y_sample,
            jnp.stack(wk_p), jnp.stack(wv_p), jnp.stack(pool_p), jnp.stack(mk_p), jnp.stack(mv_p),
            jnp.stack(wk_s), jnp.stack(wv_s), jnp.stack(pool_s))
```

```python
import numpy as np
from contextlib import ExitStack
import concourse.bass as bass
import concourse.mybir as mybir
from concourse.bass_utils import run_bass_kernel_spmd

F32 = mybir.dt.float32
BF16 = mybir.dt.bfloat16
ALU = mybir.AluOpType
AF = mybir.ActivationFunctionType
AX = mybir.AxisListType

NCORES = 8
STAGE = 99
D = 1024
SEQ_CORE = 2048
NT_P = 512
NG_P = SEQ_CORE // NT_P
RING = 14
EPS = 1e-5


class Buf:
    __slots__ = ("name", "w", "r", "al", "excl")

    def __init__(self, name, excl=False):
        self.name = name
        self.w = None
        self.r = {}
        self.al = []
        self.excl = excl


def alias(*bufs):
    for a in bufs:
        for b in bufs:
            if a is not b and b not in a.al:
                a.al.append(b)


class Prog:
    def __init__(self, nc, es, dry):
        self.nc, self.es, self.dry = nc, es, dry
        self.q = {e: [] for e in ("pe", "act", "dve", "pool", "sp")}
        self.cnt, self.sems = {}, {}
        self.waited = {e: {} for e in self.q}

    def sem(self, key):
        if key not in self.sems:
            self.sems[key] = None if self.dry else self.es.enter_context(self.nc.semaphore(key))
            self.cnt[key] = 0

    def _wait(self, eng, tok):
        if tok is None:
            return
        key, val = tok
        if self.waited[eng].get(key, 0) >= val:
            return
        self.waited[eng][key] = val
        self.q[eng].append(("w", key, val))

    def _deps(self, eng, reads, writes, extra):
        for b in reads:
            self._wait(eng, b.w)
            if b.excl:
                for k, v in b.r.items():
                    if k != eng:
                        self._wait(eng, (k, v))
        for b in writes:
            for bb in [b] + b.al:
                self._wait(eng, bb.w)
                for k, v in bb.r.items():
                    self._wait(eng, (k, v))
        for t in extra:
            self._wait(eng, t)

    def _commit(self, tok, reads, writes):
        k, v = tok
        for b in reads:
            b.r[k] = max(b.r.get(k, 0), v)
        for b in writes:
            b.w = tok
            b.r = {}

    def op(self, eng, fn, reads=(), writes=(), extra=()):
        self._deps(eng, reads, writes, extra)
        self.sem(eng)
        self.cnt[eng] += 1
        tok = (eng, self.cnt[eng])
        self.q[eng].append(("i", fn, eng, 1))
        self._commit(tok, reads, writes)
        return tok

    def mm(self, fns, reads=(), writes=(), extra=()):
        self._deps("pe", reads, writes, extra)
        for f in fns[:-1]:
            self.q["pe"].append(("i", f, None, 0))
        self.sem("pe")
        self.cnt["pe"] += 1
        tok = ("pe", self.cnt["pe"])
        self.q["pe"].append(("i", fns[-1], "pe", 1))
        self._commit(tok, reads, writes)
        return tok

    def dma(self, qeng, semkey, out, in_, reads=(), writes=(), extra=()):
        if writes:
            semkey = "dw_" + writes[0].name
        elif reads:
            semkey = "dr_" + reads[0].name
        for b in reads:
            self._wait(qeng, b.w)
        for b in writes:
            for bb in [b] + b.al:
                if not (bb.w is not None and bb.w[0] == semkey):
                    self._wait(qeng, bb.w)
                for k, v in bb.r.items():
                    self._wait(qeng, (k, v))
        for t in extra:
            self._wait(qeng, t)
        self.sem(semkey)
        self.cnt[semkey] += 16
        tok = (semkey, self.cnt[semkey])
        self.q[qeng].append(("i", (lambda e, o=out, i=in_: e.dma_start(out=o, in_=i)), semkey, 16))
        self._commit(tok, reads, writes)
        return tok

    def flush(self, block):
        def run(name):
            def f(e):
                for it in self.q[name]:
                    if it[0] == "w":
                        e.wait_ge(self.sems[it[1]], it[2])
                    else:
                        ins = it[1](e)
                        if it[3]:
                            ins.then_inc(self.sems[it[2]], it[3])
            return f
        block.tensor(run("pe"))
        block.scalar(run("act"))
        block.vector(run("dve"))
        block.gpsimd(run("pool"))
        block.sync(run("sp"))


class WStream:
    def __init__(self, P, ring_ap, sched):
        self.P, self.ring = P, ring_ap
        self.sched = sched
        self.rec = []
        self.i = 0
        self.issued = 0
        self.slots = [Buf(f"ws{i}") for i in range(RING)]
        self.src = {}

    def _issue(self, j):
        name, m = self.sched[j]
        s = j % RING
        for (dst_fn, src_ap) in self.src[name](m):
            self.P.dma("pool", f"ws{s}", dst_fn(self.ring[:, s]), src_ap, writes=[self.slots[s]])

    def get(self, name, m):
        if self.sched is None:
            self.rec.append((name, m))
            return self.ring[:, 0], self.slots[0]
        assert self.sched[self.i] == (name, m), (self.i, self.sched[self.i], name, m)
        while self.issued < min(len(self.sched), self.i + RING - 3):
            self._issue(self.issued)
            self.issued += 1
        s = self.i % RING
        self.i += 1
        return self.ring[:, s], self.slots[s]


def build(nc, es, dry, sched):
    P = Prog(nc, es, dry)

    def din(name, shape):
        return nc.dram_tensor(name, list(shape), F32, kind="ExternalInput").ap()

    def dout(name, shape):
        return nc.dram_tensor(name, list(shape), F32, kind="ExternalOutput").ap()

    if not dry:
        xp = din("xp", [128 + SEQ_CORE, D]); xs = din("xs", [128, D]); mem = din("mem", [256, D])
        ck = din("ck", [16, 128, 128]); cv = din("cv", [16, 128, 128]); spool = din("spool", [16, 15, 512])
        cmk = din("cmk", [16, 256, D]); cmv = din("cmv", [16, 256, D])
        w_in = din("w_in", [D, 1280]); w_pool = din("w_pool", [4, 128, 128]); w_out = din("w_out", [D, D])
        w_cq = din("w_cq", [D, D]); w_ck = din("w_ck", [D, D]); w_cv = din("w_cv", [D, D]); w_co = din("w_co", [D, D])
        w_up = din("w_up", [D, 4 * D]); w_down = din("w_down", [4 * D, D])
        gvec_d = din("gvec", [128, 40]); pscale_d = din("pscale", [128, 4])
        sinkp_d = din("sinkp", [128, 8]); sinks_d = din("sinks", [128, 8])
        biasg_d = din("biasg", [128, 8 * 256]); biasf_d = din("biasf", [128, 8 * 256]); biass_d = din("biass", [128, 2 * 160])
        invc_d = din("invc", [128, 64])
        yp = dout("yp", [SEQ_CORE, D]); ys = dout("ys", [128, D])
        wkp = dout("wkp", [128, 128]); wvp = dout("wvp", [128, 128]); poolp = dout("poolp", [15, 512])
        memk_o = dout("memk", [256, D]); memv_o = dout("memv", [256, D])
        wks = dout("wks", [16, 128, 128]); wvs = dout("wvs", [16, 128, 128]); pools = dout("pools", [16, 15, 512])

    def sb(name, shape, dt):
        return es.enter_context(nc.sbuf_tensor("sb_" + name, list(shape), dt))

    xT = sb("xT", [128, 8, NT_P], F32); B_xT = Buf("xT")
    hT = sb("hT", [128, 8, NT_P], BF16); B_hT = Buf("hT")
    rstd = sb("rstd", [128, NT_P], F32); B_rstd = Buf("rstd")
    lnt = sb("lnt", [128, NT_P], F32); B_lnt = Buf("lnt")
    aoT = sb("aoT", [128, 8, NT_P], BF16); B_aoT = Buf("aoT")
    qT = sb("qT", [128, 4, NT_P], BF16); B_qT = Buf("qT")
    kT = sb("kT", [128, 128 + NT_P], BF16); B_kT = Buf("kT")
    vT = sb("vT", [128, NT_P], BF16); B_vT = Buf("vT")
    vtok = sb("vtok", [128, 5, 128], BF16); B_vtok = Buf("vtok")
    kv32 = sb("kv32", [128, 2, 128], F32); B_kv32 = Buf("kv32")
    dT = sb("dT", [128, 4, NT_P], BF16); B_dT = Buf("dT")
    pexp = sb("pexp", [128, 8, 256], BF16); B_pexp = Buf("pexp")
    pT = sb("pT", [128, 8, 2, 128], BF16); B_pT = Buf("pT")
    Dg = sb("Dg", [128, 8, 128], BF16); B_Dg = Buf("Dg")
    pTc = sb("pTc", [128, 4, 2, 128], BF16); B_pTc = Buf("pTc")
    bias = sb("bias", [128, 8, 256], F32); B_bias = Buf("bias")
    biass = sb("biass", [128, 2, 160], F32); B_biass = Buf("biass")
    memkT = sb("memkT", [128, 8, 256], BF16); B_memkT = Buf("memkT")
    memv = sb("memv", [128, 2, D], BF16); B_memv = Buf("memv")
    ring = sb("ring", [128, RING, 8, 128], BF16)
    xin = [sb(f"xin{i}", [128, D], F32) for i in range(2)]; B_xin = [Buf(f"xin{i}") for i in range(2)]
    yst = [sb(f"yst{i}", [128, D], F32) for i in range(2)]; B_yst = [Buf(f"yst{i}") for i in range(2)]
    ident = sb("ident", [128, 128], BF16); identf = sb("identf", [128, 128], F32); B_const = Buf("const")
    ones = sb("ones", [128, 128], BF16)
    gvec = sb("gvec", [128, 5, 8], F32); pscale = sb("pscale", [128, 4], F32)
    sinkp = sb("sinkp", [128, 8], F32); sinks = sb("sinks", [128, 8], F32)
    invc = sb("invc", [128, 4, 16], F32)
    wpool = sb("wpool", [128, 4, 128], BF16); B_wpool = Buf("wpool")
    st = sb("st", [128, 64], F32); B_st = Buf("st")
    relu_t = [sb(f"relu{i}", [128, NT_P], BF16) for i in range(2)]; B_relu = [Buf(f"relu{i}") for i in range(2)]
    ost = sb("ost", [128, 512], F32); B_ost = Buf("ost")
    carryU = sb("carryU", [128, 4, 16], F32); B_cU = Buf("carryU")

    AR = 44 * 1024
    arena = sb("arena", [128, AR // 2], BF16)

    def av(off, nbytes, dt, pat=None, **kw):
        v = arena[:, off // 2:(off + nbytes) // 2]
        if dt is F32:
            v = v.bitcast(F32)
        if pat:
            v = v.rearrange(pat, **kw)
        return v

    hidT = av(0, 32 * NT_P * 2, BF16, "p (k t) -> p k t", k=32); B_hid = Buf("hidT")
    WU = 16 + NT_P
    U = av(0, 4 * WU * 4, F32, "p (g t) -> p g t", g=4); B_U = Buf("U")
    SA = av(4 * WU * 4, 4 * WU * 4, F32, "p (g t) -> p g t", g=4); B_SA = Buf("SA")
    SB = av(8 * WU * 4, 4 * WU * 4, F32, "p (g t) -> p g t", g=4); B_SB = Buf("SB")
    o_sb = 12 * WU * 4
    sbias = av(o_sb, 8 * 256 * 4, F32, "p (u t) -> p u t", u=8); B_sbias = Buf("sbias")
    yT = av(0, 8 * NT_P * 4, F32, "p (k t) -> p k t", k=8); B_yT = Buf("yT")
    XO = 34 * 1024
    kcT = av(XO, 16 * 128 * 2, BF16, "p (b t) -> p b t", b=16); B_kcT = Buf("kcT")
    vc = av(XO + 4096, 16 * 128 * 2, BF16, "p (b t) -> p b t", b=16); B_vc = Buf("vc")
    qs2 = av(XO + 8192, 16 * 32 * 2, BF16, "p (b t) -> p b t", b=16); B_qs2 = Buf("qs2")
    vnq = av(XO + 9216, 4 * 128 * 2, BF16, "p (i t) -> p i t", i=4); B_vnq = Buf("vnq")
    Kb = [av(i * 4096, 4096, BF16, "p (m t) -> p m t", m=2) for i in range(2)]; B_Kb = [Buf(f"Kb{i}") for i in range(2)]
    KbT = [av(8192 + i * 4096, 4096, BF16, "p (c t) -> p c t", c=8) for i in range(2)]; B_KbT = [Buf(f"KbT{i}") for i in range(2)]
    Vb = [av(16384 + i * 4096, 4096, BF16, "p (m t) -> p m t", m=2) for i in range(2)]; B_Vb = [Buf(f"Vb{i}") for i in range(2)]
    qpad = [av(24576 + i * 2048, 2048, BF16, "p (c t) -> p c t", c=8) for i in range(2)]; B_qpad = [Buf(f"qpad{i}") for i in range(2)]
    memst = av(0, 8192, F32, "p (m t) -> p m t", m=2); B_memst = Buf("memst")
    mkst = av(8192, 8192, F32, "p (m t) -> p m t", m=2); B_mkst = Buf("mkst")
    arena_bufs = [B_hid, B_U, B_SA, B_SB, B_sbias, B_yT, B_memst, B_mkst] + B_Kb + B_KbT + B_Vb + B_qpad
    alias(*arena_bufs)

    ps = [es.enter_context(nc.psum_tensor(f"ps{i}", [128, 512], F32)) for i in range(8)]
    B_ps = [Buf(f"ps{i}", excl=True) for i in range(8)]
    dctr = [0]

    def dbank():
        i = dctr[0] % 3
        dctr[0] += 1
        return ps[i], B_ps[i]
    PS_S = [3, 4]
    PS_T = [5, 6]
    PS_O = 7
    sctr = [0]
    tctr = [0]

    def sbank():
        i = PS_S[sctr[0] % 2]; sctr[0] += 1
        return ps[i], B_ps[i]

    def tbank():
        i = PS_T[tctr[0] % 2]; tctr[0] += 1
        return ps[i], B_ps[i]

    W = WStream(P, ring, sched)
    if not dry:
        def std_src(wap):
            v = wap.rearrange("(k p) (m c) -> p m k c", p=128, c=128)
            return lambda m: [((lambda s: s), v[:, m])]
        W.src["ck"] = std_src(w_ck); W.src["cv"] = std_src(w_cv)
        W.src["cq"] = std_src(w_cq); W.src["co"] = std_src(w_co); W.src["up"] = std_src(w_up)
        vin_q = w_in[:, 0:512].rearrange("(k p) (kv g d) -> p g k kv d", p=128, kv=2, g=4, d=64)
        vin_r = w_in[:, 512:1280].rearrange("(k p) (m c) -> p m k c", p=128, c=128)

        def in_src(m):
            if m < 4:
                return [((lambda s: s[:, :, 0:64]), vin_q[:, m, :, 0, :]),
                        ((lambda s: s[:, :, 64:128]), vin_q[:, m, :, 1, :])]
            return [((lambda s: s), vin_r[:, m - 4])]
        W.src["in"] = in_src
        vo_a = w_out[0:512, :].rearrange("(kv g d) (m c) -> kv d m g c", kv=2, g=4, d=64, c=128)
        vo_p = w_out[512:1024, :].rearrange("(k p) (m c) -> p m k c", p=128, c=128)

        def out_src(m):
            return [((lambda s: s[0:64, 0:4, :]), vo_a[0, :, m]),
                    ((lambda s: s[64:128, 0:4, :]), vo_a[1, :, m]),
                    ((lambda s: s[:, 4:8, :]), vo_p[:, m])]
        W.src["out"] = out_src
        vdn = w_down.rearrange("(q k p) (m c) -> p m q k c", p=128, k=8, c=128)
        W.src["down"] = lambda mq: [((lambda s: s), vdn[:, mq // 4, mq % 4])]

    if not dry:
        P.op("pool", lambda e: e.memset(identf[:], 0.0), writes=[B_const])
        P.op("pool", lambda e: e.iota(identf[:], pattern=[[1, 128]], base=0, channel_multiplier=-1,
                                      allow_small_or_imprecise_dtypes=True), writes=[B_const])
        P.op("dve", lambda e: e.tensor_single_scalar(out=ident[:], in_=identf[:], scalar=0.0, op=ALU.is_equal),
             reads=[B_const], writes=[B_const])
        P.op("dve", lambda e: e.tensor_single_scalar(out=identf[:], in_=identf[:], scalar=0.0, op=ALU.is_equal),
             writes=[B_const])
        P.op("dve", lambda e: e.memset(ones[:], 1.0), writes=[B_const])
        for (dst, src) in ((gvec[:].rearrange("p a b -> p (a b)"), gvec_d), (pscale[:], pscale_d), (sinkp[:], sinkp_d),
                           (sinks[:], sinks_d), (invc[:].rearrange("p a b -> p (a b)"), invc_d),
                           (biass[:].rearrange("p a b -> p (a b)"), biass_d)):
            P.dma("sp", "cst", dst, src[:, :], writes=[B_const])
        P.dma("sp", "biasld", bias[:].rearrange("p a b -> p (a b)"), biasf_d[:, :], writes=[B_bias])
        P.dma("pool", "wpool", wpool[:], w_pool.rearrange("g c e -> c g e"), writes=[B_wpool])

    CONST = [B_const]

    class _Stop(Exception):
        pass

    def finish():
        for key, val in P.cnt.items():
            if key not in ("pe", "act", "dve", "pool"):
                P._wait("sp", (key, val))
        for e_ in ("pe", "act", "dve", "pool"):
            if P.cnt.get(e_, 0):
                P._wait("sp", (e_, P.cnt[e_]))
        return P, W
    if STAGE == -1:
        return finish()

    def load_xT(src_rows, ntiles, dst=None, dstB=None):
        dst = xT if dst is None else dst
        dstB = B_xT if dstB is None else dstB
        for j in range(ntiles):
            xb, Bx = xin[j % 2], B_xin[j % 2]
            P.dma("sp", f"xin{j % 2}", xb[:], src_rows(j), writes=[Bx])
            for hf in range(2):
                pb, Bp = dbank()
                pv = pb[:].rearrange("p (c t) -> p c t", c=4)
                P.mm([(lambda e, c=c, pv=pv, xb=xb, hf=hf: e.transpose(out=pv[:, c, :], in_=xb[:, (hf * 4 + c) * 128:(hf * 4 + c + 1) * 128],
                                                                       identity=identf[:])) for c in range(4)],
                     reads=[Bx] + CONST, writes=[Bp])
                P.op("act" if hf == 0 else "dve",
                     (lambda e, pv=pv, hf=hf, j=j: e.activation(out=dst[:, hf * 4:hf * 4 + 4, j * 128:(j + 1) * 128], in_=pv, func=AF.Copy))
                     if hf == 0 else
                     (lambda e, pv=pv, hf=hf, j=j: e.tensor_copy(out=dst[:, hf * 4:hf * 4 + 4, j * 128:(j + 1) * 128], in_=pv)),
                     reads=[Bp], writes=[dstB])

    def norm(src, Bsrc, gi, NT, dst, Bdst):
        P.op("act", lambda e: e.activation(out=hT[:, :, :NT], in_=src[:, :, :NT], func=AF.Square),
             reads=[Bsrc], writes=[B_hT])
        pb, Bp = dbank()
        P.mm([(lambda e, k=k: e.matmul(pb[:, :NT], lhsT=ones[:], rhs=hT[:, k, :NT], start=(k == 0), stop=(k == 7)))
              for k in range(8)], reads=[B_hT] + CONST, writes=[Bp])
        P.op("act", lambda e: e.activation(out=lnt[:, :NT], in_=pb[:, :NT], func=AF.Ln, scale=1.0 / D, bias=EPS),
             reads=[Bp], writes=[B_lnt])
        P.op("act", lambda e: e.activation(out=rstd[:, :NT], in_=lnt[:, :NT], func=AF.Exp, scale=-0.5),
             reads=[B_lnt], writes=[B_rstd])
        for k in range(8):
            P.op("dve", lambda e, k=k: e.scalar_tensor_tensor(out=dst[:, k, :NT], in0=src[:, k, :NT], scalar=gvec[:, gi, k:k + 1],
                                                              in1=rstd[:, :NT], op0=ALU.mult, op1=ALU.mult),
                 reads=[Bsrc, B_rstd] + CONST, writes=[Bdst])

    def dense(wname, units, NT, rhs_fn, Brhs, evac, kgroups=1):
        for m in units:
            pb, Bp = dbank()
            fns, Bs = [], []
            for q in range(kgroups):
                slot, Bslot = W.get(wname, m * kgroups + q if kgroups > 1 else m)
                Bs.append(Bslot)
                for k in range(8):
                    fns.append(lambda e, slot=slot, k=k, q=q, pb=pb: e.matmul(
                        pb[:, :NT], lhsT=slot[:, k, :], rhs=rhs_fn(q * 8 + k),
                        start=(q == 0 and k == 0), stop=(q == kgroups - 1 and k == 7)))
            P.mm(fns, reads=Bs + [Brhs], writes=[Bp])
            evac(m, pb, Bp)

    def resid_evac(NT):
        def f(m, pb, Bp):
            P.op("dve", lambda e: e.tensor_tensor(out=xT[:, m, :NT], in0=pb[:, :NT], in1=xT[:, m, :NT], op=ALU.add),
                 reads=[Bp], writes=[B_xT])
        return f

    def softmax_units(nu, width, src_fn, Bsrc_list, sink_ap, scale, out_p):
        pass

    def diag_T(nu, nkc, p_src, Bp_src, dst_fn, Bdst, kw):
        items = [(u, kc) for u in range(nu) for kc in range(nkc)]
        for i0 in range(0, len(items), 4):
            chunk = items[i0:i0 + 4]
            pb, Bp = tbank()
            pv = pb[:].rearrange("p (s t) -> p s t", s=4)
            P.mm([(lambda e, s=s, u=u, kc=kc, pv=pv: e.matmul(pv[0:kw[kc], s, :], lhsT=p_src[:, u, kc * 128:kc * 128 + kw[kc]],
                                                              rhs=Dg[:, u, :], start=True, stop=True))
                  for s, (u, kc) in enumerate(chunk)], reads=[Bp_src, B_Dg], writes=[Bp])
            for s, (u, kc) in enumerate(chunk):
                P.op("act" if s % 2 == 0 else "dve",
                     (lambda e, s=s, u=u, kc=kc, pv=pv: e.activation(out=dst_fn(u, kc), in_=pv[0:kw[kc], s, :], func=AF.Copy))
                     if s % 2 == 0 else
                     (lambda e, s=s, u=u, kc=kc, pv=pv: e.tensor_copy(out=dst_fn(u, kc), in_=pv[0:kw[kc], s, :])),
                     reads=[Bp], writes=[Bdst])

    def attn_stats(nu, rmax_c, sink_ap, rs_c):
        pass

    def win_softmax(nu, width, sink_ap):
        P.op("dve", lambda e: e.tensor_reduce(out=st[:, 0:nu], in_=sbias[:, 0:nu, 0:width], axis=AX.X, op=ALU.max),
             reads=[B_sbias], writes=[B_st])
        P.op("dve", lambda e: e.tensor_tensor(out=st[:, 8:8 + nu], in0=st[:, 0:nu], in1=sink_ap, op=ALU.max),
             reads=[B_st] + CONST, writes=[B_st])
        P.op("dve", lambda e: e.tensor_scalar(out=st[:, 16:16 + nu], in0=st[:, 8:8 + nu], scalar1=-1.0, scalar2=None, op0=ALU.mult),
             reads=[B_st], writes=[B_st])
        P.op("dve", lambda e: e.tensor_tensor(out=st[:, 24:24 + nu], in0=sink_ap, in1=st[:, 16:16 + nu], op=ALU.add),
             reads=[B_st] + CONST, writes=[B_st])
        for u in range(nu):
            P.op("act", lambda e, u=u: e.activation(out=pexp[:, u, 0:width], in_=sbias[:, u, 0:width], func=AF.Exp,
                                                    bias=st[:, 16 + u:17 + u], scale=1.0, accum_out=st[:, 32 + u:33 + u]),
                 reads=[B_sbias, B_st], writes=[B_pexp, B_st])
        P.op("act", lambda e: e.activation(out=st[:, 40:40 + nu], in_=st[:, 24:24 + nu], func=AF.Exp),
             reads=[B_st], writes=[B_st])
        P.op("dve", lambda e: e.tensor_tensor(out=st[:, 48:48 + nu], in0=st[:, 32:32 + nu], in1=st[:, 40:40 + nu], op=ALU.add),
             reads=[B_st], writes=[B_st])
        P.op("dve", lambda e: e.reciprocal(out=st[:, 56:56 + nu], in_=st[:, 48:48 + nu]), reads=[B_st], writes=[B_st])
        for u in range(nu):
            P.op("dve", lambda e, u=u: e.tensor_scalar(out=Dg[:, u, :], in0=ident[:], scalar1=st[:, 56 + u:57 + u], scalar2=None,
                                                       op0=ALU.mult), reads=[B_st] + CONST, writes=[B_Dg])

    def cross_softmax(score_banks):
        for hp, (pb, Bp) in enumerate(score_banks):
            pv = pb[:].rearrange("p (h t) -> p h t", h=2)
            P.op("dve", lambda e, pv=pv, hp=hp: e.tensor_reduce(out=st[:, 2 * hp:2 * hp + 2], in_=pv, axis=AX.X, op=ALU.max),
                 reads=[Bp], writes=[B_st])
        P.op("dve", lambda e: e.tensor_scalar(out=st[:, 16:20], in0=st[:, 0:4], scalar1=-1.0 / 16.0, scalar2=None, op0=ALU.mult),
             reads=[B_st], writes=[B_st])
        for hp, (pb, Bp) in enumerate(score_banks):
            pv = pb[:].rearrange("p (h t) -> p h t", h=2)
            for hh in range(2):
                h = 2 * hp + hh
                P.op("act", lambda e, pv=pv, hh=hh, h=h: e.activation(out=pexp[:, h, :], in_=pv[:, hh, :], func=AF.Exp,
                                                                      bias=st[:, 16 + h:17 + h], scale=1.0 / 16.0,
                                                                      accum_out=st[:, 32 + h:33 + h]),
                     reads=[Bp, B_st], writes=[B_pexp, B_st])
        P.op("dve", lambda e: e.reciprocal(out=st[:, 56:60], in_=st[:, 32:36]), reads=[B_st], writes=[B_st])
        for h in range(4):
            P.op("dve", lambda e, h=h: e.tensor_scalar(out=Dg[:, h, :], in0=ident[:], scalar1=st[:, 56 + h:57 + h], scalar2=None,
                                                       op0=ALU.mult), reads=[B_st] + CONST, writes=[B_Dg])

    def out_tok_major(srcs, Bsrcs, ncols_each, dst_dma):
        pb, Bp = dbank()
        pv = pb[:].rearrange("p (c t) -> p c t", c=4)
        n = len(srcs)
        P.mm([(lambda e, i=i: e.transpose(out=pv[:, i, :], in_=srcs[i], identity=identf[:])) for i in range(n)],
             reads=list(Bsrcs) + CONST, writes=[Bp])
        P.op("dve", lambda e: e.tensor_copy(out=ost[:, 0:n * 128], in_=pb[:, 0:n * 128]), reads=[Bp], writes=[B_ost])
        dst_dma()

    for t in range(2):
        P.dma("sp", "memld", memst[:, t, :], mem[t * 128:(t + 1) * 128, :] if not dry else None, writes=[B_memst])
    for t in range(2):
        for hf in range(2):
            pb, Bp = dbank()
            pv = pb[:].rearrange("p (c t) -> p c t", c=4)
            P.mm([(lambda e, c=c, pv=pv, t=t, hf=hf: e.transpose(out=pv[:, c, :], in_=memst[:, t, (hf * 4 + c) * 128:(hf * 4 + c + 1) * 128],
                                                                 identity=identf[:])) for c in range(4)],
                 reads=[B_memst] + CONST, writes=[Bp])
            P.op("act", lambda e, pv=pv, hf=hf, t=t: e.activation(out=xT[:, hf * 4:hf * 4 + 4, t * 128:(t + 1) * 128], in_=pv, func=AF.Copy),
                 reads=[Bp], writes=[B_xT])
    if STAGE == -2:
        return finish()
    norm(xT, B_xT, 2, 256, hT, B_hT)
    if STAGE == -3:
        return finish()
    for (wn, is_k) in (("ck", True), ("cv", False)):
        for mh in range(2):
            yb = [sbank(), sbank()]
            for m4 in range(4):
                m = mh * 4 + m4
                slot, Bslot = W.get(wn, m)
                if is_k:
                    pb, Bp = dbank()
                    P.mm([(lambda e, k=k, slot=slot, pb=pb: e.matmul(pb[:, :256], lhsT=slot[:, k, :], rhs=hT[:, k, :256],
                                                                    start=(k == 0), stop=(k == 7))) for k in range(8)],
                         reads=[Bslot, B_hT], writes=[Bp])
                    P.op("act", lambda e, m=m, pb=pb: e.activation(out=memkT[:, m, :], in_=pb[:, :256], func=AF.Copy),
                         reads=[Bp], writes=[B_memkT])
                for t in range(2):
                    yp_, Byp = yb[t]
                    P.mm([(lambda e, k=k, slot=slot, yp_=yp_, t=t, m4=m4: e.matmul(
                        yp_[:, m4 * 128:(m4 + 1) * 128], lhsT=hT[:, k, t * 128:(t + 1) * 128], rhs=slot[:, k, :],
                        start=(k == 0), stop=(k == 7))) for k in range(8)],
                        reads=[Bslot, B_hT], writes=[Byp])
            for t in range(2):
                yp_, Byp = yb[t]
                P.op("dve", lambda e, yp_=yp_, t=t, mh=mh: e.tensor_copy(out=mkst[:, t, mh * 512:(mh + 1) * 512], in_=yp_[:, :]),
                     reads=[Byp], writes=[B_mkst])
                if not is_k:
                    P.op("act", lambda e, yp_=yp_, t=t, mh=mh: e.activation(out=memv[:, t, mh * 512:(mh + 1) * 512], in_=yp_[:, :], func=AF.Copy),
                         reads=[Byp], writes=[B_memv])
        for t in range(2):
            P.dma("sp", "memout", (memk_o if is_k else memv_o)[t * 128:(t + 1) * 128, :] if not dry else None, mkst[:, t, :],
                  reads=[B_mkst])

    def group(kind, gi):
        sample = (kind == "S")
        halo = (kind == "H")
        NT = 128 if (sample or halo) else NT_P
        ntl = NT // 128
        if sample:
            load_xT(lambda j: xs[:, :], 1)
        elif halo:
            load_xT(lambda j: xp[0:128, :], 1)
        else:
            load_xT(lambda j: xp[128 + gi * NT_P + j * 128: 128 + gi * NT_P + (j + 1) * 128, :], ntl)
        norm(xT, B_xT, 0, NT, hT, B_hT)
        last = (kind == "P" and gi == NG_P - 1)
        want32 = last or sample

        if sample:
            for hb in range(2):
                P.dma("pool", "ckld", vc[:, hb * 8:(hb + 1) * 8, :], cv[hb * 8:(hb + 1) * 8].rearrange("b s f -> s b f"), writes=[B_vc])
            kst = pexp[:].rearrange("p u t -> p (u t)").rearrange("p (b f) -> p b f", b=16)
            for hb in range(2):
                P.dma("pool", "ckld", kst[:, hb * 8:(hb + 1) * 8, :], ck[hb * 8:(hb + 1) * 8].rearrange("b s f -> s b f"), writes=[B_pexp])
            for hb in range(2):
                pb, Bp = tbank()
                pv = pb[:].bitcast(BF16).rearrange("p (b t) -> p b t", b=8)
                P.mm([(lambda e, b=b, pv=pv, hb=hb: e.transpose(out=pv[:, b, :], in_=kst[:, hb * 8 + b, :], identity=ident[:])) for b in range(8)],
                     reads=[B_pexp] + CONST, writes=[Bp])
                P.op("dve", lambda e, pv=pv, hb=hb: e.tensor_copy(out=kcT[:, hb * 8:(hb + 1) * 8, :], in_=pv), reads=[Bp], writes=[B_kcT])
            Us = U[:, :, 0:384].rearrange("p g (b c) -> p g b c", b=16)
            P.op("dve", lambda e: e.memset(U[:, :, 0:384], 0.0), writes=[B_U])
            for hb in range(2):
                P.dma("sp", "spld", xin[hb][0:120, 0:512], spool[hb * 8:(hb + 1) * 8].rearrange("b r f -> (b r) f"), writes=[B_xin[hb]])
                pb, Bp = dbank()
                pv = pb[:].rearrange("p (c t) -> p c t", c=4)
                P.mm([(lambda e, c=c, pv=pv, hb=hb: e.transpose(out=pv[:, c, 0:120], in_=xin[hb][0:120, c * 128:(c + 1) * 128],
                                                                identity=identf[0:120, 0:120])) for c in range(4)],
                     reads=[B_xin[hb]] + CONST, writes=[Bp])
                for c in range(4):
                    P.op("dve", lambda e, c=c, pv=pv, hb=hb: e.tensor_copy(
                        out=Us[:, c, hb * 8:(hb + 1) * 8, 1:16], in_=pv[:, c, 0:120].rearrange("p (b r) -> p b r", b=8)),
                        reads=[Bp], writes=[B_U])

        def in_evac(m, pb, Bp):
            if m < 4:
                P.op("act", lambda e: e.activation(out=qT[:, m, :NT], in_=pb[:, :NT], func=AF.Copy), reads=[Bp], writes=[B_qT])
            elif m == 4:
                P.op("act", lambda e: e.activation(out=kT[:, 128:128 + NT], in_=pb[:, :NT], func=AF.Copy), reads=[Bp], writes=[B_kT])
                if want32:
                    P.op("dve", lambda e: e.tensor_copy(out=kv32[:, 0, :], in_=pb[:, NT - 128:NT]), reads=[Bp], writes=[B_kv32])
            elif m == 5:
                P.op("act", lambda e: e.activation(out=vT[:, :NT], in_=pb[:, :NT], func=AF.Copy), reads=[Bp], writes=[B_vT])
                if want32:
                    P.op("dve", lambda e: e.tensor_copy(out=kv32[:, 1, :], in_=pb[:, NT - 128:NT]), reads=[Bp], writes=[B_kv32])
            else:
                g = m - 6
                if sample:
                    P.op("dve", lambda e: e.tensor_copy(out=Us[:, g, :, 16:24], in_=pb[:, 0:128].rearrange("p (b t) -> p b t", b=16)),
                         reads=[Bp], writes=[B_U])
                else:
                    P.op("dve", lambda e: e.tensor_copy(out=U[:, g, 16:16 + NT], in_=pb[:, :NT]), reads=[Bp], writes=[B_U])
        dense("in", list(range(4, 10)) if halo else list(range(10)), NT, lambda k: hT[:, k, :NT], B_hT, in_evac)

        for j0 in range(0, ntl, 4):
            pb, Bp = tbank()
            pv = pb[:].bitcast(BF16)[:, 0:512].rearrange("p (j t) -> p j t", j=4)
            P.mm([(lambda e, j=j, pv=pv: e.transpose(out=pv[:, j - j0, :], in_=vT[:, j * 128:(j + 1) * 128], identity=ident[:]))
                  for j in range(j0, min(ntl, j0 + 4))], reads=[B_vT] + CONST, writes=[Bp])
            nj = min(ntl, j0 + 4) - j0
            P.op("dve", lambda e, pv=pv, j0=j0, nj=nj: e.tensor_copy(out=vtok[:, 1 + j0:1 + j0 + nj, :], in_=pv[:, 0:nj, :]),
                 reads=[Bp], writes=[B_vtok])

        def carry():
            P.op("dve", lambda e: e.tensor_copy(out=kT[:, 0:128], in_=kT[:, NT:NT + 128]), reads=[B_kT], writes=[B_kT])
            P.op("dve", lambda e: e.tensor_copy(out=vtok[:, 0, :], in_=vtok[:, ntl, :]), reads=[B_vtok], writes=[B_vtok])

        if halo:
            carry()
            P.op("dve", lambda e: e.tensor_copy(out=carryU[:], in_=U[:, :, NT:NT + 16]), reads=[B_U], writes=[B_cU])
            return

        if STAGE == 2.1:
            raise _Stop()
        if want32:
            if last:
                def dd():
                    P.dma("sp", "kvout", wkp[:, :], ost[:, 0:128], reads=[B_ost])
                    P.dma("sp", "kvout", wvp[:, :], ost[:, 128:256], reads=[B_ost])
            else:
                def dd():
                    for t in range(8):
                        P.dma("sp", "kvout", wks[:, 120 + t, :], ost[t:128:8, 0:128], reads=[B_ost])
                        P.dma("sp", "kvout", wvs[:, 120 + t, :], ost[t:128:8, 128:256], reads=[B_ost])
                    P.dma("sp", "d2d_k", wks[:, 0:120, :], ck[:, 8:128, :])
                    P.dma("sp", "d2d_v", wvs[:, 0:120, :], cv[:, 8:128, :])
            out_tok_major([kv32[:, 0, :], kv32[:, 1, :]], [B_kv32], 128, dd)

        if STAGE == 2.15:
            raise _Stop()
        Wd = 384 if sample else 16 + NT
        if not sample:
            P.op("dve", lambda e: e.tensor_copy(out=U[:, :, 0:16], in_=carryU[:]), reads=[B_cU], writes=[B_U])
        P.op("dve", lambda e: e.tensor_tensor(out=SA[:, :, 1:Wd], in0=U[:, :, 1:Wd], in1=U[:, :, 0:Wd - 1], op=ALU.add),
             reads=[B_U], writes=[B_SA])
        P.op("dve", lambda e: e.tensor_tensor(out=SB[:, 1:4, 3:Wd], in0=SA[:, 1:4, 3:Wd], in1=SA[:, 1:4, 1:Wd - 2], op=ALU.add),
             reads=[B_SA], writes=[B_SB])
        P.op("dve", lambda e: e.tensor_tensor(out=SA[:, 2:4, 7:Wd], in0=SB[:, 2:4, 7:Wd], in1=SB[:, 2:4, 3:Wd - 4], op=ALU.add),
             reads=[B_SB], writes=[B_SA])
        P.op("dve", lambda e: e.tensor_tensor(out=SB[:, 3, 15:Wd], in0=SA[:, 3, 15:Wd], in1=SA[:, 3, 7:Wd - 8], op=ALU.add),
             reads=[B_SA], writes=[B_SB])
        for g in range(4):
            S_, BS_ = (SA, B_SA) if g % 2 == 0 else (SB, B_SB)
            if sample:
                sv = S_[:, g, 0:384].rearrange("p (b c) -> p b c", b=16)[:, :, 16:24]
                uv = U[:, g, 0:384].rearrange("p (b c) -> p b c", b=16)[:, :, 16:24]
                dv = dT[:, g, 0:128].rearrange("p (b t) -> p b t", b=16)
            else:
                sv, uv, dv = S_[:, g, 16:16 + NT], U[:, g, 16:16 + NT], dT[:, g, :NT]
            P.op("dve", lambda e, sv=sv, uv=uv, dv=dv, g=g: e.scalar_tensor_tensor(out=dv, in0=sv, scalar=1.0 / (2 << g), in1=uv,
                                                                               op0=ALU.mult, op1=ALU.subtract),
                 reads=[BS_, B_U], writes=[B_dT])
            if kind == "P" and gi == 0:
                P.op("dve", lambda e, S_=S_, g=g: e.tensor_tensor(out=st[:, 0:16], in0=S_[:, g, 16:32], in1=invc[:, g, :], op=ALU.mult),
                     reads=[BS_] + CONST, writes=[B_st])
                P.op("dve", lambda e, g=g: e.tensor_tensor(out=dT[:, g, 0:16], in0=st[:, 0:16], in1=U[:, g, 16:32], op=ALU.subtract),
                     reads=[B_st, B_U], writes=[B_dT])
        if last:
            def dd():
                P.dma("sp", "poolout", poolp[:, :], ost[113:128, :], reads=[B_ost])
            out_tok_major([U[:, g, 16 + NT - 128:16 + NT] for g in range(4)], [B_U], 128, dd)
        if sample:
            for g in range(4):
                P.op("dve", lambda e, g=g: e.tensor_copy(out=SA[:, g, 0:128].rearrange("p (b t) -> p b t", b=16), in_=Us[:, g, :, 16:24]),
                     reads=[B_U, B_dT], writes=[B_SA])

            def dd():
                for t in range(8):
                    P.dma("sp", "poolout", pools[:, 7 + t, :], ost[t:128:8, :], reads=[B_ost])
                P.dma("sp", "d2d_p", pools[:, 0:7, :], spool[:, 8:15, :])
            out_tok_major([SA[:, g, 0:128] for g in range(4)], [B_SA], 128, dd)
        else:
            P.op("dve", lambda e: e.tensor_copy(out=carryU[:], in_=U[:, :, NT:NT + 16]), reads=[B_U], writes=[B_cU])
        for g in range(4):
            pb, Bp = dbank()
            P.mm([lambda e, g=g, pb=pb: e.matmul(pb[:, :NT], lhsT=wpool[:, g, :], rhs=dT[:, g, :NT], start=True, stop=True)],
                 reads=[B_wpool, B_dT], writes=[Bp])
            P.op("act", lambda e, g=g, pb=pb: e.activation(out=aoT[:, 4 + g, :NT], in_=pb[:, :NT], func=AF.Copy, scale=pscale[:, g:g + 1]),
                 reads=[Bp] + CONST, writes=[B_aoT])

        if STAGE == 2.2:
            raise _Stop()
        if not sample:
            for j in range(ntl):
                if gi == 0 and j == 1:
                    P.dma("sp", "biasld", bias[:].rearrange("p a b -> p (a b)"), biasg_d[:, :] if not dry else None, writes=[B_bias])
                for gp in range(2):
                    bk = [sbank(), sbank()]
                    fns = []
                    for g in (2 * gp, 2 * gp + 1):
                        for kv in range(2):
                            fns.append(lambda e, kv=kv, g=g, j=j, bk=bk: e.matmul(
                                bk[kv][0][:, (g % 2) * 256:(g % 2 + 1) * 256], lhsT=qT[kv * 64:(kv + 1) * 64, g, j * 128:(j + 1) * 128],
                                rhs=kT[kv * 64:(kv + 1) * 64, j * 128:j * 128 + 256], start=True, stop=True))
                    P.mm(fns, reads=[B_qT, B_kT], writes=[bk[0][1], bk[1][1]])
                    for kv in range(2):
                        u0 = 4 * gp + kv
                        P.op("dve", lambda e, kv=kv, u0=u0, bk=bk: e.scalar_tensor_tensor(
                            out=sbias[:, u0:u0 + 3:2, :], in0=bk[kv][0][:].rearrange("p (g t) -> p g t", g=2), scalar=0.125,
                            in1=bias[:, u0:u0 + 3:2, :], op0=ALU.mult, op1=ALU.add),
                            reads=[bk[kv][1], B_bias], writes=[B_sbias])
                win_softmax(8, 256, sinkp[:, 0:8])
                diag_T(8, 2, pexp, B_pexp, lambda u, kc: pT[:, u, kc, :], B_pT, [128, 128])
                po, Bpo = ps[PS_O], B_ps[PS_O]
                pov = po[:].rearrange("p (g t) -> p g t", g=4)
                fns = []
                for g in range(4):
                    for kv in range(2):
                        for kc in range(2):
                            fns.append(lambda e, g=g, kv=kv, kc=kc, j=j: e.matmul(
                                pov[kv * 64:(kv + 1) * 64, g, :], lhsT=vtok[:, j + kc, kv * 64:(kv + 1) * 64], rhs=pT[:, 2 * g + kv, kc, :],
                                start=(kc == 0), stop=(kc == 1)))
                P.mm(fns, reads=[B_vtok, B_pT], writes=[Bpo])
                P.op("act", lambda e, j=j: e.activation(out=aoT[:, 0:4, j * 128:(j + 1) * 128], in_=pov, func=AF.Copy),
                     reads=[Bpo], writes=[B_aoT])
            carry()
        else:
            P.op("dve", lambda e: e.tensor_copy(out=qs2[:].rearrange("p b (g t) -> p b g t", g=4),
                                                in_=qT[:, :, 0:128].rearrange("p g (b t) -> p b g t", b=16)), reads=[B_qT], writes=[B_qs2])
            pb, Bp = dbank()
            pvb = pb[:].bitcast(BF16)
            P.mm([(lambda e, i=i, pvb=pvb: e.transpose(out=pvb[0:32, i * 128:(i + 1) * 128], in_=vT[:, i * 32:(i + 1) * 32], identity=ident[:]))
                  for i in range(4)], reads=[B_vT] + CONST, writes=[Bp])
            P.op("dve", lambda e, pvb=pvb: e.tensor_copy(out=vnq[0:32, :, :], in_=pvb[0:32, 0:512].rearrange("p (i t) -> p i t", i=4)),
                 reads=[Bp], writes=[B_vnq])
            for i in range(4):
                bk = [sbank(), sbank()]
                fns = []
                for kv in range(2):
                    pvk = bk[kv][0]
                    for jq in range(4):
                        b = 4 * i + jq
                        fns.append(lambda e, kv=kv, jq=jq, b=b, pvk=pvk: e.matmul(
                            pvk[32 * jq:32 * jq + 32, 0:128], lhsT=qs2[kv * 64:(kv + 1) * 64, b, :], rhs=kcT[kv * 64:(kv + 1) * 64, b, :],
                            start=True, stop=True, tile_position=(kv * 64, 32 * jq)))
                    fns.append(lambda e, kv=kv, i=i, pvk=pvk: e.matmul(
                        pvk[:, 128:160], lhsT=qs2[kv * 64:(kv + 1) * 64, 4 * i:4 * i + 4, :].rearrange("p b t -> p (b t)"),
                        rhs=kT[kv * 64:(kv + 1) * 64, 128 + 32 * i:128 + 32 * i + 32], start=True, stop=True))
                P.mm(fns, reads=[B_qs2, B_kcT, B_kT], writes=[bk[0][1], bk[1][1]])
                for kv in range(2):
                    P.op("dve", lambda e, kv=kv, i=i, bk=bk: e.scalar_tensor_tensor(
                        out=sbias[:, 2 * i + kv, 0:160], in0=bk[kv][0][:, 0:160], scalar=0.125,
                        in1=biass[:, kv, :], op0=ALU.mult, op1=ALU.add),
                        reads=[bk[kv][1]] + CONST, writes=[B_sbias])
            win_softmax(8, 160, sinks[:, 0:8])
            diag_T(8, 2, pexp, B_pexp, lambda u, kc: pT[0:(128 if kc == 0 else 32), u, kc, :], B_pT, [128, 32])
            for i in range(4):
                pb, Bp = dbank()
                fns = []
                for kv in range(2):
                    u = 2 * i + kv
                    for jq in range(4):
                        b = 4 * i + jq
                        fns.append(lambda e, kv=kv, jq=jq, b=b, u=u, pb=pb: e.matmul(
                            pb[kv * 64:(kv + 1) * 64, 32 * jq:32 * jq + 32], lhsT=vc[:, b, kv * 64:(kv + 1) * 64], rhs=pT[:, u, 0, 32 * jq:32 * jq + 32],
                            start=(jq == 0), stop=False, skip_group_check=True))
                for kv in range(2):
                    u = 2 * i + kv
                    fns.append(lambda e, kv=kv, u=u, i=i, pb=pb: e.matmul(
                        pb[kv * 64:(kv + 1) * 64, 0:128], lhsT=vnq[0:32, i, kv * 64:(kv + 1) * 64], rhs=pT[0:32, u, 1, :],
                        start=False, stop=True, skip_group_check=True))
                P.mm(fns, reads=[B_vc, B_pT, B_vnq], writes=[Bp])
                P.op("act", lambda e, pb=pb, i=i: e.activation(
                    out=aoT[:, 0:4, 32 * i:32 * i + 32].rearrange("p g (j t) -> p j g t", j=4),
                    in_=pb[:, 0:128].rearrange("p (j g t) -> p j g t", j=4, g=4), func=AF.Copy),
                    reads=[Bp], writes=[B_aoT])

        if STAGE == 2.3:
            raise _Stop()
        dense("out", list(range(8)), NT, lambda k: aoT[:, k, :NT], B_aoT, resid_evac(NT))

        if STAGE == 2.4:
            raise _Stop()
        norm(xT, B_xT, 1, NT, hT, B_hT)
        qcT, B_qcT = aoT, B_aoT
        ocT, B_ocT = hT, B_hT

        def cq_evac(m, pb, Bp):
            P.op("act", lambda e: e.activation(out=qcT[:, m, :NT], in_=pb[:, :NT], func=AF.Copy), reads=[Bp], writes=[B_qcT])
        dense("cq", list(range(8)), NT, lambda k: hT[:, k, :NT], B_hT, cq_evac)

        if not sample:
            for j in range(ntl):
                banks = [sbank(), sbank()]
                for hp, (pb, Bp) in enumerate(banks):
                    pv = pb[:].rearrange("p (h t) -> p h t", h=2)
                    fns = []
                    for hh in range(2):
                        h = 2 * hp + hh
                        for dc in range(2):
                            fns.append(lambda e, pv=pv, hh=hh, h=h, dc=dc, j=j: e.matmul(
                                pv[:, hh, :], lhsT=qcT[:, 2 * h + dc, j * 128:(j + 1) * 128], rhs=memkT[:, 2 * h + dc, :],
                                start=(dc == 0), stop=(dc == 1)))
                    P.mm(fns, reads=[B_qcT, B_memkT], writes=[Bp])
                cross_softmax(banks)
                diag_T(4, 2, pexp, B_pexp, lambda h, mc: pTc[:, h, mc, :], B_pTc, [128, 128])
                for half in range(2):
                    pb, Bp = dbank()
                    pv = pb[:].rearrange("p (c t) -> p c t", c=4)
                    fns = []
                    for cc in range(4):
                        c = half * 4 + cc
                        h = c // 2
                        for mc in range(2):
                            fns.append(lambda e, pv=pv, cc=cc, c=c, h=h, mc=mc: e.matmul(
                                pv[:, cc, :], lhsT=memv[:, mc, c * 128:(c + 1) * 128], rhs=pTc[:, h, mc, :], start=(mc == 0), stop=(mc == 1)))
                    P.mm(fns, reads=[B_memv, B_pTc], writes=[Bp])
                    P.op("act" if half == 0 else "dve",
                         (lambda e, pv=pv, half=half, j=j: e.activation(out=ocT[:, half * 4:half * 4 + 4, j * 128:(j + 1) * 128], in_=pv, func=AF.Copy))
                         if half == 0 else
                         (lambda e, pv=pv, half=half, j=j: e.tensor_copy(out=ocT[:, half * 4:half * 4 + 4, j * 128:(j + 1) * 128], in_=pv)),
                         reads=[Bp], writes=[B_ocT])
        else:
            banks = [sbank(), sbank()]
            for i in range(2):
                P.op("dve", lambda e, i=i: e.memset(qpad[i][:], 0.0), writes=[B_qpad[i]])

            def ld_k(b):
                P.dma("pool", f"kb{b % 2}", Kb[b % 2][:], cmk[b].rearrange("(m p) f -> p m f", p=128), writes=[B_Kb[b % 2]])

            def ld_v(b):
                P.dma("pool", f"vb{b % 2}", Vb[b % 2][:], cmv[b].rearrange("(m p) f -> p m f", p=128), writes=[B_Vb[b % 2]])
            ld_k(0)
            ld_v(0)
            for b in range(16):
                s2 = b % 2
                if b + 1 < 16:
                    ld_k(b + 1)
                for mt in range(2):
                    pb, Bp = tbank()
                    pv = pb[:].bitcast(BF16).rearrange("p (c t) -> p c t", c=8)
                    P.mm([(lambda e, c=c, pv=pv, mt=mt, s2=s2: e.transpose(out=pv[:, c, :], in_=Kb[s2][:, mt, c * 128:(c + 1) * 128], identity=ident[:]))
                          for c in range(8)], reads=[B_Kb[s2]] + CONST, writes=[Bp])
                    P.op("act" if mt == 0 else "dve",
                         (lambda e, pv=pv, mt=mt, s2=s2: e.activation(out=KbT[s2][:, :, mt * 128:(mt + 1) * 128], in_=pv, func=AF.Copy))
                         if mt == 0 else
                         (lambda e, pv=pv, mt=mt, s2=s2: e.tensor_copy(out=KbT[s2][:, :, mt * 128:(mt + 1) * 128], in_=pv)),
                         reads=[Bp], writes=[B_KbT[s2]])
                if b >= 2:
                    P.op("dve", lambda e, s2=s2, b=b: e.memset(qpad[s2][:, :, (b - 2) * 8:(b - 1) * 8], 0.0), writes=[B_qpad[s2]])
                P.op("dve", lambda e, s2=s2, b=b: e.tensor_copy(out=qpad[s2][:, :, b * 8:(b + 1) * 8], in_=qcT[:, :, b * 8:(b + 1) * 8]),
                     reads=[B_qcT], writes=[B_qpad[s2]])
                for hp, (pb, Bp) in enumerate(banks):
                    pv = pb[:].rearrange("p (h t) -> p h t", h=2)
                    fns = []
                    for hh in range(2):
                        h = 2 * hp + hh
                        for dc in range(2):
                            fns.append(lambda e, pv=pv, hh=hh, h=h, dc=dc, s2=s2, b=b: e.matmul(
                                pv[:, hh, :], lhsT=qpad[s2][:, 2 * h + dc, :], rhs=KbT[s2][:, 2 * h + dc, :],
                                start=(b == 0 and hh == 0 and dc == 0), stop=(b == 15 and dc == 1), skip_group_check=True))
                    P.mm(fns, reads=[B_qpad[s2], B_KbT[s2]], writes=[Bp])
            cross_softmax(banks)
            diag_T(4, 2, pexp, B_pexp, lambda h, mc: pTc[:, h, mc, :], B_pTc, [128, 128])
            pbs = [dbank(), dbank()]
            for b in range(16):
                s2 = b % 2
                if b + 1 < 16:
                    ld_v(b + 1)
                fns = []
                for c in range(8):
                    pv = pbs[c // 4][0][:].rearrange("p (c t) -> p c t", c=4)
                    h = c // 2
                    for mc in range(2):
                        fns.append(lambda e, pv=pv, c=c, h=h, mc=mc, s2=s2, b=b: e.matmul(
                            pv[:, c % 4, b * 8:(b + 1) * 8], lhsT=Vb[s2][:, mc, c * 128:(c + 1) * 128], rhs=pTc[:, h, mc, b * 8:(b + 1) * 8],
                            start=(mc == 0), stop=(mc == 1), skip_group_check=True))
                P.mm(fns, reads=[B_Vb[s2], B_pTc], writes=[pbs[0][1], pbs[1][1]])
            for half in range(2):
                pv = pbs[half][0][:].rearrange("p (c t) -> p c t", c=4)
                P.op("act" if half == 0 else "dve",
                     (lambda e, pv=pv, half=half: e.activation(out=ocT[:, half * 4:half * 4 + 4, 0:128], in_=pv, func=AF.Copy))
                     if half == 0 else
                     (lambda e, pv=pv, half=half: e.tensor_copy(out=ocT[:, half * 4:half * 4 + 4, 0:128], in_=pv)),
                     reads=[pbs[half][1]], writes=[B_ocT])
        dense("co", list(range(8)), NT, lambda k: ocT[:, k, :NT], B_ocT, resid_evac(NT))

        if STAGE == 2.5:
            raise _Stop()
        norm(xT, B_xT, 3, NT, hT, B_hT)
        uctr = [0]

        def up_evac(m, pb, Bp):
            r, Br = relu_t[uctr[0] % 2], B_relu[uctr[0] % 2]
            uctr[0] += 1
            P.op("act", lambda e: e.activation(out=r[:, :NT], in_=pb[:, :NT], func=AF.Relu), reads=[Bp], writes=[Br])
            P.op("dve", lambda e: e.tensor_tensor(out=hidT[:, m, :NT], in0=r[:, :NT], in1=r[:, :NT], op=ALU.mult), reads=[Br], writes=[B_hid])
        dense("up", list(range(32)), NT, lambda k: hT[:, k, :NT], B_hT, up_evac)
        dense("down", list(range(8)), NT, lambda k: hidT[:, k, :NT], B_hid, resid_evac(NT), kgroups=4)

        if STAGE == 2.6:
            raise _Stop()
        norm(xT, B_xT, 4, NT, yT, B_yT)
        for j in range(ntl):
            ys_, Bys = yst[j % 2], B_yst[j % 2]
            for hf in range(2):
                pb, Bp = dbank()
                pv = pb[:].rearrange("p (c t) -> p c t", c=4)
                P.mm([(lambda e, c=c, pv=pv, hf=hf, j=j: e.transpose(out=pv[:, c, :], in_=yT[:, hf * 4 + c, j * 128:(j + 1) * 128], identity=identf[:]))
                      for c in range(4)], reads=[B_yT] + CONST, writes=[Bp])
                P.op("act" if hf == 0 else "dve",
                     (lambda e, pb=pb, hf=hf, ys_=ys_: e.activation(out=ys_[:, hf * 512:(hf + 1) * 512], in_=pb[:, :], func=AF.Copy))
                     if hf == 0 else
                     (lambda e, pb=pb, hf=hf, ys_=ys_: e.tensor_copy(out=ys_[:, hf * 512:(hf + 1) * 512], in_=pb[:, :])),
                     reads=[Bp], writes=[Bys])
            if sample:
                P.dma("sp", f"yst{j % 2}", ys[:, :] if not dry else None, ys_[:], reads=[Bys])
            else:
                r0 = gi * NT_P + j * 128
                P.dma("sp", f"yst{j % 2}", yp[r0:r0 + 128, :] if not dry else None, ys_[:], reads=[Bys])

    if dry:
        pass
    try:
        if STAGE >= 1:
            group("H", 0)
        for gi in range(NG_P):
            if STAGE >= 2 + gi:
                group("P", gi)
        if STAGE >= 6:
            group("S", 0)
    except _Stop:
        pass

    return finish()


_CACHE = {}


def _build_nc():
    if "nc" in _CACHE:
        return _CACHE["nc"]
    nc0 = bass.Bass("TRN2", target_bir_lowering=False)
    with ExitStack() as es0:
        _, W0 = build_sched(nc0, es0)
    sched = W0.rec
    nc = bass.Bass("TRN2", target_bir_lowering=False)
    with ExitStack() as es:
        P, W = build(nc, es, False, sched)
        assert W.i == len(sched), (W.i, len(sched))
        block = es.enter_context(nc.Block())
        P.flush(block)
    _CACHE["nc"] = nc
    return nc


def build_sched(nc0, es0):
    return build(nc0, es0, False, None)


def _tables(half):
    slopes = 2.0 ** (-(np.arange(8) + 1.0))
    q = np.arange(128)[:, None]
    c = np.arange(256)[None, :]
    dist = q - c + 128
    valid = (dist >= 0) & (dist <= 128)
    biasg = np.empty((128, 8, 256), np.float32)
    for g in range(4):
        for kv in range(2):
            h = kv * 4 + g
            biasg[:, 2 * g + kv, :] = np.where(valid, -slopes[h] * dist, -1e30)
    biasf = biasg.copy()
    if half == 0:
        biasf[:, :, 0:128] = -1e30
    biass = np.full((128, 2, 160), -1e30, np.float32)
    for j in range(4):
        for g in range(4):
            for t in range(8):
                r = j * 32 + g * 8 + t
                for kv in range(2):
                    h = kv * 4 + g
                    cc = np.arange(128)
                    d = t + 128 - cc
                    biass[r, kv, 0:128] = np.where(cc >= t, -slopes[h] * d, -1e30)
                    for tp in range(t + 1):
                        biass[r, kv, 128 + j * 8 + tp] = -slopes[h] * (t - tp)
    invc = np.empty((128, 4, 16), np.float32)
    for g in range(4):
        w = 2 << g
        for p in range(16):
            invc[:, g, p] = 1.0 / (min(p + 1, w) if half == 0 else w)
    return biasg.reshape(128, -1), biasf.reshape(128, -1), biass.reshape(128, -1), invc.reshape(128, -1)


def _prep(x_prompt, x_sample, cache_win_k, cache_win_v, state_pool, cache_mem_k, cache_mem_v,
          mem_prompt, g_mix, w_in, attn_sinks, w_pool, pool_scale, w_out, g_cross, g_mem,
          w_cq, w_ck, w_cv, w_co, g_ffn, w_up, w_down, g_final):
    f = lambda a: np.ascontiguousarray(np.asarray(a, dtype=np.float32))
    x_prompt, x_sample = f(x_prompt), f(x_sample)
    shared = dict(w_in=f(w_in)[0], w_pool=f(w_pool)[0], w_out=f(w_out)[0], w_cq=f(w_cq)[0], w_ck=f(w_ck)[0],
                  w_cv=f(w_cv)[0], w_co=f(w_co)[0], w_up=f(w_up)[0], w_down=f(w_down)[0])
    gs = np.stack([f(g_mix)[0], f(g_cross)[0], f(g_mem)[0], f(g_ffn)[0], f(g_final)], 0)
    shared["gvec"] = np.ascontiguousarray(gs.reshape(5, 8, 128).transpose(2, 0, 1).reshape(128, 40))
    shared["pscale"] = np.ascontiguousarray(f(pool_scale)[0].reshape(4, 128).T)
    sk = f(attn_sinks)[0]
    sinkp = np.empty((128, 8), np.float32)
    for g in range(4):
        for kv in range(2):
            sinkp[:, 2 * g + kv] = sk[kv * 4 + g]
    shared["sinkp"] = sinkp
    sinks = np.empty((128, 8), np.float32)
    for r in range(128):
        g = (r % 32) // 8
        for i in range(4):
            sinks[r, 2 * i] = sk[g]
            sinks[r, 2 * i + 1] = sk[4 + g]
    shared["sinks"] = sinks
    ckf, cvf, spf = f(cache_win_k)[0], f(cache_win_v)[0], f(state_pool)[0]
    cmkf, cmvf, memf = f(cache_mem_k)[0], f(cache_mem_v)[0], f(mem_prompt)
    in_maps = []
    for c in range(NCORES):
        b, half = c // 2, c % 2
        s0 = half * SEQ_CORE
        xp = np.zeros((128 + SEQ_CORE, D), np.float32)
        xp[128:] = x_prompt[b, s0:s0 + SEQ_CORE]
        if half == 1:
            xp[:128] = x_prompt[b, s0 - 128:s0]
        biasg, biasf, biass, invc = _tables(half)
        sl = slice(16 * c, 16 * c + 16)
        m = dict(shared)
        m.update(xp=xp, xs=np.ascontiguousarray(x_sample[sl].reshape(128, D)), mem=np.ascontiguousarray(memf[b]),
                 ck=np.ascontiguousarray(ckf[sl].reshape(16, 128, 128)), cv=np.ascontiguousarray(cvf[sl].reshape(16, 128, 128)),
                 spool=np.ascontiguousarray(spf[sl]), cmk=np.ascontiguousarray(cmkf[sl].reshape(16, 256, D)),
                 cmv=np.ascontiguousarray(cmvf[sl].reshape(16, 256, D)),
                 biasg=biasg, biasf=biasf, biass=biass, invc=invc)
        in_maps.append(m)
    return in_maps


def kernel(**inputs):
    in_maps = _prep(**inputs)
    nc = _build_nc()
    res = run_bass_kernel_spmd(nc, in_maps, core_ids=list(range(NCORES))).results
    return _assemble(res)


def _assemble(res):
    B, S = 4, 4096
    y_prompt = np.empty((B, S, D), np.float32)
    y_sample = np.empty((128, 8, D), np.float32)
    wk_p = np.empty((1, B, 128, 2, 64), np.float32); wv_p = np.empty_like(wk_p)
    pool_p = np.empty((1, B, 15, 512), np.float32)
    mk_p = np.empty((1, B, 256, 4, 256), np.float32); mv_p = np.empty_like(mk_p)
    wk_s = np.empty((1, 128, 128, 2, 64), np.float32); wv_s = np.empty_like(wk_s)
    pool_s = np.empty((1, 128, 15, 512), np.float32)
    for c in range(NCORES):
        r = res[c]
        b, half = c // 2, c % 2
        y_prompt[b, half * SEQ_CORE:(half + 1) * SEQ_CORE] = r["yp"]
        sl = slice(16 * c, 16 * c + 16)
        y_sample[sl] = r["ys"].reshape(16, 8, D)
        if half == 1:
            wk_p[0, b] = r["wkp"].reshape(128, 2, 64)
            wv_p[0, b] = r["wvp"].reshape(128, 2, 64)
            pool_p[0, b] = r["poolp"]
        else:
            mk_p[0, b] = r["memk"].reshape(256, 4, 256)
            mv_p[0, b] = r["memv"].reshape(256, 4, 256)
        wk_s[0, sl] = r["wks"].reshape(16, 128, 2, 64)
        wv_s[0, sl] = r["wvs"].reshape(16, 128, 2, 64)
        pool_s[0, sl] = r["pools"]
    return (y_prompt, y_sample, wk_p, wv_p, pool_p, mk_p, mv_p, wk_s, wv_s, pool_s)
```

```python
import numpy as np
from contextlib import ExitStack
import concourse.bass as bass
import concourse.mybir as mybir
from concourse.bass_utils import run_bass_kernel_spmd

F32 = mybir.dt.float32
BF16 = mybir.dt.bfloat16
ALU = mybir.AluOpType
AF = mybir.ActivationFunctionType
AX = mybir.AxisListType

NCORES = 8
STAGE = 99
D = 1024
SEQ_CORE = 2048
NT_P = 512
NG_P = SEQ_CORE // NT_P
RING = 10
EPS = 1e-5


class Buf:
    __slots__ = ("name", "w", "r", "al", "excl")

    def __init__(self, name, excl=False):
        self.name = name
        self.w = None
        self.r = {}
        self.al = []
        self.excl = excl


def alias(*bufs):
    for a in bufs:
        for b in bufs:
            if a is not b and b not in a.al:
                a.al.append(b)


class Prog:
    def __init__(self, nc, es, dry):
        self.nc, self.es, self.dry = nc, es, dry
        self.q = {e: [] for e in ("pe", "act", "dve", "pool", "sp")}
        self.cnt, self.sems = {}, {}
        self.waited = {e: {} for e in self.q}

    def sem(self, key):
        if key not in self.sems:
            self.sems[key] = None if self.dry else self.es.enter_context(self.nc.semaphore(key))
            self.cnt[key] = 0

    def _wait(self, eng, tok):
        if tok is None:
            return
        key, val = tok
        if self.waited[eng].get(key, 0) >= val:
            return
        self.waited[eng][key] = val
        self.q[eng].append(("w", key, val))

    def _deps(self, eng, reads, writes, extra):
        for b in reads:
            self._wait(eng, b.w)
            if b.excl:
                for k, v in b.r.items():
                    if k != eng:
                        self._wait(eng, (k, v))
        for b in writes:
            for bb in [b] + b.al:
                self._wait(eng, bb.w)
                for k, v in bb.r.items():
                    self._wait(eng, (k, v))
        for t in extra:
            self._wait(eng, t)

    def _commit(self, tok, reads, writes):
        k, v = tok
        for b in reads:
            b.r[k] = max(b.r.get(k, 0), v)
        for b in writes:
            b.w = tok
            b.r = {}

    def op(self, eng, fn, reads=(), writes=(), extra=()):
        self._deps(eng, reads, writes, extra)
        self.sem(eng)
        self.cnt[eng] += 1
        tok = (eng, self.cnt[eng])
        self.q[eng].append(("i", fn, eng, 1))
        self._commit(tok, reads, writes)
        return tok

    def mm(self, fns, reads=(), writes=(), extra=()):
        self._deps("pe", reads, writes, extra)
        for f in fns[:-1]:
            self.q["pe"].append(("i", f, None, 0))
        self.sem("pe")
        self.cnt["pe"] += 1
        tok = ("pe", self.cnt["pe"])
        self.q["pe"].append(("i", fns[-1], "pe", 1))
        self._commit(tok, reads, writes)
        return tok

    def dma(self, qeng, semkey, out, in_, reads=(), writes=(), extra=()):
        if writes:
            semkey = "dw_" + writes[0].name
        elif reads:
            semkey = "dr_" + reads[0].name
        for b in reads:
            self._wait(qeng, b.w)
        for b in writes:
            for bb in [b] + b.al:
                if not (bb.w is not None and bb.w[0] == semkey):
                    self._wait(qeng, bb.w)
                for k, v in bb.r.items():
                    self._wait(qeng, (k, v))
        for t in extra:
            self._wait(qeng, t)
        self.sem(semkey)
        self.cnt[semkey] += 16
        tok = (semkey, self.cnt[semkey])
        self.q[qeng].append(("i", (lambda e, o=out, i=in_: e.dma_start(out=o, in_=i)), semkey, 16))
        self._commit(tok, reads, writes)
        return tok

    def flush(self, block):
        def run(name):
            def f(e):
                for it in self.q[name]:
                    if it[0] == "w":
                        e.wait_ge(self.sems[it[1]], it[2])
                    else:
                        ins = it[1](e)
                        if it[3]:
                            ins.then_inc(self.sems[it[2]], it[3])
            return f
        block.tensor(run("pe"))
        block.scalar(run("act"))
        block.vector(run("dve"))
        block.gpsimd(run("pool"))
        block.sync(run("sp"))


class WStream:
    def __init__(self, P, ring_ap, sched):
        self.P, self.ring = P, ring_ap
        self.sched = sched
        self.rec = []
        self.i = 0
        self.issued = 0
        self.slots = [Buf(f"ws{i}") for i in range(RING)]
        self.src = {}

    def _issue(self, j):
        name, m = self.sched[j]
        s = j % RING
        for (dst_fn, src_ap) in self.src[name](m):
            self.P.dma("pool", f"ws{s}", dst_fn(self.ring[:, s]), src_ap, writes=[self.slots[s]])

    def get(self, name, m):
        if self.sched is None:
            self.rec.append((name, m))
            return self.ring[:, 0], self.slots[0]
        assert self.sched[self.i] == (name, m), (self.i, self.sched[self.i], name, m)
        while self.issued < min(len(self.sched), self.i + RING - 3):
            self._issue(self.issued)
            self.issued += 1
        s = self.i % RING
        self.i += 1
        return self.ring[:, s], self.slots[s]


def build(nc, es, dry, sched):
    P = Prog(nc, es, dry)

    def din(name, shape):
        return nc.dram_tensor(name, list(shape), F32, kind="ExternalInput").ap()

    def dout(name, shape):
        return nc.dram_tensor(name, list(shape), F32, kind="ExternalOutput").ap()

    if not dry:
        xp = din("xp", [128 + SEQ_CORE, D]); xs = din("xs", [128, D]); mem = din("mem", [256, D])
        ck = din("ck", [16, 128, 128]); cv = din("cv", [16, 128, 128]); spool = din("spool", [16, 15, 512])
        cmk = din("cmk", [16, 256, D]); cmv = din("cmv", [16, 256, D])
        w_in = din("w_in", [D, 1280]); w_pool = din("w_pool", [4, 128, 128]); w_out = din("w_out", [D, D])
        w_cq = din("w_cq", [D, D]); w_ck = din("w_ck", [D, D]); w_cv = din("w_cv", [D, D]); w_co = din("w_co", [D, D])
        w_up = din("w_up", [D, 4 * D]); w_down = din("w_down", [4 * D, D])
        gvec_d = din("gvec", [128, 40]); pscale_d = din("pscale", [128, 4])
        sinkp_d = din("sinkp", [128, 8]); sinks_d = din("sinks", [128, 8])
        biasg_d = din("biasg", [128, 8 * 256]); biasf_d = din("biasf", [128, 8 * 256]); biass_d = din("biass", [128, 2 * 160])
        invc_d = din("invc", [128, 64])
        yp = dout("yp", [SEQ_CORE, D]); ys = dout("ys", [128, D])
        wkp = dout("wkp", [128, 128]); wvp = dout("wvp", [128, 128]); poolp = dout("poolp", [15, 512])
        memk_o = dout("memk", [256, D]); memv_o = dout("memv", [256, D])
        wks = dout("wks", [16, 128, 128]); wvs = dout("wvs", [16, 128, 128]); pools = dout("pools", [16, 15, 512])

    def sb(name, shape, dt):
        return es.enter_context(nc.sbuf_tensor("sb_" + name, list(shape), dt))

    xTs = [sb(f"xT{i}", [128, 8, NT_P], F32) for i in range(2)]; B_xTs = [Buf(f"xT{i}") for i in range(2)]
    xT, B_xT = xTs[0], B_xTs[0]
    hT = sb("hT", [128, 8, NT_P], BF16); B_hT = Buf("hT")
    rstd = sb("rstd", [128, NT_P], F32); B_rstd = Buf("rstd")
    aoT = sb("aoT", [128, 8, NT_P], BF16); B_aoT = Buf("aoT")
    qT = sb("qT", [128, 4, NT_P], BF16); B_qT = Buf("qT")
    kT = sb("kT", [128, 128 + NT_P], BF16); B_kT = Buf("kT")
    vT = sb("vT", [128, NT_P], BF16); B_vT = Buf("vT")
    vtok = sb("vtok", [128, 5, 128], BF16); B_vtok = Buf("vtok")
    kv32 = sb("kv32", [128, 2, 128], F32); B_kv32 = Buf("kv32")
    dT = sb("dT", [128, 4, NT_P], BF16); B_dT = Buf("dT")
    pexp = sb("pexp", [128, 8, 256], BF16); B_pexp = Buf("pexp")
    pT = sb("pT", [128, 8, 2, 128], BF16); B_pT = Buf("pT")
    Dg = sb("Dg", [128, 8, 128], BF16); B_Dg = Buf("Dg")
    pTc = sb("pTc", [128, 4, 2, 128], BF16); B_pTc = Buf("pTc")
    bias = sb("bias", [128, 8, 256], F32); B_bias = Buf("bias")
    biass = sb("biass", [128, 2, 160], F32); B_biass = Buf("biass")
    memkT = sb("memkT", [128, 8, 256], BF16); B_memkT = Buf("memkT")
    memv = sb("memv", [128, 2, D], BF16); B_memv = Buf("memv")
    ring = sb("ring", [128, RING, 8, 128], BF16)
    xin = [sb(f"xin{i}", [128, D], F32) for i in range(2)]; B_xin = [Buf(f"xin{i}") for i in range(2)]
    yst, B_yst = xin, B_xin
    ident = sb("ident", [128, 128], BF16); identf = sb("identf", [128, 128], F32); B_const = Buf("const")
    ones = sb("ones", [128, 128], BF16)
    gvec = sb("gvec", [128, 5, 8], F32); pscale = sb("pscale", [128, 4], F32)
    sinkp = sb("sinkp", [128, 8], F32); sinks = sb("sinks", [128, 8], F32)
    invc = sb("invc", [128, 4, 16], F32)
    wpool = sb("wpool", [128, 4, 128], BF16); B_wpool = Buf("wpool")
    st = sb("st", [128, 64], F32); B_st = Buf("st")
    relu_t = [sb(f"relu{i}", [128, NT_P], BF16) for i in range(2)]; B_relu = [Buf(f"relu{i}") for i in range(2)]
    ost = sb("ost", [128, 512], F32); B_ost = Buf("ost")
    carryU = sb("carryU", [128, 4, 16], F32); B_cU = Buf("carryU")

    R2 = 32 * NT_P * 2
    XO = R2 + 28672
    AR = XO + 10240
    arena = sb("arena", [128, AR // 2], BF16)

    def av(off, nbytes, dt, pat=None, **kw):
        v = arena[:, off // 2:(off + nbytes) // 2]
        if dt is F32:
            v = v.bitcast(F32)
        if pat:
            v = v.rearrange(pat, **kw)
        return v

    hidT = av(0, 32 * NT_P * 2, BF16, "p (k t) -> p k t", k=32); B_hid = Buf("hidT")
    yT = av(0, 8 * NT_P * 4, F32, "p (k t) -> p k t", k=8); B_yT = Buf("yT")
    WU = 16 + NT_P
    WH = 16 + 256
    U = av(R2, 4 * WU * 4, F32, "p (g t) -> p g t", g=4); B_U = Buf("U")
    SA = av(R2 + 4 * WU * 4, 4 * WH * 4, F32, "p (g t) -> p g t", g=4); B_SA = Buf("SA")
    SB = av(R2 + 4 * WU * 4 + 4 * WH * 4, 4 * WH * 4, F32, "p (g t) -> p g t", g=4); B_SB = Buf("SB")
    o_sb = R2 + 4 * WU * 4 + 8 * WH * 4
    sbias = av(o_sb, 8 * 256 * 4, F32, "p (u t) -> p u t", u=8); B_sbias = Buf("sbias")
    assert o_sb + 8192 <= XO
    kcT = av(XO, 16 * 128 * 2, BF16, "p (b t) -> p b t", b=16); B_kcT = Buf("kcT")
    vc = av(XO + 4096, 16 * 128 * 2, BF16, "p (b t) -> p b t", b=16); B_vc = Buf("vc")
    qs2 = av(XO + 8192, 16 * 32 * 2, BF16, "p (b t) -> p b t", b=16); B_qs2 = Buf("qs2")
    vnq = av(XO + 9216, 4 * 128 * 2, BF16, "p (i t) -> p i t", i=4); B_vnq = Buf("vnq")
    Kb = [av(R2 + i * 4096, 4096, BF16, "p (m t) -> p m t", m=2) for i in range(2)]; B_Kb = [Buf(f"Kb{i}") for i in range(2)]
    KbT = [av(R2 + 8192 + i * 4096, 4096, BF16, "p (c t) -> p c t", c=8) for i in range(2)]; B_KbT = [Buf(f"KbT{i}") for i in range(2)]
    Vb = [av(R2 + 16384 + i * 4096, 4096, BF16, "p (m t) -> p m t", m=2) for i in range(2)]; B_Vb = [Buf(f"Vb{i}") for i in range(2)]
    qpad = [av(R2 + 24576 + i * 2048, 2048, BF16, "p (c t) -> p c t", c=8) for i in range(2)]; B_qpad = [Buf(f"qpad{i}") for i in range(2)]
    memst = av(R2, 8192, F32, "p (m t) -> p m t", m=2); B_memst = Buf("memst")
    mkst = av(R2 + 8192, 8192, F32, "p (m t) -> p m t", m=2); B_mkst = Buf("mkst")
    alias(B_hid, B_yT)
    gX = [B_U, B_SA, B_SB, B_sbias]
    gY = B_Kb + B_KbT + B_Vb + B_qpad
    gZ = [B_memst, B_mkst]
    for ga, gb in ((gX, gY), (gX, gZ), (gY, gZ)):
        for a in ga:
            for b in gb:
                a.al.append(b)
                b.al.append(a)

    ps = [es.enter_context(nc.psum_tensor(f"ps{i}", [128, 512], F32)) for i in range(8)]
    B_ps = [Buf(f"ps{i}", excl=True) for i in range(8)]
    dctr = [0]

    def dbank():
        i = dctr[0] % 3
        dctr[0] += 1
        return ps[i], B_ps[i]
    PS_S = [3, 4]
    PS_T = [5, 6]
    PS_O = 7
    sctr = [0]
    tctr = [0]

    def sbank():
        i = PS_S[sctr[0] % 2]; sctr[0] += 1
        return ps[i], B_ps[i]

    def tbank():
        i = PS_T[tctr[0] % 2]; tctr[0] += 1
        return ps[i], B_ps[i]

    W = WStream(P, ring, sched)
    if not dry:
        def std_src(wap):
            v = wap.rearrange("(k p) (m c) -> p m k c", p=128, c=128)
            return lambda m: [((lambda s: s), v[:, m])]
        W.src["ck"] = std_src(w_ck); W.src["cv"] = std_src(w_cv)
        W.src["cq"] = std_src(w_cq); W.src["co"] = std_src(w_co); W.src["up"] = std_src(w_up)
        vin_q = w_in[:, 0:512].rearrange("(k p) (kv g d) -> p g k kv d", p=128, kv=2, g=4, d=64)
        vin_r = w_in[:, 512:1280].rearrange("(k p) (m c) -> p m k c", p=128, c=128)

        def in_src(m):
            if m < 4:
                return [((lambda s: s[:, :, 0:64]), vin_q[:, m, :, 0, :]),
                        ((lambda s: s[:, :, 64:128]), vin_q[:, m, :, 1, :])]
            return [((lambda s: s), vin_r[:, m - 4])]
        W.src["in"] = in_src
        vo_a = w_out[0:512, :].rearrange("(kv g d) (m c) -> kv d m g c", kv=2, g=4, d=64, c=128)
        vo_p = w_out[512:1024, :].rearrange("(k p) (m c) -> p m k c", p=128, c=128)

        def out_src(m):
            return [((lambda s: s[0:64, 0:4, :]), vo_a[0, :, m]),
                    ((lambda s: s[64:128, 0:4, :]), vo_a[1, :, m]),
                    ((lambda s: s[:, 4:8, :]), vo_p[:, m])]
        W.src["out"] = out_src
        vdn = w_down.rearrange("(q k p) (m c) -> p m q k c", p=128, k=8, c=128)
        W.src["down"] = lambda mq: [((lambda s: s), vdn[:, mq // 4, mq % 4])]

    if not dry:
        P.op("pool", lambda e: e.memset(identf[:], 0.0), writes=[B_const])
        P.op("pool", lambda e: e.iota(identf[:], pattern=[[1, 128]], base=0, channel_multiplier=-1,
                                      allow_small_or_imprecise_dtypes=True), writes=[B_const])
        P.op("dve", lambda e: e.tensor_single_scalar(out=ident[:], in_=identf[:], scalar=0.0, op=ALU.is_equal),
             reads=[B_const], writes=[B_const])
        P.op("dve", lambda e: e.tensor_single_scalar(out=identf[:], in_=identf[:], scalar=0.0, op=ALU.is_equal),
             writes=[B_const])
        P.op("dve", lambda e: e.memset(ones[:], 1.0), writes=[B_const])
        for (dst, src) in ((gvec[:].rearrange("p a b -> p (a b)"), gvec_d), (pscale[:], pscale_d), (sinkp[:], sinkp_d),
                           (sinks[:], sinks_d), (invc[:].rearrange("p a b -> p (a b)"), invc_d),
                           (biass[:].rearrange("p a b -> p (a b)"), biass_d)):
            P.dma("sp", "cst", dst, src[:, :], writes=[B_const])
        P.dma("sp", "biasld", bias[:].rearrange("p a b -> p (a b)"), biasf_d[:, :], writes=[B_bias])
        P.dma("pool", "wpool", wpool[:], w_pool.rearrange("g c e -> c g e"), writes=[B_wpool])

    CONST = [B_const]

    class _Stop(Exception):
        pass

    def finish():
        for key, val in P.cnt.items():
            if key not in ("pe", "act", "dve", "pool"):
                P._wait("sp", (key, val))
        for e_ in ("pe", "act", "dve", "pool"):
            if P.cnt.get(e_, 0):
                P._wait("sp", (e_, P.cnt[e_]))
        return P, W
    if STAGE == -1:
        return finish()

    def run(gen):
        if gen is not None:
            for _ in gen:
                pass

    def advance(gen, n=1, until=None):
        if gen is None:
            return
        if until is not None:
            for v in gen:
                if v == until:
                    return
            return
        for _ in range(n):
            try:
                next(gen)
            except StopIteration:
                return

    def g_load_x(src_rows, ntiles, X):
        dst, dstB = xTs[X], B_xTs[X]
        for j in range(ntiles):
            xb, Bx = xin[j % 2], B_xin[j % 2]
            P.dma("sp", "x", xb[:], src_rows(j), writes=[Bx])
            for hf in range(2):
                pb, Bp = dbank()
                pv = pb[:].rearrange("p (c t) -> p c t", c=4)
                P.mm([(lambda e, c=c, pv=pv, xb=xb, hf=hf: e.transpose(out=pv[:, c, :], in_=xb[:, (hf * 4 + c) * 128:(hf * 4 + c + 1) * 128],
                                                                       identity=identf[:])) for c in range(4)],
                     reads=[Bx] + CONST, writes=[Bp])
                P.op("act" if hf == 0 else "dve",
                     (lambda e, pv=pv, hf=hf, j=j: e.activation(out=dst[:, hf * 4:hf * 4 + 4, j * 128:(j + 1) * 128], in_=pv, func=AF.Copy))
                     if hf == 0 else
                     (lambda e, pv=pv, hf=hf, j=j: e.tensor_copy(out=dst[:, hf * 4:hf * 4 + 4, j * 128:(j + 1) * 128], in_=pv)),
                     reads=[Bp], writes=[dstB])
                yield

    def norm(src, Bsrc, gi, NT, dst, Bdst):
        P.op("act", lambda e: e.activation(out=hT[:, :, :NT], in_=src[:, :, :NT], func=AF.Square),
             reads=[Bsrc], writes=[B_hT])
        pb, Bp = dbank()
        P.mm([(lambda e, k=k: e.matmul(pb[:, :NT], lhsT=ones[:], rhs=hT[:, k, :NT], start=(k == 0), stop=(k == 7)))
              for k in range(8)], reads=[B_hT] + CONST, writes=[Bp])
        P.op("act", lambda e: e.activation(out=rstd[:, :NT], in_=pb[:, :NT], func=AF.Ln, scale=1.0 / D, bias=EPS),
             reads=[Bp], writes=[B_rstd])
        P.op("act", lambda e: e.activation(out=rstd[:, :NT], in_=rstd[:, :NT], func=AF.Exp, scale=-0.5),
             reads=[B_rstd], writes=[B_rstd])
        for k in range(8):
            P.op("dve", lambda e, k=k: e.scalar_tensor_tensor(out=dst[:, k, :NT], in0=src[:, k, :NT], scalar=gvec[:, gi, k:k + 1],
                                                              in1=rstd[:, :NT], op0=ALU.mult, op1=ALU.mult),
                 reads=[Bsrc, B_rstd] + CONST, writes=[Bdst])

    def g_dense(wname, units, NT, rhs_fn, Brhs, evac, kgroups=1):
        for m in units:
            pb, Bp = dbank()
            fns, Bs = [], []
            for q in range(kgroups):
                slot, Bslot = W.get(wname, m * kgroups + q if kgroups > 1 else m)
                Bs.append(Bslot)
                for k in range(8):
                    fns.append(lambda e, slot=slot, k=k, q=q, pb=pb: e.matmul(
                        pb[:, :NT], lhsT=slot[:, k, :], rhs=rhs_fn(q * 8 + k),
                        start=(q == 0 and k == 0), stop=(q == kgroups - 1 and k == 7)))
            P.mm(fns, reads=Bs + [Brhs], writes=[Bp])
            evac(m, pb, Bp)
            for _ in range(kgroups):
                yield

    def dense(*a, **kw):
        run(g_dense(*a, **kw))

    def resid_evac(NT, X):
        xt, Bxt = xTs[X], B_xTs[X]

        def f(m, pb, Bp):
            P.op("dve", lambda e: e.tensor_tensor(out=xt[:, m, :NT], in0=pb[:, :NT], in1=xt[:, m, :NT], op=ALU.add),
                 reads=[Bp], writes=[Bxt])
        return f

    def diag_T(nu, nkc, p_src, Bp_src, dst4, dst_fn, Bdst, kw):
        items = [(u, kc) for u in range(nu) for kc in range(nkc)]
        full = all(w == 128 for w in kw)
        for bi, i0 in enumerate(range(0, len(items), 4)):
            chunk = items[i0:i0 + 4]
            pb, Bp = tbank()
            pv = pb[:].rearrange("p (s t) -> p s t", s=4)
            P.mm([(lambda e, s=s, u=u, kc=kc, pv=pv: e.matmul(pv[0:kw[kc], s, :], lhsT=p_src[:, u, kc * 128:kc * 128 + kw[kc]],
                                                              rhs=Dg[:, u, :], start=True, stop=True))
                  for s, (u, kc) in enumerate(chunk)], reads=[Bp_src, B_Dg], writes=[Bp])
            if full:
                u0 = chunk[0][0]
                if bi % 2 == 0:
                    P.op("act", lambda e, pb=pb, u0=u0: e.activation(out=dst4(u0), in_=pb[:, 0:512], func=AF.Copy), reads=[Bp], writes=[Bdst])
                else:
                    P.op("dve", lambda e, pb=pb, u0=u0: e.tensor_copy(out=dst4(u0), in_=pb[:, 0:512]), reads=[Bp], writes=[Bdst])
            else:
                for s, (u, kc) in enumerate(chunk):
                    P.op("act" if bi % 2 == 0 else "dve",
                         (lambda e, s=s, u=u, kc=kc, pv=pv: e.activation(out=dst_fn(u, kc), in_=pv[0:kw[kc], s, :], func=AF.Copy))
                         if bi % 2 == 0 else
                         (lambda e, s=s, u=u, kc=kc, pv=pv: e.tensor_copy(out=dst_fn(u, kc), in_=pv[0:kw[kc], s, :])),
                         reads=[Bp], writes=[Bdst])
            yield

    def make_Dg(nu):
        P.op("dve", lambda e: e.tensor_tensor(out=Dg[:, 0:nu, :], in0=ident[:].unsqueeze(1).to_broadcast([128, nu, 128]),
                                              in1=st[:, 56:56 + nu].unsqueeze(2).to_broadcast([128, nu, 128]), op=ALU.mult),
             reads=[B_st] + CONST, writes=[B_Dg])

    def win_softmax(nu, width, sink_ap):
        P.op("dve", lambda e: e.tensor_reduce(out=st[:, 0:nu], in_=sbias[:, 0:nu, 0:width], axis=AX.X, op=ALU.max),
             reads=[B_sbias], writes=[B_st])
        P.op("dve", lambda e: e.tensor_tensor(out=st[:, 8:8 + nu], in0=st[:, 0:nu], in1=sink_ap, op=ALU.max),
             reads=[B_st] + CONST, writes=[B_st])
        P.op("dve", lambda e: e.tensor_scalar(out=st[:, 16:16 + nu], in0=st[:, 8:8 + nu], scalar1=-1.0, scalar2=None, op0=ALU.mult),
             reads=[B_st], writes=[B_st])
        P.op("dve", lambda e: e.tensor_tensor(out=st[:, 24:24 + nu], in0=sink_ap, in1=st[:, 16:16 + nu], op=ALU.add),
             reads=[B_st] + CONST, writes=[B_st])
        for u in range(nu):
            P.op("act", lambda e, u=u: e.activation(out=pexp[:, u, 0:width], in_=sbias[:, u, 0:width], func=AF.Exp,
                                                    bias=st[:, 16 + u:17 + u], scale=1.0, accum_out=st[:, 32 + u:33 + u]),
                 reads=[B_sbias, B_st], writes=[B_pexp, B_st])
        P.op("act", lambda e: e.activation(out=st[:, 40:40 + nu], in_=st[:, 24:24 + nu], func=AF.Exp),
             reads=[B_st], writes=[B_st])
        P.op("dve", lambda e: e.tensor_tensor(out=st[:, 48:48 + nu], in0=st[:, 32:32 + nu], in1=st[:, 40:40 + nu], op=ALU.add),
             reads=[B_st], writes=[B_st])
        P.op("dve", lambda e: e.reciprocal(out=st[:, 56:56 + nu], in_=st[:, 48:48 + nu]), reads=[B_st], writes=[B_st])
        make_Dg(nu)

    def cross_softmax(score_banks):
        for hp, (pb, Bp) in enumerate(score_banks):
            pv = pb[:].rearrange("p (h t) -> p h t", h=2)
            P.op("dve", lambda e, pv=pv, hp=hp: e.tensor_reduce(out=st[:, 2 * hp:2 * hp + 2], in_=pv, axis=AX.X, op=ALU.max),
                 reads=[Bp], writes=[B_st])
        P.op("dve", lambda e: e.tensor_scalar(out=st[:, 16:20], in0=st[:, 0:4], scalar1=-1.0 / 16.0, scalar2=None, op0=ALU.mult),
             reads=[B_st], writes=[B_st])
        for hp, (pb, Bp) in enumerate(score_banks):
            pv = pb[:].rearrange("p (h t) -> p h t", h=2)
            for hh in range(2):
                h = 2 * hp + hh
                P.op("act", lambda e, pv=pv, hh=hh, h=h: e.activation(out=pexp[:, h, :], in_=pv[:, hh, :], func=AF.Exp,
                                                                      bias=st[:, 16 + h:17 + h], scale=1.0 / 16.0,
                                                                      accum_out=st[:, 32 + h:33 + h]),
                     reads=[Bp, B_st], writes=[B_pexp, B_st])
        P.op("dve", lambda e: e.reciprocal(out=st[:, 56:60], in_=st[:, 32:36]), reads=[B_st], writes=[B_st])
        make_Dg(4)

    def out_tok_major(srcs, Bsrcs, ncols_each, dst_dma):
        pb, Bp = dbank()
        pv = pb[:].rearrange("p (c t) -> p c t", c=4)
        n = len(srcs)
        P.mm([(lambda e, i=i: e.transpose(out=pv[:, i, :], in_=srcs[i], identity=identf[:])) for i in range(n)],
             reads=list(Bsrcs) + CONST, writes=[Bp])
        P.op("dve", lambda e: e.tensor_copy(out=ost[:, 0:n * 128], in_=pb[:, 0:n * 128]), reads=[Bp], writes=[B_ost])
        dst_dma()

    for t in range(2):
        P.dma("sp", "memld", memst[:, t, :], mem[t * 128:(t + 1) * 128, :] if not dry else None, writes=[B_memst])
    for t in range(2):
        for hf in range(2):
            pb, Bp = dbank()
            pv = pb[:].rearrange("p (c t) -> p c t", c=4)
            P.mm([(lambda e, c=c, pv=pv, t=t, hf=hf: e.transpose(out=pv[:, c, :], in_=memst[:, t, (hf * 4 + c) * 128:(hf * 4 + c + 1) * 128],
                                                                 identity=identf[:])) for c in range(4)],
                 reads=[B_memst] + CONST, writes=[Bp])
            P.op("act", lambda e, pv=pv, hf=hf, t=t: e.activation(out=xT[:, hf * 4:hf * 4 + 4, t * 128:(t + 1) * 128], in_=pv, func=AF.Copy),
                 reads=[Bp], writes=[B_xT])
    if STAGE == -2:
        return finish()
    norm(xT, B_xT, 2, 256, hT, B_hT)
    if STAGE == -3:
        return finish()
    for (wn, is_k) in (("ck", True), ("cv", False)):
        for mh in range(2):
            yb = [sbank(), sbank()]
            for m4 in range(4):
                m = mh * 4 + m4
                slot, Bslot = W.get(wn, m)
                if is_k:
                    pb, Bp = dbank()
                    P.mm([(lambda e, k=k, slot=slot, pb=pb: e.matmul(pb[:, :256], lhsT=slot[:, k, :], rhs=hT[:, k, :256],
                                                                    start=(k == 0), stop=(k == 7))) for k in range(8)],
                         reads=[Bslot, B_hT], writes=[Bp])
                    P.op("act", lambda e, m=m, pb=pb: e.activation(out=memkT[:, m, :], in_=pb[:, :256], func=AF.Copy),
                         reads=[Bp], writes=[B_memkT])
                for t in range(2):
                    yp_, Byp = yb[t]
                    P.mm([(lambda e, k=k, slot=slot, yp_=yp_, t=t, m4=m4: e.matmul(
                        yp_[:, m4 * 128:(m4 + 1) * 128], lhsT=hT[:, k, t * 128:(t + 1) * 128], rhs=slot[:, k, :],
                        start=(k == 0), stop=(k == 7))) for k in range(8)],
                        reads=[Bslot, B_hT], writes=[Byp])
            for t in range(2):
                yp_, Byp = yb[t]
                P.op("dve", lambda e, yp_=yp_, t=t, mh=mh: e.tensor_copy(out=mkst[:, t, mh * 512:(mh + 1) * 512], in_=yp_[:, :]),
                     reads=[Byp], writes=[B_mkst])
                if not is_k:
                    P.op("act", lambda e, yp_=yp_, t=t, mh=mh: e.activation(out=memv[:, t, mh * 512:(mh + 1) * 512], in_=yp_[:, :], func=AF.Copy),
                         reads=[Byp], writes=[B_memv])
        for t in range(2):
            P.dma("sp", "memout", (memk_o if is_k else memv_o)[t * 128:(t + 1) * 128, :] if not dry else None, mkst[:, t, :],
                  reads=[B_mkst])

    def geom(kind):
        sample, halo = (kind == "S"), (kind == "H")
        NT = 128 if (sample or halo) else NT_P
        return sample, halo, NT, NT // 128

    def early(kind, gi, X):
        sample, halo, NT, ntl = geom(kind)
        xt, Bxt = xTs[X], B_xTs[X]
        if sample:
            yield from g_load_x(lambda j: xs[:, :], 1, X)
        elif halo:
            yield from g_load_x(lambda j: xp[0:128, :], 1, X)
        else:
            yield from g_load_x(lambda j: xp[128 + gi * NT_P + j * 128: 128 + gi * NT_P + (j + 1) * 128, :], ntl, X)
        yield "L"
        norm(xt, Bxt, 0, NT, hT, B_hT)
        last = (kind == "P" and gi == NG_P - 1)
        want32 = last or sample
        Us = U[:, :, 0:384].rearrange("p g (b c) -> p g b c", b=16)

        if sample:
            for hb in range(2):
                P.dma("pool", "ckld", vc[:, hb * 8:(hb + 1) * 8, :], cv[hb * 8:(hb + 1) * 8].rearrange("b s f -> s b f"), writes=[B_vc])
            kst = pexp[:].rearrange("p u t -> p (u t)").rearrange("p (b f) -> p b f", b=16)
            for hb in range(2):
                P.dma("pool", "ckld", kst[:, hb * 8:(hb + 1) * 8, :], ck[hb * 8:(hb + 1) * 8].rearrange("b s f -> s b f"), writes=[B_pexp])
            for hb in range(2):
                pb, Bp = tbank()
                pv = pb[:].bitcast(BF16).rearrange("p (b t) -> p b t", b=8)
                P.mm([(lambda e, b=b, pv=pv, hb=hb: e.transpose(out=pv[:, b, :], in_=kst[:, hb * 8 + b, :], identity=ident[:])) for b in range(8)],
                     reads=[B_pexp] + CONST, writes=[Bp])
                P.op("dve", lambda e, pv=pv, hb=hb: e.tensor_copy(out=kcT[:, hb * 8:(hb + 1) * 8, :], in_=pv), reads=[Bp], writes=[B_kcT])
            P.op("dve", lambda e: e.memset(U[:, :, 0:384], 0.0), writes=[B_U])
            for hb in range(2):
                P.dma("sp", "spld", xin[hb][0:120, 0:512], spool[hb * 8:(hb + 1) * 8].rearrange("b r f -> (b r) f"), writes=[B_xin[hb]])
                pb, Bp = dbank()
                pv = pb[:].rearrange("p (c t) -> p c t", c=4)
                P.mm([(lambda e, c=c, pv=pv, hb=hb: e.transpose(out=pv[:, c, 0:120], in_=xin[hb][0:120, c * 128:(c + 1) * 128],
                                                                identity=identf[0:120, 0:120])) for c in range(4)],
                     reads=[B_xin[hb]] + CONST, writes=[Bp])
                for c in range(4):
                    P.op("dve", lambda e, c=c, pv=pv, hb=hb: e.tensor_copy(
                        out=Us[:, c, hb * 8:(hb + 1) * 8, 1:16], in_=pv[:, c, 0:120].rearrange("p (b r) -> p b r", b=8)),
                        reads=[Bp], writes=[B_U])
            yield

        def in_evac(m, pb, Bp):
            if m < 4:
                P.op("act", lambda e: e.activation(out=qT[:, m, :NT], in_=pb[:, :NT], func=AF.Copy), reads=[Bp], writes=[B_qT])
            elif m == 4:
                P.op("act", lambda e: e.activation(out=kT[:, 128:128 + NT], in_=pb[:, :NT], func=AF.Copy), reads=[Bp], writes=[B_kT])
                if want32:
                    P.op("dve", lambda e: e.tensor_copy(out=kv32[:, 0, :], in_=pb[:, NT - 128:NT]), reads=[Bp], writes=[B_kv32])
            elif m == 5:
                P.op("act", lambda e: e.activation(out=vT[:, :NT], in_=pb[:, :NT], func=AF.Copy), reads=[Bp], writes=[B_vT])
                if want32:
                    P.op("dve", lambda e: e.tensor_copy(out=kv32[:, 1, :], in_=pb[:, NT - 128:NT]), reads=[Bp], writes=[B_kv32])
            else:
                g = m - 6
                if sample:
                    P.op("dve", lambda e: e.tensor_copy(out=Us[:, g, :, 16:24], in_=pb[:, 0:128].rearrange("p (b t) -> p b t", b=16)),
                         reads=[Bp], writes=[B_U])
                else:
                    P.op("dve", lambda e: e.tensor_copy(out=U[:, g, 16:16 + NT], in_=pb[:, :NT]), reads=[Bp], writes=[B_U])
        yield from g_dense("in", list(range(4, 10)) if halo else list(range(10)), NT, lambda k: hT[:, k, :NT], B_hT, in_evac)
        yield "P1"

        for j0 in range(0, ntl, 4):
            pb, Bp = tbank()
            pv = pb[:].bitcast(BF16)[:, 0:512].rearrange("p (j t) -> p j t", j=4)
            P.mm([(lambda e, j=j, pv=pv: e.transpose(out=pv[:, j - j0, :], in_=vT[:, j * 128:(j + 1) * 128], identity=ident[:]))
                  for j in range(j0, min(ntl, j0 + 4))], reads=[B_vT] + CONST, writes=[Bp])
            nj = min(ntl, j0 + 4) - j0
            P.op("dve", lambda e, pv=pv, j0=j0, nj=nj: e.tensor_copy(out=vtok[:, 1 + j0:1 + j0 + nj, :], in_=pv[:, 0:nj, :]),
                 reads=[Bp], writes=[B_vtok])
        yield

        def carry():
            P.op("dve", lambda e: e.tensor_copy(out=kT[:, 0:128], in_=kT[:, NT:NT + 128]), reads=[B_kT], writes=[B_kT])
            P.op("dve", lambda e: e.tensor_copy(out=vtok[:, 0, :], in_=vtok[:, ntl, :]), reads=[B_vtok], writes=[B_vtok])

        if halo:
            carry()
            P.op("dve", lambda e: e.tensor_copy(out=carryU[:], in_=U[:, :, NT:NT + 16]), reads=[B_U], writes=[B_cU])
            return

        if want32:
            if last:
                def dd():
                    P.dma("sp", "kvout", wkp[:, :], ost[:, 0:128], reads=[B_ost])
                    P.dma("sp", "kvout", wvp[:, :], ost[:, 128:256], reads=[B_ost])
            else:
                def dd():
                    for t in range(8):
                        P.dma("sp", "kvout", wks[:, 120 + t, :], ost[t:128:8, 0:128], reads=[B_ost])
                        P.dma("sp", "kvout", wvs[:, 120 + t, :], ost[t:128:8, 128:256], reads=[B_ost])
                    P.dma("sp", "d2d_k", wks[:, 0:120, :], ck[:, 8:128, :])
                    P.dma("sp", "d2d_v", wvs[:, 0:120, :], cv[:, 8:128, :])
            out_tok_major([kv32[:, 0, :], kv32[:, 1, :]], [B_kv32], 128, dd)
            yield

        if not sample:
            P.op("dve", lambda e: e.tensor_copy(out=U[:, :, 0:16], in_=carryU[:]), reads=[B_cU], writes=[B_U])
        for hh in range(2):
            if sample:
                Wd, c0 = 192, hh * 192
            else:
                Wd, c0 = 16 + NT // 2, hh * (NT // 2)
            Uh = U[:, :, c0:c0 + Wd]
            P.op("dve", lambda e, Uh=Uh, Wd=Wd: e.tensor_tensor(out=SA[:, :, 1:Wd], in0=Uh[:, :, 1:Wd], in1=Uh[:, :, 0:Wd - 1], op=ALU.add),
                 reads=[B_U], writes=[B_SA])
            P.op("dve", lambda e, Wd=Wd: e.tensor_tensor(out=SB[:, 1:4, 3:Wd], in0=SA[:, 1:4, 3:Wd], in1=SA[:, 1:4, 1:Wd - 2], op=ALU.add),
                 reads=[B_SA], writes=[B_SB])
            yield
            P.op("dve", lambda e, Wd=Wd: e.tensor_tensor(out=SA[:, 2:4, 7:Wd], in0=SB[:, 2:4, 7:Wd], in1=SB[:, 2:4, 3:Wd - 4], op=ALU.add),
                 reads=[B_SB], writes=[B_SA])
            P.op("dve", lambda e, Wd=Wd: e.tensor_tensor(out=SB[:, 3, 15:Wd], in0=SA[:, 3, 15:Wd], in1=SA[:, 3, 7:Wd - 8], op=ALU.add),
                 reads=[B_SA], writes=[B_SB])
            yield
            for g in range(4):
                S_, BS_ = (SA, B_SA) if g % 2 == 0 else (SB, B_SB)
                if sample:
                    sv = S_[:, g, 0:192].rearrange("p (b c) -> p b c", b=8)[:, :, 16:24]
                    uv = Uh[:, g, :].rearrange("p (b c) -> p b c", b=8)[:, :, 16:24]
                    dv = dT[:, g, hh * 64:(hh + 1) * 64].rearrange("p (b t) -> p b t", b=8)
                else:
                    sv, uv, dv = S_[:, g, 16:Wd], Uh[:, g, 16:Wd], dT[:, g, c0:c0 + NT // 2]
                P.op("dve", lambda e, sv=sv, uv=uv, dv=dv, g=g: e.scalar_tensor_tensor(out=dv, in0=sv, scalar=1.0 / (2 << g), in1=uv,
                                                                                   op0=ALU.mult, op1=ALU.subtract),
                     reads=[BS_, B_U], writes=[B_dT])
                if kind == "P" and gi == 0 and hh == 0:
                    P.op("dve", lambda e, S_=S_, g=g: e.tensor_tensor(out=st[:, 0:16], in0=S_[:, g, 16:32], in1=invc[:, g, :], op=ALU.mult),
                         reads=[BS_] + CONST, writes=[B_st])
                    P.op("dve", lambda e, g=g: e.tensor_tensor(out=dT[:, g, 0:16], in0=st[:, 0:16], in1=U[:, g, 16:32], op=ALU.subtract),
                         reads=[B_st, B_U], writes=[B_dT])
            yield
        if last:
            def dd2():
                P.dma("sp", "poolout", poolp[:, :], ost[113:128, :], reads=[B_ost])
            out_tok_major([U[:, g, 16 + NT - 128:16 + NT] for g in range(4)], [B_U], 128, dd2)
        if sample:
            for g in range(4):
                P.op("dve", lambda e, g=g: e.tensor_copy(out=SA[:, g, 0:128].rearrange("p (b t) -> p b t", b=16), in_=Us[:, g, :, 16:24]),
                     reads=[B_U, B_dT], writes=[B_SA])

            def dd3():
                for t in range(8):
                    P.dma("sp", "poolout", pools[:, 7 + t, :], ost[t:128:8, :], reads=[B_ost])
                P.dma("sp", "d2d_p", pools[:, 0:7, :], spool[:, 8:15, :])
            out_tok_major([SA[:, g, 0:128] for g in range(4)], [B_SA], 128, dd3)
        else:
            P.op("dve", lambda e: e.tensor_copy(out=carryU[:], in_=U[:, :, NT:NT + 16]), reads=[B_U], writes=[B_cU])
        yield
        for g in range(4):
            pb, Bp = dbank()
            P.mm([lambda e, g=g, pb=pb: e.matmul(pb[:, :NT], lhsT=wpool[:, g, :], rhs=dT[:, g, :NT], start=True, stop=True)],
                 reads=[B_wpool, B_dT], writes=[Bp])
            P.op("act", lambda e, g=g, pb=pb: e.activation(out=aoT[:, 4 + g, :NT], in_=pb[:, :NT], func=AF.Copy, scale=pscale[:, g:g + 1]),
                 reads=[Bp] + CONST, writes=[B_aoT])
            yield

        if not sample:
            for j in range(ntl):
                if gi == 0 and j == 1:
                    P.dma("sp", "biasld", bias[:].rearrange("p a b -> p (a b)"), biasg_d[:, :], writes=[B_bias])
                for gp in range(2):
                    bk = [sbank(), sbank()]
                    fns = []
                    for g in (2 * gp, 2 * gp + 1):
                        for kv in range(2):
                            fns.append(lambda e, kv=kv, g=g, j=j, bk=bk: e.matmul(
                                bk[kv][0][:, (g % 2) * 256:(g % 2 + 1) * 256], lhsT=qT[kv * 64:(kv + 1) * 64, g, j * 128:(j + 1) * 128],
                                rhs=kT[kv * 64:(kv + 1) * 64, j * 128:j * 128 + 256], start=True, stop=True))
                    P.mm(fns, reads=[B_qT, B_kT], writes=[bk[0][1], bk[1][1]])
                    for kv in range(2):
                        u0 = 4 * gp + kv
                        P.op("dve", lambda e, kv=kv, u0=u0, bk=bk: e.scalar_tensor_tensor(
                            out=sbias[:, u0:u0 + 3:2, :], in0=bk[kv][0][:].rearrange("p (g t) -> p g t", g=2), scalar=0.125,
                            in1=bias[:, u0:u0 + 3:2, :], op0=ALU.mult, op1=ALU.add),
                            reads=[bk[kv][1], B_bias], writes=[B_sbias])
                    yield
                win_softmax(8, 256, sinkp[:, 0:8])
                yield
                yield from diag_T(8, 2, pexp, B_pexp, lambda u0: pT[:, u0:u0 + 2, :, :].rearrange("p u k t -> p (u k t)"), None, B_pT, [128, 128])
                po, Bpo = ps[PS_O], B_ps[PS_O]
                pov = po[:].rearrange("p (g t) -> p g t", g=4)
                fns = []
                for g in range(4):
                    for kv in range(2):
                        for kc in range(2):
                            fns.append(lambda e, g=g, kv=kv, kc=kc, j=j: e.matmul(
                                pov[kv * 64:(kv + 1) * 64, g, :], lhsT=vtok[:, j + kc, kv * 64:(kv + 1) * 64], rhs=pT[:, 2 * g + kv, kc, :],
                                start=(kc == 0), stop=(kc == 1)))
                P.mm(fns, reads=[B_vtok, B_pT], writes=[Bpo])
                P.op("act", lambda e, j=j: e.activation(out=aoT[:, 0:4, j * 128:(j + 1) * 128], in_=pov, func=AF.Copy),
                     reads=[Bpo], writes=[B_aoT])
                yield
            carry()
        else:
            P.op("dve", lambda e: e.tensor_copy(out=qs2[:].rearrange("p b (g t) -> p b g t", g=4),
                                                in_=qT[:, :, 0:128].rearrange("p g (b t) -> p b g t", b=16)), reads=[B_qT], writes=[B_qs2])
            pb, Bp = dbank()
            pvb = pb[:].bitcast(BF16)
            P.mm([(lambda e, i=i, pvb=pvb: e.transpose(out=pvb[0:32, i * 128:(i + 1) * 128], in_=vT[:, i * 32:(i + 1) * 32], identity=ident[:]))
                  for i in range(4)], reads=[B_vT] + CONST, writes=[Bp])
            P.op("dve", lambda e, pvb=pvb: e.tensor_copy(out=vnq[0:32, :, :], in_=pvb[0:32, 0:512].rearrange("p (i t) -> p i t", i=4)),
                 reads=[Bp], writes=[B_vnq])
            yield
            for i in range(4):
                bk = [sbank(), sbank()]
                fns = []
                for kv in range(2):
                    pvk = bk[kv][0]
                    for jq in range(4):
                        b = 4 * i + jq
                        fns.append(lambda e, kv=kv, jq=jq, b=b, pvk=pvk: e.matmul(
                            pvk[32 * jq:32 * jq + 32, 0:128], lhsT=qs2[kv * 64:(kv + 1) * 64, b, :], rhs=kcT[kv * 64:(kv + 1) * 64, b, :],
                            start=True, stop=True, tile_position=(kv * 64, 32 * jq)))
                    fns.append(lambda e, kv=kv, i=i, pvk=pvk: e.matmul(
                        pvk[:, 128:160], lhsT=qs2[kv * 64:(kv + 1) * 64, 4 * i:4 * i + 4, :].rearrange("p b t -> p (b t)"),
                        rhs=kT[kv * 64:(kv + 1) * 64, 128 + 32 * i:128 + 32 * i + 32], start=True, stop=True))
                P.mm(fns, reads=[B_qs2, B_kcT, B_kT], writes=[bk[0][1], bk[1][1]])
                for kv in range(2):
                    P.op("dve", lambda e, kv=kv, i=i, bk=bk: e.scalar_tensor_tensor(
                        out=sbias[:, 2 * i + kv, 0:160], in0=bk[kv][0][:, 0:160], scalar=0.125,
                        in1=biass[:, kv, :], op0=ALU.mult, op1=ALU.add),
                        reads=[bk[kv][1]] + CONST, writes=[B_sbias])
                yield
            win_softmax(8, 160, sinks[:, 0:8])
            yield
            yield from diag_T(8, 2, pexp, B_pexp, None, lambda u, kc: pT[0:(128 if kc == 0 else 32), u, kc, :], B_pT, [128, 32])
            for i in range(4):
                pb, Bp = dbank()
                fns = []
                for kv in range(2):
                    u = 2 * i + kv
                    for jq in range(4):
                        b = 4 * i + jq
                        fns.append(lambda e, kv=kv, jq=jq, b=b, u=u, pb=pb: e.matmul(
                            pb[kv * 64:(kv + 1) * 64, 32 * jq:32 * jq + 32], lhsT=vc[:, b, kv * 64:(kv + 1) * 64], rhs=pT[:, u, 0, 32 * jq:32 * jq + 32],
                            start=(jq == 0), stop=False, skip_group_check=True))
                for kv in range(2):
                    u = 2 * i + kv
                    fns.append(lambda e, kv=kv, u=u, i=i, pb=pb: e.matmul(
                        pb[kv * 64:(kv + 1) * 64, 0:128], lhsT=vnq[0:32, i, kv * 64:(kv + 1) * 64], rhs=pT[0:32, u, 1, :],
                        start=False, stop=True, skip_group_check=True))
                P.mm(fns, reads=[B_vc, B_pT, B_vnq], writes=[Bp])
                P.op("act", lambda e, pb=pb, i=i: e.activation(
                    out=aoT[:, 0:4, 32 * i:32 * i + 32].rearrange("p g (j t) -> p j g t", j=4),
                    in_=pb[:, 0:128].rearrange("p (j g t) -> p j g t", j=4, g=4), func=AF.Copy),
                    reads=[Bp], writes=[B_aoT])
                yield

    def late_pre(kind, gi, X, gen):
        sample, halo, NT, ntl = geom(kind)
        xt, Bxt = xTs[X], B_xTs[X]
        loaded = [gen is None]

        def step_load():
            if not loaded[0]:
                if next(gen, "L") == "L":
                    loaded[0] = True
        dense("out", list(range(8)), NT, lambda k: aoT[:, k, :NT], B_aoT, resid_evac(NT, X))
        norm(xt, Bxt, 1, NT, hT, B_hT)
        qcT, B_qcT = aoT, B_aoT
        ocT, B_ocT = hT, B_hT

        def cq_evac(m, pb, Bp):
            P.op("act", lambda e: e.activation(out=qcT[:, m, :NT], in_=pb[:, :NT], func=AF.Copy), reads=[Bp], writes=[B_qcT])
        dense("cq", list(range(8)), NT, lambda k: hT[:, k, :NT], B_hT, cq_evac)

        if not sample:
            for j in range(ntl):
                banks = [sbank(), sbank()]
                for hp, (pb, Bp) in enumerate(banks):
                    pv = pb[:].rearrange("p (h t) -> p h t", h=2)
                    fns = []
                    for hh in range(2):
                        h = 2 * hp + hh
                        for dc in range(2):
                            fns.append(lambda e, pv=pv, hh=hh, h=h, dc=dc, j=j: e.matmul(
                                pv[:, hh, :], lhsT=qcT[:, 2 * h + dc, j * 128:(j + 1) * 128], rhs=memkT[:, 2 * h + dc, :],
                                start=(dc == 0), stop=(dc == 1)))
                    P.mm(fns, reads=[B_qcT, B_memkT], writes=[Bp])
                cross_softmax(banks)
                step_load()
                run(diag_T(4, 2, pexp, B_pexp, lambda h0: pTc[:, h0:h0 + 2, :, :].rearrange("p u k t -> p (u k t)"), None, B_pTc, [128, 128]))
                for half in range(2):
                    pb, Bp = dbank()
                    pv = pb[:].rearrange("p (c t) -> p c t", c=4)
                    fns = []
                    for cc in range(4):
                        c = half * 4 + cc
                        h = c // 2
                        for mc in range(2):
                            fns.append(lambda e, pv=pv, cc=cc, c=c, h=h, mc=mc: e.matmul(
                                pv[:, cc, :], lhsT=memv[:, mc, c * 128:(c + 1) * 128], rhs=pTc[:, h, mc, :], start=(mc == 0), stop=(mc == 1)))
                    P.mm(fns, reads=[B_memv, B_pTc], writes=[Bp])
                    P.op("act" if half == 0 else "dve",
                         (lambda e, pv=pv, half=half, j=j: e.activation(out=ocT[:, half * 4:half * 4 + 4, j * 128:(j + 1) * 128], in_=pv, func=AF.Copy))
                         if half == 0 else
                         (lambda e, pv=pv, half=half, j=j: e.tensor_copy(out=ocT[:, half * 4:half * 4 + 4, j * 128:(j + 1) * 128], in_=pv)),
                         reads=[Bp], writes=[B_ocT])
                step_load()
        else:
            banks = [sbank(), sbank()]
            for i in range(2):
                P.op("dve", lambda e, i=i: e.memset(qpad[i][:], 0.0), writes=[B_qpad[i]])

            def ld_k(b):
                P.dma("pool", f"kb{b % 2}", Kb[b % 2][:], cmk[b].rearrange("(m p) f -> p m f", p=128), writes=[B_Kb[b % 2]])

            def ld_v(b):
                P.dma("pool", f"vb{b % 2}", Vb[b % 2][:], cmv[b].rearrange("(m p) f -> p m f", p=128), writes=[B_Vb[b % 2]])
            ld_k(0)
            ld_v(0)
            for b in range(16):
                s2 = b % 2
                if b + 1 < 16:
                    ld_k(b + 1)
                for mt in range(2):
                    pb, Bp = tbank()
                    pv = pb[:].bitcast(BF16).rearrange("p (c t) -> p c t", c=8)
                    P.mm([(lambda e, c=c, pv=pv, mt=mt, s2=s2: e.transpose(out=pv[:, c, :], in_=Kb[s2][:, mt, c * 128:(c + 1) * 128], identity=ident[:]))
                          for c in range(8)], reads=[B_Kb[s2]] + CONST, writes=[Bp])
                    P.op("act" if mt == 0 else "dve",
                         (lambda e, pv=pv, mt=mt, s2=s2: e.activation(out=KbT[s2][:, :, mt * 128:(mt + 1) * 128], in_=pv, func=AF.Copy))
                         if mt == 0 else
                         (lambda e, pv=pv, mt=mt, s2=s2: e.tensor_copy(out=KbT[s2][:, :, mt * 128:(mt + 1) * 128], in_=pv)),
                         reads=[Bp], writes=[B_KbT[s2]])
                if b >= 2:
                    P.op("dve", lambda e, s2=s2, b=b: e.memset(qpad[s2][:, :, (b - 2) * 8:(b - 1) * 8], 0.0), writes=[B_qpad[s2]])
                P.op("dve", lambda e, s2=s2, b=b: e.tensor_copy(out=qpad[s2][:, :, b * 8:(b + 1) * 8], in_=qcT[:, :, b * 8:(b + 1) * 8]),
                     reads=[B_qcT], writes=[B_qpad[s2]])
                for hp, (pb, Bp) in enumerate(banks):
                    pv = pb[:].rearrange("p (h t) -> p h t", h=2)
                    fns = []
                    for hh in range(2):
                        h = 2 * hp + hh
                        for dc in range(2):
                            fns.append(lambda e, pv=pv, hh=hh, h=h, dc=dc, s2=s2, b=b: e.matmul(
                                pv[:, hh, :], lhsT=qpad[s2][:, 2 * h + dc, :], rhs=KbT[s2][:, 2 * h + dc, :],
                                start=(b == 0 and hh == 0 and dc == 0), stop=(b == 15 and dc == 1), skip_group_check=True))
                    P.mm(fns, reads=[B_qpad[s2], B_KbT[s2]], writes=[Bp])
            cross_softmax(banks)
            run(diag_T(4, 2, pexp, B_pexp, lambda h0: pTc[:, h0:h0 + 2, :, :].rearrange("p u k t -> p (u k t)"), None, B_pTc, [128, 128]))
            pbs = [dbank(), dbank()]
            for b in range(16):
                s2 = b % 2
                if b + 1 < 16:
                    ld_v(b + 1)
                fns = []
                for c in range(8):
                    pv = pbs[c // 4][0][:].rearrange("p (c t) -> p c t", c=4)
                    h = c // 2
                    for mc in range(2):
                        fns.append(lambda e, pv=pv, c=c, h=h, mc=mc, s2=s2, b=b: e.matmul(
                            pv[:, c % 4, b * 8:(b + 1) * 8], lhsT=Vb[s2][:, mc, c * 128:(c + 1) * 128], rhs=pTc[:, h, mc, b * 8:(b + 1) * 8],
                            start=(mc == 0), stop=(mc == 1), skip_group_check=True))
                P.mm(fns, reads=[B_Vb[s2], B_pTc], writes=[pbs[0][1], pbs[1][1]])
            for half in range(2):
                pv = pbs[half][0][:].rearrange("p (c t) -> p c t", c=4)
                P.op("act" if half == 0 else "dve",
                     (lambda e, pv=pv, half=half: e.activation(out=ocT[:, half * 4:half * 4 + 4, 0:128], in_=pv, func=AF.Copy))
                     if half == 0 else
                     (lambda e, pv=pv, half=half: e.tensor_copy(out=ocT[:, half * 4:half * 4 + 4, 0:128], in_=pv)),
                     reads=[pbs[half][1]], writes=[B_ocT])
        dense("co", list(range(8)), NT, lambda k: ocT[:, k, :NT], B_ocT, resid_evac(NT, X))
        while not loaded[0]:
            step_load()

    def ffn(kind, gi, X, gen):
        sample, halo, NT, ntl = geom(kind)
        xt, Bxt = xTs[X], B_xTs[X]
        norm(xt, Bxt, 3, NT, hT, B_hT)
        uctr = [0]

        def up_evac(m, pb, Bp):
            r, Br = relu_t[uctr[0] % 2], B_relu[uctr[0] % 2]
            uctr[0] += 1
            P.op("act", lambda e: e.activation(out=r[:, :NT], in_=pb[:, :NT], func=AF.Relu), reads=[Bp], writes=[Br])
            P.op("dve", lambda e: e.tensor_tensor(out=hidT[:, m, :NT], in0=r[:, :NT], in1=r[:, :NT], op=ALU.mult), reads=[Br], writes=[B_hid])
        for _ in g_dense("up", list(range(32)), NT, lambda k: hT[:, k, :NT], B_hT, up_evac):
            advance(gen, 1)
        for _ in g_dense("down", list(range(8)), NT, lambda k: hidT[:, k, :NT], B_hid, resid_evac(NT, X), kgroups=4):
            advance(gen, 1)

    def tail(kind, gi, X):
        sample, halo, NT, ntl = geom(kind)
        xt, Bxt = xTs[X], B_xTs[X]
        norm(xt, Bxt, 4, NT, yT, B_yT)
        for j in range(ntl):
            ys_, Bys = yst[j % 2], B_yst[j % 2]
            for hf in range(2):
                pb, Bp = dbank()
                pv = pb[:].rearrange("p (c t) -> p c t", c=4)
                P.mm([(lambda e, c=c, pv=pv, hf=hf, j=j: e.transpose(out=pv[:, c, :], in_=yT[:, hf * 4 + c, j * 128:(j + 1) * 128], identity=identf[:]))
                      for c in range(4)], reads=[B_yT] + CONST, writes=[Bp])
                P.op("act" if hf == 0 else "dve",
                     (lambda e, pb=pb, hf=hf, ys_=ys_: e.activation(out=ys_[:, hf * 512:(hf + 1) * 512], in_=pb[:, :], func=AF.Copy))
                     if hf == 0 else
                     (lambda e, pb=pb, hf=hf, ys_=ys_: e.tensor_copy(out=ys_[:, hf * 512:(hf + 1) * 512], in_=pb[:, :])),
                     reads=[Bp], writes=[Bys])
            if sample:
                P.dma("sp", "y", ys[:, :], ys_[:], reads=[Bys])
            else:
                r0 = gi * NT_P + j * 128
                P.dma("sp", "y", yp[r0:r0 + 128, :], ys_[:], reads=[Bys])

    order = [("P", g) for g in range(NG_P)] + [("S", 0)]
    order = order[:max(0, min(len(order), STAGE))] if STAGE < 50 else order
    run(early("H", 0, 0))
    if order:
        run(early(order[0][0], order[0][1], 0))
    for idx, (kind, gi) in enumerate(order):
        X = idx % 2
        nxt = order[idx + 1] if idx + 1 < len(order) else None
        gen = early(nxt[0], nxt[1], (idx + 1) % 2) if nxt else None
        late_pre(kind, gi, X, gen)
        advance(gen, until="P1")
        ffn(kind, gi, X, gen)
        run(gen)
        tail(kind, gi, X)

    return finish()


_CACHE = {}


def _build_nc():
    if "nc" in _CACHE:
        return _CACHE["nc"]
    nc0 = bass.Bass("TRN2", target_bir_lowering=False)
    with ExitStack() as es0:
        _, W0 = build_sched(nc0, es0)
    sched = W0.rec
    nc = bass.Bass("TRN2", target_bir_lowering=False)
    with ExitStack() as es:
        P, W = build(nc, es, False, sched)
        assert W.i == len(sched), (W.i, len(sched))
        block = es.enter_context(nc.Block())
        P.flush(block)
    _CACHE["nc"] = nc
    return nc


def build_sched(nc0, es0):
    return build(nc0, es0, False, None)


def _tables(half):
    slopes = 2.0 ** (-(np.arange(8) + 1.0))
    q = np.arange(128)[:, None]
    c = np.arange(256)[None, :]
    dist = q - c + 128
    valid = (dist >= 0) & (dist <= 128)
    biasg = np.empty((128, 8, 256), np.float32)
    for g in range(4):
        for kv in range(2):
            h = kv * 4 + g
            biasg[:, 2 * g + kv, :] = np.where(valid, -slopes[h] * dist, -1e30)
    biasf = biasg.copy()
    if half == 0:
        biasf[:, :, 0:128] = -1e30
    biass = np.full((128, 2, 160), -1e30, np.float32)
    for j in range(4):
        for g in range(4):
            for t in range(8):
                r = j * 32 + g * 8 + t
                for kv in range(2):
                    h = kv * 4 + g
                    cc = np.arange(128)
                    d = t + 128 - cc
                    biass[r, kv, 0:128] = np.where(cc >= t, -slopes[h] * d, -1e30)
                    for tp in range(t + 1):
                        biass[r, kv, 128 + j * 8 + tp] = -slopes[h] * (t - tp)
    invc = np.empty((128, 4, 16), np.float32)
    for g in range(4):
        w = 2 << g
        for p in range(16):
            invc[:, g, p] = 1.0 / (min(p + 1, w) if half == 0 else w)
    return biasg.reshape(128, -1), biasf.reshape(128, -1), biass.reshape(128, -1), invc.reshape(128, -1)


def _prep(x_prompt, x_sample, cache_win_k, cache_win_v, state_pool, cache_mem_k, cache_mem_v,
          mem_prompt, g_mix, w_in, attn_sinks, w_pool, pool_scale, w_out, g_cross, g_mem,
          w_cq, w_ck, w_cv, w_co, g_ffn, w_up, w_down, g_final):
    f = lambda a: np.ascontiguousarray(np.asarray(a, dtype=np.float32))
    x_prompt, x_sample = f(x_prompt), f(x_sample)
    shared = dict(w_in=f(w_in)[0], w_pool=f(w_pool)[0], w_out=f(w_out)[0], w_cq=f(w_cq)[0], w_ck=f(w_ck)[0],
                  w_cv=f(w_cv)[0], w_co=f(w_co)[0], w_up=f(w_up)[0], w_down=f(w_down)[0])
    gs = np.stack([f(g_mix)[0], f(g_cross)[0], f(g_mem)[0], f(g_ffn)[0], f(g_final)], 0)
    shared["gvec"] = np.ascontiguousarray(gs.reshape(5, 8, 128).transpose(2, 0, 1).reshape(128, 40))
    shared["pscale"] = np.ascontiguousarray(f(pool_scale)[0].reshape(4, 128).T)
    sk = f(attn_sinks)[0]
    sinkp = np.empty((128, 8), np.float32)
    for g in range(4):
        for kv in range(2):
            sinkp[:, 2 * g + kv] = sk[kv * 4 + g]
    shared["sinkp"] = sinkp
    sinks = np.empty((128, 8), np.float32)
    for r in range(128):
        g = (r % 32) // 8
        for i in range(4):
            sinks[r, 2 * i] = sk[g]
            sinks[r, 2 * i + 1] = sk[4 + g]
    shared["sinks"] = sinks
    ckf, cvf, spf = f(cache_win_k)[0], f(cache_win_v)[0], f(state_pool)[0]
    cmkf, cmvf, memf = f(cache_mem_k)[0], f(cache_mem_v)[0], f(mem_prompt)
    in_maps = []
    for c in range(NCORES):
        b, half = c // 2, c % 2
        s0 = half * SEQ_CORE
        xp = np.zeros((128 + SEQ_CORE, D), np.float32)
        xp[128:] = x_prompt[b, s0:s0 + SEQ_CORE]
        if half == 1:
            xp[:128] = x_prompt[b, s0 - 128:s0]
        biasg, biasf, biass, invc = _tables(half)
        sl = slice(16 * c, 16 * c + 16)
        m = dict(shared)
        m.update(xp=xp, xs=np.ascontiguousarray(x_sample[sl].reshape(128, D)), mem=np.ascontiguousarray(memf[b]),
                 ck=np.ascontiguousarray(ckf[sl].reshape(16, 128, 128)), cv=np.ascontiguousarray(cvf[sl].reshape(16, 128, 128)),
                 spool=np.ascontiguousarray(spf[sl]), cmk=np.ascontiguousarray(cmkf[sl].reshape(16, 256, D)),
                 cmv=np.ascontiguousarray(cmvf[sl].reshape(16, 256, D)),
                 biasg=biasg, biasf=biasf, biass=biass, invc=invc)
        in_maps.append(m)
    return in_maps


def kernel(**inputs):
    in_maps = _prep(**inputs)
    nc = _build_nc()
    res = run_bass_kernel_spmd(nc, in_maps, core_ids=list(range(NCORES))).results
    return _assemble(res)


def _assemble(res):
    B, S = 4, 4096
    y_prompt = np.empty((B, S, D), np.float32)
    y_sample = np.empty((128, 8, D), np.float32)
    wk_p = np.empty((1, B, 128, 2, 64), np.float32); wv_p = np.empty_like(wk_p)
    pool_p = np.empty((1, B, 15, 512), np.float32)
    mk_p = np.empty((1, B, 256, 4, 256), np.float32); mv_p = np.empty_like(mk_p)
    wk_s = np.empty((1, 128, 128, 2, 64), np.float32); wv_s = np.empty_like(wk_s)
    pool_s = np.empty((1, 128, 15, 512), np.float32)
    for c in range(NCORES):
        r = res[c]
        b, half = c // 2, c % 2
        y_prompt[b, half * SEQ_CORE:(half + 1) * SEQ_CORE] = r["yp"]
        sl = slice(16 * c, 16 * c + 16)
        y_sample[sl] = r["ys"].reshape(16, 8, D)
        if half == 1:
            wk_p[0, b] = r["wkp"].reshape(128, 2, 64)
            wv_p[0, b] = r["wvp"].reshape(128, 2, 64)
            pool_p[0, b] = r["poolp"]
        else:
            mk_p[0, b] = r["memk"].reshape(256, 4, 256)
            mv_p[0, b] = r["memv"].reshape(256, 4, 256)
        wk_s[0, sl] = r["wks"].reshape(16, 128, 2, 64)
        wv_s[0, sl] = r["wvs"].reshape(16, 128, 2, 64)
        pool_s[0, sl] = r["pools"]
    return (y_prompt, y_sample, wk_p, wv_p, pool_p, mk_p, mv_p, wk_s, wv_s, pool_s)
```

```python
import numpy as np
from contextlib import ExitStack
import concourse.bass as bass
import concourse.mybir as mybir
from concourse.bass_utils import run_bass_kernel_spmd

F32 = mybir.dt.float32
BF16 = mybir.dt.bfloat16
ALU = mybir.AluOpType
AF = mybir.ActivationFunctionType
AX = mybir.AxisListType

NCORES = 8
STAGE = 99
D = 1024
SEQ_CORE = 2048
NT_P = 512
NG_P = SEQ_CORE // NT_P
RING = 10
EPS = 1e-5


class Buf:
    __slots__ = ("name", "w", "r", "al", "excl")

    def __init__(self, name, excl=False):
        self.name = name
        self.w = None
        self.r = {}
        self.al = []
        self.excl = excl


def alias(*bufs):
    for a in bufs:
        for b in bufs:
            if a is not b and b not in a.al:
                a.al.append(b)


class Prog:
    def __init__(self, nc, es, dry):
        self.nc, self.es, self.dry = nc, es, dry
        self.q = {e: [] for e in ("pe", "act", "dve", "pool", "sp")}
        self.cnt, self.sems = {}, {}
        self.waited = {e: {} for e in self.q}

    def sem(self, key):
        if key not in self.sems:
            self.sems[key] = None if self.dry else self.es.enter_context(self.nc.semaphore(key))
            self.cnt[key] = 0

    def _wait(self, eng, tok):
        if tok is None:
            return
        key, val = tok
        if self.waited[eng].get(key, 0) >= val:
            return
        self.waited[eng][key] = val
        self.q[eng].append(("w", key, val))

    def _deps(self, eng, reads, writes, extra):
        for b in reads:
            self._wait(eng, b.w)
            if b.excl:
                for k, v in b.r.items():
                    if k != eng:
                        self._wait(eng, (k, v))
        for b in writes:
            for bb in [b] + b.al:
                self._wait(eng, bb.w)
                for k, v in bb.r.items():
                    self._wait(eng, (k, v))
        for t in extra:
            self._wait(eng, t)

    def _commit(self, tok, reads, writes):
        k, v = tok
        for b in reads:
            b.r[k] = max(b.r.get(k, 0), v)
        for b in writes:
            b.w = tok
            b.r = {}

    def op(self, eng, fn, reads=(), writes=(), extra=()):
        self._deps(eng, reads, writes, extra)
        self.sem(eng)
        self.cnt[eng] += 1
        tok = (eng, self.cnt[eng])
        self.q[eng].append(("i", fn, eng, 1))
        self._commit(tok, reads, writes)
        return tok

    def mm(self, fns, reads=(), writes=(), extra=()):
        self._deps("pe", reads, writes, extra)
        for f in fns[:-1]:
            self.q["pe"].append(("i", f, None, 0))
        self.sem("pe")
        self.cnt["pe"] += 1
        tok = ("pe", self.cnt["pe"])
        self.q["pe"].append(("i", fns[-1], "pe", 1))
        self._commit(tok, reads, writes)
        return tok

    def dma(self, qeng, semkey, out, in_, reads=(), writes=(), extra=()):
        if writes:
            semkey = "dw" + qeng[0] + "_" + writes[0].name
        elif reads:
            semkey = "dr" + qeng[0] + "_" + reads[0].name
        for b in reads:
            self._wait(qeng, b.w)
        for b in writes:
            for bb in [b] + b.al:
                if not (bb.w is not None and bb.w[0] == semkey):
                    self._wait(qeng, bb.w)
                for k, v in bb.r.items():
                    self._wait(qeng, (k, v))
        for t in extra:
            self._wait(qeng, t)
        self.sem(semkey)
        self.cnt[semkey] += 16
        tok = (semkey, self.cnt[semkey])
        self.q[qeng].append(("i", (lambda e, o=out, i=in_: e.dma_start(out=o, in_=i)), semkey, 16))
        self._commit(tok, reads, writes)
        return tok

    def flush(self, block):
        def run(name):
            def f(e):
                for it in self.q[name]:
                    if it[0] == "w":
                        e.wait_ge(self.sems[it[1]], it[2])
                    else:
                        ins = it[1](e)
                        if it[3]:
                            ins.then_inc(self.sems[it[2]], it[3])
            return f
        block.tensor(run("pe"))
        block.scalar(run("act"))
        block.vector(run("dve"))
        block.gpsimd(run("pool"))
        block.sync(run("sp"))


class WStream:
    def __init__(self, P, ring_ap, sched, scratch_fn=None):
        self.P, self.ring = P, ring_ap
        self.sched = sched
        self.rec = []
        self.i = 0
        self.issued = 0
        self.slots = [Buf(f"ws{i}") for i in range(RING)]
        self.src = {}
        self.uidx, self.wtok = {}, {}
        self.scratch = None
        if sched is not None:
            cnt = {}
            for k in sched:
                cnt[k] = cnt.get(k, 0) + 1
            for k in sched:
                if cnt[k] > 1 and k not in self.uidx:
                    self.uidx[k] = len(self.uidx)
            if scratch_fn is not None and self.uidx:
                self.scratch = scratch_fn(len(self.uidx))

    def _issue(self, j):
        key = self.sched[j]
        name, m = key
        s = j % RING
        if self.scratch is not None and key in self.wtok:
            self.P.dma("sp", f"ws{s}", self.ring[:, s], self.scratch[self.uidx[key]], writes=[self.slots[s]],
                       extra=[self.wtok[key]])
            return
        for (dst_fn, src_ap) in self.src[name](m):
            self.P.dma("pool", f"ws{s}", dst_fn(self.ring[:, s]), src_ap, writes=[self.slots[s]])
        if self.scratch is not None and key in self.uidx:
            self.wtok[key] = self.P.dma("sp", f"sw{s}", self.scratch[self.uidx[key]], self.ring[:, s], reads=[self.slots[s]])

    def get(self, name, m):
        if self.sched is None:
            self.rec.append((name, m))
            return self.ring[:, 0], self.slots[0]
        assert self.sched[self.i] == (name, m), (self.i, self.sched[self.i], name, m)
        while self.issued < min(len(self.sched), self.i + RING - 3):
            self._issue(self.issued)
            self.issued += 1
        s = self.i % RING
        self.i += 1
        return self.ring[:, s], self.slots[s]


def build(nc, es, dry, sched):
    P = Prog(nc, es, dry)

    def din(name, shape):
        return nc.dram_tensor(name, list(shape), F32, kind="ExternalInput").ap()

    def dout(name, shape):
        return nc.dram_tensor(name, list(shape), F32, kind="ExternalOutput").ap()

    if not dry:
        xp = din("xp", [128 + SEQ_CORE, D]); xs = din("xs", [128, D]); mem = din("mem", [256, D])
        ck = din("ck", [16, 128, 128]); cv = din("cv", [16, 128, 128]); spool = din("spool", [16, 15, 512])
        cmk = din("cmk", [16, 256, D]); cmv = din("cmv", [16, 256, D])
        w_in = din("w_in", [D, 1280]); w_pool = din("w_pool", [4, 128, 128]); w_out = din("w_out", [D, D])
        w_cq = din("w_cq", [D, D]); w_ck = din("w_ck", [D, D]); w_cv = din("w_cv", [D, D]); w_co = din("w_co", [D, D])
        w_up = din("w_up", [D, 4 * D]); w_down = din("w_down", [4 * D, D])
        gvec_d = din("gvec", [128, 40]); pscale_d = din("pscale", [128, 4])
        sinkp_d = din("sinkp", [128, 8]); sinks_d = din("sinks", [128, 8])
        biasg_d = din("biasg", [128, 8 * 256]); biasf_d = din("biasf", [128, 8 * 256]); biass_d = din("biass", [128, 2 * 160])
        invc_d = din("invc", [128, 64])
        yp = dout("yp", [SEQ_CORE, D]); ys = dout("ys", [128, D])
        wkp = dout("wkp", [128, 128]); wvp = dout("wvp", [128, 128]); poolp = dout("poolp", [15, 512])
        memk_o = dout("memk", [256, D]); memv_o = dout("memv", [256, D])
        wks = dout("wks", [16, 128, 128]); wvs = dout("wvs", [16, 128, 128]); pools = dout("pools", [16, 15, 512])

    def sb(name, shape, dt):
        return es.enter_context(nc.sbuf_tensor("sb_" + name, list(shape), dt))

    xTs = [sb(f"xT{i}", [128, 8, NT_P], F32) for i in range(2)]; B_xTs = [Buf(f"xT{i}") for i in range(2)]
    xT, B_xT = xTs[0], B_xTs[0]
    hT = sb("hT", [128, 8, NT_P], BF16); B_hT = Buf("hT")
    rstd = sb("rstd", [128, NT_P], F32); B_rstd = Buf("rstd")
    aoT = sb("aoT", [128, 8, NT_P], BF16); B_aoT = Buf("aoT")
    qT = sb("qT", [128, 4, NT_P], BF16); B_qT = Buf("qT")
    kT = sb("kT", [128, 128 + NT_P], BF16); B_kT = Buf("kT")
    vT = sb("vT", [128, NT_P], BF16); B_vT = Buf("vT")
    vtok = sb("vtok", [128, 5, 128], BF16); B_vtok = Buf("vtok")
    kv32 = sb("kv32", [128, 2, 128], F32); B_kv32 = Buf("kv32")
    dT = sb("dT", [128, 4, NT_P], BF16); B_dT = Buf("dT")
    pexp = sb("pexp", [128, 8, 256], BF16); B_pexp = Buf("pexp")
    pT = sb("pT", [128, 8, 2, 128], BF16); B_pT = Buf("pT")
    Dg = sb("Dg", [128, 8, 128], BF16); B_Dg = Buf("Dg")
    pTc = sb("pTc", [128, 4, 2, 128], BF16); B_pTc = Buf("pTc")
    bias = sb("bias", [128, 8, 256], F32); B_bias = Buf("bias")
    biass = sb("biass", [128, 2, 160], F32); B_biass = Buf("biass")
    memkT = sb("memkT", [128, 8, 256], BF16); B_memkT = Buf("memkT")
    memv = sb("memv", [128, 2, D], BF16); B_memv = Buf("memv")
    ring = sb("ring", [128, RING, 8, 128], BF16)
    xin = [sb(f"xin{i}", [128, D], F32) for i in range(2)]; B_xin = [Buf(f"xin{i}") for i in range(2)]
    yst, B_yst = xin, B_xin
    ident = sb("ident", [128, 128], BF16); identf = sb("identf", [128, 128], F32); B_const = Buf("const")
    ones = sb("ones", [128, 128], BF16)
    gvec = sb("gvec", [128, 5, 8], F32); pscale = sb("pscale", [128, 4], F32)
    sinkp = sb("sinkp", [128, 8], F32); sinks = sb("sinks", [128, 8], F32)
    invc = sb("invc", [128, 4, 16], F32)
    wpool = sb("wpool", [128, 4, 128], BF16); B_wpool = Buf("wpool")
    st = sb("st", [128, 64], F32); B_st = Buf("st")
    relu_t = [sb(f"relu{i}", [128, NT_P], BF16) for i in range(2)]; B_relu = [Buf(f"relu{i}") for i in range(2)]
    ost = sb("ost", [128, 512], F32); B_ost = Buf("ost")
    carryU = sb("carryU", [128, 4, 16], F32); B_cU = Buf("carryU")

    R2 = 32 * NT_P * 2
    XO = R2 + 28672
    AR = XO + 10240
    arena = sb("arena", [128, AR // 2], BF16)

    def av(off, nbytes, dt, pat=None, **kw):
        v = arena[:, off // 2:(off + nbytes) // 2]
        if dt is F32:
            v = v.bitcast(F32)
        if pat:
            v = v.rearrange(pat, **kw)
        return v

    hidT = av(0, 32 * NT_P * 2, BF16, "p (k t) -> p k t", k=32); B_hid = Buf("hidT")
    yT = av(0, 8 * NT_P * 4, F32, "p (k t) -> p k t", k=8); B_yT = Buf("yT")
    WU = 16 + NT_P
    WH = 16 + 256
    U = av(R2, 4 * WU * 4, F32, "p (g t) -> p g t", g=4); B_U = Buf("U")
    SA = av(R2 + 4 * WU * 4, 4 * WH * 4, F32, "p (g t) -> p g t", g=4); B_SA = Buf("SA")
    SB = av(R2 + 4 * WU * 4 + 4 * WH * 4, 4 * WH * 4, F32, "p (g t) -> p g t", g=4); B_SB = Buf("SB")
    o_sb = R2 + 4 * WU * 4 + 8 * WH * 4
    sbias = av(o_sb, 8 * 256 * 4, F32, "p (u t) -> p u t", u=8); B_sbias = Buf("sbias")
    assert o_sb + 8192 <= XO
    kcT = av(XO, 16 * 128 * 2, BF16, "p (b t) -> p b t", b=16); B_kcT = Buf("kcT")
    vc = av(XO + 4096, 16 * 128 * 2, BF16, "p (b t) -> p b t", b=16); B_vc = Buf("vc")
    qs2 = av(XO + 8192, 16 * 32 * 2, BF16, "p (b t) -> p b t", b=16); B_qs2 = Buf("qs2")
    vnq = av(XO + 9216, 4 * 128 * 2, BF16, "p (i t) -> p i t", i=4); B_vnq = Buf("vnq")
    Kb = [av(R2 + i * 4096, 4096, BF16, "p (m t) -> p m t", m=2) for i in range(2)]; B_Kb = [Buf(f"Kb{i}") for i in range(2)]
    KbT = [av(R2 + 8192 + i * 4096, 4096, BF16, "p (c t) -> p c t", c=8) for i in range(2)]; B_KbT = [Buf(f"KbT{i}") for i in range(2)]
    Vb = [av(R2 + 16384 + i * 4096, 4096, BF16, "p (m t) -> p m t", m=2) for i in range(2)]; B_Vb = [Buf(f"Vb{i}") for i in range(2)]
    qpad = [av(R2 + 24576 + i * 2048, 2048, BF16, "p (c t) -> p c t", c=8) for i in range(2)]; B_qpad = [Buf(f"qpad{i}") for i in range(2)]
    memst = av(R2, 8192, F32, "p (m t) -> p m t", m=2); B_memst = Buf("memst")
    mkst = av(R2 + 8192, 8192, F32, "p (m t) -> p m t", m=2); B_mkst = Buf("mkst")
    alias(B_hid, B_yT)
    gX = [B_U, B_SA, B_SB, B_sbias]
    gY = B_Kb + B_KbT + B_Vb + B_qpad
    gZ = [B_memst, B_mkst]
    for ga, gb in ((gX, gY), (gX, gZ), (gY, gZ)):
        for a in ga:
            for b in gb:
                a.al.append(b)
                b.al.append(a)

    ps = [es.enter_context(nc.psum_tensor(f"ps{i}", [128, 512], F32)) for i in range(8)]
    B_ps = [Buf(f"ps{i}", excl=True) for i in range(8)]
    dctr = [0]

    def dbank():
        i = dctr[0] % 3
        dctr[0] += 1
        return ps[i], B_ps[i]
    PS_S = [3, 4]
    PS_T = [5, 6]
    PS_O = 7
    sctr = [0]
    tctr = [0]

    def sbank():
        i = PS_S[sctr[0] % 2]; sctr[0] += 1
        return ps[i], B_ps[i]

    def tbank():
        i = PS_T[tctr[0] % 2]; tctr[0] += 1
        return ps[i], B_ps[i]

    def scratch_fn(n):
        return nc.dram_tensor("wscratch", [n, 128, 8, 128], BF16, kind="Internal").ap()
    W = WStream(P, ring, sched, scratch_fn)
    if not dry:
        def std_src(wap):
            v = wap.rearrange("(k p) (m c) -> p m k c", p=128, c=128)
            return lambda m: [((lambda s: s), v[:, m])]
        W.src["ck"] = std_src(w_ck); W.src["cv"] = std_src(w_cv)
        W.src["cq"] = std_src(w_cq); W.src["co"] = std_src(w_co); W.src["up"] = std_src(w_up)
        vin_q = w_in[:, 0:512].rearrange("(k p) (kv g d) -> p g k kv d", p=128, kv=2, g=4, d=64)
        vin_r = w_in[:, 512:1280].rearrange("(k p) (m c) -> p m k c", p=128, c=128)

        def in_src(m):
            if m < 4:
                return [((lambda s: s[:, :, 0:64]), vin_q[:, m, :, 0, :]),
                        ((lambda s: s[:, :, 64:128]), vin_q[:, m, :, 1, :])]
            return [((lambda s: s), vin_r[:, m - 4])]
        W.src["in"] = in_src
        vo_a = w_out[0:512, :].rearrange("(kv g d) (m c) -> kv d m g c", kv=2, g=4, d=64, c=128)
        vo_p = w_out[512:1024, :].rearrange("(k p) (m c) -> p m k c", p=128, c=128)

        def out_src(m):
            return [((lambda s: s[0:64, 0:4, :]), vo_a[0, :, m]),
                    ((lambda s: s[64:128, 0:4, :]), vo_a[1, :, m]),
                    ((lambda s: s[:, 4:8, :]), vo_p[:, m])]
        W.src["out"] = out_src
        vdn = w_down.rearrange("(q k p) (m c) -> p m q k c", p=128, k=8, c=128)
        W.src["down"] = lambda mq: [((lambda s: s), vdn[:, mq // 4, mq % 4])]

    if not dry:
        P.op("pool", lambda e: e.memset(identf[:], 0.0), writes=[B_const])
        P.op("pool", lambda e: e.iota(identf[:], pattern=[[1, 128]], base=0, channel_multiplier=-1,
                                      allow_small_or_imprecise_dtypes=True), writes=[B_const])
        P.op("dve", lambda e: e.tensor_single_scalar(out=ident[:], in_=identf[:], scalar=0.0, op=ALU.is_equal),
             reads=[B_const], writes=[B_const])
        P.op("dve", lambda e: e.tensor_single_scalar(out=identf[:], in_=identf[:], scalar=0.0, op=ALU.is_equal),
             writes=[B_const])
        P.op("dve", lambda e: e.memset(ones[:], 1.0), writes=[B_const])
        for (dst, src) in ((gvec[:].rearrange("p a b -> p (a b)"), gvec_d), (pscale[:], pscale_d), (sinkp[:], sinkp_d),
                           (sinks[:], sinks_d), (invc[:].rearrange("p a b -> p (a b)"), invc_d),
                           (biass[:].rearrange("p a b -> p (a b)"), biass_d)):
            P.dma("sp", "cst", dst, src[:, :], writes=[B_const])
        P.dma("sp", "biasld", bias[:].rearrange("p a b -> p (a b)"), biasf_d[:, :], writes=[B_bias])
        P.dma("pool", "wpool", wpool[:], w_pool.rearrange("g c e -> c g e"), writes=[B_wpool])

    CONST = [B_const]

    class _Stop(Exception):
        pass

    def finish():
        for key, val in P.cnt.items():
            if key not in ("pe", "act", "dve", "pool"):
                P._wait("sp", (key, val))
        for e_ in ("pe", "act", "dve", "pool"):
            if P.cnt.get(e_, 0):
                P._wait("sp", (e_, P.cnt[e_]))
        return P, W
    if STAGE == -1:
        return finish()

    def run(gen):
        if gen is not None:
            for _ in gen:
                pass

    def advance(gen, n=1, until=None):
        if gen is None:
            return
        if until is not None:
            for v in gen:
                if v == until:
                    return
            return
        for _ in range(n):
            try:
                next(gen)
            except StopIteration:
                return

    def g_load_x(src_rows, ntiles, X):
        dst, dstB = xTs[X], B_xTs[X]
        for j in range(ntiles):
            xb, Bx = xin[j % 2], B_xin[j % 2]
            P.dma("sp", "x", xb[:], src_rows(j), writes=[Bx])
            for hf in range(2):
                pb, Bp = dbank()
                pv = pb[:].rearrange("p (c t) -> p c t", c=4)
                P.mm([(lambda e, c=c, pv=pv, xb=xb, hf=hf: e.transpose(out=pv[:, c, :], in_=xb[:, (hf * 4 + c) * 128:(hf * 4 + c + 1) * 128],
                                                                       identity=identf[:])) for c in range(4)],
                     reads=[Bx] + CONST, writes=[Bp])
                P.op("act" if hf == 0 else "dve",
                     (lambda e, pv=pv, hf=hf, j=j: e.activation(out=dst[:, hf * 4:hf * 4 + 4, j * 128:(j + 1) * 128], in_=pv, func=AF.Copy))
                     if hf == 0 else
                     (lambda e, pv=pv, hf=hf, j=j: e.tensor_copy(out=dst[:, hf * 4:hf * 4 + 4, j * 128:(j + 1) * 128], in_=pv)),
                     reads=[Bp], writes=[dstB])
                yield

    def norm(src, Bsrc, gi, NT, dst, Bdst):
        P.op("act", lambda e: e.activation(out=hT[:, :, :NT], in_=src[:, :, :NT], func=AF.Square),
             reads=[Bsrc], writes=[B_hT])
        pb, Bp = dbank()
        P.mm([(lambda e, k=k: e.matmul(pb[:, :NT], lhsT=ones[:], rhs=hT[:, k, :NT], start=(k == 0), stop=(k == 7)))
              for k in range(8)], reads=[B_hT] + CONST, writes=[Bp])
        P.op("act", lambda e: e.activation(out=rstd[:, :NT], in_=pb[:, :NT], func=AF.Ln, scale=1.0 / D, bias=EPS),
             reads=[Bp], writes=[B_rstd])
        P.op("act", lambda e: e.activation(out=rstd[:, :NT], in_=rstd[:, :NT], func=AF.Exp, scale=-0.5),
             reads=[B_rstd], writes=[B_rstd])
        for k in range(8):
            P.op("dve", lambda e, k=k: e.scalar_tensor_tensor(out=dst[:, k, :NT], in0=src[:, k, :NT], scalar=gvec[:, gi, k:k + 1],
                                                              in1=rstd[:, :NT], op0=ALU.mult, op1=ALU.mult),
                 reads=[Bsrc, B_rstd] + CONST, writes=[Bdst])

    def g_dense(wname, units, NT, rhs_fn, Brhs, evac, kgroups=1):
        for m in units:
            pb, Bp = dbank()
            fns, Bs = [], []
            for q in range(kgroups):
                slot, Bslot = W.get(wname, m * kgroups + q if kgroups > 1 else m)
                Bs.append(Bslot)
                for k in range(8):
                    fns.append(lambda e, slot=slot, k=k, q=q, pb=pb: e.matmul(
                        pb[:, :NT], lhsT=slot[:, k, :], rhs=rhs_fn(q * 8 + k),
                        start=(q == 0 and k == 0), stop=(q == kgroups - 1 and k == 7)))
            P.mm(fns, reads=Bs + [Brhs], writes=[Bp])
            evac(m, pb, Bp)
            for _ in range(kgroups):
                yield

    def dense(*a, **kw):
        run(g_dense(*a, **kw))

    def resid_evac(NT, X):
        xt, Bxt = xTs[X], B_xTs[X]

        def f(m, pb, Bp):
            P.op("dve", lambda e: e.tensor_tensor(out=xt[:, m, :NT], in0=pb[:, :NT], in1=xt[:, m, :NT], op=ALU.add),
                 reads=[Bp], writes=[Bxt])
        return f

    def diag_T(nu, nkc, p_src, Bp_src, dst4, dst_fn, Bdst, kw):
        items = [(u, kc) for u in range(nu) for kc in range(nkc)]
        full = all(w == 128 for w in kw)
        for bi, i0 in enumerate(range(0, len(items), 4)):
            chunk = items[i0:i0 + 4]
            pb, Bp = tbank()
            pv = pb[:].rearrange("p (s t) -> p s t", s=4)
            P.mm([(lambda e, s=s, u=u, kc=kc, pv=pv: e.matmul(pv[0:kw[kc], s, :], lhsT=p_src[:, u, kc * 128:kc * 128 + kw[kc]],
                                                              rhs=Dg[:, u, :], start=True, stop=True))
                  for s, (u, kc) in enumerate(chunk)], reads=[Bp_src, B_Dg], writes=[Bp])
            if full:
                u0 = chunk[0][0]
                if bi % 2 == 0:
                    P.op("act", lambda e, pb=pb, u0=u0: e.activation(out=dst4(u0), in_=pb[:, 0:512], func=AF.Copy), reads=[Bp], writes=[Bdst])
                else:
                    P.op("dve", lambda e, pb=pb, u0=u0: e.tensor_copy(out=dst4(u0), in_=pb[:, 0:512]), reads=[Bp], writes=[Bdst])
            else:
                for s, (u, kc) in enumerate(chunk):
                    P.op("act" if bi % 2 == 0 else "dve",
                         (lambda e, s=s, u=u, kc=kc, pv=pv: e.activation(out=dst_fn(u, kc), in_=pv[0:kw[kc], s, :], func=AF.Copy))
                         if bi % 2 == 0 else
                         (lambda e, s=s, u=u, kc=kc, pv=pv: e.tensor_copy(out=dst_fn(u, kc), in_=pv[0:kw[kc], s, :])),
                         reads=[Bp], writes=[Bdst])
            yield

    def make_Dg(nu):
        P.op("dve", lambda e: e.tensor_tensor(out=Dg[:, 0:nu, :], in0=ident[:].unsqueeze(1).to_broadcast([128, nu, 128]),
                                              in1=st[:, 56:56 + nu].unsqueeze(2).to_broadcast([128, nu, 128]), op=ALU.mult),
             reads=[B_st] + CONST, writes=[B_Dg])

    def win_softmax(nu, width, sink_ap):
        P.op("dve", lambda e: e.tensor_reduce(out=st[:, 0:nu], in_=sbias[:, 0:nu, 0:width], axis=AX.X, op=ALU.max),
             reads=[B_sbias], writes=[B_st])
        P.op("dve", lambda e: e.tensor_tensor(out=st[:, 8:8 + nu], in0=st[:, 0:nu], in1=sink_ap, op=ALU.max),
             reads=[B_st] + CONST, writes=[B_st])
        P.op("dve", lambda e: e.tensor_scalar(out=st[:, 16:16 + nu], in0=st[:, 8:8 + nu], scalar1=-1.0, scalar2=None, op0=ALU.mult),
             reads=[B_st], writes=[B_st])
        P.op("dve", lambda e: e.tensor_tensor(out=st[:, 24:24 + nu], in0=sink_ap, in1=st[:, 16:16 + nu], op=ALU.add),
             reads=[B_st] + CONST, writes=[B_st])
        for u in range(nu):
            P.op("act", lambda e, u=u: e.activation(out=pexp[:, u, 0:width], in_=sbias[:, u, 0:width], func=AF.Exp,
                                                    bias=st[:, 16 + u:17 + u], scale=1.0, accum_out=st[:, 32 + u:33 + u]),
                 reads=[B_sbias, B_st], writes=[B_pexp, B_st])
        P.op("act", lambda e: e.activation(out=st[:, 40:40 + nu], in_=st[:, 24:24 + nu], func=AF.Exp),
             reads=[B_st], writes=[B_st])
        P.op("dve", lambda e: e.tensor_tensor(out=st[:, 48:48 + nu], in0=st[:, 32:32 + nu], in1=st[:, 40:40 + nu], op=ALU.add),
             reads=[B_st], writes=[B_st])
        P.op("dve", lambda e: e.reciprocal(out=st[:, 56:56 + nu], in_=st[:, 48:48 + nu]), reads=[B_st], writes=[B_st])
        make_Dg(nu)

    def cross_softmax(score_banks):
        for hp, (pb, Bp) in enumerate(score_banks):
            pv = pb[:].rearrange("p (h t) -> p h t", h=2)
            P.op("dve", lambda e, pv=pv, hp=hp: e.tensor_reduce(out=st[:, 2 * hp:2 * hp + 2], in_=pv, axis=AX.X, op=ALU.max),
                 reads=[Bp], writes=[B_st])
        P.op("dve", lambda e: e.tensor_scalar(out=st[:, 16:20], in0=st[:, 0:4], scalar1=-1.0 / 16.0, scalar2=None, op0=ALU.mult),
             reads=[B_st], writes=[B_st])
        for hp, (pb, Bp) in enumerate(score_banks):
            pv = pb[:].rearrange("p (h t) -> p h t", h=2)
            for hh in range(2):
                h = 2 * hp + hh
                P.op("act", lambda e, pv=pv, hh=hh, h=h: e.activation(out=pexp[:, h, :], in_=pv[:, hh, :], func=AF.Exp,
                                                                      bias=st[:, 16 + h:17 + h], scale=1.0 / 16.0,
                                                                      accum_out=st[:, 32 + h:33 + h]),
                     reads=[Bp, B_st], writes=[B_pexp, B_st])
        P.op("dve", lambda e: e.reciprocal(out=st[:, 56:60], in_=st[:, 32:36]), reads=[B_st], writes=[B_st])
        make_Dg(4)

    def out_tok_major(srcs, Bsrcs, ncols_each, dst_dma):
        pb, Bp = dbank()
        pv = pb[:].rearrange("p (c t) -> p c t", c=4)
        n = len(srcs)
        P.mm([(lambda e, i=i: e.transpose(out=pv[:, i, :], in_=srcs[i], identity=identf[:])) for i in range(n)],
             reads=list(Bsrcs) + CONST, writes=[Bp])
        P.op("dve", lambda e: e.tensor_copy(out=ost[:, 0:n * 128], in_=pb[:, 0:n * 128]), reads=[Bp], writes=[B_ost])
        dst_dma()

    for t in range(2):
        P.dma("sp", "memld", memst[:, t, :], mem[t * 128:(t + 1) * 128, :] if not dry else None, writes=[B_memst])
    for t in range(2):
        for hf in range(2):
            pb, Bp = dbank()
            pv = pb[:].rearrange("p (c t) -> p c t", c=4)
            P.mm([(lambda e, c=c, pv=pv, t=t, hf=hf: e.transpose(out=pv[:, c, :], in_=memst[:, t, (hf * 4 + c) * 128:(hf * 4 + c + 1) * 128],
                                                                 identity=identf[:])) for c in range(4)],
                 reads=[B_memst] + CONST, writes=[Bp])
            P.op("act", lambda e, pv=pv, hf=hf, t=t: e.activation(out=xT[:, hf * 4:hf * 4 + 4, t * 128:(t + 1) * 128], in_=pv, func=AF.Copy),
                 reads=[Bp], writes=[B_xT])
    if STAGE == -2:
        return finish()
    norm(xT, B_xT, 2, 256, hT, B_hT)
    if STAGE == -3:
        return finish()
    for (wn, is_k) in (("ck", True), ("cv", False)):
        for mh in range(2):
            yb = [sbank(), sbank()]
            for m4 in range(4):
                m = mh * 4 + m4
                slot, Bslot = W.get(wn, m)
                if is_k:
                    pb, Bp = dbank()
                    P.mm([(lambda e, k=k, slot=slot, pb=pb: e.matmul(pb[:, :256], lhsT=slot[:, k, :], rhs=hT[:, k, :256],
                                                                    start=(k == 0), stop=(k == 7))) for k in range(8)],
                         reads=[Bslot, B_hT], writes=[Bp])
                    P.op("act", lambda e, m=m, pb=pb: e.activation(out=memkT[:, m, :], in_=pb[:, :256], func=AF.Copy),
                         reads=[Bp], writes=[B_memkT])
                for t in range(2):
                    yp_, Byp = yb[t]
                    P.mm([(lambda e, k=k, slot=slot, yp_=yp_, t=t, m4=m4: e.matmul(
                        yp_[:, m4 * 128:(m4 + 1) * 128], lhsT=hT[:, k, t * 128:(t + 1) * 128], rhs=slot[:, k, :],
                        start=(k == 0), stop=(k == 7))) for k in range(8)],
                        reads=[Bslot, B_hT], writes=[Byp])
            for t in range(2):
                yp_, Byp = yb[t]
                P.op("dve", lambda e, yp_=yp_, t=t, mh=mh: e.tensor_copy(out=mkst[:, t, mh * 512:(mh + 1) * 512], in_=yp_[:, :]),
                     reads=[Byp], writes=[B_mkst])
                if not is_k:
                    P.op("act", lambda e, yp_=yp_, t=t, mh=mh: e.activation(out=memv[:, t, mh * 512:(mh + 1) * 512], in_=yp_[:, :], func=AF.Copy),
                         reads=[Byp], writes=[B_memv])
        for t in range(2):
            P.dma("sp", "memout", (memk_o if is_k else memv_o)[t * 128:(t + 1) * 128, :] if not dry else None, mkst[:, t, :],
                  reads=[B_mkst])

    def geom(kind):
        sample, halo = (kind == "S"), (kind == "H")
        NT = 128 if (sample or halo) else NT_P
        return sample, halo, NT, NT // 128

    def early(kind, gi, X):
        sample, halo, NT, ntl = geom(kind)
        xt, Bxt = xTs[X], B_xTs[X]
        if sample:
            yield from g_load_x(lambda j: xs[:, :], 1, X)
        elif halo:
            yield from g_load_x(lambda j: xp[0:128, :], 1, X)
        else:
            yield from g_load_x(lambda j: xp[128 + gi * NT_P + j * 128: 128 + gi * NT_P + (j + 1) * 128, :], ntl, X)
        yield "L"
        norm(xt, Bxt, 0, NT, hT, B_hT)
        last = (kind == "P" and gi == NG_P - 1)
        want32 = last or sample
        Us = U[:, :, 0:384].rearrange("p g (b c) -> p g b c", b=16)

        if sample:
            for hb in range(2):
                P.dma("pool", "ckld", vc[:, hb * 8:(hb + 1) * 8, :], cv[hb * 8:(hb + 1) * 8].rearrange("b s f -> s b f"), writes=[B_vc])
            kst = pexp[:].rearrange("p u t -> p (u t)").rearrange("p (b f) -> p b f", b=16)
            for hb in range(2):
                P.dma("pool", "ckld", kst[:, hb * 8:(hb + 1) * 8, :], ck[hb * 8:(hb + 1) * 8].rearrange("b s f -> s b f"), writes=[B_pexp])
            for hb in range(2):
                pb, Bp = tbank()
                pv = pb[:].bitcast(BF16).rearrange("p (b t) -> p b t", b=8)
                P.mm([(lambda e, b=b, pv=pv, hb=hb: e.transpose(out=pv[:, b, :], in_=kst[:, hb * 8 + b, :], identity=ident[:])) for b in range(8)],
                     reads=[B_pexp] + CONST, writes=[Bp])
                P.op("dve", lambda e, pv=pv, hb=hb: e.tensor_copy(out=kcT[:, hb * 8:(hb + 1) * 8, :], in_=pv), reads=[Bp], writes=[B_kcT])
            P.op("dve", lambda e: e.memset(U[:, :, 0:384], 0.0), writes=[B_U])
            for hb in range(2):
                P.dma("sp", "spld", xin[hb][0:120, 0:512], spool[hb * 8:(hb + 1) * 8].rearrange("b r f -> (b r) f"), writes=[B_xin[hb]])
                pb, Bp = dbank()
                pv = pb[:].rearrange("p (c t) -> p c t", c=4)
                P.mm([(lambda e, c=c, pv=pv, hb=hb: e.transpose(out=pv[:, c, 0:120], in_=xin[hb][0:120, c * 128:(c + 1) * 128],
                                                                identity=identf[0:120, 0:120])) for c in range(4)],
                     reads=[B_xin[hb]] + CONST, writes=[Bp])
                for c in range(4):
                    P.op("dve", lambda e, c=c, pv=pv, hb=hb: e.tensor_copy(
                        out=Us[:, c, hb * 8:(hb + 1) * 8, 1:16], in_=pv[:, c, 0:120].rearrange("p (b r) -> p b r", b=8)),
                        reads=[Bp], writes=[B_U])
            yield

        def in_evac(m, pb, Bp):
            if m < 4:
                P.op("act", lambda e: e.activation(out=qT[:, m, :NT], in_=pb[:, :NT], func=AF.Copy), reads=[Bp], writes=[B_qT])
            elif m == 4:
                P.op("act", lambda e: e.activation(out=kT[:, 128:128 + NT], in_=pb[:, :NT], func=AF.Copy), reads=[Bp], writes=[B_kT])
                if want32:
                    P.op("dve", lambda e: e.tensor_copy(out=kv32[:, 0, :], in_=pb[:, NT - 128:NT]), reads=[Bp], writes=[B_kv32])
            elif m == 5:
                P.op("act", lambda e: e.activation(out=vT[:, :NT], in_=pb[:, :NT], func=AF.Copy), reads=[Bp], writes=[B_vT])
                if want32:
                    P.op("dve", lambda e: e.tensor_copy(out=kv32[:, 1, :], in_=pb[:, NT - 128:NT]), reads=[Bp], writes=[B_kv32])
            else:
                g = m - 6
                if sample:
                    P.op("dve", lambda e: e.tensor_copy(out=Us[:, g, :, 16:24], in_=pb[:, 0:128].rearrange("p (b t) -> p b t", b=16)),
                         reads=[Bp], writes=[B_U])
                else:
                    P.op("dve", lambda e: e.tensor_copy(out=U[:, g, 16:16 + NT], in_=pb[:, :NT]), reads=[Bp], writes=[B_U])
        yield from g_dense("in", list(range(4, 10)) if halo else list(range(10)), NT, lambda k: hT[:, k, :NT], B_hT, in_evac)
        yield "P1"

        for j0 in range(0, ntl, 4):
            pb, Bp = tbank()
            pv = pb[:].bitcast(BF16)[:, 0:512].rearrange("p (j t) -> p j t", j=4)
            P.mm([(lambda e, j=j, pv=pv: e.transpose(out=pv[:, j - j0, :], in_=vT[:, j * 128:(j + 1) * 128], identity=ident[:]))
                  for j in range(j0, min(ntl, j0 + 4))], reads=[B_vT] + CONST, writes=[Bp])
            nj = min(ntl, j0 + 4) - j0
            P.op("dve", lambda e, pv=pv, j0=j0, nj=nj: e.tensor_copy(out=vtok[:, 1 + j0:1 + j0 + nj, :], in_=pv[:, 0:nj, :]),
                 reads=[Bp], writes=[B_vtok])
        yield

        def carry():
            P.op("dve", lambda e: e.tensor_copy(out=kT[:, 0:128], in_=kT[:, NT:NT + 128]), reads=[B_kT], writes=[B_kT])
            P.op("dve", lambda e: e.tensor_copy(out=vtok[:, 0, :], in_=vtok[:, ntl, :]), reads=[B_vtok], writes=[B_vtok])

        if halo:
            carry()
            P.op("dve", lambda e: e.tensor_copy(out=carryU[:], in_=U[:, :, NT:NT + 16]), reads=[B_U], writes=[B_cU])
            return

        if want32:
            if last:
                def dd():
                    P.dma("sp", "kvout", wkp[:, :], ost[:, 0:128], reads=[B_ost])
                    P.dma("sp", "kvout", wvp[:, :], ost[:, 128:256], reads=[B_ost])
            else:
                def dd():
                    for t in range(8):
                        P.dma("sp", "kvout", wks[:, 120 + t, :], ost[t:128:8, 0:128], reads=[B_ost])
                        P.dma("sp", "kvout", wvs[:, 120 + t, :], ost[t:128:8, 128:256], reads=[B_ost])
                    P.dma("sp", "d2d_k", wks[:, 0:120, :], ck[:, 8:128, :])
                    P.dma("sp", "d2d_v", wvs[:, 0:120, :], cv[:, 8:128, :])
            out_tok_major([kv32[:, 0, :], kv32[:, 1, :]], [B_kv32], 128, dd)
            yield

        if not sample:
            P.op("dve", lambda e: e.tensor_copy(out=U[:, :, 0:16], in_=carryU[:]), reads=[B_cU], writes=[B_U])
        for hh in range(2):
            if sample:
                Wd, c0 = 192, hh * 192
            else:
                Wd, c0 = 16 + NT // 2, hh * (NT // 2)
            Uh = U[:, :, c0:c0 + Wd]
            P.op("dve", lambda e, Uh=Uh, Wd=Wd: e.tensor_tensor(out=SA[:, :, 1:Wd], in0=Uh[:, :, 1:Wd], in1=Uh[:, :, 0:Wd - 1], op=ALU.add),
                 reads=[B_U], writes=[B_SA])
            P.op("dve", lambda e, Wd=Wd: e.tensor_tensor(out=SB[:, 1:4, 3:Wd], in0=SA[:, 1:4, 3:Wd], in1=SA[:, 1:4, 1:Wd - 2], op=ALU.add),
                 reads=[B_SA], writes=[B_SB])
            yield
            P.op("dve", lambda e, Wd=Wd: e.tensor_tensor(out=SA[:, 2:4, 7:Wd], in0=SB[:, 2:4, 7:Wd], in1=SB[:, 2:4, 3:Wd - 4], op=ALU.add),
                 reads=[B_SB], writes=[B_SA])
            P.op("dve", lambda e, Wd=Wd: e.tensor_tensor(out=SB[:, 3, 15:Wd], in0=SA[:, 3, 15:Wd], in1=SA[:, 3, 7:Wd - 8], op=ALU.add),
                 reads=[B_SA], writes=[B_SB])
            yield
            for g in range(4):
                S_, BS_ = (SA, B_SA) if g % 2 == 0 else (SB, B_SB)
                if sample:
                    sv = S_[:, g, 0:192].rearrange("p (b c) -> p b c", b=8)[:, :, 16:24]
                    uv = Uh[:, g, :].rearrange("p (b c) -> p b c", b=8)[:, :, 16:24]
                    dv = dT[:, g, hh * 64:(hh + 1) * 64].rearrange("p (b t) -> p b t", b=8)
                else:
                    sv, uv, dv = S_[:, g, 16:Wd], Uh[:, g, 16:Wd], dT[:, g, c0:c0 + NT // 2]
                P.op("dve", lambda e, sv=sv, uv=uv, dv=dv, g=g: e.scalar_tensor_tensor(out=dv, in0=sv, scalar=1.0 / (2 << g), in1=uv,
                                                                                   op0=ALU.mult, op1=ALU.subtract),
                     reads=[BS_, B_U], writes=[B_dT])
                if kind == "P" and gi == 0 and hh == 0:
                    P.op("dve", lambda e, S_=S_, g=g: e.tensor_tensor(out=st[:, 0:16], in0=S_[:, g, 16:32], in1=invc[:, g, :], op=ALU.mult),
                         reads=[BS_] + CONST, writes=[B_st])
                    P.op("dve", lambda e, g=g: e.tensor_tensor(out=dT[:, g, 0:16], in0=st[:, 0:16], in1=U[:, g, 16:32], op=ALU.subtract),
                         reads=[B_st, B_U], writes=[B_dT])
            yield
        if last:
            def dd2():
                P.dma("sp", "poolout", poolp[:, :], ost[113:128, :], reads=[B_ost])
            out_tok_major([U[:, g, 16 + NT - 128:16 + NT] for g in range(4)], [B_U], 128, dd2)
        if sample:
            for g in range(4):
                P.op("dve", lambda e, g=g: e.tensor_copy(out=SA[:, g, 0:128].rearrange("p (b t) -> p b t", b=16), in_=Us[:, g, :, 16:24]),
                     reads=[B_U, B_dT], writes=[B_SA])

            def dd3():
                for t in range(8):
                    P.dma("sp", "poolout", pools[:, 7 + t, :], ost[t:128:8, :], reads=[B_ost])
                P.dma("sp", "d2d_p", pools[:, 0:7, :], spool[:, 8:15, :])
            out_tok_major([SA[:, g, 0:128] for g in range(4)], [B_SA], 128, dd3)
        else:
            P.op("dve", lambda e: e.tensor_copy(out=carryU[:], in_=U[:, :, NT:NT + 16]), reads=[B_U], writes=[B_cU])
        yield
        for g in range(4):
            pb, Bp = dbank()
            P.mm([lambda e, g=g, pb=pb: e.matmul(pb[:, :NT], lhsT=wpool[:, g, :], rhs=dT[:, g, :NT], start=True, stop=True)],
                 reads=[B_wpool, B_dT], writes=[Bp])
            P.op("act", lambda e, g=g, pb=pb: e.activation(out=aoT[:, 4 + g, :NT], in_=pb[:, :NT], func=AF.Copy, scale=pscale[:, g:g + 1]),
                 reads=[Bp] + CONST, writes=[B_aoT])
            yield

        if not sample:
            for j in range(ntl):
                if gi == 0 and j == 1:
                    P.dma("sp", "biasld", bias[:].rearrange("p a b -> p (a b)"), biasg_d[:, :], writes=[B_bias])
                for gp in range(2):
                    bk = [sbank(), sbank()]
                    fns = []
                    for g in (2 * gp, 2 * gp + 1):
                        for kv in range(2):
                            fns.append(lambda e, kv=kv, g=g, j=j, bk=bk: e.matmul(
                                bk[kv][0][:, (g % 2) * 256:(g % 2 + 1) * 256], lhsT=qT[kv * 64:(kv + 1) * 64, g, j * 128:(j + 1) * 128],
                                rhs=kT[kv * 64:(kv + 1) * 64, j * 128:j * 128 + 256], start=True, stop=True))
                    P.mm(fns, reads=[B_qT, B_kT], writes=[bk[0][1], bk[1][1]])
                    for kv in range(2):
                        u0 = 4 * gp + kv
                        P.op("dve", lambda e, kv=kv, u0=u0, bk=bk: e.scalar_tensor_tensor(
                            out=sbias[:, u0:u0 + 3:2, :], in0=bk[kv][0][:].rearrange("p (g t) -> p g t", g=2), scalar=0.125,
                            in1=bias[:, u0:u0 + 3:2, :], op0=ALU.mult, op1=ALU.add),
                            reads=[bk[kv][1], B_bias], writes=[B_sbias])
                    yield
                win_softmax(8, 256, sinkp[:, 0:8])
                yield
                yield from diag_T(8, 2, pexp, B_pexp, lambda u0: pT[:, u0:u0 + 2, :, :].rearrange("p u k t -> p (u k t)"), None, B_pT, [128, 128])
                po, Bpo = ps[PS_O], B_ps[PS_O]
                pov = po[:].rearrange("p (g t) -> p g t", g=4)
                fns = []
                for g in range(4):
                    for kv in range(2):
                        for kc in range(2):
                            fns.append(lambda e, g=g, kv=kv, kc=kc, j=j: e.matmul(
                                pov[kv * 64:(kv + 1) * 64, g, :], lhsT=vtok[:, j + kc, kv * 64:(kv + 1) * 64], rhs=pT[:, 2 * g + kv, kc, :],
                                start=(kc == 0), stop=(kc == 1)))
                P.mm(fns, reads=[B_vtok, B_pT], writes=[Bpo])
                P.op("act", lambda e, j=j: e.activation(out=aoT[:, 0:4, j * 128:(j + 1) * 128], in_=pov, func=AF.Copy),
                     reads=[Bpo], writes=[B_aoT])
                yield
            carry()
        else:
            P.op("dve", lambda e: e.tensor_copy(out=qs2[:].rearrange("p b (g t) -> p b g t", g=4),
                                                in_=qT[:, :, 0:128].rearrange("p g (b t) -> p b g t", b=16)), reads=[B_qT], writes=[B_qs2])
            pb, Bp = dbank()
            pvb = pb[:].bitcast(BF16)
            P.mm([(lambda e, i=i, pvb=pvb: e.transpose(out=pvb[0:32, i * 128:(i + 1) * 128], in_=vT[:, i * 32:(i + 1) * 32], identity=ident[:]))
                  for i in range(4)], reads=[B_vT] + CONST, writes=[Bp])
            P.op("dve", lambda e, pvb=pvb: e.tensor_copy(out=vnq[0:32, :, :], in_=pvb[0:32, 0:512].rearrange("p (i t) -> p i t", i=4)),
                 reads=[Bp], writes=[B_vnq])
            yield
            for i in range(4):
                bk = [sbank(), sbank()]
                fns = []
                for kv in range(2):
                    pvk = bk[kv][0]
                    for jq in range(4):
                        b = 4 * i + jq
                        fns.append(lambda e, kv=kv, jq=jq, b=b, pvk=pvk: e.matmul(
                            pvk[32 * jq:32 * jq + 32, 0:128], lhsT=qs2[kv * 64:(kv + 1) * 64, b, :], rhs=kcT[kv * 64:(kv + 1) * 64, b, :],
                            start=True, stop=True, tile_position=(kv * 64, 32 * jq)))
                    fns.append(lambda e, kv=kv, i=i, pvk=pvk: e.matmul(
                        pvk[:, 128:160], lhsT=qs2[kv * 64:(kv + 1) * 64, 4 * i:4 * i + 4, :].rearrange("p b t -> p (b t)"),
                        rhs=kT[kv * 64:(kv + 1) * 64, 128 + 32 * i:128 + 32 * i + 32], start=True, stop=True))
                P.mm(fns, reads=[B_qs2, B_kcT, B_kT], writes=[bk[0][1], bk[1][1]])
                for kv in range(2):
                    P.op("dve", lambda e, kv=kv, i=i, bk=bk: e.scalar_tensor_tensor(
                        out=sbias[:, 2 * i + kv, 0:160], in0=bk[kv][0][:, 0:160], scalar=0.125,
                        in1=biass[:, kv, :], op0=ALU.mult, op1=ALU.add),
                        reads=[bk[kv][1]] + CONST, writes=[B_sbias])
                yield
            win_softmax(8, 160, sinks[:, 0:8])
            yield
            yield from diag_T(8, 2, pexp, B_pexp, None, lambda u, kc: pT[0:(128 if kc == 0 else 32), u, kc, :], B_pT, [128, 32])
            for i in range(4):
                pb, Bp = dbank()
                fns = []
                for kv in range(2):
                    u = 2 * i + kv
                    for jq in range(4):
                        b = 4 * i + jq
                        fns.append(lambda e, kv=kv, jq=jq, b=b, u=u, pb=pb: e.matmul(
                            pb[kv * 64:(kv + 1) * 64, 32 * jq:32 * jq + 32], lhsT=vc[:, b, kv * 64:(kv + 1) * 64], rhs=pT[:, u, 0, 32 * jq:32 * jq + 32],
                            start=(jq == 0), stop=False, skip_group_check=True))
                for kv in range(2):
                    u = 2 * i + kv
                    fns.append(lambda e, kv=kv, u=u, i=i, pb=pb: e.matmul(
                        pb[kv * 64:(kv + 1) * 64, 0:128], lhsT=vnq[0:32, i, kv * 64:(kv + 1) * 64], rhs=pT[0:32, u, 1, :],
                        start=False, stop=True, skip_group_check=True))
                P.mm(fns, reads=[B_vc, B_pT, B_vnq], writes=[Bp])
                P.op("act", lambda e, pb=pb, i=i: e.activation(
                    out=aoT[:, 0:4, 32 * i:32 * i + 32].rearrange("p g (j t) -> p j g t", j=4),
                    in_=pb[:, 0:128].rearrange("p (j g t) -> p j g t", j=4, g=4), func=AF.Copy),
                    reads=[Bp], writes=[B_aoT])
                yield

    def late_pre(kind, gi, X, gen):
        sample, halo, NT, ntl = geom(kind)
        xt, Bxt = xTs[X], B_xTs[X]
        loaded = [gen is None]

        def step_load():
            if not loaded[0]:
                if next(gen, "L") == "L":
                    loaded[0] = True
        dense("out", list(range(8)), NT, lambda k: aoT[:, k, :NT], B_aoT, resid_evac(NT, X))
        norm(xt, Bxt, 1, NT, hT, B_hT)
        qcT, B_qcT = aoT, B_aoT
        ocT, B_ocT = hT, B_hT

        def cq_evac(m, pb, Bp):
            P.op("act", lambda e: e.activation(out=qcT[:, m, :NT], in_=pb[:, :NT], func=AF.Copy), reads=[Bp], writes=[B_qcT])
        dense("cq", list(range(8)), NT, lambda k: hT[:, k, :NT], B_hT, cq_evac)

        if not sample:
            for j in range(ntl):
                banks = [sbank(), sbank()]
                for hp, (pb, Bp) in enumerate(banks):
                    pv = pb[:].rearrange("p (h t) -> p h t", h=2)
                    fns = []
                    for hh in range(2):
                        h = 2 * hp + hh
                        for dc in range(2):
                            fns.append(lambda e, pv=pv, hh=hh, h=h, dc=dc, j=j: e.matmul(
                                pv[:, hh, :], lhsT=qcT[:, 2 * h + dc, j * 128:(j + 1) * 128], rhs=memkT[:, 2 * h + dc, :],
                                start=(dc == 0), stop=(dc == 1)))
                    P.mm(fns, reads=[B_qcT, B_memkT], writes=[Bp])
                cross_softmax(banks)
                step_load()
                run(diag_T(4, 2, pexp, B_pexp, lambda h0: pTc[:, h0:h0 + 2, :, :].rearrange("p u k t -> p (u k t)"), None, B_pTc, [128, 128]))
                for half in range(2):
                    pb, Bp = dbank()
                    pv = pb[:].rearrange("p (c t) -> p c t", c=4)
                    fns = []
                    for cc in range(4):
                        c = half * 4 + cc
                        h = c // 2
                        for mc in range(2):
                            fns.append(lambda e, pv=pv, cc=cc, c=c, h=h, mc=mc: e.matmul(
                                pv[:, cc, :], lhsT=memv[:, mc, c * 128:(c + 1) * 128], rhs=pTc[:, h, mc, :], start=(mc == 0), stop=(mc == 1)))
                    P.mm(fns, reads=[B_memv, B_pTc], writes=[Bp])
                    P.op("act" if half == 0 else "dve",
                         (lambda e, pv=pv, half=half, j=j: e.activation(out=ocT[:, half * 4:half * 4 + 4, j * 128:(j + 1) * 128], in_=pv, func=AF.Copy))
                         if half == 0 else
                         (lambda e, pv=pv, half=half, j=j: e.tensor_copy(out=ocT[:, half * 4:half * 4 + 4, j * 128:(j + 1) * 128], in_=pv)),
                         reads=[Bp], writes=[B_ocT])
                step_load()
        else:
            banks = [sbank(), sbank()]
            for i in range(2):
                P.op("dve", lambda e, i=i: e.memset(qpad[i][:], 0.0), writes=[B_qpad[i]])

            def ld_k(b):
                P.dma("pool", f"kb{b % 2}", Kb[b % 2][:], cmk[b].rearrange("(m p) f -> p m f", p=128), writes=[B_Kb[b % 2]])

            def ld_v(b):
                P.dma("pool", f"vb{b % 2}", Vb[b % 2][:], cmv[b].rearrange("(m p) f -> p m f", p=128), writes=[B_Vb[b % 2]])
            ld_k(0)
            ld_v(0)
            for b in range(16):
                s2 = b % 2
                if b + 1 < 16:
                    ld_k(b + 1)
                for mt in range(2):
                    pb, Bp = tbank()
                    pv = pb[:].bitcast(BF16).rearrange("p (c t) -> p c t", c=8)
                    P.mm([(lambda e, c=c, pv=pv, mt=mt, s2=s2: e.transpose(out=pv[:, c, :], in_=Kb[s2][:, mt, c * 128:(c + 1) * 128], identity=ident[:]))
                          for c in range(8)], reads=[B_Kb[s2]] + CONST, writes=[Bp])
                    P.op("act" if mt == 0 else "dve",
                         (lambda e, pv=pv, mt=mt, s2=s2: e.activation(out=KbT[s2][:, :, mt * 128:(mt + 1) * 128], in_=pv, func=AF.Copy))
                         if mt == 0 else
                         (lambda e, pv=pv, mt=mt, s2=s2: e.tensor_copy(out=KbT[s2][:, :, mt * 128:(mt + 1) * 128], in_=pv)),
                         reads=[Bp], writes=[B_KbT[s2]])
                if b >= 2:
                    P.op("dve", lambda e, s2=s2, b=b: e.memset(qpad[s2][:, :, (b - 2) * 8:(b - 1) * 8], 0.0), writes=[B_qpad[s2]])
                P.op("dve", lambda e, s2=s2, b=b: e.tensor_copy(out=qpad[s2][:, :, b * 8:(b + 1) * 8], in_=qcT[:, :, b * 8:(b + 1) * 8]),
                     reads=[B_qcT], writes=[B_qpad[s2]])
                for hp, (pb, Bp) in enumerate(banks):
                    pv = pb[:].rearrange("p (h t) -> p h t", h=2)
                    fns = []
                    for hh in range(2):
                        h = 2 * hp + hh
                        for dc in range(2):
                            fns.append(lambda e, pv=pv, hh=hh, h=h, dc=dc, s2=s2, b=b: e.matmul(
                                pv[:, hh, :], lhsT=qpad[s2][:, 2 * h + dc, :], rhs=KbT[s2][:, 2 * h + dc, :],
                                start=(b == 0 and hh == 0 and dc == 0), stop=(b == 15 and dc == 1), skip_group_check=True))
                    P.mm(fns, reads=[B_qpad[s2], B_KbT[s2]], writes=[Bp])
            cross_softmax(banks)
            run(diag_T(4, 2, pexp, B_pexp, lambda h0: pTc[:, h0:h0 + 2, :, :].rearrange("p u k t -> p (u k t)"), None, B_pTc, [128, 128]))
            pbs = [dbank(), dbank()]
            for b in range(16):
                s2 = b % 2
                if b + 1 < 16:
                    ld_v(b + 1)
                fns = []
                for c in range(8):
                    pv = pbs[c // 4][0][:].rearrange("p (c t) -> p c t", c=4)
                    h = c // 2
                    for mc in range(2):
                        fns.append(lambda e, pv=pv, c=c, h=h, mc=mc, s2=s2, b=b: e.matmul(
                            pv[:, c % 4, b * 8:(b + 1) * 8], lhsT=Vb[s2][:, mc, c * 128:(c + 1) * 128], rhs=pTc[:, h, mc, b * 8:(b + 1) * 8],
                            start=(mc == 0), stop=(mc == 1), skip_group_check=True))
                P.mm(fns, reads=[B_Vb[s2], B_pTc], writes=[pbs[0][1], pbs[1][1]])
            for half in range(2):
                pv = pbs[half][0][:].rearrange("p (c t) -> p c t", c=4)
                P.op("act" if half == 0 else "dve",
                     (lambda e, pv=pv, half=half: e.activation(out=ocT[:, half * 4:half * 4 + 4, 0:128], in_=pv, func=AF.Copy))
                     if half == 0 else
                     (lambda e, pv=pv, half=half: e.tensor_copy(out=ocT[:, half * 4:half * 4 + 4, 0:128], in_=pv)),
                     reads=[pbs[half][1]], writes=[B_ocT])
        dense("co", list(range(8)), NT, lambda k: ocT[:, k, :NT], B_ocT, resid_evac(NT, X))
        while not loaded[0]:
            step_load()

    def ffn(kind, gi, X, gen):
        sample, halo, NT, ntl = geom(kind)
        xt, Bxt = xTs[X], B_xTs[X]
        norm(xt, Bxt, 3, NT, hT, B_hT)
        uctr = [0]

        def up_evac(m, pb, Bp):
            r, Br = relu_t[uctr[0] % 2], B_relu[uctr[0] % 2]
            uctr[0] += 1
            P.op("act", lambda e: e.activation(out=r[:, :NT], in_=pb[:, :NT], func=AF.Relu), reads=[Bp], writes=[Br])
            P.op("dve", lambda e: e.tensor_tensor(out=hidT[:, m, :NT], in0=r[:, :NT], in1=r[:, :NT], op=ALU.mult), reads=[Br], writes=[B_hid])
        for _ in g_dense("up", list(range(32)), NT, lambda k: hT[:, k, :NT], B_hT, up_evac):
            advance(gen, 1)
        for _ in g_dense("down", list(range(8)), NT, lambda k: hidT[:, k, :NT], B_hid, resid_evac(NT, X), kgroups=4):
            advance(gen, 1)

    def tail(kind, gi, X):
        sample, halo, NT, ntl = geom(kind)
        xt, Bxt = xTs[X], B_xTs[X]
        norm(xt, Bxt, 4, NT, yT, B_yT)
        for j in range(ntl):
            ys_, Bys = yst[j % 2], B_yst[j % 2]
            for hf in range(2):
                pb, Bp = dbank()
                pv = pb[:].rearrange("p (c t) -> p c t", c=4)
                P.mm([(lambda e, c=c, pv=pv, hf=hf, j=j: e.transpose(out=pv[:, c, :], in_=yT[:, hf * 4 + c, j * 128:(j + 1) * 128], identity=identf[:]))
                      for c in range(4)], reads=[B_yT] + CONST, writes=[Bp])
                P.op("act" if hf == 0 else "dve",
                     (lambda e, pb=pb, hf=hf, ys_=ys_: e.activation(out=ys_[:, hf * 512:(hf + 1) * 512], in_=pb[:, :], func=AF.Copy))
                     if hf == 0 else
                     (lambda e, pb=pb, hf=hf, ys_=ys_: e.tensor_copy(out=ys_[:, hf * 512:(hf + 1) * 512], in_=pb[:, :])),
                     reads=[Bp], writes=[Bys])
            if sample:
                P.dma("sp", "y", ys[:, :], ys_[:], reads=[Bys])
            else:
                r0 = gi * NT_P + j * 128
                P.dma("sp", "y", yp[r0:r0 + 128, :], ys_[:], reads=[Bys])

    order = [("P", g) for g in range(NG_P)] + [("S", 0)]
    order = order[:max(0, min(len(order), STAGE))] if STAGE < 50 else order
    run(early("H", 0, 0))
    if order:
        run(early(order[0][0], order[0][1], 0))
    for idx, (kind, gi) in enumerate(order):
        X = idx % 2
        nxt = order[idx + 1] if idx + 1 < len(order) else None
        gen = early(nxt[0], nxt[1], (idx + 1) % 2) if nxt else None
        late_pre(kind, gi, X, gen)
        advance(gen, until="P1")
        ffn(kind, gi, X, gen)
        run(gen)
        tail(kind, gi, X)

    return finish()


_CACHE = {}


def _build_nc():
    if "nc" in _CACHE:
        return _CACHE["nc"]
    nc0 = bass.Bass("TRN2", target_bir_lowering=False)
    with ExitStack() as es0:
        _, W0 = build_sched(nc0, es0)
    sched = W0.rec
    nc = bass.Bass("TRN2", target_bir_lowering=False)
    with ExitStack() as es:
        P, W = build(nc, es, False, sched)
        assert W.i == len(sched), (W.i, len(sched))
        block = es.enter_context(nc.Block())
        P.flush(block)
    _CACHE["nc"] = nc
    return nc


def build_sched(nc0, es0):
    return build(nc0, es0, False, None)


def _tables(half):
    slopes = 2.0 ** (-(np.arange(8) + 1.0))
    q = np.arange(128)[:, None]
    c = np.arange(256)[None, :]
    dist = q - c + 128
    valid = (dist >= 0) & (dist <= 128)
    biasg = np.empty((128, 8, 256), np.float32)
    for g in range(4):
        for kv in range(2):
            h = kv * 4 + g
            biasg[:, 2 * g + kv, :] = np.where(valid, -slopes[h] * dist, -1e30)
    biasf = biasg.copy()
    if half == 0:
        biasf[:, :, 0:128] = -1e30
    biass = np.full((128, 2, 160), -1e30, np.float32)
    for j in range(4):
        for g in range(4):
            for t in range(8):
                r = j * 32 + g * 8 + t
                for kv in range(2):
                    h = kv * 4 + g
                    cc = np.arange(128)
                    d = t + 128 - cc
                    biass[r, kv, 0:128] = np.where(cc >= t, -slopes[h] * d, -1e30)
                    for tp in range(t + 1):
                        biass[r, kv, 128 + j * 8 + tp] = -slopes[h] * (t - tp)
    invc = np.empty((128, 4, 16), np.float32)
    for g in range(4):
        w = 2 << g
        for p in range(16):
            invc[:, g, p] = 1.0 / (min(p + 1, w) if half == 0 else w)
    return biasg.reshape(128, -1), biasf.reshape(128, -1), biass.reshape(128, -1), invc.reshape(128, -1)


def _prep(x_prompt, x_sample, cache_win_k, cache_win_v, state_pool, cache_mem_k, cache_mem_v,
          mem_prompt, g_mix, w_in, attn_sinks, w_pool, pool_scale, w_out, g_cross, g_mem,
          w_cq, w_ck, w_cv, w_co, g_ffn, w_up, w_down, g_final):
    f = lambda a: np.ascontiguousarray(np.asarray(a, dtype=np.float32))
    x_prompt, x_sample = f(x_prompt), f(x_sample)
    shared = dict(w_in=f(w_in)[0], w_pool=f(w_pool)[0], w_out=f(w_out)[0], w_cq=f(w_cq)[0], w_ck=f(w_ck)[0],
                  w_cv=f(w_cv)[0], w_co=f(w_co)[0], w_up=f(w_up)[0], w_down=f(w_down)[0])
    gs = np.stack([f(g_mix)[0], f(g_cross)[0], f(g_mem)[0], f(g_ffn)[0], f(g_final)], 0)
    shared["gvec"] = np.ascontiguousarray(gs.reshape(5, 8, 128).transpose(2, 0, 1).reshape(128, 40))
    shared["pscale"] = np.ascontiguousarray(f(pool_scale)[0].reshape(4, 128).T)
    sk = f(attn_sinks)[0]
    sinkp = np.empty((128, 8), np.float32)
    for g in range(4):
        for kv in range(2):
            sinkp[:, 2 * g + kv] = sk[kv * 4 + g]
    shared["sinkp"] = sinkp
    sinks = np.empty((128, 8), np.float32)
    for r in range(128):
        g = (r % 32) // 8
        for i in range(4):
            sinks[r, 2 * i] = sk[g]
            sinks[r, 2 * i + 1] = sk[4 + g]
    shared["sinks"] = sinks
    ckf, cvf, spf = f(cache_win_k)[0], f(cache_win_v)[0], f(state_pool)[0]
    cmkf, cmvf, memf = f(cache_mem_k)[0], f(cache_mem_v)[0], f(mem_prompt)
    in_maps = []
    for c in range(NCORES):
        b, half = c // 2, c % 2
        s0 = half * SEQ_CORE
        xp = np.zeros((128 + SEQ_CORE, D), np.float32)
        xp[128:] = x_prompt[b, s0:s0 + SEQ_CORE]
        if half == 1:
            xp[:128] = x_prompt[b, s0 - 128:s0]
        biasg, biasf, biass, invc = _tables(half)
        sl = slice(16 * c, 16 * c + 16)
        m = dict(shared)
        m.update(xp=xp, xs=np.ascontiguousarray(x_sample[sl].reshape(128, D)), mem=np.ascontiguousarray(memf[b]),
                 ck=np.ascontiguousarray(ckf[sl].reshape(16, 128, 128)), cv=np.ascontiguousarray(cvf[sl].reshape(16, 128, 128)),
                 spool=np.ascontiguousarray(spf[sl]), cmk=np.ascontiguousarray(cmkf[sl].reshape(16, 256, D)),
                 cmv=np.ascontiguousarray(cmvf[sl].reshape(16, 256, D)),
                 biasg=biasg, biasf=biasf, biass=biass, invc=invc)
        in_maps.append(m)
    return in_maps


def kernel(**inputs):
    in_maps = _prep(**inputs)
    nc = _build_nc()
    res = run_bass_kernel_spmd(nc, in_maps, core_ids=list(range(NCORES))).results
    return _assemble(res)


def _assemble(res):
    B, S = 4, 4096
    y_prompt = np.empty((B, S, D), np.float32)
    y_sample = np.empty((128, 8, D), np.float32)
    wk_p = np.empty((1, B, 128, 2, 64), np.float32); wv_p = np.empty_like(wk_p)
    pool_p = np.empty((1, B, 15, 512), np.float32)
    mk_p = np.empty((1, B, 256, 4, 256), np.float32); mv_p = np.empty_like(mk_p)
    wk_s = np.empty((1, 128, 128, 2, 64), np.float32); wv_s = np.empty_like(wk_s)
    pool_s = np.empty((1, 128, 15, 512), np.float32)
    for c in range(NCORES):
        r = res[c]
        b, half = c // 2, c % 2
        y_prompt[b, half * SEQ_CORE:(half + 1) * SEQ_CORE] = r["yp"]
        sl = slice(16 * c, 16 * c + 16)
        y_sample[sl] = r["ys"].reshape(16, 8, D)
        if half == 1:
            wk_p[0, b] = r["wkp"].reshape(128, 2, 64)
            wv_p[0, b] = r["wvp"].reshape(128, 2, 64)
            pool_p[0, b] = r["poolp"]
        else:
            mk_p[0, b] = r["memk"].reshape(256, 4, 256)
            mv_p[0, b] = r["memv"].reshape(256, 4, 256)
        wk_s[0, sl] = r["wks"].reshape(16, 128, 2, 64)
        wv_s[0, sl] = r["wvs"].reshape(16, 128, 2, 64)
        pool_s[0, sl] = r["pools"]
    return (y_prompt, y_sample, wk_p, wv_p, pool_p, mk_p, mv_p, wk_s, wv_s, pool_s)
```

```python
import numpy as np
from contextlib import ExitStack
import concourse.bass as bass
import concourse.mybir as mybir
from concourse.bass_utils import run_bass_kernel_spmd

F32 = mybir.dt.float32
BF16 = mybir.dt.bfloat16
ALU = mybir.AluOpType
AF = mybir.ActivationFunctionType
AX = mybir.AxisListType

NCORES = 8
STAGE = 99
D = 1024
SEQ_CORE = 2048
NT_P = 512
NG_P = SEQ_CORE // NT_P
RING = 10
EPS = 1e-5


class Buf:
    __slots__ = ("name", "w", "r", "al", "excl")

    def __init__(self, name, excl=False):
        self.name = name
        self.w = None
        self.r = {}
        self.al = []
        self.excl = excl


def alias(*bufs):
    for a in bufs:
        for b in bufs:
            if a is not b and b not in a.al:
                a.al.append(b)


class Prog:
    def __init__(self, nc, es, dry):
        self.nc, self.es, self.dry = nc, es, dry
        self.q = {e: [] for e in ("pe", "act", "dve", "pool", "sp")}
        self.cnt, self.sems = {}, {}
        self.waited = {e: {} for e in self.q}

    def sem(self, key):
        if key not in self.sems:
            self.sems[key] = None if self.dry else self.es.enter_context(self.nc.semaphore(key))
            self.cnt[key] = 0

    def _wait(self, eng, tok):
        if tok is None:
            return
        key, val = tok
        if self.waited[eng].get(key, 0) >= val:
            return
        self.waited[eng][key] = val
        self.q[eng].append(("w", key, val))

    def _deps(self, eng, reads, writes, extra):
        for b in reads:
            self._wait(eng, b.w)
            if b.excl:
                for k, v in b.r.items():
                    if k != eng:
                        self._wait(eng, (k, v))
        for b in writes:
            for bb in [b] + b.al:
                self._wait(eng, bb.w)
                for k, v in bb.r.items():
                    self._wait(eng, (k, v))
        for t in extra:
            self._wait(eng, t)

    def _commit(self, tok, reads, writes):
        k, v = tok
        for b in reads:
            b.r[k] = max(b.r.get(k, 0), v)
        for b in writes:
            b.w = tok
            b.r = {}

    def op(self, eng, fn, reads=(), writes=(), extra=()):
        self._deps(eng, reads, writes, extra)
        self.sem(eng)
        self.cnt[eng] += 1
        tok = (eng, self.cnt[eng])
        self.q[eng].append(("i", fn, eng, 1))
        self._commit(tok, reads, writes)
        return tok

    def mm(self, fns, reads=(), writes=(), extra=()):
        self._deps("pe", reads, writes, extra)
        for f in fns[:-1]:
            self.q["pe"].append(("i", f, None, 0))
        self.sem("pe")
        self.cnt["pe"] += 1
        tok = ("pe", self.cnt["pe"])
        self.q["pe"].append(("i", fns[-1], "pe", 1))
        self._commit(tok, reads, writes)
        return tok

    def dma(self, qeng, semkey, out, in_, reads=(), writes=(), extra=()):
        if writes:
            semkey = "dw" + qeng[0] + "_" + writes[0].name
        elif reads:
            semkey = "dr" + qeng[0] + "_" + reads[0].name
        for b in reads:
            self._wait(qeng, b.w)
        for b in writes:
            for bb in [b] + b.al:
                if not (bb.w is not None and bb.w[0] == semkey):
                    self._wait(qeng, bb.w)
                for k, v in bb.r.items():
                    self._wait(qeng, (k, v))
        for t in extra:
            self._wait(qeng, t)
        self.sem(semkey)
        self.cnt[semkey] += 16
        tok = (semkey, self.cnt[semkey])
        self.q[qeng].append(("i", (lambda e, o=out, i=in_: e.dma_start(out=o, in_=i)), semkey, 16))
        self._commit(tok, reads, writes)
        return tok

    def flush(self, block):
        def run(name):
            def f(e):
                for it in self.q[name]:
                    if it[0] == "w":
                        e.wait_ge(self.sems[it[1]], it[2])
                    else:
                        ins = it[1](e)
                        if it[3]:
                            ins.then_inc(self.sems[it[2]], it[3])
            return f
        block.tensor(run("pe"))
        block.scalar(run("act"))
        block.vector(run("dve"))
        block.gpsimd(run("pool"))
        block.sync(run("sp"))


class WStream:
    def __init__(self, P, ring_ap, sched, scratch_fn=None):
        self.P, self.ring = P, ring_ap
        self.sched = sched
        self.rec = []
        self.i = 0
        self.issued = 0
        self.slots = [Buf(f"ws{i}") for i in range(RING)]
        self.src = {}
        self.uidx, self.wtok = {}, {}
        self.scratch = None
        if sched is not None:
            cnt = {}
            for k in sched:
                cnt[k] = cnt.get(k, 0) + 1
            for k in sched:
                if cnt[k] > 1 and k not in self.uidx:
                    self.uidx[k] = len(self.uidx)
            if scratch_fn is not None and self.uidx:
                self.scratch = scratch_fn(len(self.uidx))

    def _issue(self, j):
        key = self.sched[j]
        name, m = key
        s = j % RING
        if self.scratch is not None and key in self.wtok:
            self.P.dma("sp", f"ws{s}", self.ring[:, s], self.scratch[self.uidx[key]], writes=[self.slots[s]],
                       extra=[self.wtok[key]])
            return
        for (dst_fn, src_ap) in self.src[name](m):
            self.P.dma("pool", f"ws{s}", dst_fn(self.ring[:, s]), src_ap, writes=[self.slots[s]])
        if self.scratch is not None and key in self.uidx:
            self.wtok[key] = self.P.dma("sp", f"sw{s}", self.scratch[self.uidx[key]], self.ring[:, s], reads=[self.slots[s]])

    def get(self, name, m):
        if self.sched is None:
            self.rec.append((name, m))
            return self.ring[:, 0], self.slots[0]
        assert self.sched[self.i] == (name, m), (self.i, self.sched[self.i], name, m)
        while self.issued < min(len(self.sched), self.i + RING - 3):
            self._issue(self.issued)
            self.issued += 1
        s = self.i % RING
        self.i += 1
        return self.ring[:, s], self.slots[s]


def build(nc, es, dry, sched):
    P = Prog(nc, es, dry)

    def din(name, shape):
        return nc.dram_tensor(name, list(shape), F32, kind="ExternalInput").ap()

    def dout(name, shape):
        return nc.dram_tensor(name, list(shape), F32, kind="ExternalOutput").ap()

    if not dry:
        xp = din("xp", [128 + SEQ_CORE, D]); xs = din("xs", [128, D]); mem = din("mem", [256, D])
        ck = din("ck", [16, 128, 128]); cv = din("cv", [16, 128, 128]); spool = din("spool", [16, 15, 512])
        cmk = din("cmk", [16, 256, D]); cmv = din("cmv", [16, 256, D])
        w_in = din("w_in", [D, 1280]); w_pool = din("w_pool", [4, 128, 128]); w_out = din("w_out", [D, D])
        w_cq = din("w_cq", [D, D]); w_ck = din("w_ck", [D, D]); w_cv = din("w_cv", [D, D]); w_co = din("w_co", [D, D])
        w_up = din("w_up", [D, 4 * D]); w_down = din("w_down", [4 * D, D])
        gvec_d = din("gvec", [128, 40]); pscale_d = din("pscale", [128, 4])
        sinkp_d = din("sinkp", [128, 8]); sinks_d = din("sinks", [128, 8])
        biasg_d = din("biasg", [128, 8 * 256]); biasf_d = din("biasf", [128, 8 * 256]); biass_d = din("biass", [128, 2 * 160])
        invc_d = din("invc", [128, 64])
        yp = dout("yp", [SEQ_CORE, D]); ys = dout("ys", [128, D])
        wkp = dout("wkp", [128, 128]); wvp = dout("wvp", [128, 128]); poolp = dout("poolp", [15, 512])
        memk_o = dout("memk", [256, D]); memv_o = dout("memv", [256, D])
        wks = dout("wks", [16, 128, 128]); wvs = dout("wvs", [16, 128, 128]); pools = dout("pools", [16, 15, 512])

    def sb(name, shape, dt):
        return es.enter_context(nc.sbuf_tensor("sb_" + name, list(shape), dt))

    xTs = [sb(f"xT{i}", [128, 8, NT_P], F32) for i in range(2)]; B_xTs = [Buf(f"xT{i}") for i in range(2)]
    xT, B_xT = xTs[0], B_xTs[0]
    hT = sb("hT", [128, 8, NT_P], BF16); B_hT = Buf("hT")
    rstd = sb("rstd", [128, NT_P], F32); B_rstd = Buf("rstd")
    aoT = sb("aoT", [128, 8, NT_P], BF16); B_aoT = Buf("aoT")
    qT = sb("qT", [128, 4, NT_P], BF16); B_qT = Buf("qT")
    kT = sb("kT", [128, 128 + NT_P], BF16); B_kT = Buf("kT")
    vT = sb("vT", [128, NT_P], BF16); B_vT = Buf("vT")
    vtok = sb("vtok", [128, 5, 128], BF16); B_vtok = Buf("vtok")
    kv32 = sb("kv32", [128, 2, 128], F32); B_kv32 = Buf("kv32")
    dT = sb("dT", [128, 4, NT_P], BF16); B_dT = Buf("dT")
    pexp = sb("pexp", [128, 8, 256], BF16); B_pexp = Buf("pexp")
    pT = sb("pT", [128, 8, 2, 128], BF16); B_pT = Buf("pT")
    Dg = sb("Dg", [128, 8, 128], BF16); B_Dg = Buf("Dg")
    pTc = sb("pTc", [128, 4, 2, 128], BF16); B_pTc = Buf("pTc")
    bias = sb("bias", [128, 8, 256], F32); B_bias = Buf("bias")
    biass = sb("biass", [128, 2, 160], F32); B_biass = Buf("biass")
    memkT = sb("memkT", [128, 8, 256], BF16); B_memkT = Buf("memkT")
    memv = sb("memv", [128, 2, D], BF16); B_memv = Buf("memv")
    ring = sb("ring", [128, RING, 8, 128], BF16)
    xin = [sb(f"xin{i}", [128, D], F32) for i in range(2)]; B_xin = [Buf(f"xin{i}") for i in range(2)]
    yst, B_yst = xin, B_xin
    ident = sb("ident", [128, 128], BF16); identf = sb("identf", [128, 128], F32); B_const = Buf("const")
    ones = sb("ones", [128, 128], BF16)
    gvec = sb("gvec", [128, 5, 8], F32); pscale = sb("pscale", [128, 4], F32)
    sinkp = sb("sinkp", [128, 8], F32); sinks = sb("sinks", [128, 8], F32)
    invc = sb("invc", [128, 4, 16], F32)
    wpool = sb("wpool", [128, 4, 128], BF16); B_wpool = Buf("wpool")
    st = sb("st", [128, 64], F32); B_st = Buf("st")
    relu_t = [sb(f"relu{i}", [128, NT_P], BF16) for i in range(2)]; B_relu = [Buf(f"relu{i}") for i in range(2)]
    ost = sb("ost", [128, 512], F32); B_ost = Buf("ost")
    carryU = sb("carryU", [128, 4, 16], F32); B_cU = Buf("carryU")
    sqb = [sb(f"sq{i}", [128, NT_P], BF16) for i in range(2)]; B_sq = [Buf(f"sq{i}") for i in range(2)]
    rstd2 = sb("rstd2", [128, NT_P], F32); B_rstd2 = Buf("rstd2")

    R2 = 32 * NT_P * 2
    XO = R2 + 28672
    AR = XO + 10240
    arena = sb("arena", [128, AR // 2], BF16)

    def av(off, nbytes, dt, pat=None, **kw):
        v = arena[:, off // 2:(off + nbytes) // 2]
        if dt is F32:
            v = v.bitcast(F32)
        if pat:
            v = v.rearrange(pat, **kw)
        return v

    hidT = av(0, 32 * NT_P * 2, BF16, "p (k t) -> p k t", k=32); B_hid = Buf("hidT")
    yT = av(0, 8 * NT_P * 4, F32, "p (k t) -> p k t", k=8); B_yT = Buf("yT")
    WU = 16 + NT_P
    WH = 16 + 256
    U = av(R2, 4 * WU * 4, F32, "p (g t) -> p g t", g=4); B_U = Buf("U")
    SA = av(R2 + 4 * WU * 4, 4 * WH * 4, F32, "p (g t) -> p g t", g=4); B_SA = Buf("SA")
    SB = av(R2 + 4 * WU * 4 + 4 * WH * 4, 4 * WH * 4, F32, "p (g t) -> p g t", g=4); B_SB = Buf("SB")
    o_sb = R2 + 4 * WU * 4 + 8 * WH * 4
    sbias = av(o_sb, 8 * 256 * 4, F32, "p (u t) -> p u t", u=8); B_sbias = Buf("sbias")
    assert o_sb + 8192 <= XO
    kcT = av(XO, 16 * 128 * 2, BF16, "p (b t) -> p b t", b=16); B_kcT = Buf("kcT")
    vc = av(XO + 4096, 16 * 128 * 2, BF16, "p (b t) -> p b t", b=16); B_vc = Buf("vc")
    qs2 = av(XO + 8192, 16 * 32 * 2, BF16, "p (b t) -> p b t", b=16); B_qs2 = Buf("qs2")
    vnq = av(XO + 9216, 4 * 128 * 2, BF16, "p (i t) -> p i t", i=4); B_vnq = Buf("vnq")
    Kb = [av(R2 + i * 4096, 4096, BF16, "p (m t) -> p m t", m=2) for i in range(2)]; B_Kb = [Buf(f"Kb{i}") for i in range(2)]
    KbT = [av(R2 + 8192 + i * 4096, 4096, BF16, "p (c t) -> p c t", c=8) for i in range(2)]; B_KbT = [Buf(f"KbT{i}") for i in range(2)]
    Vb = [av(R2 + 16384 + i * 4096, 4096, BF16, "p (m t) -> p m t", m=2) for i in range(2)]; B_Vb = [Buf(f"Vb{i}") for i in range(2)]
    qpad = [av(R2 + 24576 + i * 2048, 2048, BF16, "p (c t) -> p c t", c=8) for i in range(2)]; B_qpad = [Buf(f"qpad{i}") for i in range(2)]
    memst = av(0, 8192, F32, "p (m t) -> p m t", m=2); B_memst = Buf("memst")
    mkst = av(8192, 8192, F32, "p (m t) -> p m t", m=2); B_mkst = Buf("mkst")
    alias(B_hid, B_yT)
    gX = [B_U, B_SA, B_SB, B_sbias]
    gY = B_Kb + B_KbT + B_Vb + B_qpad
    gZ = [B_memst, B_mkst]
    for ga, gb in ((gX, gY), ([B_hid, B_yT], gZ)):
        for a in ga:
            for b in gb:
                a.al.append(b)
                b.al.append(a)

    ps = [es.enter_context(nc.psum_tensor(f"ps{i}", [128, 512], F32)) for i in range(8)]
    B_ps = [Buf(f"ps{i}", excl=True) for i in range(8)]
    dctr = [0]

    def dbank():
        i = dctr[0] % 3
        dctr[0] += 1
        return ps[i], B_ps[i]
    PS_S = [3, 4]
    PS_T = [5, 6]
    PS_O = 7
    sctr = [0]
    tctr = [0]

    def sbank():
        i = PS_S[sctr[0] % 2]; sctr[0] += 1
        return ps[i], B_ps[i]

    def tbank():
        i = PS_T[tctr[0] % 2]; tctr[0] += 1
        return ps[i], B_ps[i]

    def scratch_fn(n):
        return nc.dram_tensor("wscratch", [n, 128, 8, 128], BF16, kind="Internal").ap()
    W = WStream(P, ring, sched, scratch_fn)
    if not dry:
        def std_src(wap):
            v = wap.rearrange("(k p) (m c) -> p m k c", p=128, c=128)
            return lambda m: [((lambda s: s), v[:, m])]
        W.src["ck"] = std_src(w_ck); W.src["cv"] = std_src(w_cv)
        W.src["cq"] = std_src(w_cq); W.src["co"] = std_src(w_co); W.src["up"] = std_src(w_up)
        vin_q = w_in[:, 0:512].rearrange("(k p) (kv g d) -> p g k kv d", p=128, kv=2, g=4, d=64)
        vin_r = w_in[:, 512:1280].rearrange("(k p) (m c) -> p m k c", p=128, c=128)

        def in_src(m):
            if m < 4:
                return [((lambda s: s[:, :, 0:64]), vin_q[:, m, :, 0, :]),
                        ((lambda s: s[:, :, 64:128]), vin_q[:, m, :, 1, :])]
            return [((lambda s: s), vin_r[:, m - 4])]
        W.src["in"] = in_src
        vo_a = w_out[0:512, :].rearrange("(kv g d) (m c) -> kv d m g c", kv=2, g=4, d=64, c=128)
        vo_p = w_out[512:1024, :].rearrange("(k p) (m c) -> p m k c", p=128, c=128)

        def out_src(m):
            return [((lambda s: s[0:64, 0:4, :]), vo_a[0, :, m]),
                    ((lambda s: s[64:128, 0:4, :]), vo_a[1, :, m]),
                    ((lambda s: s[:, 4:8, :]), vo_p[:, m])]
        W.src["out"] = out_src
        vdn = w_down.rearrange("(q k p) (m c) -> p m q k c", p=128, k=8, c=128)
        W.src["down"] = lambda mq: [((lambda s: s), vdn[:, mq // 4, mq % 4])]

    if not dry:
        P.op("pool", lambda e: e.memset(identf[:], 0.0), writes=[B_const])
        P.op("pool", lambda e: e.iota(identf[:], pattern=[[1, 128]], base=0, channel_multiplier=-1,
                                      allow_small_or_imprecise_dtypes=True), writes=[B_const])
        P.op("dve", lambda e: e.tensor_single_scalar(out=ident[:], in_=identf[:], scalar=0.0, op=ALU.is_equal),
             reads=[B_const], writes=[B_const])
        P.op("dve", lambda e: e.tensor_single_scalar(out=identf[:], in_=identf[:], scalar=0.0, op=ALU.is_equal),
             writes=[B_const])
        P.op("dve", lambda e: e.memset(ones[:], 1.0), writes=[B_const])
        for (dst, src) in ((gvec[:].rearrange("p a b -> p (a b)"), gvec_d), (pscale[:], pscale_d), (sinkp[:], sinkp_d),
                           (sinks[:], sinks_d), (invc[:].rearrange("p a b -> p (a b)"), invc_d),
                           (biass[:].rearrange("p a b -> p (a b)"), biass_d)):
            P.dma("sp", "cst", dst, src[:, :], writes=[B_const])
        P.dma("sp", "biasld", bias[:].rearrange("p a b -> p (a b)"), biasf_d[:, :], writes=[B_bias])
        P.dma("pool", "wpool", wpool[:], w_pool.rearrange("g c e -> c g e"), writes=[B_wpool])

    CONST = [B_const]

    class _Stop(Exception):
        pass

    def finish():
        for key, val in P.cnt.items():
            if key not in ("pe", "act", "dve", "pool"):
                P._wait("sp", (key, val))
        for e_ in ("pe", "act", "dve", "pool"):
            if P.cnt.get(e_, 0):
                P._wait("sp", (e_, P.cnt[e_]))
        return P, W
    if STAGE == -1:
        return finish()

    def run(gen):
        if gen is not None:
            for _ in gen:
                pass

    def advance(gen, n=1, until=None):
        if gen is None:
            return
        if until is not None:
            for v in gen:
                if v == until:
                    return
            return
        for _ in range(n):
            try:
                next(gen)
            except StopIteration:
                return

    def g_load_x(src_rows, ntiles, X):
        dst, dstB = xTs[X], B_xTs[X]
        for j in range(ntiles):
            xb, Bx = xin[j % 2], B_xin[j % 2]
            P.dma("sp", "x", xb[:], src_rows(j), writes=[Bx])
            for hf in range(2):
                pb, Bp = dbank()
                pv = pb[:].rearrange("p (c t) -> p c t", c=4)
                P.mm([(lambda e, c=c, pv=pv, xb=xb, hf=hf: e.transpose(out=pv[:, c, :], in_=xb[:, (hf * 4 + c) * 128:(hf * 4 + c + 1) * 128],
                                                                       identity=identf[:])) for c in range(4)],
                     reads=[Bx] + CONST, writes=[Bp])
                P.op("act" if hf == 0 else "dve",
                     (lambda e, pv=pv, hf=hf, j=j: e.activation(out=dst[:, hf * 4:hf * 4 + 4, j * 128:(j + 1) * 128], in_=pv, func=AF.Copy))
                     if hf == 0 else
                     (lambda e, pv=pv, hf=hf, j=j: e.tensor_copy(out=dst[:, hf * 4:hf * 4 + 4, j * 128:(j + 1) * 128], in_=pv)),
                     reads=[Bp], writes=[dstB])
                yield

    def norm(src, Bsrc, gi, NT, dst, Bdst):
        P.op("act", lambda e: e.activation(out=hT[:, :, :NT], in_=src[:, :, :NT], func=AF.Square),
             reads=[Bsrc], writes=[B_hT])
        pb, Bp = dbank()
        P.mm([(lambda e, k=k: e.matmul(pb[:, :NT], lhsT=ones[:], rhs=hT[:, k, :NT], start=(k == 0), stop=(k == 7)))
              for k in range(8)], reads=[B_hT] + CONST, writes=[Bp])
        P.op("act", lambda e: e.activation(out=rstd[:, :NT], in_=pb[:, :NT], func=AF.Ln, scale=1.0 / D, bias=EPS),
             reads=[Bp], writes=[B_rstd])
        P.op("act", lambda e: e.activation(out=rstd[:, :NT], in_=rstd[:, :NT], func=AF.Exp, scale=-0.5),
             reads=[B_rstd], writes=[B_rstd])
        for k in range(8):
            P.op("dve", lambda e, k=k: e.scalar_tensor_tensor(out=dst[:, k, :NT], in0=src[:, k, :NT], scalar=gvec[:, gi, k:k + 1],
                                                              in1=rstd[:, :NT], op0=ALU.mult, op1=ALU.mult),
                 reads=[Bsrc, B_rstd] + CONST, writes=[Bdst])

    def g_dense(wname, units, NT, rhs_fn, Brhs, evac, kgroups=1):
        for m in units:
            pb, Bp = dbank()
            fns, Bs = [], []
            for q in range(kgroups):
                slot, Bslot = W.get(wname, m * kgroups + q if kgroups > 1 else m)
                Bs.append(Bslot)
                for k in range(8):
                    fns.append(lambda e, slot=slot, k=k, q=q, pb=pb: e.matmul(
                        pb[:, :NT], lhsT=slot[:, k, :], rhs=rhs_fn(q * 8 + k),
                        start=(q == 0 and k == 0), stop=(q == kgroups - 1 and k == 7)))
            P.mm(fns, reads=Bs + [Brhs], writes=[Bp])
            evac(m, pb, Bp)
            for _ in range(kgroups):
                yield

    def dense(*a, **kw):
        run(g_dense(*a, **kw))

    def resid_evac(NT, X, prep=None, scale2=False):
        xt, Bxt = xTs[X], B_xTs[X]

        def f(m, pb, Bp):
            if scale2:
                P.op("dve", lambda e: e.tensor_tensor(out=pb[:, :NT], in0=pb[:, :NT], in1=rstd2[:, :NT], op=ALU.mult),
                     reads=[Bp, B_rstd2], writes=[Bp])
            P.op("dve", lambda e: e.tensor_tensor(out=xt[:, m, :NT], in0=pb[:, :NT], in1=xt[:, m, :NT], op=ALU.add),
                 reads=[Bp], writes=[Bxt])
            if prep is not None:
                prep[0](m)
        return f

    def make_prep(X, NT, gidx, exp_scale, rdst, Brdst):
        xt, Bxt = xTs[X], B_xTs[X]
        sp_, Bsp = ps[PS_O], B_ps[PS_O]
        pend = []

        def emit_mm(k):
            P.mm([lambda e, k=k: e.matmul(sp_[:, :NT], lhsT=ones[:], rhs=sqb[k % 2][:, :NT], start=(k == 0), stop=(k == 7))],
                 reads=[B_sq[k % 2]] + CONST, writes=[Bsp])

        def after(m):
            while len(pend) >= 2:
                emit_mm(pend.pop(0))
            P.op("act", lambda e: e.activation(out=hT[:, m, :NT], in_=xt[:, m, :NT], func=AF.Copy, scale=gvec[:, gidx, m:m + 1]),
                 reads=[Bxt] + CONST, writes=[B_hT])
            P.op("act", lambda e: e.activation(out=sqb[m % 2][:, :NT], in_=xt[:, m, :NT], func=AF.Square),
                 reads=[Bxt], writes=[B_sq[m % 2]])
            pend.append(m)

        def flush():
            while pend:
                emit_mm(pend.pop(0))
            P.op("act", lambda e: e.activation(out=rdst[:, :NT], in_=sp_[:, :NT], func=AF.Ln, scale=1.0 / D, bias=EPS),
                 reads=[Bsp], writes=[Brdst])
            P.op("act", lambda e: e.activation(out=rdst[:, :NT], in_=rdst[:, :NT], func=AF.Exp, scale=exp_scale),
                 reads=[Brdst], writes=[Brdst])
        return after, flush

    def diag_T(nu, nkc, p_src, Bp_src, dst4, dst_fn, Bdst, kw):
        items = [(u, kc) for u in range(nu) for kc in range(nkc)]
        full = all(w == 128 for w in kw)
        for bi, i0 in enumerate(range(0, len(items), 4)):
            chunk = items[i0:i0 + 4]
            pb, Bp = tbank()
            pv = pb[:].rearrange("p (s t) -> p s t", s=4)
            P.mm([(lambda e, s=s, u=u, kc=kc, pv=pv: e.matmul(pv[0:kw[kc], s, :], lhsT=p_src[:, u, kc * 128:kc * 128 + kw[kc]],
                                                              rhs=Dg[:, u, :], start=True, stop=True))
                  for s, (u, kc) in enumerate(chunk)], reads=[Bp_src, B_Dg], writes=[Bp])
            if full:
                u0 = chunk[0][0]
                if bi % 2 == 0:
                    P.op("act", lambda e, pb=pb, u0=u0: e.activation(out=dst4(u0), in_=pb[:, 0:512], func=AF.Copy), reads=[Bp], writes=[Bdst])
                else:
                    P.op("dve", lambda e, pb=pb, u0=u0: e.tensor_copy(out=dst4(u0), in_=pb[:, 0:512]), reads=[Bp], writes=[Bdst])
            else:
                for s, (u, kc) in enumerate(chunk):
                    P.op("act" if bi % 2 == 0 else "dve",
                         (lambda e, s=s, u=u, kc=kc, pv=pv: e.activation(out=dst_fn(u, kc), in_=pv[0:kw[kc], s, :], func=AF.Copy))
                         if bi % 2 == 0 else
                         (lambda e, s=s, u=u, kc=kc, pv=pv: e.tensor_copy(out=dst_fn(u, kc), in_=pv[0:kw[kc], s, :])),
                         reads=[Bp], writes=[Bdst])
            yield

    def make_Dg(nu):
        P.op("dve", lambda e: e.tensor_tensor(out=Dg[:, 0:nu, :], in0=ident[:].unsqueeze(1).to_broadcast([128, nu, 128]),
                                              in1=st[:, 56:56 + nu].unsqueeze(2).to_broadcast([128, nu, 128]), op=ALU.mult),
             reads=[B_st] + CONST, writes=[B_Dg])

    def win_softmax(nu, width, sink_ap):
        P.op("dve", lambda e: e.tensor_reduce(out=st[:, 0:nu], in_=sbias[:, 0:nu, 0:width], axis=AX.X, op=ALU.max),
             reads=[B_sbias], writes=[B_st])
        P.op("dve", lambda e: e.tensor_tensor(out=st[:, 8:8 + nu], in0=st[:, 0:nu], in1=sink_ap, op=ALU.max),
             reads=[B_st] + CONST, writes=[B_st])
        P.op("dve", lambda e: e.tensor_scalar(out=st[:, 16:16 + nu], in0=st[:, 8:8 + nu], scalar1=-1.0, scalar2=None, op0=ALU.mult),
             reads=[B_st], writes=[B_st])
        P.op("dve", lambda e: e.tensor_tensor(out=st[:, 24:24 + nu], in0=sink_ap, in1=st[:, 16:16 + nu], op=ALU.add),
             reads=[B_st] + CONST, writes=[B_st])
        for u in range(nu):
            P.op("act", lambda e, u=u: e.activation(out=pexp[:, u, 0:width], in_=sbias[:, u, 0:width], func=AF.Exp,
                                                    bias=st[:, 16 + u:17 + u], scale=1.0, accum_out=st[:, 32 + u:33 + u]),
                 reads=[B_sbias, B_st], writes=[B_pexp, B_st])
        P.op("act", lambda e: e.activation(out=st[:, 40:40 + nu], in_=st[:, 24:24 + nu], func=AF.Exp),
             reads=[B_st], writes=[B_st])
        P.op("dve", lambda e: e.tensor_tensor(out=st[:, 48:48 + nu], in0=st[:, 32:32 + nu], in1=st[:, 40:40 + nu], op=ALU.add),
             reads=[B_st], writes=[B_st])
        P.op("dve", lambda e: e.reciprocal(out=st[:, 56:56 + nu], in_=st[:, 48:48 + nu]), reads=[B_st], writes=[B_st])
        make_Dg(nu)

    def cross_softmax(score_banks):
        for hp, (pb, Bp) in enumerate(score_banks):
            pv = pb[:].rearrange("p (h t) -> p h t", h=2)
            P.op("dve", lambda e, pv=pv, hp=hp: e.tensor_reduce(out=st[:, 2 * hp:2 * hp + 2], in_=pv, axis=AX.X, op=ALU.max),
                 reads=[Bp], writes=[B_st])
        P.op("dve", lambda e: e.tensor_scalar(out=st[:, 16:20], in0=st[:, 0:4], scalar1=-1.0 / 16.0, scalar2=None, op0=ALU.mult),
             reads=[B_st], writes=[B_st])
        for hp, (pb, Bp) in enumerate(score_banks):
            pv = pb[:].rearrange("p (h t) -> p h t", h=2)
            for hh in range(2):
                h = 2 * hp + hh
                P.op("act", lambda e, pv=pv, hh=hh, h=h: e.activation(out=pexp[:, h, :], in_=pv[:, hh, :], func=AF.Exp,
                                                                      bias=st[:, 16 + h:17 + h], scale=1.0 / 16.0,
                                                                      accum_out=st[:, 32 + h:33 + h]),
                     reads=[Bp, B_st], writes=[B_pexp, B_st])
        P.op("dve", lambda e: e.reciprocal(out=st[:, 56:60], in_=st[:, 32:36]), reads=[B_st], writes=[B_st])
        make_Dg(4)

    def out_tok_major(srcs, Bsrcs, ncols_each, dst_dma):
        pb, Bp = dbank()
        pv = pb[:].rearrange("p (c t) -> p c t", c=4)
        n = len(srcs)
        P.mm([(lambda e, i=i: e.transpose(out=pv[:, i, :], in_=srcs[i], identity=identf[:])) for i in range(n)],
             reads=list(Bsrcs) + CONST, writes=[Bp])
        P.op("dve", lambda e: e.tensor_copy(out=ost[:, 0:n * 128], in_=pb[:, 0:n * 128]), reads=[Bp], writes=[B_ost])
        dst_dma()

    def mem_setup(gen):
        mx, Bmx = xTs[1], B_xTs[1]
        for t in range(2):
            P.dma("sp", "memld", memst[:, t, :], mem[t * 128:(t + 1) * 128, :], writes=[B_memst])
        for t in range(2):
            for hf in range(2):
                pb, Bp = dbank()
                pv = pb[:].rearrange("p (c t) -> p c t", c=4)
                P.mm([(lambda e, c=c, pv=pv, t=t, hf=hf: e.transpose(out=pv[:, c, :], in_=memst[:, t, (hf * 4 + c) * 128:(hf * 4 + c + 1) * 128],
                                                                     identity=identf[:])) for c in range(4)],
                     reads=[B_memst] + CONST, writes=[Bp])
                P.op("act", lambda e, pv=pv, hf=hf, t=t: e.activation(out=mx[:, hf * 4:hf * 4 + 4, t * 128:(t + 1) * 128], in_=pv, func=AF.Copy),
                     reads=[Bp], writes=[Bmx])
        norm(mx, Bmx, 2, 256, hT, B_hT)
        for (wn, is_k) in (("ck", True), ("cv", False)):
            for m in range(8):
                slot, Bslot = W.get(wn, m)
                if is_k:
                    pb, Bp = dbank()
                    P.mm([(lambda e, k=k, slot=slot, pb=pb: e.matmul(pb[:, :256], lhsT=slot[:, k, :], rhs=hT[:, k, :256],
                                                                    start=(k == 0), stop=(k == 7))) for k in range(8)],
                         reads=[Bslot, B_hT], writes=[Bp])
                    P.op("act", lambda e, m=m, pb=pb: e.activation(out=memkT[:, m, :], in_=pb[:, :256], func=AF.Copy),
                         reads=[Bp], writes=[B_memkT])
                pb, Bp = dbank()
                fns = []
                for t in range(2):
                    for k in range(8):
                        fns.append(lambda e, k=k, slot=slot, pb=pb, t=t: e.matmul(
                            pb[:, t * 128:(t + 1) * 128], lhsT=hT[:, k, t * 128:(t + 1) * 128], rhs=slot[:, k, :],
                            start=(k == 0), stop=(k == 7)))
                P.mm(fns, reads=[Bslot, B_hT], writes=[Bp])
                pv2 = pb[:, 0:256].rearrange("p (t c) -> p t c", t=2)
                P.op("dve", lambda e, pv2=pv2, m=m: e.tensor_copy(out=mkst[:, :, m * 128:(m + 1) * 128], in_=pv2),
                     reads=[Bp], writes=[B_mkst])
                if not is_k:
                    P.op("act", lambda e, pv2=pv2, m=m: e.activation(out=memv[:, :, m * 128:(m + 1) * 128], in_=pv2, func=AF.Copy),
                         reads=[Bp], writes=[B_memv])
                advance(gen, 3)
            for t in range(2):
                P.dma("sp", "memout", (memk_o if is_k else memv_o)[t * 128:(t + 1) * 128, :], mkst[:, t, :], reads=[B_mkst])

    def geom(kind):
        sample, halo = (kind == "S"), (kind == "H")
        NT = 128 if (sample or halo) else NT_P
        return sample, halo, NT, NT // 128

    def early(kind, gi, X):
        sample, halo, NT, ntl = geom(kind)
        xt, Bxt = xTs[X], B_xTs[X]
        if sample:
            yield from g_load_x(lambda j: xs[:, :], 1, X)
        elif halo:
            yield from g_load_x(lambda j: xp[0:128, :], 1, X)
        else:
            yield from g_load_x(lambda j: xp[128 + gi * NT_P + j * 128: 128 + gi * NT_P + (j + 1) * 128, :], ntl, X)
        yield "L"
        norm(xt, Bxt, 0, NT, hT, B_hT)
        last = (kind == "P" and gi == NG_P - 1)
        want32 = last or sample
        Us = U[:, :, 0:384].rearrange("p g (b c) -> p g b c", b=16)

        if sample:
            for hb in range(2):
                P.dma("pool", "ckld", vc[:, hb * 8:(hb + 1) * 8, :], cv[hb * 8:(hb + 1) * 8].rearrange("b s f -> s b f"), writes=[B_vc])
            kst = pexp[:].rearrange("p u t -> p (u t)").rearrange("p (b f) -> p b f", b=16)
            for hb in range(2):
                P.dma("pool", "ckld", kst[:, hb * 8:(hb + 1) * 8, :], ck[hb * 8:(hb + 1) * 8].rearrange("b s f -> s b f"), writes=[B_pexp])
            for hb in range(2):
                pb, Bp = tbank()
                pv = pb[:].bitcast(BF16).rearrange("p (b t) -> p b t", b=8)
                P.mm([(lambda e, b=b, pv=pv, hb=hb: e.transpose(out=pv[:, b, :], in_=kst[:, hb * 8 + b, :], identity=ident[:])) for b in range(8)],
                     reads=[B_pexp] + CONST, writes=[Bp])
                P.op("dve", lambda e, pv=pv, hb=hb: e.tensor_copy(out=kcT[:, hb * 8:(hb + 1) * 8, :], in_=pv), reads=[Bp], writes=[B_kcT])
            P.op("dve", lambda e: e.memset(U[:, :, 0:384], 0.0), writes=[B_U])
            for hb in range(2):
                P.dma("sp", "spld", xin[hb][0:120, 0:512], spool[hb * 8:(hb + 1) * 8].rearrange("b r f -> (b r) f"), writes=[B_xin[hb]])
                pb, Bp = dbank()
                pv = pb[:].rearrange("p (c t) -> p c t", c=4)
                P.mm([(lambda e, c=c, pv=pv, hb=hb: e.transpose(out=pv[:, c, 0:120], in_=xin[hb][0:120, c * 128:(c + 1) * 128],
                                                                identity=identf[0:120, 0:120])) for c in range(4)],
                     reads=[B_xin[hb]] + CONST, writes=[Bp])
                for c in range(4):
                    P.op("dve", lambda e, c=c, pv=pv, hb=hb: e.tensor_copy(
                        out=Us[:, c, hb * 8:(hb + 1) * 8, 1:16], in_=pv[:, c, 0:120].rearrange("p (b r) -> p b r", b=8)),
                        reads=[Bp], writes=[B_U])
            yield

        def in_evac(m, pb, Bp):
            if m < 4:
                P.op("act", lambda e: e.activation(out=qT[:, m, :NT], in_=pb[:, :NT], func=AF.Copy), reads=[Bp], writes=[B_qT])
            elif m == 4:
                P.op("act", lambda e: e.activation(out=kT[:, 128:128 + NT], in_=pb[:, :NT], func=AF.Copy), reads=[Bp], writes=[B_kT])
                if want32:
                    P.op("dve", lambda e: e.tensor_copy(out=kv32[:, 0, :], in_=pb[:, NT - 128:NT]), reads=[Bp], writes=[B_kv32])
            elif m == 5:
                P.op("act", lambda e: e.activation(out=vT[:, :NT], in_=pb[:, :NT], func=AF.Copy), reads=[Bp], writes=[B_vT])
                if want32:
                    P.op("dve", lambda e: e.tensor_copy(out=kv32[:, 1, :], in_=pb[:, NT - 128:NT]), reads=[Bp], writes=[B_kv32])
            else:
                g = m - 6
                if sample:
                    P.op("dve", lambda e: e.tensor_copy(out=Us[:, g, :, 16:24], in_=pb[:, 0:128].rearrange("p (b t) -> p b t", b=16)),
                         reads=[Bp], writes=[B_U])
                else:
                    P.op("dve", lambda e: e.tensor_copy(out=U[:, g, 16:16 + NT], in_=pb[:, :NT]), reads=[Bp], writes=[B_U])
        yield from g_dense("in", list(range(4, 10)) if halo else list(range(10)), NT, lambda k: hT[:, k, :NT], B_hT, in_evac)
        yield "P1"

        for j0 in range(0, ntl, 4):
            pb, Bp = tbank()
            pv = pb[:].bitcast(BF16)[:, 0:512].rearrange("p (j t) -> p j t", j=4)
            P.mm([(lambda e, j=j, pv=pv: e.transpose(out=pv[:, j - j0, :], in_=vT[:, j * 128:(j + 1) * 128], identity=ident[:]))
                  for j in range(j0, min(ntl, j0 + 4))], reads=[B_vT] + CONST, writes=[Bp])
            nj = min(ntl, j0 + 4) - j0
            P.op("dve", lambda e, pv=pv, j0=j0, nj=nj: e.tensor_copy(out=vtok[:, 1 + j0:1 + j0 + nj, :], in_=pv[:, 0:nj, :]),
                 reads=[Bp], writes=[B_vtok])
        yield

        def carry():
            P.op("dve", lambda e: e.tensor_copy(out=kT[:, 0:128], in_=kT[:, NT:NT + 128]), reads=[B_kT], writes=[B_kT])
            P.op("dve", lambda e: e.tensor_copy(out=vtok[:, 0, :], in_=vtok[:, ntl, :]), reads=[B_vtok], writes=[B_vtok])

        if halo:
            carry()
            P.op("dve", lambda e: e.tensor_copy(out=carryU[:], in_=U[:, :, NT:NT + 16]), reads=[B_U], writes=[B_cU])
            return

        if want32:
            if last:
                def dd():
                    P.dma("sp", "kvout", wkp[:, :], ost[:, 0:128], reads=[B_ost])
                    P.dma("sp", "kvout", wvp[:, :], ost[:, 128:256], reads=[B_ost])
            else:
                def dd():
                    for t in range(8):
                        P.dma("sp", "kvout", wks[:, 120 + t, :], ost[t:128:8, 0:128], reads=[B_ost])
                        P.dma("sp", "kvout", wvs[:, 120 + t, :], ost[t:128:8, 128:256], reads=[B_ost])
                    P.dma("sp", "d2d_k", wks[:, 0:120, :], ck[:, 8:128, :])
                    P.dma("sp", "d2d_v", wvs[:, 0:120, :], cv[:, 8:128, :])
            out_tok_major([kv32[:, 0, :], kv32[:, 1, :]], [B_kv32], 128, dd)
            yield

        if not sample:
            P.op("dve", lambda e: e.tensor_copy(out=U[:, :, 0:16], in_=carryU[:]), reads=[B_cU], writes=[B_U])
        for hh in range(2):
            if sample:
                Wd, c0 = 192, hh * 192
            else:
                Wd, c0 = 16 + NT // 2, hh * (NT // 2)
            Uh = U[:, :, c0:c0 + Wd]
            P.op("dve", lambda e, Uh=Uh, Wd=Wd: e.tensor_tensor(out=SA[:, :, 1:Wd], in0=Uh[:, :, 1:Wd], in1=Uh[:, :, 0:Wd - 1], op=ALU.add),
                 reads=[B_U], writes=[B_SA])
            P.op("dve", lambda e, Wd=Wd: e.tensor_tensor(out=SB[:, 1:4, 3:Wd], in0=SA[:, 1:4, 3:Wd], in1=SA[:, 1:4, 1:Wd - 2], op=ALU.add),
                 reads=[B_SA], writes=[B_SB])
            yield
            P.op("dve", lambda e, Wd=Wd: e.tensor_tensor(out=SA[:, 2:4, 7:Wd], in0=SB[:, 2:4, 7:Wd], in1=SB[:, 2:4, 3:Wd - 4], op=ALU.add),
                 reads=[B_SB], writes=[B_SA])
            P.op("dve", lambda e, Wd=Wd: e.tensor_tensor(out=SB[:, 3, 15:Wd], in0=SA[:, 3, 15:Wd], in1=SA[:, 3, 7:Wd - 8], op=ALU.add),
                 reads=[B_SA], writes=[B_SB])
            yield
            for g in range(4):
                S_, BS_ = (SA, B_SA) if g % 2 == 0 else (SB, B_SB)
                if sample:
                    sv = S_[:, g, 0:192].rearrange("p (b c) -> p b c", b=8)[:, :, 16:24]
                    uv = Uh[:, g, :].rearrange("p (b c) -> p b c", b=8)[:, :, 16:24]
                    dv = dT[:, g, hh * 64:(hh + 1) * 64].rearrange("p (b t) -> p b t", b=8)
                else:
                    sv, uv, dv = S_[:, g, 16:Wd], Uh[:, g, 16:Wd], dT[:, g, c0:c0 + NT // 2]
                P.op("dve", lambda e, sv=sv, uv=uv, dv=dv, g=g: e.scalar_tensor_tensor(out=dv, in0=sv, scalar=1.0 / (2 << g), in1=uv,
                                                                                   op0=ALU.mult, op1=ALU.subtract),
                     reads=[BS_, B_U], writes=[B_dT])
                if kind == "P" and gi == 0 and hh == 0:
                    P.op("dve", lambda e, S_=S_, g=g: e.tensor_tensor(out=st[:, 0:16], in0=S_[:, g, 16:32], in1=invc[:, g, :], op=ALU.mult),
                         reads=[BS_] + CONST, writes=[B_st])
                    P.op("dve", lambda e, g=g: e.tensor_tensor(out=dT[:, g, 0:16], in0=st[:, 0:16], in1=U[:, g, 16:32], op=ALU.subtract),
                         reads=[B_st, B_U], writes=[B_dT])
            yield
        if last:
            def dd2():
                P.dma("sp", "poolout", poolp[:, :], ost[113:128, :], reads=[B_ost])
            out_tok_major([U[:, g, 16 + NT - 128:16 + NT] for g in range(4)], [B_U], 128, dd2)
        if sample:
            for g in range(4):
                P.op("dve", lambda e, g=g: e.tensor_copy(out=SA[:, g, 0:128].rearrange("p (b t) -> p b t", b=16), in_=Us[:, g, :, 16:24]),
                     reads=[B_U, B_dT], writes=[B_SA])

            def dd3():
                for t in range(8):
                    P.dma("sp", "poolout", pools[:, 7 + t, :], ost[t:128:8, :], reads=[B_ost])
                P.dma("sp", "d2d_p", pools[:, 0:7, :], spool[:, 8:15, :])
            out_tok_major([SA[:, g, 0:128] for g in range(4)], [B_SA], 128, dd3)
        else:
            P.op("dve", lambda e: e.tensor_copy(out=carryU[:], in_=U[:, :, NT:NT + 16]), reads=[B_U], writes=[B_cU])
        yield
        for g in range(4):
            pb, Bp = dbank()
            P.mm([lambda e, g=g, pb=pb: e.matmul(pb[:, :NT], lhsT=wpool[:, g, :], rhs=dT[:, g, :NT], start=True, stop=True)],
                 reads=[B_wpool, B_dT], writes=[Bp])
            P.op("act", lambda e, g=g, pb=pb: e.activation(out=aoT[:, 4 + g, :NT], in_=pb[:, :NT], func=AF.Copy, scale=pscale[:, g:g + 1]),
                 reads=[Bp] + CONST, writes=[B_aoT])
            yield

        if not sample:
            for j in range(ntl):
                if gi == 0 and j == 1:
                    P.dma("sp", "biasld", bias[:].rearrange("p a b -> p (a b)"), biasg_d[:, :], writes=[B_bias])
                for gp in range(2):
                    bk = [sbank(), sbank()]
                    fns = []
                    for g in (2 * gp, 2 * gp + 1):
                        for kv in range(2):
                            fns.append(lambda e, kv=kv, g=g, j=j, bk=bk: e.matmul(
                                bk[kv][0][:, (g % 2) * 256:(g % 2 + 1) * 256], lhsT=qT[kv * 64:(kv + 1) * 64, g, j * 128:(j + 1) * 128],
                                rhs=kT[kv * 64:(kv + 1) * 64, j * 128:j * 128 + 256], start=True, stop=True))
                    P.mm(fns, reads=[B_qT, B_kT], writes=[bk[0][1], bk[1][1]])
                    for kv in range(2):
                        u0 = 4 * gp + kv
                        P.op("dve", lambda e, kv=kv, u0=u0, bk=bk: e.scalar_tensor_tensor(
                            out=sbias[:, u0:u0 + 3:2, :], in0=bk[kv][0][:].rearrange("p (g t) -> p g t", g=2), scalar=0.125,
                            in1=bias[:, u0:u0 + 3:2, :], op0=ALU.mult, op1=ALU.add),
                            reads=[bk[kv][1], B_bias], writes=[B_sbias])
                    yield
                win_softmax(8, 256, sinkp[:, 0:8])
                yield
                yield from diag_T(8, 2, pexp, B_pexp, lambda u0: pT[:, u0:u0 + 2, :, :].rearrange("p u k t -> p (u k t)"), None, B_pT, [128, 128])
                po, Bpo = ps[PS_O], B_ps[PS_O]
                pov = po[:].rearrange("p (g t) -> p g t", g=4)
                fns = []
                for g in range(4):
                    for kv in range(2):
                        for kc in range(2):
                            fns.append(lambda e, g=g, kv=kv, kc=kc, j=j: e.matmul(
                                pov[kv * 64:(kv + 1) * 64, g, :], lhsT=vtok[:, j + kc, kv * 64:(kv + 1) * 64], rhs=pT[:, 2 * g + kv, kc, :],
                                start=(kc == 0), stop=(kc == 1)))
                P.mm(fns, reads=[B_vtok, B_pT], writes=[Bpo])
                P.op("act", lambda e, j=j: e.activation(out=aoT[:, 0:4, j * 128:(j + 1) * 128], in_=pov, func=AF.Copy),
                     reads=[Bpo], writes=[B_aoT])
                yield
            carry()
        else:
            P.op("dve", lambda e: e.tensor_copy(out=qs2[:].rearrange("p b (g t) -> p b g t", g=4),
                                                in_=qT[:, :, 0:128].rearrange("p g (b t) -> p b g t", b=16)), reads=[B_qT], writes=[B_qs2])
            pb, Bp = dbank()
            pvb = pb[:].bitcast(BF16)
            P.mm([(lambda e, i=i, pvb=pvb: e.transpose(out=pvb[0:32, i * 128:(i + 1) * 128], in_=vT[:, i * 32:(i + 1) * 32], identity=ident[:]))
                  for i in range(4)], reads=[B_vT] + CONST, writes=[Bp])
            P.op("dve", lambda e, pvb=pvb: e.tensor_copy(out=vnq[0:32, :, :], in_=pvb[0:32, 0:512].rearrange("p (i t) -> p i t", i=4)),
                 reads=[Bp], writes=[B_vnq])
            yield
            for i in range(4):
                bk = [sbank(), sbank()]
                fns = []
                for kv in range(2):
                    pvk = bk[kv][0]
                    for jq in range(4):
                        b = 4 * i + jq
                        fns.append(lambda e, kv=kv, jq=jq, b=b, pvk=pvk: e.matmul(
                            pvk[32 * jq:32 * jq + 32, 0:128], lhsT=qs2[kv * 64:(kv + 1) * 64, b, :], rhs=kcT[kv * 64:(kv + 1) * 64, b, :],
                            start=True, stop=True, tile_position=(kv * 64, 32 * jq)))
                    fns.append(lambda e, kv=kv, i=i, pvk=pvk: e.matmul(
                        pvk[:, 128:160], lhsT=qs2[kv * 64:(kv + 1) * 64, 4 * i:4 * i + 4, :].rearrange("p b t -> p (b t)"),
                        rhs=kT[kv * 64:(kv + 1) * 64, 128 + 32 * i:128 + 32 * i + 32], start=True, stop=True))
                P.mm(fns, reads=[B_qs2, B_kcT, B_kT], writes=[bk[0][1], bk[1][1]])
                for kv in range(2):
                    P.op("dve", lambda e, kv=kv, i=i, bk=bk: e.scalar_tensor_tensor(
                        out=sbias[:, 2 * i + kv, 0:160], in0=bk[kv][0][:, 0:160], scalar=0.125,
                        in1=biass[:, kv, :], op0=ALU.mult, op1=ALU.add),
                        reads=[bk[kv][1]] + CONST, writes=[B_sbias])
                yield
            win_softmax(8, 160, sinks[:, 0:8])
            yield
            yield from diag_T(8, 2, pexp, B_pexp, None, lambda u, kc: pT[0:(128 if kc == 0 else 32), u, kc, :], B_pT, [128, 32])
            for i in range(4):
                pb, Bp = dbank()
                fns = []
                for kv in range(2):
                    u = 2 * i + kv
                    for jq in range(4):
                        b = 4 * i + jq
                        fns.append(lambda e, kv=kv, jq=jq, b=b, u=u, pb=pb: e.matmul(
                            pb[kv * 64:(kv + 1) * 64, 32 * jq:32 * jq + 32], lhsT=vc[:, b, kv * 64:(kv + 1) * 64], rhs=pT[:, u, 0, 32 * jq:32 * jq + 32],
                            start=(jq == 0), stop=False, skip_group_check=True))
                for kv in range(2):
                    u = 2 * i + kv
                    fns.append(lambda e, kv=kv, u=u, i=i, pb=pb: e.matmul(
                        pb[kv * 64:(kv + 1) * 64, 0:128], lhsT=vnq[0:32, i, kv * 64:(kv + 1) * 64], rhs=pT[0:32, u, 1, :],
                        start=False, stop=True, skip_group_check=True))
                P.mm(fns, reads=[B_vc, B_pT, B_vnq], writes=[Bp])
                P.op("act", lambda e, pb=pb, i=i: e.activation(
                    out=aoT[:, 0:4, 32 * i:32 * i + 32].rearrange("p g (j t) -> p j g t", j=4),
                    in_=pb[:, 0:128].rearrange("p (j g t) -> p j g t", j=4, g=4), func=AF.Copy),
                    reads=[Bp], writes=[B_aoT])
                yield

    def late_pre(kind, gi, X, gen):
        sample, halo, NT, ntl = geom(kind)
        xt, Bxt = xTs[X], B_xTs[X]
        loaded = [gen is None]

        def step_load():
            if not loaded[0]:
                if next(gen, "L") == "L":
                    loaded[0] = True
        prep2 = make_prep(X, NT, 1, -0.5, rstd, B_rstd)
        dense("out", list(range(8)), NT, lambda k: aoT[:, k, :NT], B_aoT, resid_evac(NT, X, prep=prep2))
        prep2[1]()
        qcT, B_qcT = aoT, B_aoT
        ocT, B_ocT = hidT, B_hid

        def cq_evac(m, pb, Bp):
            P.op("dve", lambda e: e.tensor_tensor(out=qcT[:, m, :NT], in0=pb[:, :NT], in1=rstd[:, :NT], op=ALU.mult),
                 reads=[Bp, B_rstd], writes=[B_qcT])
        dense("cq", list(range(8)), NT, lambda k: hT[:, k, :NT], B_hT, cq_evac)

        if not sample:
            for j in range(ntl):
                banks = [sbank(), sbank()]
                for hp, (pb, Bp) in enumerate(banks):
                    pv = pb[:].rearrange("p (h t) -> p h t", h=2)
                    fns = []
                    for hh in range(2):
                        h = 2 * hp + hh
                        for dc in range(2):
                            fns.append(lambda e, pv=pv, hh=hh, h=h, dc=dc, j=j: e.matmul(
                                pv[:, hh, :], lhsT=qcT[:, 2 * h + dc, j * 128:(j + 1) * 128], rhs=memkT[:, 2 * h + dc, :],
                                start=(dc == 0), stop=(dc == 1)))
                    P.mm(fns, reads=[B_qcT, B_memkT], writes=[Bp])
                cross_softmax(banks)
                step_load()
                run(diag_T(4, 2, pexp, B_pexp, lambda h0: pTc[:, h0:h0 + 2, :, :].rearrange("p u k t -> p (u k t)"), None, B_pTc, [128, 128]))
                for half in range(2):
                    pb, Bp = dbank()
                    pv = pb[:].rearrange("p (c t) -> p c t", c=4)
                    fns = []
                    for cc in range(4):
                        c = half * 4 + cc
                        h = c // 2
                        for mc in range(2):
                            fns.append(lambda e, pv=pv, cc=cc, c=c, h=h, mc=mc: e.matmul(
                                pv[:, cc, :], lhsT=memv[:, mc, c * 128:(c + 1) * 128], rhs=pTc[:, h, mc, :], start=(mc == 0), stop=(mc == 1)))
                    P.mm(fns, reads=[B_memv, B_pTc], writes=[Bp])
                    P.op("act" if half == 0 else "dve",
                         (lambda e, pv=pv, half=half, j=j: e.activation(out=ocT[:, half * 4:half * 4 + 4, j * 128:(j + 1) * 128], in_=pv, func=AF.Copy))
                         if half == 0 else
                         (lambda e, pv=pv, half=half, j=j: e.tensor_copy(out=ocT[:, half * 4:half * 4 + 4, j * 128:(j + 1) * 128], in_=pv)),
                         reads=[Bp], writes=[B_ocT])
                step_load()
        else:
            banks = [sbank(), sbank()]
            for i in range(2):
                P.op("dve", lambda e, i=i: e.memset(qpad[i][:], 0.0), writes=[B_qpad[i]])

            def ld_k(b):
                P.dma("pool", f"kb{b % 2}", Kb[b % 2][:], cmk[b].rearrange("(m p) f -> p m f", p=128), writes=[B_Kb[b % 2]])

            def ld_v(b):
                P.dma("pool", f"vb{b % 2}", Vb[b % 2][:], cmv[b].rearrange("(m p) f -> p m f", p=128), writes=[B_Vb[b % 2]])
            ld_k(0)
            ld_v(0)
            for b in range(16):
                s2 = b % 2
                if b + 1 < 16:
                    ld_k(b + 1)
                for mt in range(2):
                    pb, Bp = tbank()
                    pv = pb[:].bitcast(BF16).rearrange("p (c t) -> p c t", c=8)
                    P.mm([(lambda e, c=c, pv=pv, mt=mt, s2=s2: e.transpose(out=pv[:, c, :], in_=Kb[s2][:, mt, c * 128:(c + 1) * 128], identity=ident[:]))
                          for c in range(8)], reads=[B_Kb[s2]] + CONST, writes=[Bp])
                    P.op("act" if mt == 0 else "dve",
                         (lambda e, pv=pv, mt=mt, s2=s2: e.activation(out=KbT[s2][:, :, mt * 128:(mt + 1) * 128], in_=pv, func=AF.Copy))
                         if mt == 0 else
                         (lambda e, pv=pv, mt=mt, s2=s2: e.tensor_copy(out=KbT[s2][:, :, mt * 128:(mt + 1) * 128], in_=pv)),
                         reads=[Bp], writes=[B_KbT[s2]])
                if b >= 2:
                    P.op("dve", lambda e, s2=s2, b=b: e.memset(qpad[s2][:, :, (b - 2) * 8:(b - 1) * 8], 0.0), writes=[B_qpad[s2]])
                P.op("dve", lambda e, s2=s2, b=b: e.tensor_copy(out=qpad[s2][:, :, b * 8:(b + 1) * 8], in_=qcT[:, :, b * 8:(b + 1) * 8]),
                     reads=[B_qcT], writes=[B_qpad[s2]])
                for hp, (pb, Bp) in enumerate(banks):
                    pv = pb[:].rearrange("p (h t) -> p h t", h=2)
                    fns = []
                    for hh in range(2):
                        h = 2 * hp + hh
                        for dc in range(2):
                            fns.append(lambda e, pv=pv, hh=hh, h=h, dc=dc, s2=s2, b=b: e.matmul(
                                pv[:, hh, :], lhsT=qpad[s2][:, 2 * h + dc, :], rhs=KbT[s2][:, 2 * h + dc, :],
                                start=(b == 0 and hh == 0 and dc == 0), stop=(b == 15 and dc == 1), skip_group_check=True))
                    P.mm(fns, reads=[B_qpad[s2], B_KbT[s2]], writes=[Bp])
            cross_softmax(banks)
            run(diag_T(4, 2, pexp, B_pexp, lambda h0: pTc[:, h0:h0 + 2, :, :].rearrange("p u k t -> p (u k t)"), None, B_pTc, [128, 128]))
            pbs = [dbank(), dbank()]
            for b in range(16):
                s2 = b % 2
                if b + 1 < 16:
                    ld_v(b + 1)
                fns = []
                for c in range(8):
                    pv = pbs[c // 4][0][:].rearrange("p (c t) -> p c t", c=4)
                    h = c // 2
                    for mc in range(2):
                        fns.append(lambda e, pv=pv, c=c, h=h, mc=mc, s2=s2, b=b: e.matmul(
                            pv[:, c % 4, b * 8:(b + 1) * 8], lhsT=Vb[s2][:, mc, c * 128:(c + 1) * 128], rhs=pTc[:, h, mc, b * 8:(b + 1) * 8],
                            start=(mc == 0), stop=(mc == 1), skip_group_check=True))
                P.mm(fns, reads=[B_Vb[s2], B_pTc], writes=[pbs[0][1], pbs[1][1]])
            for half in range(2):
                pv = pbs[half][0][:].rearrange("p (c t) -> p c t", c=4)
                P.op("act" if half == 0 else "dve",
                     (lambda e, pv=pv, half=half: e.activation(out=ocT[:, half * 4:half * 4 + 4, 0:128], in_=pv, func=AF.Copy))
                     if half == 0 else
                     (lambda e, pv=pv, half=half: e.tensor_copy(out=ocT[:, half * 4:half * 4 + 4, 0:128], in_=pv)),
                     reads=[pbs[half][1]], writes=[B_ocT])
        while not loaded[0]:
            step_load()
        advance(gen, until="P1")
        prep3 = make_prep(X, NT, 3, -1.0, rstd2, B_rstd2)
        dense("co", list(range(8)), NT, lambda k: ocT[:, k, :NT], B_ocT, resid_evac(NT, X, prep=prep3))
        prep3[1]()

    def ffn(kind, gi, X, gen):
        sample, halo, NT, ntl = geom(kind)
        xt, Bxt = xTs[X], B_xTs[X]
        uctr = [0]

        def up_evac(m, pb, Bp):
            r, Br = relu_t[uctr[0] % 2], B_relu[uctr[0] % 2]
            uctr[0] += 1
            P.op("act", lambda e: e.activation(out=r[:, :NT], in_=pb[:, :NT], func=AF.Relu), reads=[Bp], writes=[Br])
            P.op("dve", lambda e: e.tensor_tensor(out=hidT[:, m, :NT], in0=r[:, :NT], in1=r[:, :NT], op=ALU.mult), reads=[Br], writes=[B_hid])
        for _ in g_dense("up", list(range(32)), NT, lambda k: hT[:, k, :NT], B_hT, up_evac):
            advance(gen, 1)
        for _ in g_dense("down", list(range(8)), NT, lambda k: hidT[:, k, :NT], B_hid, resid_evac(NT, X, scale2=True), kgroups=4):
            advance(gen, 1)

    def tail(kind, gi, X):
        sample, halo, NT, ntl = geom(kind)
        xt, Bxt = xTs[X], B_xTs[X]
        norm(xt, Bxt, 4, NT, yT, B_yT)
        for j in range(ntl):
            ys_, Bys = yst[j % 2], B_yst[j % 2]
            for hf in range(2):
                pb, Bp = dbank()
                pv = pb[:].rearrange("p (c t) -> p c t", c=4)
                P.mm([(lambda e, c=c, pv=pv, hf=hf, j=j: e.transpose(out=pv[:, c, :], in_=yT[:, hf * 4 + c, j * 128:(j + 1) * 128], identity=identf[:]))
                      for c in range(4)], reads=[B_yT] + CONST, writes=[Bp])
                P.op("act" if hf == 0 else "dve",
                     (lambda e, pb=pb, hf=hf, ys_=ys_: e.activation(out=ys_[:, hf * 512:(hf + 1) * 512], in_=pb[:, :], func=AF.Copy))
                     if hf == 0 else
                     (lambda e, pb=pb, hf=hf, ys_=ys_: e.tensor_copy(out=ys_[:, hf * 512:(hf + 1) * 512], in_=pb[:, :])),
                     reads=[Bp], writes=[Bys])
            if sample:
                P.dma("sp", "y", ys[:, :], ys_[:], reads=[Bys])
            else:
                r0 = gi * NT_P + j * 128
                P.dma("sp", "y", yp[r0:r0 + 128, :], ys_[:], reads=[Bys])

    order = [("P", g) for g in range(NG_P)] + [("S", 0)]
    order = order[:max(0, min(len(order), STAGE))] if STAGE < 50 else order
    run(early("H", 0, 0))
    gen0 = early(order[0][0], order[0][1], 0) if order else None
    advance(gen0, until="P1")
    mem_setup(gen0)
    run(gen0)
    for idx, (kind, gi) in enumerate(order):
        X = idx % 2
        nxt = order[idx + 1] if idx + 1 < len(order) else None
        gen = early(nxt[0], nxt[1], (idx + 1) % 2) if nxt else None
        late_pre(kind, gi, X, gen)
        ffn(kind, gi, X, gen)
        run(gen)
        tail(kind, gi, X)

    return finish()


_CACHE = {}


def _build_nc():
    if "nc" in _CACHE:
        return _CACHE["nc"]
    nc0 = bass.Bass("TRN2", target_bir_lowering=False)
    with ExitStack() as es0:
        _, W0 = build_sched(nc0, es0)
    sched = W0.rec
    nc = bass.Bass("TRN2", target_bir_lowering=False)
    with ExitStack() as es:
        P, W = build(nc, es, False, sched)
        assert W.i == len(sched), (W.i, len(sched))
        block = es.enter_context(nc.Block())
        P.flush(block)
    _CACHE["nc"] = nc
    return nc


def build_sched(nc0, es0):
    return build(nc0, es0, False, None)


def _tables(half):
    slopes = 2.0 ** (-(np.arange(8) + 1.0))
    q = np.arange(128)[:, None]
    c = np.arange(256)[None, :]
    dist = q - c + 128
    valid = (dist >= 0) & (dist <= 128)
    biasg = np.empty((128, 8, 256), np.float32)
    for g in range(4):
        for kv in range(2):
            h = kv * 4 + g
            biasg[:, 2 * g + kv, :] = np.where(valid, -slopes[h] * dist, -1e30)
    biasf = biasg.copy()
    if half == 0:
        biasf[:, :, 0:128] = -1e30
    biass = np.full((128, 2, 160), -1e30, np.float32)
    for j in range(4):
        for g in range(4):
            for t in range(8):
                r = j * 32 + g * 8 + t
                for kv in range(2):
                    h = kv * 4 + g
                    cc = np.arange(128)
                    d = t + 128 - cc
                    biass[r, kv, 0:128] = np.where(cc >= t, -slopes[h] * d, -1e30)
                    for tp in range(t + 1):
                        biass[r, kv, 128 + j * 8 + tp] = -slopes[h] * (t - tp)
    invc = np.empty((128, 4, 16), np.float32)
    for g in range(4):
        w = 2 << g
        for p in range(16):
            invc[:, g, p] = 1.0 / (min(p + 1, w) if half == 0 else w)
    return biasg.reshape(128, -1), biasf.reshape(128, -1), biass.reshape(128, -1), invc.reshape(128, -1)


def _prep(x_prompt, x_sample, cache_win_k, cache_win_v, state_pool, cache_mem_k, cache_mem_v,
          mem_prompt, g_mix, w_in, attn_sinks, w_pool, pool_scale, w_out, g_cross, g_mem,
          w_cq, w_ck, w_cv, w_co, g_ffn, w_up, w_down, g_final):
    f = lambda a: np.ascontiguousarray(np.asarray(a, dtype=np.float32))
    x_prompt, x_sample = f(x_prompt), f(x_sample)
    shared = dict(w_in=f(w_in)[0], w_pool=f(w_pool)[0], w_out=f(w_out)[0], w_cq=f(w_cq)[0], w_ck=f(w_ck)[0],
                  w_cv=f(w_cv)[0], w_co=f(w_co)[0], w_up=f(w_up)[0], w_down=f(w_down)[0])
    gs = np.stack([f(g_mix)[0], f(g_cross)[0], f(g_mem)[0], f(g_ffn)[0], f(g_final)], 0)
    shared["gvec"] = np.ascontiguousarray(gs.reshape(5, 8, 128).transpose(2, 0, 1).reshape(128, 40))
    shared["pscale"] = np.ascontiguousarray(f(pool_scale)[0].reshape(4, 128).T)
    sk = f(attn_sinks)[0]
    sinkp = np.empty((128, 8), np.float32)
    for g in range(4):
        for kv in range(2):
            sinkp[:, 2 * g + kv] = sk[kv * 4 + g]
    shared["sinkp"] = sinkp
    sinks = np.empty((128, 8), np.float32)
    for r in range(128):
        g = (r % 32) // 8
        for i in range(4):
            sinks[r, 2 * i] = sk[g]
            sinks[r, 2 * i + 1] = sk[4 + g]
    shared["sinks"] = sinks
    ckf, cvf, spf = f(cache_win_k)[0], f(cache_win_v)[0], f(state_pool)[0]
    cmkf, cmvf, memf = f(cache_mem_k)[0], f(cache_mem_v)[0], f(mem_prompt)
    in_maps = []
    for c in range(NCORES):
        b, half = c // 2, c % 2
        s0 = half * SEQ_CORE
        xp = np.zeros((128 + SEQ_CORE, D), np.float32)
        xp[128:] = x_prompt[b, s0:s0 + SEQ_CORE]
        if half == 1:
            xp[:128] = x_prompt[b, s0 - 128:s0]
        biasg, biasf, biass, invc = _tables(half)
        sl = slice(16 * c, 16 * c + 16)
        m = dict(shared)
        m.update(xp=xp, xs=np.ascontiguousarray(x_sample[sl].reshape(128, D)), mem=np.ascontiguousarray(memf[b]),
                 ck=np.ascontiguousarray(ckf[sl].reshape(16, 128, 128)), cv=np.ascontiguousarray(cvf[sl].reshape(16, 128, 128)),
                 spool=np.ascontiguousarray(spf[sl]), cmk=np.ascontiguousarray(cmkf[sl].reshape(16, 256, D)),
                 cmv=np.ascontiguousarray(cmvf[sl].reshape(16, 256, D)),
                 biasg=biasg, biasf=biasf, biass=biass, invc=invc)
        in_maps.append(m)
    return in_maps


def kernel(**inputs):
    in_maps = _prep(**inputs)
    nc = _build_nc()
    res = run_bass_kernel_spmd(nc, in_maps, core_ids=list(range(NCORES))).results
    return _assemble(res)


def _assemble(res):
    B, S = 4, 4096
    y_prompt = np.empty((B, S, D), np.float32)
    y_sample = np.empty((128, 8, D), np.float32)
    wk_p = np.empty((1, B, 128, 2, 64), np.float32); wv_p = np.empty_like(wk_p)
    pool_p = np.empty((1, B, 15, 512), np.float32)
    mk_p = np.empty((1, B, 256, 4, 256), np.float32); mv_p = np.empty_like(mk_p)
    wk_s = np.empty((1, 128, 128, 2, 64), np.float32); wv_s = np.empty_like(wk_s)
    pool_s = np.empty((1, 128, 15, 512), np.float32)
    for c in range(NCORES):
        r = res[c]
        b, half = c // 2, c % 2
        y_prompt[b, half * SEQ_CORE:(half + 1) * SEQ_CORE] = r["yp"]
        sl = slice(16 * c, 16 * c + 16)
        y_sample[sl] = r["ys"].reshape(16, 8, D)
        if half == 1:
            wk_p[0, b] = r["wkp"].reshape(128, 2, 64)
            wv_p[0, b] = r["wvp"].reshape(128, 2, 64)
            pool_p[0, b] = r["poolp"]
        else:
            mk_p[0, b] = r["memk"].reshape(256, 4, 256)
            mv_p[0, b] = r["memv"].reshape(256, 4, 256)
        wk_s[0, sl] = r["wks"].reshape(16, 128, 2, 64)
        wv_s[0, sl] = r["wvs"].reshape(16, 128, 2, 64)
        pool_s[0, sl] = r["pools"]
    return (y_prompt, y_sample, wk_p, wv_p, pool_p, mk_p, mv_p, wk_s, wv_s, pool_s)
```

```python
import numpy as np
from contextlib import ExitStack
import concourse.bass as bass
import concourse.mybir as mybir
from concourse.bass_utils import run_bass_kernel_spmd

F32 = mybir.dt.float32
BF16 = mybir.dt.bfloat16
ALU = mybir.AluOpType
AF = mybir.ActivationFunctionType
AX = mybir.AxisListType

NCORES = 8
STAGE = 99
TAIL_OVERLAP = True
D = 1024
SEQ_CORE = 2048
NT_P = 512
NG_P = SEQ_CORE // NT_P
RING = 10
EPS = 1e-5


class Buf:
    __slots__ = ("name", "w", "r", "al", "excl")

    def __init__(self, name, excl=False):
        self.name = name
        self.w = None
        self.r = {}
        self.al = []
        self.excl = excl


def alias(*bufs):
    for a in bufs:
        for b in bufs:
            if a is not b and b not in a.al:
                a.al.append(b)


class Prog:
    def __init__(self, nc, es, dry):
        self.nc, self.es, self.dry = nc, es, dry
        self.q = {e: [] for e in ("pe", "act", "dve", "pool", "sp")}
        self.cnt, self.sems = {}, {}
        self.waited = {e: {} for e in self.q}

    def sem(self, key):
        if key not in self.sems:
            self.sems[key] = None if self.dry else self.es.enter_context(self.nc.semaphore(key))
            self.cnt[key] = 0

    def _wait(self, eng, tok):
        if tok is None:
            return
        key, val = tok
        if self.waited[eng].get(key, 0) >= val:
            return
        self.waited[eng][key] = val
        self.q[eng].append(("w", key, val))

    def _deps(self, eng, reads, writes, extra):
        for b in reads:
            self._wait(eng, b.w)
            if b.excl:
                for k, v in b.r.items():
                    if k != eng:
                        self._wait(eng, (k, v))
        for b in writes:
            for bb in [b] + b.al:
                self._wait(eng, bb.w)
                for k, v in bb.r.items():
                    self._wait(eng, (k, v))
        for t in extra:
            self._wait(eng, t)

    def _commit(self, tok, reads, writes):
        k, v = tok
        for b in reads:
            b.r[k] = max(b.r.get(k, 0), v)
        for b in writes:
            b.w = tok
            b.r = {}

    def op(self, eng, fn, reads=(), writes=(), extra=()):
        self._deps(eng, reads, writes, extra)
        self.sem(eng)
        self.cnt[eng] += 1
        tok = (eng, self.cnt[eng])
        self.q[eng].append(("i", fn, eng, 1))
        self._commit(tok, reads, writes)
        return tok

    def mm(self, fns, reads=(), writes=(), extra=()):
        self._deps("pe", reads, writes, extra)
        for f in fns[:-1]:
            self.q["pe"].append(("i", f, None, 0))
        self.sem("pe")
        self.cnt["pe"] += 1
        tok = ("pe", self.cnt["pe"])
        self.q["pe"].append(("i", fns[-1], "pe", 1))
        self._commit(tok, reads, writes)
        return tok

    def dma(self, qeng, semkey, out, in_, reads=(), writes=(), extra=()):
        if writes:
            semkey = "dw" + qeng[0] + "_" + writes[0].name
        elif reads:
            semkey = "dr" + qeng[0] + "_" + reads[0].name
        for b in reads:
            self._wait(qeng, b.w)
        for b in writes:
            for bb in [b] + b.al:
                if not (bb.w is not None and bb.w[0] == semkey):
                    self._wait(qeng, bb.w)
                for k, v in bb.r.items():
                    self._wait(qeng, (k, v))
        for t in extra:
            self._wait(qeng, t)
        self.sem(semkey)
        self.cnt[semkey] += 16
        tok = (semkey, self.cnt[semkey])
        self.q[qeng].append(("i", (lambda e, o=out, i=in_: e.dma_start(out=o, in_=i)), semkey, 16))
        self._commit(tok, reads, writes)
        return tok

    def flush(self, block):
        def run(name):
            def f(e):
                for it in self.q[name]:
                    if it[0] == "w":
                        e.wait_ge(self.sems[it[1]], it[2])
                    else:
                        ins = it[1](e)
                        if it[3]:
                            ins.then_inc(self.sems[it[2]], it[3])
            return f
        block.tensor(run("pe"))
        block.scalar(run("act"))
        block.vector(run("dve"))
        block.gpsimd(run("pool"))
        block.sync(run("sp"))


class WStream:
    def __init__(self, P, ring_ap, sched, scratch_fn=None):
        self.P, self.ring = P, ring_ap
        self.sched = sched
        self.rec = []
        self.i = 0
        self.issued = 0
        self.slots = [Buf(f"ws{i}") for i in range(RING)]
        self.src = {}
        self.uidx, self.wtok = {}, {}
        self.scratch = None
        if sched is not None:
            cnt = {}
            for k in sched:
                cnt[k] = cnt.get(k, 0) + 1
            for k in sched:
                if cnt[k] > 1 and k not in self.uidx:
                    self.uidx[k] = len(self.uidx)
            if scratch_fn is not None and self.uidx:
                self.scratch = scratch_fn(len(self.uidx))

    def _issue(self, j):
        key = self.sched[j]
        name, m = key
        s = j % RING
        if self.scratch is not None and key in self.wtok:
            self.P.dma("sp", f"ws{s}", self.ring[:, s], self.scratch[self.uidx[key]], writes=[self.slots[s]],
                       extra=[self.wtok[key]])
            return
        for (dst_fn, src_ap) in self.src[name](m):
            self.P.dma("pool", f"ws{s}", dst_fn(self.ring[:, s]), src_ap, writes=[self.slots[s]])
        if self.scratch is not None and key in self.uidx:
            self.wtok[key] = self.P.dma("sp", f"sw{s}", self.scratch[self.uidx[key]], self.ring[:, s], reads=[self.slots[s]])

    def get(self, name, m):
        if self.sched is None:
            self.rec.append((name, m))
            return self.ring[:, 0], self.slots[0]
        assert self.sched[self.i] == (name, m), (self.i, self.sched[self.i], name, m)
        while self.issued < min(len(self.sched), self.i + RING - 3):
            self._issue(self.issued)
            self.issued += 1
        s = self.i % RING
        self.i += 1
        return self.ring[:, s], self.slots[s]


def build(nc, es, dry, sched):
    P = Prog(nc, es, dry)

    def din(name, shape):
        return nc.dram_tensor(name, list(shape), F32, kind="ExternalInput").ap()

    def dout(name, shape):
        return nc.dram_tensor(name, list(shape), F32, kind="ExternalOutput").ap()

    if not dry:
        xp = din("xp", [128 + SEQ_CORE, D]); xs = din("xs", [128, D]); mem = din("mem", [256, D])
        ck = din("ck", [16, 128, 128]); cv = din("cv", [16, 128, 128]); spool = din("spool", [16, 15, 512])
        cmk = din("cmk", [16, 256, D]); cmv = din("cmv", [16, 256, D])
        w_in = din("w_in", [D, 1280]); w_pool = din("w_pool", [4, 128, 128]); w_out = din("w_out", [D, D])
        w_cq = din("w_cq", [D, D]); w_ck = din("w_ck", [D, D]); w_cv = din("w_cv", [D, D]); w_co = din("w_co", [D, D])
        w_up = din("w_up", [D, 4 * D]); w_down = din("w_down", [4 * D, D])
        gvec_d = din("gvec", [128, 40]); pscale_d = din("pscale", [128, 4])
        sinkp_d = din("sinkp", [128, 8]); sinks_d = din("sinks", [128, 8])
        biasg_d = din("biasg", [128, 8 * 256]); biasf_d = din("biasf", [128, 8 * 256]); biass_d = din("biass", [128, 2 * 160])
        invc_d = din("invc", [128, 64])
        yp = dout("yp", [SEQ_CORE, D]); ys = dout("ys", [128, D])
        wkp = dout("wkp", [128, 128]); wvp = dout("wvp", [128, 128]); poolp = dout("poolp", [15, 512])
        memk_o = dout("memk", [256, D]); memv_o = dout("memv", [256, D])
        wks = dout("wks", [16, 128, 128]); wvs = dout("wvs", [16, 128, 128]); pools = dout("pools", [16, 15, 512])

    def sb(name, shape, dt):
        return es.enter_context(nc.sbuf_tensor("sb_" + name, list(shape), dt))

    xTs = [sb(f"xT{i}", [128, 8, NT_P], F32) for i in range(2)]; B_xTs = [Buf(f"xT{i}") for i in range(2)]
    xT, B_xT = xTs[0], B_xTs[0]
    hT = sb("hT", [128, 8, NT_P], BF16); B_hT = Buf("hT")
    rstd = sb("rstd", [128, NT_P], F32); B_rstd = Buf("rstd")
    aoT = sb("aoT", [128, 8, NT_P], BF16); B_aoT = Buf("aoT")
    qT = sb("qT", [128, 4, NT_P], BF16); B_qT = Buf("qT")
    kT = sb("kT", [128, 128 + NT_P], BF16); B_kT = Buf("kT")
    vT = sb("vT", [128, NT_P], BF16); B_vT = Buf("vT")
    vtok = sb("vtok", [128, 5, 128], BF16); B_vtok = Buf("vtok")
    kv32 = sb("kv32", [128, 2, 128], F32); B_kv32 = Buf("kv32")
    dT = sb("dT", [128, 4, NT_P], BF16); B_dT = Buf("dT")
    pexp = sb("pexp", [128, 8, 256], BF16); B_pexpH = [Buf("pexp0"), Buf("pexp1")]; B_pexp = B_pexpH[0]
    pT = sb("pT", [128, 8, 2, 128], BF16); B_pTH = [Buf("pT0"), Buf("pT1")]; B_pT = B_pTH[0]
    Dg = sb("Dg", [128, 8, 128], BF16); B_DgH = [Buf("Dg0"), Buf("Dg1")]; B_Dg = B_DgH[0]
    pTc = sb("pTc", [128, 4, 2, 128], BF16); B_pTc = Buf("pTc")
    bias = sb("bias", [128, 8, 256], F32); B_bias = Buf("bias")
    biass = sb("biass", [128, 2, 160], F32); B_biass = Buf("biass")
    memkT = sb("memkT", [128, 8, 256], BF16); B_memkT = Buf("memkT")
    memv = sb("memv", [128, 2, D], BF16); B_memv = Buf("memv")
    ring = sb("ring", [128, RING, 8, 128], BF16)
    xin = [sb(f"xin{i}", [128, D], F32) for i in range(2)]; B_xin = [Buf(f"xin{i}") for i in range(2)]
    yst1 = sb("yst1", [128, D], F32); yst, B_yst = [None, yst1], [None, Buf("yst1")]
    ident = sb("ident", [128, 128], BF16); identf = sb("identf", [128, 128], F32); B_const = Buf("const")
    ones = sb("ones", [128, 128], BF16)
    gvec = sb("gvec", [128, 5, 8], F32); pscale = sb("pscale", [128, 4], F32)
    sinkp = sb("sinkp", [128, 8], F32); sinks = sb("sinks", [128, 8], F32)
    invc = sb("invc", [128, 4, 16], F32)
    wpool = sb("wpool", [128, 4, 128], BF16); B_wpool = Buf("wpool")
    st = sb("st", [128, 64], F32); B_stH = [Buf("st0"), Buf("st1")]; B_st = B_stH[0]
    relu_t = [sb(f"relu{i}", [128, NT_P], BF16) for i in range(2)]; B_relu = [Buf(f"relu{i}") for i in range(2)]
    ost = sb("ost", [128, 512], F32); B_ost = Buf("ost")
    carryU = sb("carryU", [128, 4, 16], F32); B_cU = Buf("carryU")
    sqb = [sb(f"sq{i}", [128, NT_P], BF16) for i in range(2)]; B_sq = [Buf(f"sq{i}") for i in range(2)]
    rstd2 = sb("rstd2", [128, NT_P], F32); B_rstd2 = Buf("rstd2")

    R2 = 32 * NT_P * 2
    XO = R2 + 28672
    AR = XO + 10240
    arena = sb("arena", [128, AR // 2], BF16)

    def av(off, nbytes, dt, pat=None, **kw):
        v = arena[:, off // 2:(off + nbytes) // 2]
        if dt is F32:
            v = v.bitcast(F32)
        if pat:
            v = v.rearrange(pat, **kw)
        return v

    hidT = av(0, 32 * NT_P * 2, BF16, "p (k t) -> p k t", k=32); B_hid = Buf("hidT")
    yT = av(0, 8 * NT_P * 4, F32, "p (k t) -> p k t", k=8); B_yT = Buf("yT")
    WU = 16 + NT_P
    WH = 16 + 256
    U = av(R2, 4 * WU * 4, F32, "p (g t) -> p g t", g=4); B_U = Buf("U")
    SA = av(R2 + 4 * WU * 4, 4 * WH * 4, F32, "p (g t) -> p g t", g=4); B_SA = Buf("SA")
    SB = av(R2 + 4 * WU * 4 + 4 * WH * 4, 4 * WH * 4, F32, "p (g t) -> p g t", g=4); B_SB = Buf("SB")
    o_sb = R2 + 4 * WU * 4 + 8 * WH * 4
    sbias = av(o_sb, 8 * 256 * 4, F32, "p (u t) -> p u t", u=8); B_sbiasH = [Buf("sbias0"), Buf("sbias1")]; B_sbias = B_sbiasH[0]
    assert o_sb + 8192 <= XO
    kcT = av(XO, 16 * 128 * 2, BF16, "p (b t) -> p b t", b=16); B_kcT = Buf("kcT")
    vc = av(XO + 4096, 16 * 128 * 2, BF16, "p (b t) -> p b t", b=16); B_vc = Buf("vc")
    qs2 = av(XO + 8192, 16 * 32 * 2, BF16, "p (b t) -> p b t", b=16); B_qs2 = Buf("qs2")
    vnq = av(XO + 9216, 4 * 128 * 2, BF16, "p (i t) -> p i t", i=4); B_vnq = Buf("vnq")
    Kb = [av(R2 + i * 4096, 4096, BF16, "p (m t) -> p m t", m=2) for i in range(2)]; B_Kb = [Buf(f"Kb{i}") for i in range(2)]
    KbT = [av(R2 + 8192 + i * 4096, 4096, BF16, "p (c t) -> p c t", c=8) for i in range(2)]; B_KbT = [Buf(f"KbT{i}") for i in range(2)]
    Vb = [av(R2 + 16384 + i * 4096, 4096, BF16, "p (m t) -> p m t", m=2) for i in range(2)]; B_Vb = [Buf(f"Vb{i}") for i in range(2)]
    qpad = [av(R2 + 24576 + i * 2048, 2048, BF16, "p (c t) -> p c t", c=8) for i in range(2)]; B_qpad = [Buf(f"qpad{i}") for i in range(2)]
    memst = av(0, 8192, F32, "p (m t) -> p m t", m=2); B_memst = Buf("memst")
    mkst = av(8192, 8192, F32, "p (m t) -> p m t", m=2); B_mkst = Buf("mkst")
    alias(B_hid, B_yT)
    gX = [B_U, B_SA, B_SB] + B_sbiasH
    gY = B_Kb + B_KbT + B_Vb + B_qpad
    gZ = [B_memst, B_mkst]
    for ga, gb in ((gX, gY), ([B_hid, B_yT], gZ)):
        for a in ga:
            for b in gb:
                a.al.append(b)
                b.al.append(a)

    ps = [es.enter_context(nc.psum_tensor(f"ps{i}", [128, 512], F32)) for i in range(8)]
    B_ps = [Buf(f"ps{i}", excl=True) for i in range(8)]
    dctr = [0]

    def dbank():
        i = dctr[0] % 3
        dctr[0] += 1
        return ps[i], B_ps[i]
    PS_S = [3, 4]
    PS_T = [5, 6]
    PS_O = 7
    sctr = [0]
    tctr = [0]

    def sbank():
        i = PS_S[sctr[0] % 2]; sctr[0] += 1
        return ps[i], B_ps[i]

    def tbank():
        i = PS_T[tctr[0] % 2]; tctr[0] += 1
        return ps[i], B_ps[i]

    def scratch_fn(n):
        return nc.dram_tensor("wscratch", [n, 128, 8, 128], BF16, kind="Internal").ap()
    W = WStream(P, ring, sched, scratch_fn)
    if not dry:
        def std_src(wap):
            v = wap.rearrange("(k p) (m c) -> p m k c", p=128, c=128)
            return lambda m: [((lambda s: s), v[:, m])]
        W.src["ck"] = std_src(w_ck); W.src["cv"] = std_src(w_cv)
        W.src["cq"] = std_src(w_cq); W.src["co"] = std_src(w_co); W.src["up"] = std_src(w_up)
        vin_q = w_in[:, 0:512].rearrange("(k p) (kv g d) -> p g k kv d", p=128, kv=2, g=4, d=64)
        vin_r = w_in[:, 512:1280].rearrange("(k p) (m c) -> p m k c", p=128, c=128)

        def in_src(m):
            if m < 4:
                return [((lambda s: s[:, :, 0:64]), vin_q[:, m, :, 0, :]),
                        ((lambda s: s[:, :, 64:128]), vin_q[:, m, :, 1, :])]
            return [((lambda s: s), vin_r[:, m - 4])]
        W.src["in"] = in_src
        vo_a = w_out[0:512, :].rearrange("(kv g d) (m c) -> kv d m g c", kv=2, g=4, d=64, c=128)
        vo_p = w_out[512:1024, :].rearrange("(k p) (m c) -> p m k c", p=128, c=128)

        def out_src(m):
            return [((lambda s: s[0:64, 0:4, :]), vo_a[0, :, m]),
                    ((lambda s: s[64:128, 0:4, :]), vo_a[1, :, m]),
                    ((lambda s: s[:, 4:8, :]), vo_p[:, m])]
        W.src["out"] = out_src
        vdn = w_down.rearrange("(q k p) (m c) -> p m q k c", p=128, k=8, c=128)
        W.src["down"] = lambda mq: [((lambda s: s), vdn[:, mq // 4, mq % 4])]

    if not dry:
        P.op("pool", lambda e: e.memset(identf[:], 0.0), writes=[B_const])
        P.op("pool", lambda e: e.iota(identf[:], pattern=[[1, 128]], base=0, channel_multiplier=-1,
                                      allow_small_or_imprecise_dtypes=True), writes=[B_const])
        P.op("dve", lambda e: e.tensor_single_scalar(out=ident[:], in_=identf[:], scalar=0.0, op=ALU.is_equal),
             reads=[B_const], writes=[B_const])
        P.op("dve", lambda e: e.tensor_single_scalar(out=identf[:], in_=identf[:], scalar=0.0, op=ALU.is_equal),
             writes=[B_const])
        P.op("dve", lambda e: e.memset(ones[:], 1.0), writes=[B_const])
        for (dst, src) in ((gvec[:].rearrange("p a b -> p (a b)"), gvec_d), (pscale[:], pscale_d), (sinkp[:], sinkp_d),
                           (sinks[:], sinks_d), (invc[:].rearrange("p a b -> p (a b)"), invc_d),
                           (biass[:].rearrange("p a b -> p (a b)"), biass_d)):
            P.dma("sp", "cst", dst, src[:, :], writes=[B_const])
        P.dma("sp", "biasld", bias[:].rearrange("p a b -> p (a b)"), biasf_d[:, :], writes=[B_bias])
        P.dma("pool", "wpool", wpool[:], w_pool.rearrange("g c e -> c g e"), writes=[B_wpool])

    CONST = [B_const]

    class _Stop(Exception):
        pass

    def finish():
        for key, val in P.cnt.items():
            if key not in ("pe", "act", "dve", "pool"):
                P._wait("sp", (key, val))
        for e_ in ("pe", "act", "dve", "pool"):
            if P.cnt.get(e_, 0):
                P._wait("sp", (e_, P.cnt[e_]))
        return P, W
    if STAGE == -1:
        return finish()

    def run(gen):
        if gen is not None:
            for _ in gen:
                pass

    def advance(gen, n=1, until=None):
        if gen is None:
            return
        if until is not None:
            for v in gen:
                if v == until:
                    return
            return
        for _ in range(n):
            try:
                next(gen)
            except StopIteration:
                return

    def g_load_x(src_rows, ntiles, X):
        dst, dstB = xTs[X], B_xTs[X]
        for j in range(ntiles):
            xb, Bx = xin[j % 2], B_xin[j % 2]
            P.dma("sp", "x", xb[:], src_rows(j), writes=[Bx])
            for hf in range(2):
                pb, Bp = dbank()
                pv = pb[:].rearrange("p (c t) -> p c t", c=4)
                P.mm([(lambda e, c=c, pv=pv, xb=xb, hf=hf: e.transpose(out=pv[:, c, :], in_=xb[:, (hf * 4 + c) * 128:(hf * 4 + c + 1) * 128],
                                                                       identity=identf[:])) for c in range(4)],
                     reads=[Bx] + CONST, writes=[Bp])
                P.op("act" if hf == 0 else "dve",
                     (lambda e, pv=pv, hf=hf, j=j: e.activation(out=dst[:, hf * 4:hf * 4 + 4, j * 128:(j + 1) * 128], in_=pv, func=AF.Copy))
                     if hf == 0 else
                     (lambda e, pv=pv, hf=hf, j=j: e.tensor_copy(out=dst[:, hf * 4:hf * 4 + 4, j * 128:(j + 1) * 128], in_=pv)),
                     reads=[Bp], writes=[dstB])
                yield

    def norm(src, Bsrc, gi, NT, dst, Bdst):
        P.op("act", lambda e: e.activation(out=hT[:, :, :NT], in_=src[:, :, :NT], func=AF.Square),
             reads=[Bsrc], writes=[B_hT])
        pb, Bp = dbank()
        P.mm([(lambda e, k=k: e.matmul(pb[:, :NT], lhsT=ones[:], rhs=hT[:, k, :NT], start=(k == 0), stop=(k == 7)))
              for k in range(8)], reads=[B_hT] + CONST, writes=[Bp])
        P.op("act", lambda e: e.activation(out=rstd[:, :NT], in_=pb[:, :NT], func=AF.Ln, scale=1.0 / D, bias=EPS),
             reads=[Bp], writes=[B_rstd])
        P.op("act", lambda e: e.activation(out=rstd[:, :NT], in_=rstd[:, :NT], func=AF.Exp, scale=-0.5),
             reads=[B_rstd], writes=[B_rstd])
        for k in range(8):
            P.op("dve", lambda e, k=k: e.scalar_tensor_tensor(out=dst[:, k, :NT], in0=src[:, k, :NT], scalar=gvec[:, gi, k:k + 1],
                                                              in1=rstd[:, :NT], op0=ALU.mult, op1=ALU.mult),
                 reads=[Bsrc, B_rstd] + CONST, writes=[Bdst])

    def g_dense(wname, units, NT, rhs_fn, Brhs, evac, kgroups=1):
        for m in units:
            pb, Bp = dbank()
            fns, Bs = [], []
            for q in range(kgroups):
                slot, Bslot = W.get(wname, m * kgroups + q if kgroups > 1 else m)
                Bs.append(Bslot)
                for k in range(8):
                    fns.append(lambda e, slot=slot, k=k, q=q, pb=pb: e.matmul(
                        pb[:, :NT], lhsT=slot[:, k, :], rhs=rhs_fn(q * 8 + k),
                        start=(q == 0 and k == 0), stop=(q == kgroups - 1 and k == 7)))
            P.mm(fns, reads=Bs + [Brhs], writes=[Bp])
            evac(m, pb, Bp)
            for _ in range(kgroups):
                yield

    def dense(*a, **kw):
        run(g_dense(*a, **kw))

    def resid_evac(NT, X, prep=None, scale2=False):
        xt, Bxt = xTs[X], B_xTs[X]

        def f(m, pb, Bp):
            if scale2:
                P.op("dve", lambda e: e.tensor_tensor(out=pb[:, :NT], in0=pb[:, :NT], in1=rstd2[:, :NT], op=ALU.mult),
                     reads=[Bp, B_rstd2], writes=[Bp])
            P.op("dve", lambda e: e.tensor_tensor(out=xt[:, m, :NT], in0=pb[:, :NT], in1=xt[:, m, :NT], op=ALU.add),
                 reads=[Bp], writes=[Bxt])
            if prep is not None:
                prep[0](m)
        return f

    def make_prep(X, NT, gidx, exp_scale, rdst, Brdst):
        xt, Bxt = xTs[X], B_xTs[X]
        sp_, Bsp = ps[PS_O], B_ps[PS_O]
        pend = []

        def emit_mm(k):
            P.mm([lambda e, k=k: e.matmul(sp_[:, :NT], lhsT=ones[:], rhs=sqb[k % 2][:, :NT], start=(k == 0), stop=(k == 7))],
                 reads=[B_sq[k % 2]] + CONST, writes=[Bsp])

        def after(m):
            while len(pend) >= 2:
                emit_mm(pend.pop(0))
            P.op("act", lambda e: e.activation(out=hT[:, m, :NT], in_=xt[:, m, :NT], func=AF.Copy, scale=gvec[:, gidx, m:m + 1]),
                 reads=[Bxt] + CONST, writes=[B_hT])
            P.op("act", lambda e: e.activation(out=sqb[m % 2][:, :NT], in_=xt[:, m, :NT], func=AF.Square),
                 reads=[Bxt], writes=[B_sq[m % 2]])
            pend.append(m)

        def flush():
            while pend:
                emit_mm(pend.pop(0))
            P.op("act", lambda e: e.activation(out=rdst[:, :NT], in_=sp_[:, :NT], func=AF.Ln, scale=1.0 / D, bias=EPS),
                 reads=[Bsp], writes=[Brdst])
            P.op("act", lambda e: e.activation(out=rdst[:, :NT], in_=rdst[:, :NT], func=AF.Exp, scale=exp_scale),
                 reads=[Brdst], writes=[Brdst])
        return after, flush

    def diag_T(u0, nu, nkc, p_src, Bp_src, BDg, dst4, dst_fn, Bdst, kw):
        items = [(u, kc) for u in range(u0, u0 + nu) for kc in range(nkc)]
        full = all(w == 128 for w in kw)
        for bi, i0 in enumerate(range(0, len(items), 4)):
            chunk = items[i0:i0 + 4]
            pb, Bp = tbank()
            pv = pb[:].rearrange("p (s t) -> p s t", s=4)
            P.mm([(lambda e, s=s, u=u, kc=kc, pv=pv: e.matmul(pv[0:kw[kc], s, :], lhsT=p_src[:, u, kc * 128:kc * 128 + kw[kc]],
                                                              rhs=Dg[:, u, :], start=True, stop=True))
                  for s, (u, kc) in enumerate(chunk)], reads=list(Bp_src) + list(BDg), writes=[Bp])
            if full:
                uu = chunk[0][0]
                if bi % 2 == 0:
                    P.op("act", lambda e, pb=pb, uu=uu: e.activation(out=dst4(uu), in_=pb[:, 0:512], func=AF.Copy), reads=[Bp], writes=list(Bdst))
                else:
                    P.op("dve", lambda e, pb=pb, uu=uu: e.tensor_copy(out=dst4(uu), in_=pb[:, 0:512]), reads=[Bp], writes=list(Bdst))
            else:
                for s, (u, kc) in enumerate(chunk):
                    P.op("act" if bi % 2 == 0 else "dve",
                         (lambda e, s=s, u=u, kc=kc, pv=pv: e.activation(out=dst_fn(u, kc), in_=pv[0:kw[kc], s, :], func=AF.Copy))
                         if bi % 2 == 0 else
                         (lambda e, s=s, u=u, kc=kc, pv=pv: e.tensor_copy(out=dst_fn(u, kc), in_=pv[0:kw[kc], s, :])),
                         reads=[Bp], writes=list(Bdst))
            yield

    def make_Dg(u0, nu, Bst, BDg):
        P.op("dve", lambda e: e.tensor_tensor(out=Dg[:, u0:u0 + nu, :], in0=ident[:].unsqueeze(1).to_broadcast([128, nu, 128]),
                                              in1=st[:, 56 + u0:56 + u0 + nu].unsqueeze(2).to_broadcast([128, nu, 128]), op=ALU.mult),
             reads=list(Bst) + CONST, writes=list(BDg))

    def win_softmax(u0, nu, width, sink_ap, hs):
        Bsb = [B_sbiasH[h] for h in hs]; Bst = [B_stH[h] for h in hs]
        Bpe = [B_pexpH[h] for h in hs]; BDg = [B_DgH[h] for h in hs]
        c = lambda base: slice(base + u0, base + u0 + nu)
        P.op("dve", lambda e: e.tensor_reduce(out=st[:, c(0)], in_=sbias[:, u0:u0 + nu, 0:width], axis=AX.X, op=ALU.max),
             reads=Bsb, writes=Bst)
        P.op("dve", lambda e: e.tensor_tensor(out=st[:, c(8)], in0=st[:, c(0)], in1=sink_ap, op=ALU.max),
             reads=Bst + CONST, writes=Bst)
        P.op("dve", lambda e: e.tensor_scalar(out=st[:, c(16)], in0=st[:, c(8)], scalar1=-1.0, scalar2=None, op0=ALU.mult),
             reads=Bst, writes=Bst)
        P.op("dve", lambda e: e.tensor_tensor(out=st[:, c(24)], in0=sink_ap, in1=st[:, c(16)], op=ALU.add),
             reads=Bst + CONST, writes=Bst)
        for u in range(u0, u0 + nu):
            P.op("act", lambda e, u=u: e.activation(out=pexp[:, u, 0:width], in_=sbias[:, u, 0:width], func=AF.Exp,
                                                    bias=st[:, 16 + u:17 + u], scale=1.0, accum_out=st[:, 32 + u:33 + u]),
                 reads=Bsb + Bst, writes=Bpe + Bst)
        P.op("act", lambda e: e.activation(out=st[:, c(40)], in_=st[:, c(24)], func=AF.Exp),
             reads=Bst, writes=Bst)
        yield
        P.op("dve", lambda e: e.tensor_tensor(out=st[:, c(48)], in0=st[:, c(32)], in1=st[:, c(40)], op=ALU.add),
             reads=Bst, writes=Bst)
        P.op("dve", lambda e: e.reciprocal(out=st[:, c(56)], in_=st[:, c(48)]), reads=Bst, writes=Bst)
        make_Dg(u0, nu, Bst, BDg)

    def cross_softmax(score_banks):
        for hp, (pb, Bp) in enumerate(score_banks):
            pv = pb[:].rearrange("p (h t) -> p h t", h=2)
            P.op("dve", lambda e, pv=pv, hp=hp: e.tensor_reduce(out=st[:, 2 * hp:2 * hp + 2], in_=pv, axis=AX.X, op=ALU.max),
                 reads=[Bp], writes=[B_st])
        P.op("dve", lambda e: e.tensor_scalar(out=st[:, 16:20], in0=st[:, 0:4], scalar1=-1.0 / 16.0, scalar2=None, op0=ALU.mult),
             reads=[B_st], writes=[B_st])
        for hp, (pb, Bp) in enumerate(score_banks):
            pv = pb[:].rearrange("p (h t) -> p h t", h=2)
            for hh in range(2):
                h = 2 * hp + hh
                P.op("act", lambda e, pv=pv, hh=hh, h=h: e.activation(out=pexp[:, h, :], in_=pv[:, hh, :], func=AF.Exp,
                                                                      bias=st[:, 16 + h:17 + h], scale=1.0 / 16.0,
                                                                      accum_out=st[:, 32 + h:33 + h]),
                     reads=[Bp, B_st], writes=[B_pexp, B_st])
        P.op("dve", lambda e: e.reciprocal(out=st[:, 56:60], in_=st[:, 32:36]), reads=[B_st], writes=[B_st])
        make_Dg(0, 4, [B_st], [B_Dg])

    def out_tok_major(srcs, Bsrcs, ncols_each, dst_dma):
        pb, Bp = dbank()
        pv = pb[:].rearrange("p (c t) -> p c t", c=4)
        n = len(srcs)
        P.mm([(lambda e, i=i: e.transpose(out=pv[:, i, :], in_=srcs[i], identity=identf[:])) for i in range(n)],
             reads=list(Bsrcs) + CONST, writes=[Bp])
        P.op("dve", lambda e: e.tensor_copy(out=ost[:, 0:n * 128], in_=pb[:, 0:n * 128]), reads=[Bp], writes=[B_ost])
        dst_dma()

    def mem_setup(gen):
        mx, Bmx = xTs[1], B_xTs[1]
        for t in range(2):
            P.dma("sp", "memld", memst[:, t, :], mem[t * 128:(t + 1) * 128, :], writes=[B_memst])
        for t in range(2):
            for hf in range(2):
                pb, Bp = dbank()
                pv = pb[:].rearrange("p (c t) -> p c t", c=4)
                P.mm([(lambda e, c=c, pv=pv, t=t, hf=hf: e.transpose(out=pv[:, c, :], in_=memst[:, t, (hf * 4 + c) * 128:(hf * 4 + c + 1) * 128],
                                                                     identity=identf[:])) for c in range(4)],
                     reads=[B_memst] + CONST, writes=[Bp])
                P.op("act", lambda e, pv=pv, hf=hf, t=t: e.activation(out=mx[:, hf * 4:hf * 4 + 4, t * 128:(t + 1) * 128], in_=pv, func=AF.Copy),
                     reads=[Bp], writes=[Bmx])
        norm(mx, Bmx, 2, 256, hT, B_hT)
        for (wn, is_k) in (("ck", True), ("cv", False)):
            for m in range(8):
                slot, Bslot = W.get(wn, m)
                if is_k:
                    pb, Bp = dbank()
                    P.mm([(lambda e, k=k, slot=slot, pb=pb: e.matmul(pb[:, :256], lhsT=slot[:, k, :], rhs=hT[:, k, :256],
                                                                    start=(k == 0), stop=(k == 7))) for k in range(8)],
                         reads=[Bslot, B_hT], writes=[Bp])
                    P.op("act", lambda e, m=m, pb=pb: e.activation(out=memkT[:, m, :], in_=pb[:, :256], func=AF.Copy),
                         reads=[Bp], writes=[B_memkT])
                pb, Bp = dbank()
                fns = []
                for t in range(2):
                    for k in range(8):
                        fns.append(lambda e, k=k, slot=slot, pb=pb, t=t: e.matmul(
                            pb[:, t * 128:(t + 1) * 128], lhsT=hT[:, k, t * 128:(t + 1) * 128], rhs=slot[:, k, :],
                            start=(k == 0), stop=(k == 7)))
                P.mm(fns, reads=[Bslot, B_hT], writes=[Bp])
                pv2 = pb[:, 0:256].rearrange("p (t c) -> p t c", t=2)
                P.op("dve", lambda e, pv2=pv2, m=m: e.tensor_copy(out=mkst[:, :, m * 128:(m + 1) * 128], in_=pv2),
                     reads=[Bp], writes=[B_mkst])
                if not is_k:
                    P.op("act", lambda e, pv2=pv2, m=m: e.activation(out=memv[:, :, m * 128:(m + 1) * 128], in_=pv2, func=AF.Copy),
                         reads=[Bp], writes=[B_memv])
                advance(gen, 3)
            for t in range(2):
                P.dma("pool", "memout", (memk_o if is_k else memv_o)[t * 128:(t + 1) * 128, :], mkst[:, t, :], reads=[B_mkst])

    def geom(kind):
        sample, halo = (kind == "S"), (kind == "H")
        NT = 128 if (sample or halo) else NT_P
        return sample, halo, NT, NT // 128

    def early(kind, gi, X):
        sample, halo, NT, ntl = geom(kind)
        xt, Bxt = xTs[X], B_xTs[X]
        if sample:
            yield from g_load_x(lambda j: xs[:, :], 1, X)
        elif halo:
            yield from g_load_x(lambda j: xp[0:128, :], 1, X)
        else:
            yield from g_load_x(lambda j: xp[128 + gi * NT_P + j * 128: 128 + gi * NT_P + (j + 1) * 128, :], ntl, X)
        yield "L"
        norm(xt, Bxt, 0, NT, hT, B_hT)
        last = (kind == "P" and gi == NG_P - 1)
        want32 = last or sample
        Us = U[:, :, 0:384].rearrange("p g (b c) -> p g b c", b=16)

        if sample:
            P.op("dve", lambda e: e.memset(U[:, :, 0:384], 0.0), writes=[B_U])
            for hb in range(2):
                P.dma("sp", "spld", xin[hb][0:120, 0:512], spool[hb * 8:(hb + 1) * 8].rearrange("b r f -> (b r) f"), writes=[B_xin[hb]])
                pb, Bp = dbank()
                pv = pb[:].rearrange("p (c t) -> p c t", c=4)
                P.mm([(lambda e, c=c, pv=pv, hb=hb: e.transpose(out=pv[:, c, 0:120], in_=xin[hb][0:120, c * 128:(c + 1) * 128],
                                                                identity=identf[0:120, 0:120])) for c in range(4)],
                     reads=[B_xin[hb]] + CONST, writes=[Bp])
                for c in range(4):
                    P.op("dve", lambda e, c=c, pv=pv, hb=hb: e.tensor_copy(
                        out=Us[:, c, hb * 8:(hb + 1) * 8, 1:16], in_=pv[:, c, 0:120].rearrange("p (b r) -> p b r", b=8)),
                        reads=[Bp], writes=[B_U])
            yield

        def in_evac(m, pb, Bp):
            if m < 4:
                P.op("act", lambda e: e.activation(out=qT[:, m, :NT], in_=pb[:, :NT], func=AF.Copy), reads=[Bp], writes=[B_qT])
            elif m == 4:
                P.op("act", lambda e: e.activation(out=kT[:, 128:128 + NT], in_=pb[:, :NT], func=AF.Copy), reads=[Bp], writes=[B_kT])
                if want32:
                    P.op("dve", lambda e: e.tensor_copy(out=kv32[:, 0, :], in_=pb[:, NT - 128:NT]), reads=[Bp], writes=[B_kv32])
            elif m == 5:
                P.op("act", lambda e: e.activation(out=vT[:, :NT], in_=pb[:, :NT], func=AF.Copy), reads=[Bp], writes=[B_vT])
                if want32:
                    P.op("dve", lambda e: e.tensor_copy(out=kv32[:, 1, :], in_=pb[:, NT - 128:NT]), reads=[Bp], writes=[B_kv32])
            else:
                g = m - 6
                if sample:
                    P.op("dve", lambda e: e.tensor_copy(out=Us[:, g, :, 16:24], in_=pb[:, 0:128].rearrange("p (b t) -> p b t", b=16)),
                         reads=[Bp], writes=[B_U])
                else:
                    P.op("dve", lambda e: e.tensor_copy(out=U[:, g, 16:16 + NT], in_=pb[:, :NT]), reads=[Bp], writes=[B_U])
        yield from g_dense("in", list(range(4, 10)) if halo else list(range(10)), NT, lambda k: hT[:, k, :NT], B_hT, in_evac)
        yield "P1"

        if sample:
            for hb in range(2):
                P.dma("pool", "ckld", vc[:, hb * 8:(hb + 1) * 8, :], cv[hb * 8:(hb + 1) * 8].rearrange("b s f -> s b f"), writes=[B_vc])
            kst = pexp[:].rearrange("p u t -> p (u t)").rearrange("p (b f) -> p b f", b=16)
            for hb in range(2):
                P.dma("pool", "ckld", kst[:, hb * 8:(hb + 1) * 8, :], ck[hb * 8:(hb + 1) * 8].rearrange("b s f -> s b f"), writes=[B_pexpH[hb]])
            for hb in range(2):
                pb, Bp = tbank()
                pv = pb[:].bitcast(BF16).rearrange("p (b t) -> p b t", b=8)
                P.mm([(lambda e, b=b, pv=pv, hb=hb: e.transpose(out=pv[:, b, :], in_=kst[:, hb * 8 + b, :], identity=ident[:])) for b in range(8)],
                     reads=[B_pexpH[hb]] + CONST, writes=[Bp])
                P.op("dve", lambda e, pv=pv, hb=hb: e.tensor_copy(out=kcT[:, hb * 8:(hb + 1) * 8, :], in_=pv), reads=[Bp], writes=[B_kcT])
            yield

        for j0 in range(0, ntl, 4):
            pb, Bp = tbank()
            pv = pb[:].bitcast(BF16)[:, 0:512].rearrange("p (j t) -> p j t", j=4)
            P.mm([(lambda e, j=j, pv=pv: e.transpose(out=pv[:, j - j0, :], in_=vT[:, j * 128:(j + 1) * 128], identity=ident[:]))
                  for j in range(j0, min(ntl, j0 + 4))], reads=[B_vT] + CONST, writes=[Bp])
            nj = min(ntl, j0 + 4) - j0
            P.op("dve", lambda e, pv=pv, j0=j0, nj=nj: e.tensor_copy(out=vtok[:, 1 + j0:1 + j0 + nj, :], in_=pv[:, 0:nj, :]),
                 reads=[Bp], writes=[B_vtok])
        yield

        def carry():
            P.op("dve", lambda e: e.tensor_copy(out=kT[:, 0:128], in_=kT[:, NT:NT + 128]), reads=[B_kT], writes=[B_kT])
            P.op("dve", lambda e: e.tensor_copy(out=vtok[:, 0, :], in_=vtok[:, ntl, :]), reads=[B_vtok], writes=[B_vtok])

        if halo:
            carry()
            P.op("dve", lambda e: e.tensor_copy(out=carryU[:], in_=U[:, :, NT:NT + 16]), reads=[B_U], writes=[B_cU])
            return

        if want32:
            if last:
                def dd():
                    P.dma("pool", "kvout", wkp[:, :], ost[:, 0:128], reads=[B_ost])
                    P.dma("pool", "kvout", wvp[:, :], ost[:, 128:256], reads=[B_ost])
            else:
                def dd():
                    for t in range(8):
                        P.dma("pool", "kvout", wks[:, 120 + t, :], ost[t:128:8, 0:128], reads=[B_ost])
                        P.dma("pool", "kvout", wvs[:, 120 + t, :], ost[t:128:8, 128:256], reads=[B_ost])
                    P.dma("sp", "d2d_k", wks[:, 0:120, :], ck[:, 8:128, :])
                    P.dma("sp", "d2d_v", wvs[:, 0:120, :], cv[:, 8:128, :])
            out_tok_major([kv32[:, 0, :], kv32[:, 1, :]], [B_kv32], 128, dd)
            yield

        if not sample:
            P.op("dve", lambda e: e.tensor_copy(out=U[:, :, 0:16], in_=carryU[:]), reads=[B_cU], writes=[B_U])
        for hh in range(2):
            if sample:
                Wd, c0 = 192, hh * 192
            else:
                Wd, c0 = 16 + NT // 2, hh * (NT // 2)
            Uh = U[:, :, c0:c0 + Wd]
            P.op("dve", lambda e, Uh=Uh, Wd=Wd: e.tensor_tensor(out=SA[:, :, 1:Wd], in0=Uh[:, :, 1:Wd], in1=Uh[:, :, 0:Wd - 1], op=ALU.add),
                 reads=[B_U], writes=[B_SA])
            P.op("dve", lambda e, Wd=Wd: e.tensor_tensor(out=SB[:, 1:4, 3:Wd], in0=SA[:, 1:4, 3:Wd], in1=SA[:, 1:4, 1:Wd - 2], op=ALU.add),
                 reads=[B_SA], writes=[B_SB])
            yield
            P.op("dve", lambda e, Wd=Wd: e.tensor_tensor(out=SA[:, 2:4, 7:Wd], in0=SB[:, 2:4, 7:Wd], in1=SB[:, 2:4, 3:Wd - 4], op=ALU.add),
                 reads=[B_SB], writes=[B_SA])
            P.op("dve", lambda e, Wd=Wd: e.tensor_tensor(out=SB[:, 3, 15:Wd], in0=SA[:, 3, 15:Wd], in1=SA[:, 3, 7:Wd - 8], op=ALU.add),
                 reads=[B_SA], writes=[B_SB])
            yield
            for g in range(4):
                S_, BS_ = (SA, B_SA) if g % 2 == 0 else (SB, B_SB)
                if sample:
                    sv = S_[:, g, 0:192].rearrange("p (b c) -> p b c", b=8)[:, :, 16:24]
                    uv = Uh[:, g, :].rearrange("p (b c) -> p b c", b=8)[:, :, 16:24]
                    dv = dT[:, g, hh * 64:(hh + 1) * 64].rearrange("p (b t) -> p b t", b=8)
                else:
                    sv, uv, dv = S_[:, g, 16:Wd], Uh[:, g, 16:Wd], dT[:, g, c0:c0 + NT // 2]
                P.op("dve", lambda e, sv=sv, uv=uv, dv=dv, g=g: e.scalar_tensor_tensor(out=dv, in0=sv, scalar=1.0 / (2 << g), in1=uv,
                                                                                   op0=ALU.mult, op1=ALU.subtract),
                     reads=[BS_, B_U], writes=[B_dT])
                if kind == "P" and gi == 0 and hh == 0:
                    P.op("dve", lambda e, S_=S_, g=g: e.tensor_tensor(out=st[:, 0:16], in0=S_[:, g, 16:32], in1=invc[:, g, :], op=ALU.mult),
                         reads=[BS_] + CONST, writes=B_stH)
                    P.op("dve", lambda e, g=g: e.tensor_tensor(out=dT[:, g, 0:16], in0=st[:, 0:16], in1=U[:, g, 16:32], op=ALU.subtract),
                         reads=B_stH + [B_U], writes=[B_dT])
            yield
        if last:
            def dd2():
                P.dma("pool", "poolout", poolp[:, :], ost[113:128, :], reads=[B_ost])
            out_tok_major([U[:, g, 16 + NT - 128:16 + NT] for g in range(4)], [B_U], 128, dd2)
        if sample:
            for g in range(4):
                P.op("dve", lambda e, g=g: e.tensor_copy(out=SA[:, g, 0:128].rearrange("p (b t) -> p b t", b=16), in_=Us[:, g, :, 16:24]),
                     reads=[B_U, B_dT], writes=[B_SA])

            def dd3():
                for t in range(8):
                    P.dma("pool", "poolout", pools[:, 7 + t, :], ost[t:128:8, :], reads=[B_ost])
                P.dma("sp", "d2d_p", pools[:, 0:7, :], spool[:, 8:15, :])
            out_tok_major([SA[:, g, 0:128] for g in range(4)], [B_SA], 128, dd3)
        else:
            P.op("dve", lambda e: e.tensor_copy(out=carryU[:], in_=U[:, :, NT:NT + 16]), reads=[B_U], writes=[B_cU])
        yield
        for g in range(4):
            pb, Bp = dbank()
            P.mm([lambda e, g=g, pb=pb: e.matmul(pb[:, :NT], lhsT=wpool[:, g, :], rhs=dT[:, g, :NT], start=True, stop=True)],
                 reads=[B_wpool, B_dT], writes=[Bp])
            P.op("act", lambda e, g=g, pb=pb: e.activation(out=aoT[:, 4 + g, :NT], in_=pb[:, :NT], func=AF.Copy, scale=pscale[:, g:g + 1]),
                 reads=[Bp] + CONST, writes=[B_aoT])
            yield

        if not sample:
            for j in range(ntl):
                if gi == 0 and j == 1:
                    P.dma("sp", "biasld", bias[:].rearrange("p a b -> p (a b)"), biasg_d[:, :], writes=[B_bias])
                for gp in range(2):
                    bk = [sbank(), sbank()]
                    fns = []
                    for g in (2 * gp, 2 * gp + 1):
                        for kv in range(2):
                            fns.append(lambda e, kv=kv, g=g, j=j, bk=bk: e.matmul(
                                bk[kv][0][:, (g % 2) * 256:(g % 2 + 1) * 256], lhsT=qT[kv * 64:(kv + 1) * 64, g, j * 128:(j + 1) * 128],
                                rhs=kT[kv * 64:(kv + 1) * 64, j * 128:j * 128 + 256], start=True, stop=True))
                    P.mm(fns, reads=[B_qT, B_kT], writes=[bk[0][1], bk[1][1]])
                    for kv in range(2):
                        u0 = 4 * gp + kv
                        P.op("dve", lambda e, kv=kv, u0=u0, bk=bk: e.scalar_tensor_tensor(
                            out=sbias[:, u0:u0 + 3:2, :], in0=bk[kv][0][:].rearrange("p (g t) -> p g t", g=2), scalar=0.125,
                            in1=bias[:, u0:u0 + 3:2, :], op0=ALU.mult, op1=ALU.add),
                            reads=[bk[kv][1], B_bias], writes=[B_sbiasH[gp]])
                    yield
                sm = [win_softmax(4 * gp, 4, 256, sinkp[:, 4 * gp:4 * gp + 4], [gp]) for gp in range(2)]
                next(sm[0])
                yield
                next(sm[1])
                yield
                yield
                run(sm[0])
                yield
                run(sm[1])
                yield
                for gp in range(2):
                    yield from diag_T(4 * gp, 4, 2, pexp, [B_pexpH[gp]], [B_DgH[gp]],
                                      lambda u0: pT[:, u0:u0 + 2, :, :].rearrange("p u k t -> p (u k t)"), None, [B_pTH[gp]], [128, 128])
                yield
                po, Bpo = ps[PS_O], B_ps[PS_O]
                pov = po[:].rearrange("p (g t) -> p g t", g=4)
                fns = []
                for g in range(4):
                    for kv in range(2):
                        for kc in range(2):
                            fns.append(lambda e, g=g, kv=kv, kc=kc, j=j: e.matmul(
                                pov[kv * 64:(kv + 1) * 64, g, :], lhsT=vtok[:, j + kc, kv * 64:(kv + 1) * 64], rhs=pT[:, 2 * g + kv, kc, :],
                                start=(kc == 0), stop=(kc == 1)))
                P.mm(fns, reads=[B_vtok] + B_pTH, writes=[Bpo])
                P.op("act", lambda e, j=j: e.activation(out=aoT[:, 0:4, j * 128:(j + 1) * 128], in_=pov, func=AF.Copy),
                     reads=[Bpo], writes=[B_aoT])
                yield
            carry()
        else:
            P.op("dve", lambda e: e.tensor_copy(out=qs2[:].rearrange("p b (g t) -> p b g t", g=4),
                                                in_=qT[:, :, 0:128].rearrange("p g (b t) -> p b g t", b=16)), reads=[B_qT], writes=[B_qs2])
            pb, Bp = dbank()
            pvb = pb[:].bitcast(BF16)
            P.mm([(lambda e, i=i, pvb=pvb: e.transpose(out=pvb[0:32, i * 128:(i + 1) * 128], in_=vT[:, i * 32:(i + 1) * 32], identity=ident[:]))
                  for i in range(4)], reads=[B_vT] + CONST, writes=[Bp])
            P.op("dve", lambda e, pvb=pvb: e.tensor_copy(out=vnq[0:32, :, :], in_=pvb[0:32, 0:512].rearrange("p (i t) -> p i t", i=4)),
                 reads=[Bp], writes=[B_vnq])
            yield
            for i in range(4):
                bk = [sbank(), sbank()]
                fns = []
                for kv in range(2):
                    pvk = bk[kv][0]
                    for jq in range(4):
                        b = 4 * i + jq
                        fns.append(lambda e, kv=kv, jq=jq, b=b, pvk=pvk: e.matmul(
                            pvk[32 * jq:32 * jq + 32, 0:128], lhsT=qs2[kv * 64:(kv + 1) * 64, b, :], rhs=kcT[kv * 64:(kv + 1) * 64, b, :],
                            start=True, stop=True, tile_position=(kv * 64, 32 * jq)))
                    fns.append(lambda e, kv=kv, i=i, pvk=pvk: e.matmul(
                        pvk[:, 128:160], lhsT=qs2[kv * 64:(kv + 1) * 64, 4 * i:4 * i + 4, :].rearrange("p b t -> p (b t)"),
                        rhs=kT[kv * 64:(kv + 1) * 64, 128 + 32 * i:128 + 32 * i + 32], start=True, stop=True))
                P.mm(fns, reads=[B_qs2, B_kcT, B_kT], writes=[bk[0][1], bk[1][1]])
                for kv in range(2):
                    P.op("dve", lambda e, kv=kv, i=i, bk=bk: e.scalar_tensor_tensor(
                        out=sbias[:, 2 * i + kv, 0:160], in0=bk[kv][0][:, 0:160], scalar=0.125,
                        in1=biass[:, kv, :], op0=ALU.mult, op1=ALU.add),
                        reads=[bk[kv][1]] + CONST, writes=[B_sbiasH[i // 2]])
                yield
            smx = win_softmax(0, 8, 160, sinks[:, 0:8], [0, 1])
            next(smx)
            yield
            yield
            run(smx)
            yield
            yield from diag_T(0, 8, 2, pexp, B_pexpH, B_DgH, None, lambda u, kc: pT[0:(128 if kc == 0 else 32), u, kc, :], B_pTH, [128, 32])
            for i in range(4):
                pb, Bp = dbank()
                fns = []
                for kv in range(2):
                    u = 2 * i + kv
                    for jq in range(4):
                        b = 4 * i + jq
                        fns.append(lambda e, kv=kv, jq=jq, b=b, u=u, pb=pb: e.matmul(
                            pb[kv * 64:(kv + 1) * 64, 32 * jq:32 * jq + 32], lhsT=vc[:, b, kv * 64:(kv + 1) * 64], rhs=pT[:, u, 0, 32 * jq:32 * jq + 32],
                            start=(jq == 0), stop=False, skip_group_check=True))
                for kv in range(2):
                    u = 2 * i + kv
                    fns.append(lambda e, kv=kv, u=u, i=i, pb=pb: e.matmul(
                        pb[kv * 64:(kv + 1) * 64, 0:128], lhsT=vnq[0:32, i, kv * 64:(kv + 1) * 64], rhs=pT[0:32, u, 1, :],
                        start=False, stop=True, skip_group_check=True))
                P.mm(fns, reads=[B_vc, B_vnq] + B_pTH, writes=[Bp])
                P.op("act", lambda e, pb=pb, i=i: e.activation(
                    out=aoT[:, 0:4, 32 * i:32 * i + 32].rearrange("p g (j t) -> p j g t", j=4),
                    in_=pb[:, 0:128].rearrange("p (j g t) -> p j g t", j=4, g=4), func=AF.Copy),
                    reads=[Bp], writes=[B_aoT])
                yield

    def late_pre(kind, gi, X, gen, tgen=None):
        sample, halo, NT, ntl = geom(kind)
        xt, Bxt = xTs[X], B_xTs[X]
        loaded = [gen is None]
        xfree = [tgen is None]

        def step_tail(n):
            for _ in range(n):
                if tgen is not None and next(tgen, "END") == "XDONE":
                    xfree[0] = True

        def step_load():
            if not loaded[0] and xfree[0]:
                if next(gen, "L") == "L":
                    loaded[0] = True
        p1done = [gen is None]

        def step_p1(n):
            for _ in range(n):
                if not p1done[0]:
                    if next(gen, "P1") == "P1":
                        p1done[0] = True
        prep2 = make_prep(X, NT, 1, -0.5, rstd, B_rstd)
        for _ in g_dense("out", list(range(8)), NT, lambda k: aoT[:, k, :NT], B_aoT, resid_evac(NT, X, prep=prep2)):
            step_load()
            step_tail(2)
        prep2[1]()
        qcT, B_qcT = aoT, B_aoT
        ocT, B_ocT = hidT, B_hid

        def cq_evac(m, pb, Bp):
            P.op("dve", lambda e: e.tensor_tensor(out=qcT[:, m, :NT], in0=pb[:, :NT], in1=rstd[:, :NT], op=ALU.mult),
                 reads=[Bp, B_rstd], writes=[B_qcT])
        for _ in g_dense("cq", list(range(8)), NT, lambda k: hT[:, k, :NT], B_hT, cq_evac):
            step_load()
            step_tail(2)
        while tgen is not None and not xfree[0]:
            step_tail(1)
        while not loaded[0]:
            step_load()
        run(tgen)

        if not sample:
            for j in range(ntl):
                banks = [sbank(), sbank()]
                for hp, (pb, Bp) in enumerate(banks):
                    pv = pb[:].rearrange("p (h t) -> p h t", h=2)
                    fns = []
                    for hh in range(2):
                        h = 2 * hp + hh
                        for dc in range(2):
                            fns.append(lambda e, pv=pv, hh=hh, h=h, dc=dc, j=j: e.matmul(
                                pv[:, hh, :], lhsT=qcT[:, 2 * h + dc, j * 128:(j + 1) * 128], rhs=memkT[:, 2 * h + dc, :],
                                start=(dc == 0), stop=(dc == 1)))
                    P.mm(fns, reads=[B_qcT, B_memkT], writes=[Bp])
                cross_softmax(banks)
                step_p1(2)
                run(diag_T(0, 4, 2, pexp, [B_pexp], [B_Dg], lambda h0: pTc[:, h0:h0 + 2, :, :].rearrange("p u k t -> p (u k t)"), None, [B_pTc], [128, 128]))
                for half in range(2):
                    pb, Bp = dbank()
                    pv = pb[:].rearrange("p (c t) -> p c t", c=4)
                    fns = []
                    for cc in range(4):
                        c = half * 4 + cc
                        h = c // 2
                        for mc in range(2):
                            fns.append(lambda e, pv=pv, cc=cc, c=c, h=h, mc=mc: e.matmul(
                                pv[:, cc, :], lhsT=memv[:, mc, c * 128:(c + 1) * 128], rhs=pTc[:, h, mc, :], start=(mc == 0), stop=(mc == 1)))
                    P.mm(fns, reads=[B_memv, B_pTc], writes=[Bp])
                    P.op("act" if half == 0 else "dve",
                         (lambda e, pv=pv, half=half, j=j: e.activation(out=ocT[:, half * 4:half * 4 + 4, j * 128:(j + 1) * 128], in_=pv, func=AF.Copy))
                         if half == 0 else
                         (lambda e, pv=pv, half=half, j=j: e.tensor_copy(out=ocT[:, half * 4:half * 4 + 4, j * 128:(j + 1) * 128], in_=pv)),
                         reads=[Bp], writes=[B_ocT])
                step_p1(2)
        else:
            banks = [sbank(), sbank()]
            for i in range(2):
                P.op("dve", lambda e, i=i: e.memset(qpad[i][:], 0.0), writes=[B_qpad[i]])

            def ld_k(b):
                P.dma("pool", f"kb{b % 2}", Kb[b % 2][:], cmk[b].rearrange("(m p) f -> p m f", p=128), writes=[B_Kb[b % 2]])

            def ld_v(b):
                P.dma("pool", f"vb{b % 2}", Vb[b % 2][:], cmv[b].rearrange("(m p) f -> p m f", p=128), writes=[B_Vb[b % 2]])
            ld_k(0)
            ld_v(0)
            for b in range(16):
                s2 = b % 2
                if b + 1 < 16:
                    ld_k(b + 1)
                for mt in range(2):
                    pb, Bp = tbank()
                    pv = pb[:].bitcast(BF16).rearrange("p (c t) -> p c t", c=8)
                    P.mm([(lambda e, c=c, pv=pv, mt=mt, s2=s2: e.transpose(out=pv[:, c, :], in_=Kb[s2][:, mt, c * 128:(c + 1) * 128], identity=ident[:]))
                          for c in range(8)], reads=[B_Kb[s2]] + CONST, writes=[Bp])
                    P.op("act" if mt == 0 else "dve",
                         (lambda e, pv=pv, mt=mt, s2=s2: e.activation(out=KbT[s2][:, :, mt * 128:(mt + 1) * 128], in_=pv, func=AF.Copy))
                         if mt == 0 else
                         (lambda e, pv=pv, mt=mt, s2=s2: e.tensor_copy(out=KbT[s2][:, :, mt * 128:(mt + 1) * 128], in_=pv)),
                         reads=[Bp], writes=[B_KbT[s2]])
                if b >= 2:
                    P.op("dve", lambda e, s2=s2, b=b: e.memset(qpad[s2][:, :, (b - 2) * 8:(b - 1) * 8], 0.0), writes=[B_qpad[s2]])
                P.op("dve", lambda e, s2=s2, b=b: e.tensor_copy(out=qpad[s2][:, :, b * 8:(b + 1) * 8], in_=qcT[:, :, b * 8:(b + 1) * 8]),
                     reads=[B_qcT], writes=[B_qpad[s2]])
                for hp, (pb, Bp) in enumerate(banks):
                    pv = pb[:].rearrange("p (h t) -> p h t", h=2)
                    fns = []
                    for hh in range(2):
                        h = 2 * hp + hh
                        for dc in range(2):
                            fns.append(lambda e, pv=pv, hh=hh, h=h, dc=dc, s2=s2, b=b: e.matmul(
                                pv[:, hh, :], lhsT=qpad[s2][:, 2 * h + dc, :], rhs=KbT[s2][:, 2 * h + dc, :],
                                start=(b == 0 and hh == 0 and dc == 0), stop=(b == 15 and dc == 1), skip_group_check=True))
                    P.mm(fns, reads=[B_qpad[s2], B_KbT[s2]], writes=[Bp])
            cross_softmax(banks)
            run(diag_T(0, 4, 2, pexp, [B_pexp], [B_Dg], lambda h0: pTc[:, h0:h0 + 2, :, :].rearrange("p u k t -> p (u k t)"), None, [B_pTc], [128, 128]))
            pbs = [dbank(), dbank()]
            for b in range(16):
                s2 = b % 2
                if b + 1 < 16:
                    ld_v(b + 1)
                fns = []
                for c in range(8):
                    pv = pbs[c // 4][0][:].rearrange("p (c t) -> p c t", c=4)
                    h = c // 2
                    for mc in range(2):
                        fns.append(lambda e, pv=pv, c=c, h=h, mc=mc, s2=s2, b=b: e.matmul(
                            pv[:, c % 4, b * 8:(b + 1) * 8], lhsT=Vb[s2][:, mc, c * 128:(c + 1) * 128], rhs=pTc[:, h, mc, b * 8:(b + 1) * 8],
                            start=(mc == 0), stop=(mc == 1), skip_group_check=True))
                P.mm(fns, reads=[B_Vb[s2], B_pTc], writes=[pbs[0][1], pbs[1][1]])
            for half in range(2):
                pv = pbs[half][0][:].rearrange("p (c t) -> p c t", c=4)
                P.op("act" if half == 0 else "dve",
                     (lambda e, pv=pv, half=half: e.activation(out=ocT[:, half * 4:half * 4 + 4, 0:128], in_=pv, func=AF.Copy))
                     if half == 0 else
                     (lambda e, pv=pv, half=half: e.tensor_copy(out=ocT[:, half * 4:half * 4 + 4, 0:128], in_=pv)),
                     reads=[pbs[half][1]], writes=[B_ocT])
        while not p1done[0]:
            step_p1(1)
        prep3 = make_prep(X, NT, 3, -1.0, rstd2, B_rstd2)
        dense("co", list(range(8)), NT, lambda k: ocT[:, k, :NT], B_ocT, resid_evac(NT, X, prep=prep3))
        prep3[1]()

    def ffn(kind, gi, X, gen):
        sample, halo, NT, ntl = geom(kind)
        xt, Bxt = xTs[X], B_xTs[X]
        uctr = [0]

        def up_evac(m, pb, Bp):
            r, Br = relu_t[uctr[0] % 2], B_relu[uctr[0] % 2]
            uctr[0] += 1
            P.op("act", lambda e: e.activation(out=r[:, :NT], in_=pb[:, :NT], func=AF.Relu), reads=[Bp], writes=[Br])
            P.op("pool", lambda e: e.tensor_tensor(out=hidT[:, m, :NT], in0=r[:, :NT], in1=r[:, :NT], op=ALU.mult), reads=[Br], writes=[B_hid])
        for _ in g_dense("up", list(range(32)), NT, lambda k: hT[:, k, :NT], B_hT, up_evac):
            advance(gen, 1)
        for _ in g_dense("down", list(range(8)), NT, lambda k: hidT[:, k, :NT], B_hid, resid_evac(NT, X, scale2=True), kgroups=4):
            advance(gen, 1)

    def g_tail(kind, gi, X):
        sample, halo, NT, ntl = geom(kind)
        xt, Bxt = xTs[X], B_xTs[X]
        sq = hidT[:, 16:24, :]
        P.op("act", lambda e: e.activation(out=sq[:, :, :NT], in_=xt[:, :, :NT], func=AF.Square), reads=[Bxt], writes=[B_hid])
        yield
        pb0, Bp0 = dbank()
        P.mm([(lambda e, k=k: e.matmul(pb0[:, :NT], lhsT=ones[:], rhs=sq[:, k, :NT], start=(k == 0), stop=(k == 7)))
              for k in range(8)], reads=[B_hid] + CONST, writes=[Bp0])
        P.op("act", lambda e: e.activation(out=rstd2[:, :NT], in_=pb0[:, :NT], func=AF.Ln, scale=1.0 / D, bias=EPS),
             reads=[Bp0], writes=[B_rstd2])
        P.op("act", lambda e: e.activation(out=rstd2[:, :NT], in_=rstd2[:, :NT], func=AF.Exp, scale=-0.5),
             reads=[B_rstd2], writes=[B_rstd2])
        yield
        for k in range(8):
            P.op("dve", lambda e, k=k: e.scalar_tensor_tensor(out=yT[:, k, :NT], in0=xt[:, k, :NT], scalar=gvec[:, 4, k:k + 1],
                                                              in1=rstd2[:, :NT], op0=ALU.mult, op1=ALU.mult),
                 reads=[Bxt, B_rstd2] + CONST, writes=[B_yT])
            if k % 2 == 1 and k < 7:
                yield
        yield "XDONE"
        for j in range(ntl):
            ys_, Bys = yst[1], B_yst[1]
            for hf in range(2):
                pb, Bp = dbank()
                pv = pb[:].rearrange("p (c t) -> p c t", c=4)
                P.mm([(lambda e, c=c, pv=pv, hf=hf, j=j: e.transpose(out=pv[:, c, :], in_=yT[:, hf * 4 + c, j * 128:(j + 1) * 128], identity=identf[:]))
                      for c in range(4)], reads=[B_yT] + CONST, writes=[Bp])
                P.op("act" if hf == 0 else "dve",
                     (lambda e, pb=pb, hf=hf, ys_=ys_: e.activation(out=ys_[:, hf * 512:(hf + 1) * 512], in_=pb[:, :], func=AF.Copy))
                     if hf == 0 else
                     (lambda e, pb=pb, hf=hf, ys_=ys_: e.tensor_copy(out=ys_[:, hf * 512:(hf + 1) * 512], in_=pb[:, :])),
                     reads=[Bp], writes=[Bys])
                yield
            if sample:
                P.dma("pool", "y", ys[:, :], ys_[:], reads=[Bys])
            else:
                r0 = gi * NT_P + j * 128
                P.dma("pool", "y", yp[r0:r0 + 128, :], ys_[:], reads=[Bys])

    order = [("P", g) for g in range(NG_P)] + [("S", 0)]
    order = order[:max(0, min(len(order), STAGE))] if STAGE < 50 else order
    run(early("H", 0, 0))
    gen0 = early(order[0][0], order[0][1], 0) if order else None
    advance(gen0, until="P1")
    mem_setup(gen0)
    run(gen0)
    tgen = None
    for idx, (kind, gi) in enumerate(order):
        X = idx % 2
        nxt = order[idx + 1] if idx + 1 < len(order) else None
        gen = early(nxt[0], nxt[1], (idx + 1) % 2) if nxt else None
        late_pre(kind, gi, X, gen, tgen)
        ffn(kind, gi, X, gen)
        run(gen)
        tgen = g_tail(kind, gi, X)
        if not TAIL_OVERLAP:
            run(tgen)
            tgen = None
    run(tgen)

    return finish()


_CACHE = {}


def _build_nc():
    if "nc" in _CACHE:
        return _CACHE["nc"]
    nc0 = bass.Bass("TRN2", target_bir_lowering=False)
    with ExitStack() as es0:
        _, W0 = build_sched(nc0, es0)
    sched = W0.rec
    nc = bass.Bass("TRN2", target_bir_lowering=False)
    with ExitStack() as es:
        P, W = build(nc, es, False, sched)
        assert W.i == len(sched), (W.i, len(sched))
        block = es.enter_context(nc.Block())
        P.flush(block)
    _CACHE["nc"] = nc
    return nc


def build_sched(nc0, es0):
    return build(nc0, es0, False, None)


def _tables(half):
    slopes = 2.0 ** (-(np.arange(8) + 1.0))
    q = np.arange(128)[:, None]
    c = np.arange(256)[None, :]
    dist = q - c + 128
    valid = (dist >= 0) & (dist <= 128)
    biasg = np.empty((128, 8, 256), np.float32)
    for g in range(4):
        for kv in range(2):
            h = kv * 4 + g
            biasg[:, 2 * g + kv, :] = np.where(valid, -slopes[h] * dist, -1e30)
    biasf = biasg.copy()
    if half == 0:
        biasf[:, :, 0:128] = -1e30
    biass = np.full((128, 2, 160), -1e30, np.float32)
    for j in range(4):
        for g in range(4):
            for t in range(8):
                r = j * 32 + g * 8 + t
                for kv in range(2):
                    h = kv * 4 + g
                    cc = np.arange(128)
                    d = t + 128 - cc
                    biass[r, kv, 0:128] = np.where(cc >= t, -slopes[h] * d, -1e30)
                    for tp in range(t + 1):
                        biass[r, kv, 128 + j * 8 + tp] = -slopes[h] * (t - tp)
    invc = np.empty((128, 4, 16), np.float32)
    for g in range(4):
        w = 2 << g
        for p in range(16):
            invc[:, g, p] = 1.0 / (min(p + 1, w) if half == 0 else w)
    return biasg.reshape(128, -1), biasf.reshape(128, -1), biass.reshape(128, -1), invc.reshape(128, -1)


def _prep(x_prompt, x_sample, cache_win_k, cache_win_v, state_pool, cache_mem_k, cache_mem_v,
          mem_prompt, g_mix, w_in, attn_sinks, w_pool, pool_scale, w_out, g_cross, g_mem,
          w_cq, w_ck, w_cv, w_co, g_ffn, w_up, w_down, g_final):
    f = lambda a: np.ascontiguousarray(np.asarray(a, dtype=np.float32))
    x_prompt, x_sample = f(x_prompt), f(x_sample)
    shared = dict(w_in=f(w_in)[0], w_pool=f(w_pool)[0], w_out=f(w_out)[0], w_cq=f(w_cq)[0], w_ck=f(w_ck)[0],
                  w_cv=f(w_cv)[0], w_co=f(w_co)[0], w_up=f(w_up)[0], w_down=f(w_down)[0])
    gs = np.stack([f(g_mix)[0], f(g_cross)[0], f(g_mem)[0], f(g_ffn)[0], f(g_final)], 0)
    shared["gvec"] = np.ascontiguousarray(gs.reshape(5, 8, 128).transpose(2, 0, 1).reshape(128, 40))
    shared["pscale"] = np.ascontiguousarray(f(pool_scale)[0].reshape(4, 128).T)
    sk = f(attn_sinks)[0]
    sinkp = np.empty((128, 8), np.float32)
    for g in range(4):
        for kv in range(2):
            sinkp[:, 2 * g + kv] = sk[kv * 4 + g]
    shared["sinkp"] = sinkp
    sinks = np.empty((128, 8), np.float32)
    for r in range(128):
        g = (r % 32) // 8
        for i in range(4):
            sinks[r, 2 * i] = sk[g]
            sinks[r, 2 * i + 1] = sk[4 + g]
    shared["sinks"] = sinks
    ckf, cvf, spf = f(cache_win_k)[0], f(cache_win_v)[0], f(state_pool)[0]
    cmkf, cmvf, memf = f(cache_mem_k)[0], f(cache_mem_v)[0], f(mem_prompt)
    in_maps = []
    for c in range(NCORES):
        b, half = c // 2, c % 2
        s0 = half * SEQ_CORE
        xp = np.zeros((128 + SEQ_CORE, D), np.float32)
        xp[128:] = x_prompt[b, s0:s0 + SEQ_CORE]
        if half == 1:
            xp[:128] = x_prompt[b, s0 - 128:s0]
        biasg, biasf, biass, invc = _tables(half)
        sl = slice(16 * c, 16 * c + 16)
        m = dict(shared)
        m.update(xp=xp, xs=np.ascontiguousarray(x_sample[sl].reshape(128, D)), mem=np.ascontiguousarray(memf[b]),
                 ck=np.ascontiguousarray(ckf[sl].reshape(16, 128, 128)), cv=np.ascontiguousarray(cvf[sl].reshape(16, 128, 128)),
                 spool=np.ascontiguousarray(spf[sl]), cmk=np.ascontiguousarray(cmkf[sl].reshape(16, 256, D)),
                 cmv=np.ascontiguousarray(cmvf[sl].reshape(16, 256, D)),
                 biasg=biasg, biasf=biasf, biass=biass, invc=invc)
        in_maps.append(m)
    return in_maps


def kernel(**inputs):
    in_maps = _prep(**inputs)
    nc = _build_nc()
    res = run_bass_kernel_spmd(nc, in_maps, core_ids=list(range(NCORES))).results
    return _assemble(res)


def _assemble(res):
    B, S = 4, 4096
    y_prompt = np.empty((B, S, D), np.float32)
    y_sample = np.empty((128, 8, D), np.float32)
    wk_p = np.empty((1, B, 128, 2, 64), np.float32); wv_p = np.empty_like(wk_p)
    pool_p = np.empty((1, B, 15, 512), np.float32)
    mk_p = np.empty((1, B, 256, 4, 256), np.float32); mv_p = np.empty_like(mk_p)
    wk_s = np.empty((1, 128, 128, 2, 64), np.float32); wv_s = np.empty_like(wk_s)
    pool_s = np.empty((1, 128, 15, 512), np.float32)
    for c in range(NCORES):
        r = res[c]
        b, half = c // 2, c % 2
        y_prompt[b, half * SEQ_CORE:(half + 1) * SEQ_CORE] = r["yp"]
        sl = slice(16 * c, 16 * c + 16)
        y_sample[sl] = r["ys"].reshape(16, 8, D)
        if half == 1:
            wk_p[0, b] = r["wkp"].reshape(128, 2, 64)
            wv_p[0, b] = r["wvp"].reshape(128, 2, 64)
            pool_p[0, b] = r["poolp"]
        else:
            mk_p[0, b] = r["memk"].reshape(256, 4, 256)
            mv_p[0, b] = r["memv"].reshape(256, 4, 256)
        wk_s[0, sl] = r["wks"].reshape(16, 128, 2, 64)
        wv_s[0, sl] = r["wvs"].reshape(16, 128, 2, 64)
        pool_s[0, sl] = r["pools"]
    return (y_prompt, y_sample, wk_p, wv_p, pool_p, mk_p, mv_p, wk_s, wv_s, pool_s)
```

```python
import numpy as np
from contextlib import ExitStack
import concourse.bass as bass
import concourse.mybir as mybir
from concourse.bass_utils import run_bass_kernel_spmd

F32 = mybir.dt.float32
BF16 = mybir.dt.bfloat16
ALU = mybir.AluOpType
AF = mybir.ActivationFunctionType
AX = mybir.AxisListType

NCORES = 8
STAGE = 99
TAIL_OVERLAP = True
D = 1024
SEQ_CORE = 2048
NT_P = 512
NG_P = SEQ_CORE // NT_P
RING = 10
EPS = 1e-5


class Buf:
    __slots__ = ("name", "w", "r", "al", "excl")

    def __init__(self, name, excl=False):
        self.name = name
        self.w = None
        self.r = {}
        self.al = []
        self.excl = excl


def alias(*bufs):
    for a in bufs:
        for b in bufs:
            if a is not b and b not in a.al:
                a.al.append(b)


class Prog:
    def __init__(self, nc, es, dry):
        self.nc, self.es, self.dry = nc, es, dry
        self.q = {e: [] for e in ("pe", "act", "dve", "pool", "sp")}
        self.cnt, self.sems = {}, {}
        self.waited = {e: {} for e in self.q}

    def sem(self, key):
        if key not in self.sems:
            self.sems[key] = None if self.dry else self.es.enter_context(self.nc.semaphore(key))
            self.cnt[key] = 0

    def _wait(self, eng, tok):
        if tok is None:
            return
        key, val = tok
        if self.waited[eng].get(key, 0) >= val:
            return
        self.waited[eng][key] = val
        self.q[eng].append(("w", key, val))

    def _deps(self, eng, reads, writes, extra):
        for b in reads:
            self._wait(eng, b.w)
            if b.excl:
                for k, v in b.r.items():
                    if k != eng:
                        self._wait(eng, (k, v))
        for b in writes:
            for bb in [b] + b.al:
                self._wait(eng, bb.w)
                for k, v in bb.r.items():
                    self._wait(eng, (k, v))
        for t in extra:
            self._wait(eng, t)

    def _commit(self, tok, reads, writes):
        k, v = tok
        for b in reads:
            b.r[k] = max(b.r.get(k, 0), v)
        for b in writes:
            b.w = tok
            b.r = {}

    def op(self, eng, fn, reads=(), writes=(), extra=()):
        self._deps(eng, reads, writes, extra)
        self.sem(eng)
        self.cnt[eng] += 1
        tok = (eng, self.cnt[eng])
        self.q[eng].append(("i", fn, eng, 1))
        self._commit(tok, reads, writes)
        return tok

    def mm(self, fns, reads=(), writes=(), extra=()):
        self._deps("pe", reads, writes, extra)
        for f in fns[:-1]:
            self.q["pe"].append(("i", f, None, 0))
        self.sem("pe")
        self.cnt["pe"] += 1
        tok = ("pe", self.cnt["pe"])
        self.q["pe"].append(("i", fns[-1], "pe", 1))
        self._commit(tok, reads, writes)
        return tok

    def dma(self, qeng, semkey, out, in_, reads=(), writes=(), extra=()):
        if writes:
            semkey = "dw" + qeng[0] + "_" + writes[0].name
        elif reads:
            semkey = "dr" + qeng[0] + "_" + reads[0].name
        for b in reads:
            self._wait(qeng, b.w)
        for b in writes:
            for bb in [b] + b.al:
                if not (bb.w is not None and bb.w[0] == semkey):
                    self._wait(qeng, bb.w)
                for k, v in bb.r.items():
                    self._wait(qeng, (k, v))
        for t in extra:
            self._wait(qeng, t)
        self.sem(semkey)
        self.cnt[semkey] += 16
        tok = (semkey, self.cnt[semkey])
        self.q[qeng].append(("i", (lambda e, o=out, i=in_: e.dma_start(out=o, in_=i)), semkey, 16))
        self._commit(tok, reads, writes)
        return tok

    def flush(self, block):
        def run(name):
            def f(e):
                for it in self.q[name]:
                    if it[0] == "w":
                        e.wait_ge(self.sems[it[1]], it[2])
                    else:
                        ins = it[1](e)
                        if it[3]:
                            ins.then_inc(self.sems[it[2]], it[3])
            return f
        block.tensor(run("pe"))
        block.scalar(run("act"))
        block.vector(run("dve"))
        block.gpsimd(run("pool"))
        block.sync(run("sp"))


class WStream:
    def __init__(self, P, ring_ap, sched, scratch_fn=None):
        self.P, self.ring = P, ring_ap
        self.sched = sched
        self.rec = []
        self.i = 0
        self.issued = 0
        self.slots = [Buf(f"ws{i}") for i in range(RING)]
        self.src = {}
        self.uidx, self.wtok = {}, {}
        self.scratch = None
        if sched is not None:
            cnt = {}
            for k in sched:
                cnt[k] = cnt.get(k, 0) + 1
            for k in sched:
                if cnt[k] > 1 and k not in self.uidx:
                    self.uidx[k] = len(self.uidx)
            if scratch_fn is not None and self.uidx:
                self.scratch = scratch_fn(len(self.uidx))

    def _issue(self, j):
        key = self.sched[j]
        name, m = key
        s = j % RING
        if self.scratch is not None and key in self.wtok:
            self.P.dma("sp", f"ws{s}", self.ring[:, s], self.scratch[self.uidx[key]], writes=[self.slots[s]],
                       extra=[self.wtok[key]])
            return
        for (dst_fn, src_ap) in self.src[name](m):
            self.P.dma("pool", f"ws{s}", dst_fn(self.ring[:, s]), src_ap, writes=[self.slots[s]])
        if self.scratch is not None and key in self.uidx:
            self.wtok[key] = self.P.dma("sp", f"sw{s}", self.scratch[self.uidx[key]], self.ring[:, s], reads=[self.slots[s]])

    def get(self, name, m):
        if self.sched is None:
            self.rec.append((name, m))
            return self.ring[:, 0], self.slots[0]
        assert self.sched[self.i] == (name, m), (self.i, self.sched[self.i], name, m)
        while self.issued < min(len(self.sched), self.i + RING - 3):
            self._issue(self.issued)
            self.issued += 1
        s = self.i % RING
        self.i += 1
        return self.ring[:, s], self.slots[s]


def build(nc, es, dry, sched):
    P = Prog(nc, es, dry)

    def din(name, shape):
        return nc.dram_tensor(name, list(shape), F32, kind="ExternalInput").ap()

    def dout(name, shape):
        return nc.dram_tensor(name, list(shape), F32, kind="ExternalOutput").ap()

    if not dry:
        xp = din("xp", [128 + SEQ_CORE, D]); xs = din("xs", [128, D]); mem = din("mem", [256, D])
        ck = din("ck", [16, 128, 128]); cv = din("cv", [16, 128, 128]); spool = din("spool", [16, 15, 512])
        cmk = din("cmk", [16, 256, D]); cmv = din("cmv", [16, 256, D])
        w_in = din("w_in", [D, 1280]); w_pool = din("w_pool", [4, 128, 128]); w_out = din("w_out", [D, D])
        w_cq = din("w_cq", [D, D]); w_ck = din("w_ck", [D, D]); w_cv = din("w_cv", [D, D]); w_co = din("w_co", [D, D])
        w_up = din("w_up", [D, 4 * D]); w_down = din("w_down", [4 * D, D])
        gvec_d = din("gvec", [128, 40]); pscale_d = din("pscale", [128, 4])
        sinkp_d = din("sinkp", [128, 8]); sinks_d = din("sinks", [128, 8])
        biasg_d = din("biasg", [128, 8 * 256]); biasf_d = din("biasf", [128, 8 * 256]); biass_d = din("biass", [128, 2 * 160])
        invc_d = din("invc", [128, 64])
        yp = dout("yp", [SEQ_CORE, D]); ys = dout("ys", [128, D])
        wkp = dout("wkp", [128, 128]); wvp = dout("wvp", [128, 128]); poolp = dout("poolp", [15, 512])
        memk_o = dout("memk", [256, D]); memv_o = dout("memv", [256, D])
        wks = dout("wks", [16, 128, 128]); wvs = dout("wvs", [16, 128, 128]); pools = dout("pools", [16, 15, 512])

    def sb(name, shape, dt):
        return es.enter_context(nc.sbuf_tensor("sb_" + name, list(shape), dt))

    xTs = [sb(f"xT{i}", [128, 8, NT_P], F32) for i in range(2)]; B_xTs = [Buf(f"xT{i}") for i in range(2)]
    xT, B_xT = xTs[0], B_xTs[0]
    hT = sb("hT", [128, 8, NT_P], BF16); B_hT = Buf("hT")
    rstd = sb("rstd", [128, NT_P], F32); B_rstd = Buf("rstd")
    aoT = sb("aoT", [128, 8, NT_P], BF16); B_aoT = Buf("aoT")
    qT = sb("qT", [128, 4, NT_P], BF16); B_qT = Buf("qT")
    kT = sb("kT", [128, 128 + NT_P], BF16); B_kT = Buf("kT")
    vT = sb("vT", [128, NT_P], BF16); B_vT = Buf("vT")
    vtok = sb("vtok", [128, 5, 128], BF16); B_vtok = Buf("vtok")
    kv32 = sb("kv32", [128, 2, 128], F32); B_kv32 = Buf("kv32")
    dT = sb("dT", [128, 4, NT_P], BF16); B_dT = Buf("dT")
    pexp = sb("pexp", [128, 8, 256], BF16); B_pexpH = [Buf("pexp0"), Buf("pexp1")]; B_pexp = B_pexpH[0]
    pT = sb("pT", [128, 8, 2, 128], BF16); B_pTH = [Buf("pT0"), Buf("pT1")]; B_pT = B_pTH[0]
    Dg = sb("Dg", [128, 8, 128], BF16); B_DgH = [Buf("Dg0"), Buf("Dg1")]; B_Dg = B_DgH[0]
    pTc = sb("pTc", [128, 4, 2, 128], BF16); B_pTc = Buf("pTc")
    bias = sb("bias", [128, 8, 256], F32); B_bias = Buf("bias")
    biass = sb("biass", [128, 2, 160], F32); B_biass = Buf("biass")
    memkT = sb("memkT", [128, 8, 256], BF16); B_memkT = Buf("memkT")
    memv = sb("memv", [128, 2, D], BF16); B_memv = Buf("memv")
    ring = sb("ring", [128, RING, 8, 128], BF16)
    xin = [sb(f"xin{i}", [128, D], F32) for i in range(2)]; B_xin = [Buf(f"xin{i}") for i in range(2)]
    yst1 = sb("yst1", [128, D], F32); yst, B_yst = [None, yst1], [None, Buf("yst1")]
    ident = sb("ident", [128, 128], BF16); identf = sb("identf", [128, 128], F32); B_const = Buf("const")
    ones = sb("ones", [128, 128], BF16)
    gvec = sb("gvec", [128, 5, 8], F32); pscale = sb("pscale", [128, 4], F32)
    sinkp = sb("sinkp", [128, 8], F32); sinks = sb("sinks", [128, 8], F32)
    invc = sb("invc", [128, 4, 16], F32)
    wpool = sb("wpool", [128, 4, 128], BF16); B_wpool = Buf("wpool")
    st = sb("st", [128, 64], F32); B_stH = [Buf("st0"), Buf("st1")]; B_st = B_stH[0]
    relu_t = [sb(f"relu{i}", [128, NT_P], BF16) for i in range(2)]; B_relu = [Buf(f"relu{i}") for i in range(2)]
    ost = sb("ost", [128, 512], F32); B_ost = Buf("ost")
    carryU = sb("carryU", [128, 4, 16], F32); B_cU = Buf("carryU")
    sqb = [sb(f"sq{i}", [128, NT_P], BF16) for i in range(2)]; B_sq = [Buf(f"sq{i}") for i in range(2)]
    rstd2 = sb("rstd2", [128, NT_P], F32); B_rstd2 = Buf("rstd2")

    R2 = 32 * NT_P * 2
    XO = R2 + 28672
    AR = XO + 10240
    arena = sb("arena", [128, AR // 2], BF16)

    def av(off, nbytes, dt, pat=None, **kw):
        v = arena[:, off // 2:(off + nbytes) // 2]
        if dt is F32:
            v = v.bitcast(F32)
        if pat:
            v = v.rearrange(pat, **kw)
        return v

    hidT = av(0, 32 * NT_P * 2, BF16, "p (k t) -> p k t", k=32); B_hid = Buf("hidT")
    yT = av(0, 8 * NT_P * 4, F32, "p (k t) -> p k t", k=8); B_yT = Buf("yT")
    WU = 16 + NT_P
    WH = 16 + 256
    U = av(R2, 4 * WU * 4, F32, "p (g t) -> p g t", g=4); B_U = Buf("U")
    SA = av(R2 + 4 * WU * 4, 4 * WH * 4, F32, "p (g t) -> p g t", g=4); B_SA = Buf("SA")
    SB = av(R2 + 4 * WU * 4 + 4 * WH * 4, 4 * WH * 4, F32, "p (g t) -> p g t", g=4); B_SB = Buf("SB")
    o_sb = R2 + 4 * WU * 4 + 8 * WH * 4
    sbias = av(o_sb, 8 * 256 * 4, F32, "p (u t) -> p u t", u=8); B_sbiasH = [Buf("sbias0"), Buf("sbias1")]; B_sbias = B_sbiasH[0]
    assert o_sb + 8192 <= XO
    kcT = av(XO, 16 * 128 * 2, BF16, "p (b t) -> p b t", b=16); B_kcT = Buf("kcT")
    vc = av(XO + 4096, 16 * 128 * 2, BF16, "p (b t) -> p b t", b=16); B_vc = Buf("vc")
    qs2 = av(XO + 8192, 16 * 32 * 2, BF16, "p (b t) -> p b t", b=16); B_qs2 = Buf("qs2")
    vnq = av(XO + 9216, 4 * 128 * 2, BF16, "p (i t) -> p i t", i=4); B_vnq = Buf("vnq")
    Kb = [av(R2 + i * 4096, 4096, BF16, "p (m t) -> p m t", m=2) for i in range(2)]; B_Kb = [Buf(f"Kb{i}") for i in range(2)]
    KbT = [av(R2 + 8192 + i * 4096, 4096, BF16, "p (c t) -> p c t", c=8) for i in range(2)]; B_KbT = [Buf(f"KbT{i}") for i in range(2)]
    Vb = [av(R2 + 16384 + i * 4096, 4096, BF16, "p (m t) -> p m t", m=2) for i in range(2)]; B_Vb = [Buf(f"Vb{i}") for i in range(2)]
    qpad = [av(R2 + 24576 + i * 2048, 2048, BF16, "p (c t) -> p c t", c=8) for i in range(2)]; B_qpad = [Buf(f"qpad{i}") for i in range(2)]
    memst = av(0, 8192, F32, "p (m t) -> p m t", m=2); B_memst = Buf("memst")
    mkst = av(8192, 8192, F32, "p (m t) -> p m t", m=2); B_mkst = Buf("mkst")
    alias(B_hid, B_yT)
    gX = [B_U, B_SA, B_SB] + B_sbiasH
    gY = B_Kb + B_KbT + B_Vb + B_qpad
    gZ = [B_memst, B_mkst]
    for ga, gb in ((gX, gY), ([B_hid, B_yT], gZ)):
        for a in ga:
            for b in gb:
                a.al.append(b)
                b.al.append(a)

    ps = [es.enter_context(nc.psum_tensor(f"ps{i}", [128, 512], F32)) for i in range(8)]
    B_ps = [Buf(f"ps{i}", excl=True) for i in range(8)]
    dctr = [0]

    def dbank():
        i = dctr[0] % 3
        dctr[0] += 1
        return ps[i], B_ps[i]
    PS_S = [3, 4]
    PS_T = [5, 6]
    PS_O = 7
    sctr = [0]
    tctr = [0]

    def sbank():
        i = PS_S[sctr[0] % 2]; sctr[0] += 1
        return ps[i], B_ps[i]

    def tbank():
        i = PS_T[tctr[0] % 2]; tctr[0] += 1
        return ps[i], B_ps[i]

    def scratch_fn(n):
        return nc.dram_tensor("wscratch", [n, 128, 8, 128], BF16, kind="Internal").ap()
    W = WStream(P, ring, sched, scratch_fn)
    if not dry:
        def std_src(wap):
            v = wap.rearrange("(k p) (m c) -> p m k c", p=128, c=128)
            return lambda m: [((lambda s: s), v[:, m])]
        W.src["ck"] = std_src(w_ck); W.src["cv"] = std_src(w_cv)
        W.src["cq"] = std_src(w_cq); W.src["co"] = std_src(w_co); W.src["up"] = std_src(w_up)
        vin_q = w_in[:, 0:512].rearrange("(k p) (kv g d) -> p g k kv d", p=128, kv=2, g=4, d=64)
        vin_r = w_in[:, 512:1280].rearrange("(k p) (m c) -> p m k c", p=128, c=128)

        def in_src(m):
            if m < 4:
                return [((lambda s: s[:, :, 0:64]), vin_q[:, m, :, 0, :]),
                        ((lambda s: s[:, :, 64:128]), vin_q[:, m, :, 1, :])]
            return [((lambda s: s), vin_r[:, m - 4])]
        W.src["in"] = in_src
        vo_a = w_out[0:512, :].rearrange("(kv g d) (m c) -> kv d m g c", kv=2, g=4, d=64, c=128)
        vo_p = w_out[512:1024, :].rearrange("(k p) (m c) -> p m k c", p=128, c=128)

        def out_src(m):
            return [((lambda s: s[0:64, 0:4, :]), vo_a[0, :, m]),
                    ((lambda s: s[64:128, 0:4, :]), vo_a[1, :, m]),
                    ((lambda s: s[:, 4:8, :]), vo_p[:, m])]
        W.src["out"] = out_src
        vdn = w_down.rearrange("(q k p) (m c) -> p m q k c", p=128, k=8, c=128)
        W.src["down"] = lambda mq: [((lambda s: s), vdn[:, mq // 4, mq % 4])]

    if not dry:
        P.op("pool", lambda e: e.memset(identf[:], 0.0), writes=[B_const])
        P.op("pool", lambda e: e.iota(identf[:], pattern=[[1, 128]], base=0, channel_multiplier=-1,
                                      allow_small_or_imprecise_dtypes=True), writes=[B_const])
        P.op("dve", lambda e: e.tensor_single_scalar(out=ident[:], in_=identf[:], scalar=0.0, op=ALU.is_equal),
             reads=[B_const], writes=[B_const])
        P.op("dve", lambda e: e.tensor_single_scalar(out=identf[:], in_=identf[:], scalar=0.0, op=ALU.is_equal),
             writes=[B_const])
        P.op("dve", lambda e: e.memset(ones[:], 1.0), writes=[B_const])
        for (dst, src) in ((gvec[:].rearrange("p a b -> p (a b)"), gvec_d), (pscale[:], pscale_d), (sinkp[:], sinkp_d),
                           (sinks[:], sinks_d), (invc[:].rearrange("p a b -> p (a b)"), invc_d),
                           (biass[:].rearrange("p a b -> p (a b)"), biass_d)):
            P.dma("sp", "cst", dst, src[:, :], writes=[B_const])
        P.dma("sp", "biasld", bias[:].rearrange("p a b -> p (a b)"), biasf_d[:, :], writes=[B_bias])
        P.dma("pool", "wpool", wpool[:], w_pool.rearrange("g c e -> c g e"), writes=[B_wpool])

    CONST = [B_const]

    class _Stop(Exception):
        pass

    def finish():
        for key, val in P.cnt.items():
            if key not in ("pe", "act", "dve", "pool"):
                P._wait("sp", (key, val))
        for e_ in ("pe", "act", "dve", "pool"):
            if P.cnt.get(e_, 0):
                P._wait("sp", (e_, P.cnt[e_]))
        return P, W
    if STAGE == -1:
        return finish()

    def run(gen):
        if gen is not None:
            for _ in gen:
                pass

    def advance(gen, n=1, until=None):
        if gen is None:
            return
        if until is not None:
            for v in gen:
                if v == until:
                    return
            return
        for _ in range(n):
            try:
                next(gen)
            except StopIteration:
                return

    def g_load_x(src_rows, ntiles, X):
        dst, dstB = xTs[X], B_xTs[X]
        for j in range(ntiles):
            xb, Bx = xin[j % 2], B_xin[j % 2]
            P.dma("sp", "x", xb[:], src_rows(j), writes=[Bx])
            for hf in range(2):
                pb, Bp = dbank()
                pv = pb[:].rearrange("p (c t) -> p c t", c=4)
                P.mm([(lambda e, c=c, pv=pv, xb=xb, hf=hf: e.transpose(out=pv[:, c, :], in_=xb[:, (hf * 4 + c) * 128:(hf * 4 + c + 1) * 128],
                                                                       identity=identf[:])) for c in range(4)],
                     reads=[Bx] + CONST, writes=[Bp])
                P.op("act" if hf == 0 else "dve",
                     (lambda e, pv=pv, hf=hf, j=j: e.activation(out=dst[:, hf * 4:hf * 4 + 4, j * 128:(j + 1) * 128], in_=pv, func=AF.Copy))
                     if hf == 0 else
                     (lambda e, pv=pv, hf=hf, j=j: e.tensor_copy(out=dst[:, hf * 4:hf * 4 + 4, j * 128:(j + 1) * 128], in_=pv)),
                     reads=[Bp], writes=[dstB])
                yield

    def norm(src, Bsrc, gi, NT, dst, Bdst):
        P.op("act", lambda e: e.activation(out=hT[:, :, :NT], in_=src[:, :, :NT], func=AF.Square),
             reads=[Bsrc], writes=[B_hT])
        pb, Bp = dbank()
        P.mm([(lambda e, k=k: e.matmul(pb[:, :NT], lhsT=ones[:], rhs=hT[:, k, :NT], start=(k == 0), stop=(k == 7)))
              for k in range(8)], reads=[B_hT] + CONST, writes=[Bp])
        P.op("act", lambda e: e.activation(out=rstd[:, :NT], in_=pb[:, :NT], func=AF.Ln, scale=1.0 / D, bias=EPS),
             reads=[Bp], writes=[B_rstd])
        P.op("act", lambda e: e.activation(out=rstd[:, :NT], in_=rstd[:, :NT], func=AF.Exp, scale=-0.5),
             reads=[B_rstd], writes=[B_rstd])
        for k in range(8):
            P.op("dve", lambda e, k=k: e.scalar_tensor_tensor(out=dst[:, k, :NT], in0=src[:, k, :NT], scalar=gvec[:, gi, k:k + 1],
                                                              in1=rstd[:, :NT], op0=ALU.mult, op1=ALU.mult),
                 reads=[Bsrc, B_rstd] + CONST, writes=[Bdst])

    def g_dense(wname, units, NT, rhs_fn, Brhs, evac, kgroups=1):
        for m in units:
            pb, Bp = dbank()
            fns, Bs = [], []
            for q in range(kgroups):
                slot, Bslot = W.get(wname, m * kgroups + q if kgroups > 1 else m)
                Bs.append(Bslot)
                for k in range(8):
                    fns.append(lambda e, slot=slot, k=k, q=q, pb=pb: e.matmul(
                        pb[:, :NT], lhsT=slot[:, k, :], rhs=rhs_fn(q * 8 + k),
                        start=(q == 0 and k == 0), stop=(q == kgroups - 1 and k == 7)))
            P.mm(fns, reads=Bs + [Brhs], writes=[Bp])
            evac(m, pb, Bp)
            for _ in range(kgroups):
                yield

    def dense(*a, **kw):
        run(g_dense(*a, **kw))

    def resid_evac(NT, X, prep=None, scale2=False):
        xt, Bxt = xTs[X], B_xTs[X]

        def f(m, pb, Bp):
            if scale2:
                P.op("dve", lambda e: e.tensor_tensor(out=pb[:, :NT], in0=pb[:, :NT], in1=rstd2[:, :NT], op=ALU.mult),
                     reads=[Bp, B_rstd2], writes=[Bp])
            P.op("dve", lambda e: e.tensor_tensor(out=xt[:, m, :NT], in0=pb[:, :NT], in1=xt[:, m, :NT], op=ALU.add),
                 reads=[Bp], writes=[Bxt])
            if prep is not None:
                prep[0](m)
        return f

    def make_prep(X, NT, gidx, exp_scale, rdst, Brdst):
        xt, Bxt = xTs[X], B_xTs[X]
        sp_, Bsp = ps[PS_O], B_ps[PS_O]
        pend = []

        def emit_mm(k):
            P.mm([lambda e, k=k: e.matmul(sp_[:, :NT], lhsT=ones[:], rhs=sqb[k % 2][:, :NT], start=(k == 0), stop=(k == 7))],
                 reads=[B_sq[k % 2]] + CONST, writes=[Bsp])

        def after(m):
            while len(pend) >= 2:
                emit_mm(pend.pop(0))
            P.op("act", lambda e: e.activation(out=hT[:, m, :NT], in_=xt[:, m, :NT], func=AF.Copy, scale=gvec[:, gidx, m:m + 1]),
                 reads=[Bxt] + CONST, writes=[B_hT])
            P.op("act", lambda e: e.activation(out=sqb[m % 2][:, :NT], in_=xt[:, m, :NT], func=AF.Square),
                 reads=[Bxt], writes=[B_sq[m % 2]])
            pend.append(m)

        def flush():
            while pend:
                emit_mm(pend.pop(0))
            P.op("act", lambda e: e.activation(out=rdst[:, :NT], in_=sp_[:, :NT], func=AF.Ln, scale=1.0 / D, bias=EPS),
                 reads=[Bsp], writes=[Brdst])
            P.op("act", lambda e: e.activation(out=rdst[:, :NT], in_=rdst[:, :NT], func=AF.Exp, scale=exp_scale),
                 reads=[Brdst], writes=[Brdst])
        return after, flush

    def diag_T(u0, nu, nkc, p_src, Bp_src, BDg, dst4, dst_fn, Bdst, kw):
        items = [(u, kc) for u in range(u0, u0 + nu) for kc in range(nkc)]
        full = all(w == 128 for w in kw)
        for bi, i0 in enumerate(range(0, len(items), 4)):
            chunk = items[i0:i0 + 4]
            pb, Bp = tbank()
            pv = pb[:].rearrange("p (s t) -> p s t", s=4)
            P.mm([(lambda e, s=s, u=u, kc=kc, pv=pv: e.matmul(pv[0:kw[kc], s, :], lhsT=p_src[:, u, kc * 128:kc * 128 + kw[kc]],
                                                              rhs=Dg[:, u, :], start=True, stop=True))
                  for s, (u, kc) in enumerate(chunk)], reads=list(Bp_src) + list(BDg), writes=[Bp])
            if full:
                uu = chunk[0][0]
                if bi % 2 == 0:
                    P.op("act", lambda e, pb=pb, uu=uu: e.activation(out=dst4(uu), in_=pb[:, 0:512], func=AF.Copy), reads=[Bp], writes=list(Bdst))
                else:
                    P.op("dve", lambda e, pb=pb, uu=uu: e.tensor_copy(out=dst4(uu), in_=pb[:, 0:512]), reads=[Bp], writes=list(Bdst))
            else:
                for s, (u, kc) in enumerate(chunk):
                    P.op("act" if bi % 2 == 0 else "dve",
                         (lambda e, s=s, u=u, kc=kc, pv=pv: e.activation(out=dst_fn(u, kc), in_=pv[0:kw[kc], s, :], func=AF.Copy))
                         if bi % 2 == 0 else
                         (lambda e, s=s, u=u, kc=kc, pv=pv: e.tensor_copy(out=dst_fn(u, kc), in_=pv[0:kw[kc], s, :])),
                         reads=[Bp], writes=list(Bdst))
            yield

    def make_Dg(u0, nu, Bst, BDg):
        P.op("dve", lambda e: e.tensor_tensor(out=Dg[:, u0:u0 + nu, :], in0=ident[:].unsqueeze(1).to_broadcast([128, nu, 128]),
                                              in1=st[:, 56 + u0:56 + u0 + nu].unsqueeze(2).to_broadcast([128, nu, 128]), op=ALU.mult),
             reads=list(Bst) + CONST, writes=list(BDg))

    def win_softmax(u0, nu, width, sink_ap, hs):
        Bsb = [B_sbiasH[h] for h in hs]; Bst = [B_stH[h] for h in hs]
        Bpe = [B_pexpH[h] for h in hs]; BDg = [B_DgH[h] for h in hs]
        c = lambda base: slice(base + u0, base + u0 + nu)
        P.op("dve", lambda e: e.tensor_reduce(out=st[:, c(0)], in_=sbias[:, u0:u0 + nu, 0:width], axis=AX.X, op=ALU.max),
             reads=Bsb, writes=Bst)
        P.op("dve", lambda e: e.tensor_tensor(out=st[:, c(8)], in0=st[:, c(0)], in1=sink_ap, op=ALU.max),
             reads=Bst + CONST, writes=Bst)
        P.op("dve", lambda e: e.tensor_scalar(out=st[:, c(16)], in0=st[:, c(8)], scalar1=-1.0, scalar2=None, op0=ALU.mult),
             reads=Bst, writes=Bst)
        P.op("dve", lambda e: e.tensor_tensor(out=st[:, c(24)], in0=sink_ap, in1=st[:, c(16)], op=ALU.add),
             reads=Bst + CONST, writes=Bst)
        for u in range(u0, u0 + nu):
            P.op("act", lambda e, u=u: e.activation(out=pexp[:, u, 0:width], in_=sbias[:, u, 0:width], func=AF.Exp,
                                                    bias=st[:, 16 + u:17 + u], scale=1.0, accum_out=st[:, 32 + u:33 + u]),
                 reads=Bsb + Bst, writes=Bpe + Bst)
        P.op("act", lambda e: e.activation(out=st[:, c(40)], in_=st[:, c(24)], func=AF.Exp),
             reads=Bst, writes=Bst)
        yield
        P.op("dve", lambda e: e.tensor_tensor(out=st[:, c(48)], in0=st[:, c(32)], in1=st[:, c(40)], op=ALU.add),
             reads=Bst, writes=Bst)
        P.op("dve", lambda e: e.reciprocal(out=st[:, c(56)], in_=st[:, c(48)]), reads=Bst, writes=Bst)
        make_Dg(u0, nu, Bst, BDg)

    def cross_softmax(score_banks):
        for hp, (pb, Bp) in enumerate(score_banks):
            pv = pb[:].rearrange("p (h t) -> p h t", h=2)
            P.op("dve", lambda e, pv=pv, hp=hp: e.tensor_reduce(out=st[:, 2 * hp:2 * hp + 2], in_=pv, axis=AX.X, op=ALU.max),
                 reads=[Bp], writes=[B_st])
        P.op("dve", lambda e: e.tensor_scalar(out=st[:, 16:20], in0=st[:, 0:4], scalar1=-1.0 / 16.0, scalar2=None, op0=ALU.mult),
             reads=[B_st], writes=[B_st])
        for hp, (pb, Bp) in enumerate(score_banks):
            pv = pb[:].rearrange("p (h t) -> p h t", h=2)
            for hh in range(2):
                h = 2 * hp + hh
                P.op("act", lambda e, pv=pv, hh=hh, h=h: e.activation(out=pexp[:, h, :], in_=pv[:, hh, :], func=AF.Exp,
                                                                      bias=st[:, 16 + h:17 + h], scale=1.0 / 16.0,
                                                                      accum_out=st[:, 32 + h:33 + h]),
                     reads=[Bp, B_st], writes=[B_pexp, B_st])
        P.op("dve", lambda e: e.reciprocal(out=st[:, 56:60], in_=st[:, 32:36]), reads=[B_st], writes=[B_st])
        make_Dg(0, 4, [B_st], [B_Dg])

    def out_tok_major(srcs, Bsrcs, ncols_each, dst_dma):
        pb, Bp = dbank()
        pv = pb[:].rearrange("p (c t) -> p c t", c=4)
        n = len(srcs)
        P.mm([(lambda e, i=i: e.transpose(out=pv[:, i, :], in_=srcs[i], identity=identf[:])) for i in range(n)],
             reads=list(Bsrcs) + CONST, writes=[Bp])
        P.op("dve", lambda e: e.tensor_copy(out=ost[:, 0:n * 128], in_=pb[:, 0:n * 128]), reads=[Bp], writes=[B_ost])
        dst_dma()

    def mem_setup(gen):
        mx, Bmx = xTs[1], B_xTs[1]
        for t in range(2):
            P.dma("sp", "memld", memst[:, t, :], mem[t * 128:(t + 1) * 128, :], writes=[B_memst])
        for t in range(2):
            for hf in range(2):
                pb, Bp = dbank()
                pv = pb[:].rearrange("p (c t) -> p c t", c=4)
                P.mm([(lambda e, c=c, pv=pv, t=t, hf=hf: e.transpose(out=pv[:, c, :], in_=memst[:, t, (hf * 4 + c) * 128:(hf * 4 + c + 1) * 128],
                                                                     identity=identf[:])) for c in range(4)],
                     reads=[B_memst] + CONST, writes=[Bp])
                P.op("act", lambda e, pv=pv, hf=hf, t=t: e.activation(out=mx[:, hf * 4:hf * 4 + 4, t * 128:(t + 1) * 128], in_=pv, func=AF.Copy),
                     reads=[Bp], writes=[Bmx])
        norm(mx, Bmx, 2, 256, hT, B_hT)
        for (wn, is_k) in (("ck", True), ("cv", False)):
            for m in range(8):
                slot, Bslot = W.get(wn, m)
                if is_k:
                    pb, Bp = dbank()
                    P.mm([(lambda e, k=k, slot=slot, pb=pb: e.matmul(pb[:, :256], lhsT=slot[:, k, :], rhs=hT[:, k, :256],
                                                                    start=(k == 0), stop=(k == 7))) for k in range(8)],
                         reads=[Bslot, B_hT], writes=[Bp])
                    P.op("act", lambda e, m=m, pb=pb: e.activation(out=memkT[:, m, :], in_=pb[:, :256], func=AF.Copy),
                         reads=[Bp], writes=[B_memkT])
                pb, Bp = dbank()
                fns = []
                for t in range(2):
                    for k in range(8):
                        fns.append(lambda e, k=k, slot=slot, pb=pb, t=t: e.matmul(
                            pb[:, t * 128:(t + 1) * 128], lhsT=hT[:, k, t * 128:(t + 1) * 128], rhs=slot[:, k, :],
                            start=(k == 0), stop=(k == 7)))
                P.mm(fns, reads=[Bslot, B_hT], writes=[Bp])
                pv2 = pb[:, 0:256].rearrange("p (t c) -> p t c", t=2)
                P.op("dve", lambda e, pv2=pv2, m=m: e.tensor_copy(out=mkst[:, :, m * 128:(m + 1) * 128], in_=pv2),
                     reads=[Bp], writes=[B_mkst])
                if not is_k:
                    P.op("act", lambda e, pv2=pv2, m=m: e.activation(out=memv[:, :, m * 128:(m + 1) * 128], in_=pv2, func=AF.Copy),
                         reads=[Bp], writes=[B_memv])
                advance(gen, 3)
            for t in range(2):
                P.dma("pool", "memout", (memk_o if is_k else memv_o)[t * 128:(t + 1) * 128, :], mkst[:, t, :], reads=[B_mkst])

    def geom(kind):
        sample, halo = (kind == "S"), (kind == "H")
        NT = 128 if (sample or halo) else NT_P
        return sample, halo, NT, NT // 128

    def early(kind, gi, X):
        sample, halo, NT, ntl = geom(kind)
        xt, Bxt = xTs[X], B_xTs[X]
        if sample:
            yield from g_load_x(lambda j: xs[:, :], 1, X)
        elif halo:
            yield from g_load_x(lambda j: xp[0:128, :], 1, X)
        else:
            yield from g_load_x(lambda j: xp[128 + gi * NT_P + j * 128: 128 + gi * NT_P + (j + 1) * 128, :], ntl, X)
        yield "L"
        norm(xt, Bxt, 0, NT, hT, B_hT)
        last = (kind == "P" and gi == NG_P - 1)
        want32 = last or sample
        Us = U[:, :, 0:384].rearrange("p g (b c) -> p g b c", b=16)

        if sample:
            P.op("dve", lambda e: e.memset(U[:, :, 0:384], 0.0), writes=[B_U])
            for hb in range(2):
                P.dma("sp", "spld", xin[hb][0:120, 0:512], spool[hb * 8:(hb + 1) * 8].rearrange("b r f -> (b r) f"), writes=[B_xin[hb]])
                pb, Bp = dbank()
                pv = pb[:].rearrange("p (c t) -> p c t", c=4)
                P.mm([(lambda e, c=c, pv=pv, hb=hb: e.transpose(out=pv[:, c, 0:120], in_=xin[hb][0:120, c * 128:(c + 1) * 128],
                                                                identity=identf[0:120, 0:120])) for c in range(4)],
                     reads=[B_xin[hb]] + CONST, writes=[Bp])
                for c in range(4):
                    P.op("dve", lambda e, c=c, pv=pv, hb=hb: e.tensor_copy(
                        out=Us[:, c, hb * 8:(hb + 1) * 8, 1:16], in_=pv[:, c, 0:120].rearrange("p (b r) -> p b r", b=8)),
                        reads=[Bp], writes=[B_U])
            yield

        def in_evac(m, pb, Bp):
            if m < 4:
                P.op("act", lambda e: e.activation(out=qT[:, m, :NT], in_=pb[:, :NT], func=AF.Copy), reads=[Bp], writes=[B_qT])
            elif m == 4:
                P.op("act", lambda e: e.activation(out=kT[:, 128:128 + NT], in_=pb[:, :NT], func=AF.Copy), reads=[Bp], writes=[B_kT])
                if want32:
                    P.op("dve", lambda e: e.tensor_copy(out=kv32[:, 0, :], in_=pb[:, NT - 128:NT]), reads=[Bp], writes=[B_kv32])
            elif m == 5:
                P.op("act", lambda e: e.activation(out=vT[:, :NT], in_=pb[:, :NT], func=AF.Copy), reads=[Bp], writes=[B_vT])
                if want32:
                    P.op("dve", lambda e: e.tensor_copy(out=kv32[:, 1, :], in_=pb[:, NT - 128:NT]), reads=[Bp], writes=[B_kv32])
            else:
                g = m - 6
                if sample:
                    P.op("dve", lambda e: e.tensor_copy(out=Us[:, g, :, 16:24], in_=pb[:, 0:128].rearrange("p (b t) -> p b t", b=16)),
                         reads=[Bp], writes=[B_U])
                else:
                    P.op("dve", lambda e: e.tensor_copy(out=U[:, g, 16:16 + NT], in_=pb[:, :NT]), reads=[Bp], writes=[B_U])
        yield from g_dense("in", list(range(4, 10)) if halo else list(range(10)), NT, lambda k: hT[:, k, :NT], B_hT, in_evac)
        yield "P1"

        if sample:
            for hb in range(2):
                P.dma("pool", "ckld", vc[:, hb * 8:(hb + 1) * 8, :], cv[hb * 8:(hb + 1) * 8].rearrange("b s f -> s b f"), writes=[B_vc])
            kst = pexp[:].rearrange("p u t -> p (u t)").rearrange("p (b f) -> p b f", b=16)
            for hb in range(2):
                P.dma("pool", "ckld", kst[:, hb * 8:(hb + 1) * 8, :], ck[hb * 8:(hb + 1) * 8].rearrange("b s f -> s b f"), writes=[B_pexpH[hb]])
            for hb in range(2):
                pb, Bp = tbank()
                pv = pb[:].bitcast(BF16).rearrange("p (b t) -> p b t", b=8)
                P.mm([(lambda e, b=b, pv=pv, hb=hb: e.transpose(out=pv[:, b, :], in_=kst[:, hb * 8 + b, :], identity=ident[:])) for b in range(8)],
                     reads=[B_pexpH[hb]] + CONST, writes=[Bp])
                P.op("dve", lambda e, pv=pv, hb=hb: e.tensor_copy(out=kcT[:, hb * 8:(hb + 1) * 8, :], in_=pv), reads=[Bp], writes=[B_kcT])
            yield

        for j0 in range(0, ntl, 4):
            pb, Bp = tbank()
            pv = pb[:].bitcast(BF16)[:, 0:512].rearrange("p (j t) -> p j t", j=4)
            P.mm([(lambda e, j=j, pv=pv: e.transpose(out=pv[:, j - j0, :], in_=vT[:, j * 128:(j + 1) * 128], identity=ident[:]))
                  for j in range(j0, min(ntl, j0 + 4))], reads=[B_vT] + CONST, writes=[Bp])
            nj = min(ntl, j0 + 4) - j0
            P.op("dve", lambda e, pv=pv, j0=j0, nj=nj: e.tensor_copy(out=vtok[:, 1 + j0:1 + j0 + nj, :], in_=pv[:, 0:nj, :]),
                 reads=[Bp], writes=[B_vtok])
        yield

        def carry():
            P.op("dve", lambda e: e.tensor_copy(out=kT[:, 0:128], in_=kT[:, NT:NT + 128]), reads=[B_kT], writes=[B_kT])
            P.op("dve", lambda e: e.tensor_copy(out=vtok[:, 0, :], in_=vtok[:, ntl, :]), reads=[B_vtok], writes=[B_vtok])

        if halo:
            carry()
            P.op("dve", lambda e: e.tensor_copy(out=carryU[:], in_=U[:, :, NT:NT + 16]), reads=[B_U], writes=[B_cU])
            return

        if want32:
            if last:
                def dd():
                    P.dma("pool", "kvout", wkp[:, :], ost[:, 0:128], reads=[B_ost])
                    P.dma("pool", "kvout", wvp[:, :], ost[:, 128:256], reads=[B_ost])
            else:
                def dd():
                    for t in range(8):
                        P.dma("pool", "kvout", wks[:, 120 + t, :], ost[t:128:8, 0:128], reads=[B_ost])
                        P.dma("pool", "kvout", wvs[:, 120 + t, :], ost[t:128:8, 128:256], reads=[B_ost])
                    P.dma("sp", "d2d_k", wks[:, 0:120, :], ck[:, 8:128, :])
                    P.dma("sp", "d2d_v", wvs[:, 0:120, :], cv[:, 8:128, :])
            out_tok_major([kv32[:, 0, :], kv32[:, 1, :]], [B_kv32], 128, dd)
            yield

        if not sample:
            P.op("dve", lambda e: e.tensor_copy(out=U[:, :, 0:16], in_=carryU[:]), reads=[B_cU], writes=[B_U])
        for hh in range(2):
            if sample:
                Wd, c0 = 192, hh * 192
            else:
                Wd, c0 = 16 + NT // 2, hh * (NT // 2)
            Uh = U[:, :, c0:c0 + Wd]
            P.op("dve", lambda e, Uh=Uh, Wd=Wd: e.tensor_tensor(out=SA[:, :, 1:Wd], in0=Uh[:, :, 1:Wd], in1=Uh[:, :, 0:Wd - 1], op=ALU.add),
                 reads=[B_U], writes=[B_SA])
            P.op("dve", lambda e, Wd=Wd: e.tensor_tensor(out=SB[:, 1:4, 3:Wd], in0=SA[:, 1:4, 3:Wd], in1=SA[:, 1:4, 1:Wd - 2], op=ALU.add),
                 reads=[B_SA], writes=[B_SB])
            yield
            P.op("dve", lambda e, Wd=Wd: e.tensor_tensor(out=SA[:, 2:4, 7:Wd], in0=SB[:, 2:4, 7:Wd], in1=SB[:, 2:4, 3:Wd - 4], op=ALU.add),
                 reads=[B_SB], writes=[B_SA])
            P.op("dve", lambda e, Wd=Wd: e.tensor_tensor(out=SB[:, 3, 15:Wd], in0=SA[:, 3, 15:Wd], in1=SA[:, 3, 7:Wd - 8], op=ALU.add),
                 reads=[B_SA], writes=[B_SB])
            yield
            for g in range(4):
                S_, BS_ = (SA, B_SA) if g % 2 == 0 else (SB, B_SB)
                if sample:
                    sv = S_[:, g, 0:192].rearrange("p (b c) -> p b c", b=8)[:, :, 16:24]
                    uv = Uh[:, g, :].rearrange("p (b c) -> p b c", b=8)[:, :, 16:24]
                    dv = dT[:, g, hh * 64:(hh + 1) * 64].rearrange("p (b t) -> p b t", b=8)
                else:
                    sv, uv, dv = S_[:, g, 16:Wd], Uh[:, g, 16:Wd], dT[:, g, c0:c0 + NT // 2]
                P.op("dve", lambda e, sv=sv, uv=uv, dv=dv, g=g: e.scalar_tensor_tensor(out=dv, in0=sv, scalar=1.0 / (2 << g), in1=uv,
                                                                                   op0=ALU.mult, op1=ALU.subtract),
                     reads=[BS_, B_U], writes=[B_dT])
                if kind == "P" and gi == 0 and hh == 0:
                    P.op("dve", lambda e, S_=S_, g=g: e.tensor_tensor(out=st[:, 0:16], in0=S_[:, g, 16:32], in1=invc[:, g, :], op=ALU.mult),
                         reads=[BS_] + CONST, writes=B_stH)
                    P.op("dve", lambda e, g=g: e.tensor_tensor(out=dT[:, g, 0:16], in0=st[:, 0:16], in1=U[:, g, 16:32], op=ALU.subtract),
                         reads=B_stH + [B_U], writes=[B_dT])
            yield
        if last:
            def dd2():
                P.dma("pool", "poolout", poolp[:, :], ost[113:128, :], reads=[B_ost])
            out_tok_major([U[:, g, 16 + NT - 128:16 + NT] for g in range(4)], [B_U], 128, dd2)
        if sample:
            for g in range(4):
                P.op("dve", lambda e, g=g: e.tensor_copy(out=SA[:, g, 0:128].rearrange("p (b t) -> p b t", b=16), in_=Us[:, g, :, 16:24]),
                     reads=[B_U, B_dT], writes=[B_SA])

            def dd3():
                for t in range(8):
                    P.dma("pool", "poolout", pools[:, 7 + t, :], ost[t:128:8, :], reads=[B_ost])
                P.dma("sp", "d2d_p", pools[:, 0:7, :], spool[:, 8:15, :])
            out_tok_major([SA[:, g, 0:128] for g in range(4)], [B_SA], 128, dd3)
        else:
            P.op("dve", lambda e: e.tensor_copy(out=carryU[:], in_=U[:, :, NT:NT + 16]), reads=[B_U], writes=[B_cU])
        yield
        for g in range(4):
            pb, Bp = dbank()
            P.mm([lambda e, g=g, pb=pb: e.matmul(pb[:, :NT], lhsT=wpool[:, g, :], rhs=dT[:, g, :NT], start=True, stop=True)],
                 reads=[B_wpool, B_dT], writes=[Bp])
            P.op("act", lambda e, g=g, pb=pb: e.activation(out=aoT[:, 4 + g, :NT], in_=pb[:, :NT], func=AF.Copy, scale=pscale[:, g:g + 1]),
                 reads=[Bp] + CONST, writes=[B_aoT])
            yield

        if not sample:
            for j in range(ntl):
                if gi == 0 and j == 1:
                    P.dma("sp", "biasld", bias[:].rearrange("p a b -> p (a b)"), biasg_d[:, :], writes=[B_bias])
                for gp in range(2):
                    bk = [sbank(), sbank()]
                    fns = []
                    for g in (2 * gp, 2 * gp + 1):
                        for kv in range(2):
                            fns.append(lambda e, kv=kv, g=g, j=j, bk=bk: e.matmul(
                                bk[kv][0][:, (g % 2) * 256:(g % 2 + 1) * 256], lhsT=qT[kv * 64:(kv + 1) * 64, g, j * 128:(j + 1) * 128],
                                rhs=kT[kv * 64:(kv + 1) * 64, j * 128:j * 128 + 256], start=True, stop=True))
                    P.mm(fns, reads=[B_qT, B_kT], writes=[bk[0][1], bk[1][1]])
                    for kv in range(2):
                        u0 = 4 * gp + kv
                        P.op("dve", lambda e, kv=kv, u0=u0, bk=bk: e.scalar_tensor_tensor(
                            out=sbias[:, u0:u0 + 3:2, :], in0=bk[kv][0][:].rearrange("p (g t) -> p g t", g=2), scalar=0.125,
                            in1=bias[:, u0:u0 + 3:2, :], op0=ALU.mult, op1=ALU.add),
                            reads=[bk[kv][1], B_bias], writes=[B_sbiasH[gp]])
                    yield
                sm = [win_softmax(4 * gp, 4, 256, sinkp[:, 4 * gp:4 * gp + 4], [gp]) for gp in range(2)]
                next(sm[0])
                yield
                next(sm[1])
                yield
                yield
                run(sm[0])
                yield
                run(sm[1])
                yield
                for gp in range(2):
                    yield from diag_T(4 * gp, 4, 2, pexp, [B_pexpH[gp]], [B_DgH[gp]],
                                      lambda u0: pT[:, u0:u0 + 2, :, :].rearrange("p u k t -> p (u k t)"), None, [B_pTH[gp]], [128, 128])
                yield
                po, Bpo = ps[PS_O], B_ps[PS_O]
                pov = po[:].rearrange("p (g t) -> p g t", g=4)
                fns = []
                for g in range(4):
                    for kv in range(2):
                        for kc in range(2):
                            fns.append(lambda e, g=g, kv=kv, kc=kc, j=j: e.matmul(
                                pov[kv * 64:(kv + 1) * 64, g, :], lhsT=vtok[:, j + kc, kv * 64:(kv + 1) * 64], rhs=pT[:, 2 * g + kv, kc, :],
                                start=(kc == 0), stop=(kc == 1)))
                P.mm(fns, reads=[B_vtok] + B_pTH, writes=[Bpo])
                P.op("act", lambda e, j=j: e.activation(out=aoT[:, 0:4, j * 128:(j + 1) * 128], in_=pov, func=AF.Copy),
                     reads=[Bpo], writes=[B_aoT])
                yield
            carry()
        else:
            P.op("dve", lambda e: e.tensor_copy(out=qs2[:].rearrange("p b (g t) -> p b g t", g=4),
                                                in_=qT[:, :, 0:128].rearrange("p g (b t) -> p b g t", b=16)), reads=[B_qT], writes=[B_qs2])
            pb, Bp = dbank()
            pvb = pb[:].bitcast(BF16)
            P.mm([(lambda e, i=i, pvb=pvb: e.transpose(out=pvb[0:32, i * 128:(i + 1) * 128], in_=vT[:, i * 32:(i + 1) * 32], identity=ident[:]))
                  for i in range(4)], reads=[B_vT] + CONST, writes=[Bp])
            P.op("dve", lambda e, pvb=pvb: e.tensor_copy(out=vnq[0:32, :, :], in_=pvb[0:32, 0:512].rearrange("p (i t) -> p i t", i=4)),
                 reads=[Bp], writes=[B_vnq])
            yield
            for i in range(4):
                bk = [sbank(), sbank()]
                fns = []
                for kv in range(2):
                    pvk = bk[kv][0]
                    for jq in range(4):
                        b = 4 * i + jq
                        fns.append(lambda e, kv=kv, jq=jq, b=b, pvk=pvk: e.matmul(
                            pvk[32 * jq:32 * jq + 32, 0:128], lhsT=qs2[kv * 64:(kv + 1) * 64, b, :], rhs=kcT[kv * 64:(kv + 1) * 64, b, :],
                            start=True, stop=True, tile_position=(kv * 64, 32 * jq)))
                    fns.append(lambda e, kv=kv, i=i, pvk=pvk: e.matmul(
                        pvk[:, 128:160], lhsT=qs2[kv * 64:(kv + 1) * 64, 4 * i:4 * i + 4, :].rearrange("p b t -> p (b t)"),
                        rhs=kT[kv * 64:(kv + 1) * 64, 128 + 32 * i:128 + 32 * i + 32], start=True, stop=True))
                P.mm(fns, reads=[B_qs2, B_kcT, B_kT], writes=[bk[0][1], bk[1][1]])
                for kv in range(2):
                    P.op("dve", lambda e, kv=kv, i=i, bk=bk: e.scalar_tensor_tensor(
                        out=sbias[:, 2 * i + kv, 0:160], in0=bk[kv][0][:, 0:160], scalar=0.125,
                        in1=biass[:, kv, :], op0=ALU.mult, op1=ALU.add),
                        reads=[bk[kv][1]] + CONST, writes=[B_sbiasH[i // 2]])
                yield
            smx = win_softmax(0, 8, 160, sinks[:, 0:8], [0, 1])
            next(smx)
            yield
            yield
            run(smx)
            yield
            yield from diag_T(0, 8, 2, pexp, B_pexpH, B_DgH, None, lambda u, kc: pT[0:(128 if kc == 0 else 32), u, kc, :], B_pTH, [128, 32])
            for i in range(4):
                pb, Bp = dbank()
                fns = []
                for kv in range(2):
                    u = 2 * i + kv
                    for jq in range(4):
                        b = 4 * i + jq
                        fns.append(lambda e, kv=kv, jq=jq, b=b, u=u, pb=pb: e.matmul(
                            pb[kv * 64:(kv + 1) * 64, 32 * jq:32 * jq + 32], lhsT=vc[:, b, kv * 64:(kv + 1) * 64], rhs=pT[:, u, 0, 32 * jq:32 * jq + 32],
                            start=(jq == 0), stop=False, skip_group_check=True))
                for kv in range(2):
                    u = 2 * i + kv
                    fns.append(lambda e, kv=kv, u=u, i=i, pb=pb: e.matmul(
                        pb[kv * 64:(kv + 1) * 64, 0:128], lhsT=vnq[0:32, i, kv * 64:(kv + 1) * 64], rhs=pT[0:32, u, 1, :],
                        start=False, stop=True, skip_group_check=True))
                P.mm(fns, reads=[B_vc, B_vnq] + B_pTH, writes=[Bp])
                P.op("act", lambda e, pb=pb, i=i: e.activation(
                    out=aoT[:, 0:4, 32 * i:32 * i + 32].rearrange("p g (j t) -> p j g t", j=4),
                    in_=pb[:, 0:128].rearrange("p (j g t) -> p j g t", j=4, g=4), func=AF.Copy),
                    reads=[Bp], writes=[B_aoT])
                yield

    def late_pre(kind, gi, X, gen, tgen=None):
        sample, halo, NT, ntl = geom(kind)
        xt, Bxt = xTs[X], B_xTs[X]
        loaded = [gen is None]
        xfree = [tgen is None]

        def step_tail(n):
            for _ in range(n):
                if tgen is not None and next(tgen, "END") == "XDONE":
                    xfree[0] = True

        def step_load():
            if not loaded[0] and xfree[0]:
                if next(gen, "L") == "L":
                    loaded[0] = True
        p1done = [gen is None]

        def step_p1(n):
            for _ in range(n):
                if not p1done[0]:
                    if next(gen, "P1") == "P1":
                        p1done[0] = True
        prep2 = make_prep(X, NT, 1, -0.5, rstd, B_rstd)
        for _ in g_dense("out", list(range(8)), NT, lambda k: aoT[:, k, :NT], B_aoT, resid_evac(NT, X, prep=prep2)):
            step_load()
            step_tail(2)
        prep2[1]()
        qcT, B_qcT = aoT, B_aoT
        ocT, B_ocT = hidT, B_hid

        def cq_evac(m, pb, Bp):
            P.op("dve", lambda e: e.tensor_tensor(out=qcT[:, m, :NT], in0=pb[:, :NT], in1=rstd[:, :NT], op=ALU.mult),
                 reads=[Bp, B_rstd], writes=[B_qcT])
        for _ in g_dense("cq", list(range(8)), NT, lambda k: hT[:, k, :NT], B_hT, cq_evac):
            step_load()
            step_tail(2)
        while tgen is not None and not xfree[0]:
            step_tail(1)
        while not loaded[0]:
            step_load()
        run(tgen)

        if not sample:
            for j in range(ntl):
                banks = [sbank(), sbank()]
                for hp, (pb, Bp) in enumerate(banks):
                    pv = pb[:].rearrange("p (h t) -> p h t", h=2)
                    fns = []
                    for hh in range(2):
                        h = 2 * hp + hh
                        for dc in range(2):
                            fns.append(lambda e, pv=pv, hh=hh, h=h, dc=dc, j=j: e.matmul(
                                pv[:, hh, :], lhsT=qcT[:, 2 * h + dc, j * 128:(j + 1) * 128], rhs=memkT[:, 2 * h + dc, :],
                                start=(dc == 0), stop=(dc == 1)))
                    P.mm(fns, reads=[B_qcT, B_memkT], writes=[Bp])
                cross_softmax(banks)
                step_p1(2)
                run(diag_T(0, 4, 2, pexp, [B_pexp], [B_Dg], lambda h0: pTc[:, h0:h0 + 2, :, :].rearrange("p u k t -> p (u k t)"), None, [B_pTc], [128, 128]))
                for half in range(2):
                    pb, Bp = dbank()
                    pv = pb[:].rearrange("p (c t) -> p c t", c=4)
                    fns = []
                    for cc in range(4):
                        c = half * 4 + cc
                        h = c // 2
                        for mc in range(2):
                            fns.append(lambda e, pv=pv, cc=cc, c=c, h=h, mc=mc: e.matmul(
                                pv[:, cc, :], lhsT=memv[:, mc, c * 128:(c + 1) * 128], rhs=pTc[:, h, mc, :], start=(mc == 0), stop=(mc == 1)))
                    P.mm(fns, reads=[B_memv, B_pTc], writes=[Bp])
                    P.op("act" if half == 0 else "dve",
                         (lambda e, pv=pv, half=half, j=j: e.activation(out=ocT[:, half * 4:half * 4 + 4, j * 128:(j + 1) * 128], in_=pv, func=AF.Copy))
                         if half == 0 else
                         (lambda e, pv=pv, half=half, j=j: e.tensor_copy(out=ocT[:, half * 4:half * 4 + 4, j * 128:(j + 1) * 128], in_=pv)),
                         reads=[Bp], writes=[B_ocT])
                step_p1(2)
        else:
            banks = [sbank(), sbank()]
            for i in range(2):
                P.op("dve", lambda e, i=i: e.memset(qpad[i][:], 0.0), writes=[B_qpad[i]])

            def xslot(c0):
                return xTs[0][:, :, c0:c0 + 128].bitcast(BF16)
            KX = [xslot(128), xslot(256)]
            VX = [xslot(384)]
            B_KX = [Buf("KX0"), Buf("KX1")]
            B_VX = [Buf("VX0")]
            for bb_ in B_KX + B_VX:
                bb_.al.append(B_xTs[0])
                B_xTs[0].al.append(bb_)
            NK, NV = 4, 3

            def kslot(b):
                i = b % NK
                return (("a", Kb[i], B_Kb[i]) if i < 2 else ("x", KX[i - 2], B_KX[i - 2]))

            def vslot(b):
                i = b % NV
                return (("a", Vb[i], B_Vb[i]) if i < 2 else ("x", VX[i - 2], B_VX[i - 2]))

            def ld(slot, src):
                kind_, ap_, B_ = slot
                if kind_ == "a":
                    P.dma("pool", "c", ap_[:], src.rearrange("(m p) f -> p m f", p=128), writes=[B_])
                else:
                    for m_ in range(2):
                        P.dma("pool", "c", ap_[:, m_ * 4:(m_ + 1) * 4, :],
                              src[m_ * 128:(m_ + 1) * 128, :].rearrange("p (q j) -> p q j", j=256), writes=[B_])

            def tile_of(slot, mt, c):
                kind_, ap_, B_ = slot
                if kind_ == "a":
                    return ap_[:, mt, c * 128:(c + 1) * 128]
                return ap_[:, mt * 4 + c // 2, (c % 2) * 128:(c % 2 + 1) * 128]

            for b in range(3):
                ld(kslot(b), cmk[b])
            for b in range(3):
                ld(vslot(b), cmv[b])
            def Tstage(b):
                s2 = b % 2
                ks = kslot(b)
                if b + 3 < 16:
                    ld(kslot(b + 3), cmk[b + 3])
                for mt in range(2):
                    pb, Bp = tbank()
                    pv = pb[:].bitcast(BF16).rearrange("p (c t) -> p c t", c=8)
                    P.mm([(lambda e, c=c, pv=pv, mt=mt, ks=ks: e.transpose(out=pv[:, c, :], in_=tile_of(ks, mt, c), identity=ident[:]))
                          for c in range(8)], reads=[ks[2]] + CONST, writes=[Bp])
                    P.op("act" if mt == 0 else "dve",
                         (lambda e, pv=pv, mt=mt, s2=s2: e.activation(out=KbT[s2][:, :, mt * 128:(mt + 1) * 128], in_=pv, func=AF.Copy))
                         if mt == 0 else
                         (lambda e, pv=pv, mt=mt, s2=s2: e.tensor_copy(out=KbT[s2][:, :, mt * 128:(mt + 1) * 128], in_=pv)),
                         reads=[Bp], writes=[B_KbT[s2]])
                if b >= 2:
                    P.op("dve", lambda e, s2=s2, b=b: e.memset(qpad[s2][:, :, (b - 2) * 8:(b - 1) * 8], 0.0), writes=[B_qpad[s2]])
                P.op("dve", lambda e, s2=s2, b=b: e.tensor_copy(out=qpad[s2][:, :, b * 8:(b + 1) * 8], in_=qcT[:, :, b * 8:(b + 1) * 8]),
                     reads=[B_qcT], writes=[B_qpad[s2]])

            def Sstage(b):
                s2 = b % 2
                for hp, (pb, Bp) in enumerate(banks):
                    pv = pb[:].rearrange("p (h t) -> p h t", h=2)
                    fns = []
                    for hh in range(2):
                        h = 2 * hp + hh
                        for dc in range(2):
                            fns.append(lambda e, pv=pv, hh=hh, h=h, dc=dc, s2=s2, b=b: e.matmul(
                                pv[:, hh, :], lhsT=qpad[s2][:, 2 * h + dc, :], rhs=KbT[s2][:, 2 * h + dc, :],
                                start=(b == 0 and hh == 0 and dc == 0), stop=(b == 15 and dc == 1), skip_group_check=True))
                    P.mm(fns, reads=[B_qpad[s2], B_KbT[s2]], writes=[Bp])

            Tstage(0)
            for b in range(16):
                if b + 1 < 16:
                    Tstage(b + 1)
                Sstage(b)
            cross_softmax(banks)
            run(diag_T(0, 4, 2, pexp, [B_pexp], [B_Dg], lambda h0: pTc[:, h0:h0 + 2, :, :].rearrange("p u k t -> p (u k t)"), None, [B_pTc], [128, 128]))
            pbs = [dbank(), dbank()]
            for b in range(16):
                vs = vslot(b)
                fns = []
                for c in range(8):
                    pv = pbs[c // 4][0][:].rearrange("p (c t) -> p c t", c=4)
                    h = c // 2
                    for mc in range(2):
                        fns.append(lambda e, pv=pv, c=c, h=h, mc=mc, vs=vs, b=b: e.matmul(
                            pv[:, c % 4, b * 8:(b + 1) * 8], lhsT=tile_of(vs, mc, c), rhs=pTc[:, h, mc, b * 8:(b + 1) * 8],
                            start=(mc == 0), stop=(mc == 1), skip_group_check=True))
                P.mm(fns, reads=[vs[2], B_pTc], writes=[pbs[0][1], pbs[1][1]])
                if b + 3 < 16:
                    ld(vslot(b + 3), cmv[b + 3])
            for half in range(2):
                pv = pbs[half][0][:].rearrange("p (c t) -> p c t", c=4)
                P.op("act" if half == 0 else "dve",
                     (lambda e, pv=pv, half=half: e.activation(out=ocT[:, half * 4:half * 4 + 4, 0:128], in_=pv, func=AF.Copy))
                     if half == 0 else
                     (lambda e, pv=pv, half=half: e.tensor_copy(out=ocT[:, half * 4:half * 4 + 4, 0:128], in_=pv)),
                     reads=[pbs[half][1]], writes=[B_ocT])
        while not p1done[0]:
            step_p1(1)
        prep3 = make_prep(X, NT, 3, -1.0, rstd2, B_rstd2)
        dense("co", list(range(8)), NT, lambda k: ocT[:, k, :NT], B_ocT, resid_evac(NT, X, prep=prep3))
        prep3[1]()

    def ffn(kind, gi, X, gen):
        sample, halo, NT, ntl = geom(kind)
        xt, Bxt = xTs[X], B_xTs[X]
        uctr = [0]

        def up_evac(m, pb, Bp):
            r, Br = relu_t[uctr[0] % 2], B_relu[uctr[0] % 2]
            uctr[0] += 1
            P.op("act", lambda e: e.activation(out=r[:, :NT], in_=pb[:, :NT], func=AF.Relu), reads=[Bp], writes=[Br])
            P.op("pool", lambda e: e.tensor_tensor(out=hidT[:, m, :NT], in0=r[:, :NT], in1=r[:, :NT], op=ALU.mult), reads=[Br], writes=[B_hid])
        for _ in g_dense("up", list(range(32)), NT, lambda k: hT[:, k, :NT], B_hT, up_evac):
            advance(gen, 1)
        for _ in g_dense("down", list(range(8)), NT, lambda k: hidT[:, k, :NT], B_hid, resid_evac(NT, X, scale2=True), kgroups=4):
            advance(gen, 1)

    def g_tail(kind, gi, X):
        sample, halo, NT, ntl = geom(kind)
        xt, Bxt = xTs[X], B_xTs[X]
        sq = hidT[:, 16:24, :]
        P.op("act", lambda e: e.activation(out=sq[:, :, :NT], in_=xt[:, :, :NT], func=AF.Square), reads=[Bxt], writes=[B_hid])
        yield
        pb0, Bp0 = dbank()
        P.mm([(lambda e, k=k: e.matmul(pb0[:, :NT], lhsT=ones[:], rhs=sq[:, k, :NT], start=(k == 0), stop=(k == 7)))
              for k in range(8)], reads=[B_hid] + CONST, writes=[Bp0])
        P.op("act", lambda e: e.activation(out=rstd2[:, :NT], in_=pb0[:, :NT], func=AF.Ln, scale=1.0 / D, bias=EPS),
             reads=[Bp0], writes=[B_rstd2])
        P.op("act", lambda e: e.activation(out=rstd2[:, :NT], in_=rstd2[:, :NT], func=AF.Exp, scale=-0.5),
             reads=[B_rstd2], writes=[B_rstd2])
        yield
        for k in range(8):
            P.op("dve", lambda e, k=k: e.scalar_tensor_tensor(out=yT[:, k, :NT], in0=xt[:, k, :NT], scalar=gvec[:, 4, k:k + 1],
                                                              in1=rstd2[:, :NT], op0=ALU.mult, op1=ALU.mult),
                 reads=[Bxt, B_rstd2] + CONST, writes=[B_yT])
            if k % 2 == 1 and k < 7:
                yield
        yield "XDONE"
        for j in range(ntl):
            ys_, Bys = yst[1], B_yst[1]
            for hf in range(2):
                pb, Bp = dbank()
                pv = pb[:].rearrange("p (c t) -> p c t", c=4)
                P.mm([(lambda e, c=c, pv=pv, hf=hf, j=j: e.transpose(out=pv[:, c, :], in_=yT[:, hf * 4 + c, j * 128:(j + 1) * 128], identity=identf[:]))
                      for c in range(4)], reads=[B_yT] + CONST, writes=[Bp])
                P.op("act" if hf == 0 else "dve",
                     (lambda e, pb=pb, hf=hf, ys_=ys_: e.activation(out=ys_[:, hf * 512:(hf + 1) * 512], in_=pb[:, :], func=AF.Copy))
                     if hf == 0 else
                     (lambda e, pb=pb, hf=hf, ys_=ys_: e.tensor_copy(out=ys_[:, hf * 512:(hf + 1) * 512], in_=pb[:, :])),
                     reads=[Bp], writes=[Bys])
                yield
            if sample:
                P.dma("pool", "y", ys[:, :], ys_[:], reads=[Bys])
            else:
                r0 = gi * NT_P + j * 128
                P.dma("pool", "y", yp[r0:r0 + 128, :], ys_[:], reads=[Bys])

    order = [("P", g) for g in range(NG_P)] + [("S", 0)]
    order = order[:max(0, min(len(order), STAGE))] if STAGE < 50 else order
    run(early("H", 0, 0))
    gen0 = early(order[0][0], order[0][1], 0) if order else None
    advance(gen0, until="P1")
    mem_setup(gen0)
    run(gen0)
    tgen = None
    for idx, (kind, gi) in enumerate(order):
        X = idx % 2
        nxt = order[idx + 1] if idx + 1 < len(order) else None
        gen = early(nxt[0], nxt[1], (idx + 1) % 2) if nxt else None
        late_pre(kind, gi, X, gen, tgen)
        ffn(kind, gi, X, gen)
        run(gen)
        tgen = g_tail(kind, gi, X)
        if not TAIL_OVERLAP:
            run(tgen)
            tgen = None
    run(tgen)

    return finish()


_CACHE = {}


def _build_nc():
    if "nc" in _CACHE:
        return _CACHE["nc"]
    nc0 = bass.Bass("TRN2", target_bir_lowering=False)
    with ExitStack() as es0:
        _, W0 = build_sched(nc0, es0)
    sched = W0.rec
    nc = bass.Bass("TRN2", target_bir_lowering=False)
    with ExitStack() as es:
        P, W = build(nc, es, False, sched)
        assert W.i == len(sched), (W.i, len(sched))
        block = es.enter_context(nc.Block())
        P.flush(block)
    _CACHE["nc"] = nc
    return nc


def build_sched(nc0, es0):
    return build(nc0, es0, False, None)


def _tables(half):
    slopes = 2.0 ** (-(np.arange(8) + 1.0))
    q = np.arange(128)[:, None]
    c = np.arange(256)[None, :]
    dist = q - c + 128
    valid = (dist >= 0) & (dist <= 128)
    biasg = np.empty((128, 8, 256), np.float32)
    for g in range(4):
        for kv in range(2):
            h = kv * 4 + g
            biasg[:, 2 * g + kv, :] = np.where(valid, -slopes[h] * dist, -1e30)
    biasf = biasg.copy()
    if half == 0:
        biasf[:, :, 0:128] = -1e30
    biass = np.full((128, 2, 160), -1e30, np.float32)
    for j in range(4):
        for g in range(4):
            for t in range(8):
                r = j * 32 + g * 8 + t
                for kv in range(2):
                    h = kv * 4 + g
                    cc = np.arange(128)
                    d = t + 128 - cc
                    biass[r, kv, 0:128] = np.where(cc >= t, -slopes[h] * d, -1e30)
                    for tp in range(t + 1):
                        biass[r, kv, 128 + j * 8 + tp] = -slopes[h] * (t - tp)
    invc = np.empty((128, 4, 16), np.float32)
    for g in range(4):
        w = 2 << g
        for p in range(16):
            invc[:, g, p] = 1.0 / (min(p + 1, w) if half == 0 else w)
    return biasg.reshape(128, -1), biasf.reshape(128, -1), biass.reshape(128, -1), invc.reshape(128, -1)


def _prep(x_prompt, x_sample, cache_win_k, cache_win_v, state_pool, cache_mem_k, cache_mem_v,
          mem_prompt, g_mix, w_in, attn_sinks, w_pool, pool_scale, w_out, g_cross, g_mem,
          w_cq, w_ck, w_cv, w_co, g_ffn, w_up, w_down, g_final):
    f = lambda a: np.ascontiguousarray(np.asarray(a, dtype=np.float32))
    x_prompt, x_sample = f(x_prompt), f(x_sample)
    shared = dict(w_in=f(w_in)[0], w_pool=f(w_pool)[0], w_out=f(w_out)[0], w_cq=f(w_cq)[0], w_ck=f(w_ck)[0],
                  w_cv=f(w_cv)[0], w_co=f(w_co)[0], w_up=f(w_up)[0], w_down=f(w_down)[0])
    gs = np.stack([f(g_mix)[0], f(g_cross)[0], f(g_mem)[0], f(g_ffn)[0], f(g_final)], 0)
    shared["gvec"] = np.ascontiguousarray(gs.reshape(5, 8, 128).transpose(2, 0, 1).reshape(128, 40))
    shared["pscale"] = np.ascontiguousarray(f(pool_scale)[0].reshape(4, 128).T)
    sk = f(attn_sinks)[0]
    sinkp = np.empty((128, 8), np.float32)
    for g in range(4):
        for kv in range(2):
            sinkp[:, 2 * g + kv] = sk[kv * 4 + g]
    shared["sinkp"] = sinkp
    sinks = np.empty((128, 8), np.float32)
    for r in range(128):
        g = (r % 32) // 8
        for i in range(4):
            sinks[r, 2 * i] = sk[g]
            sinks[r, 2 * i + 1] = sk[4 + g]
    shared["sinks"] = sinks
    ckf, cvf, spf = f(cache_win_k)[0], f(cache_win_v)[0], f(state_pool)[0]
    cmkf, cmvf, memf = f(cache_mem_k)[0], f(cache_mem_v)[0], f(mem_prompt)
    in_maps = []
    for c in range(NCORES):
        b, half = c // 2, c % 2
        s0 = half * SEQ_CORE
        xp = np.zeros((128 + SEQ_CORE, D), np.float32)
        xp[128:] = x_prompt[b, s0:s0 + SEQ_CORE]
        if half == 1:
            xp[:128] = x_prompt[b, s0 - 128:s0]
        biasg, biasf, biass, invc = _tables(half)
        sl = slice(16 * c, 16 * c + 16)
        m = dict(shared)
        m.update(xp=xp, xs=np.ascontiguousarray(x_sample[sl].reshape(128, D)), mem=np.ascontiguousarray(memf[b]),
                 ck=np.ascontiguousarray(ckf[sl].reshape(16, 128, 128)), cv=np.ascontiguousarray(cvf[sl].reshape(16, 128, 128)),
                 spool=np.ascontiguousarray(spf[sl]), cmk=np.ascontiguousarray(cmkf[sl].reshape(16, 256, D)),
                 cmv=np.ascontiguousarray(cmvf[sl].reshape(16, 256, D)),
                 biasg=biasg, biasf=biasf, biass=biass, invc=invc)
        in_maps.append(m)
    return in_maps


def kernel(**inputs):
    in_maps = _prep(**inputs)
    nc = _build_nc()
    res = run_bass_kernel_spmd(nc, in_maps, core_ids=list(range(NCORES))).results
    return _assemble(res)


def _assemble(res):
    B, S = 4, 4096
    y_prompt = np.empty((B, S, D), np.float32)
    y_sample = np.empty((128, 8, D), np.float32)
    wk_p = np.empty((1, B, 128, 2, 64), np.float32); wv_p = np.empty_like(wk_p)
    pool_p = np.empty((1, B, 15, 512), np.float32)
    mk_p = np.empty((1, B, 256, 4, 256), np.float32); mv_p = np.empty_like(mk_p)
    wk_s = np.empty((1, 128, 128, 2, 64), np.float32); wv_s = np.empty_like(wk_s)
    pool_s = np.empty((1, 128, 15, 512), np.float32)
    for c in range(NCORES):
        r = res[c]
        b, half = c // 2, c % 2
        y_prompt[b, half * SEQ_CORE:(half + 1) * SEQ_CORE] = r["yp"]
        sl = slice(16 * c, 16 * c + 16)
        y_sample[sl] = r["ys"].reshape(16, 8, D)
        if half == 1:
            wk_p[0, b] = r["wkp"].reshape(128, 2, 64)
            wv_p[0, b] = r["wvp"].reshape(128, 2, 64)
            pool_p[0, b] = r["poolp"]
        else:
            mk_p[0, b] = r["memk"].reshape(256, 4, 256)
            mv_p[0, b] = r["memv"].reshape(256, 4, 256)
        wk_s[0, sl] = r["wks"].reshape(16, 128, 2, 64)
        wv_s[0, sl] = r["wvs"].reshape(16, 128, 2, 64)
        pool_s[0, sl] = r["pools"]
    return (y_prompt, y_sample, wk_p, wv_p, pool_p, mk_p, mv_p, wk_s, wv_s, pool_s)
```

```python
import numpy as np
from contextlib import ExitStack
import concourse.bass as bass
import concourse.mybir as mybir
from concourse.bass_utils import run_bass_kernel_spmd

F32 = mybir.dt.float32
BF16 = mybir.dt.bfloat16
ALU = mybir.AluOpType
AF = mybir.ActivationFunctionType
AX = mybir.AxisListType

NCORES = 8
STAGE = 99
TAIL_OVERLAP = True
D = 1024
SEQ_CORE = 2048
NT_P = 512
NG_P = SEQ_CORE // NT_P
RING = 10
EPS = 1e-5


class Buf:
    __slots__ = ("name", "w", "r", "al", "excl")

    def __init__(self, name, excl=False):
        self.name = name
        self.w = None
        self.r = {}
        self.al = []
        self.excl = excl


def alias(*bufs):
    for a in bufs:
        for b in bufs:
            if a is not b and b not in a.al:
                a.al.append(b)


class Prog:
    def __init__(self, nc, es, dry):
        self.nc, self.es, self.dry = nc, es, dry
        self.q = {e: [] for e in ("pe", "act", "dve", "pool", "sp")}
        self.cnt, self.sems = {}, {}
        self.waited = {e: {} for e in self.q}

    def sem(self, key):
        if key not in self.sems:
            self.sems[key] = None if self.dry else self.es.enter_context(self.nc.semaphore(key))
            self.cnt[key] = 0

    def _wait(self, eng, tok):
        if tok is None:
            return
        key, val = tok
        if self.waited[eng].get(key, 0) >= val:
            return
        self.waited[eng][key] = val
        self.q[eng].append(("w", key, val))

    def _deps(self, eng, reads, writes, extra):
        for b in reads:
            self._wait(eng, b.w)
            if b.excl:
                for k, v in b.r.items():
                    if k != eng:
                        self._wait(eng, (k, v))
        for b in writes:
            for bb in [b] + b.al:
                self._wait(eng, bb.w)
                for k, v in bb.r.items():
                    self._wait(eng, (k, v))
        for t in extra:
            self._wait(eng, t)

    def _commit(self, tok, reads, writes):
        k, v = tok
        for b in reads:
            b.r[k] = max(b.r.get(k, 0), v)
        for b in writes:
            b.w = tok
            b.r = {}

    def op(self, eng, fn, reads=(), writes=(), extra=()):
        self._deps(eng, reads, writes, extra)
        self.sem(eng)
        self.cnt[eng] += 1
        tok = (eng, self.cnt[eng])
        self.q[eng].append(("i", fn, eng, 1))
        self._commit(tok, reads, writes)
        return tok

    def mm(self, fns, reads=(), writes=(), extra=()):
        self._deps("pe", reads, writes, extra)
        for f in fns[:-1]:
            self.q["pe"].append(("i", f, None, 0))
        self.sem("pe")
        self.cnt["pe"] += 1
        tok = ("pe", self.cnt["pe"])
        self.q["pe"].append(("i", fns[-1], "pe", 1))
        self._commit(tok, reads, writes)
        return tok

    def dma(self, qeng, semkey, out, in_, reads=(), writes=(), extra=()):
        if writes:
            semkey = "dw" + qeng[0] + "_" + writes[0].name
        elif reads:
            semkey = "dr" + qeng[0] + "_" + reads[0].name
        for b in reads:
            self._wait(qeng, b.w)
        for b in writes:
            for bb in [b] + b.al:
                if not (bb.w is not None and bb.w[0] == semkey):
                    self._wait(qeng, bb.w)
                for k, v in bb.r.items():
                    self._wait(qeng, (k, v))
        for t in extra:
            self._wait(qeng, t)
        self.sem(semkey)
        self.cnt[semkey] += 16
        tok = (semkey, self.cnt[semkey])
        self.q[qeng].append(("i", (lambda e, o=out, i=in_: e.dma_start(out=o, in_=i)), semkey, 16))
        self._commit(tok, reads, writes)
        return tok

    def flush(self, block):
        def run(name):
            def f(e):
                for it in self.q[name]:
                    if it[0] == "w":
                        e.wait_ge(self.sems[it[1]], it[2])
                    else:
                        ins = it[1](e)
                        if it[3]:
                            ins.then_inc(self.sems[it[2]], it[3])
            return f
        block.tensor(run("pe"))
        block.scalar(run("act"))
        block.vector(run("dve"))
        block.gpsimd(run("pool"))
        block.sync(run("sp"))


class WStream:
    def __init__(self, P, ring_ap, sched, scratch_fn=None):
        self.P, self.ring = P, ring_ap
        self.sched = sched
        self.rec = []
        self.i = 0
        self.issued = 0
        self.slots = [Buf(f"ws{i}") for i in range(RING)]
        self.src = {}
        self.uidx, self.wtok = {}, {}
        self.scratch = None
        if sched is not None:
            cnt = {}
            for k in sched:
                cnt[k] = cnt.get(k, 0) + 1
            for k in sched:
                if cnt[k] > 1 and k not in self.uidx:
                    self.uidx[k] = len(self.uidx)
            if scratch_fn is not None and self.uidx:
                self.scratch = scratch_fn(len(self.uidx))

    def _issue(self, j):
        key = self.sched[j]
        name, m = key
        s = j % RING
        if self.scratch is not None and key in self.wtok:
            self.P.dma("sp", f"ws{s}", self.ring[:, s], self.scratch[self.uidx[key]], writes=[self.slots[s]],
                       extra=[self.wtok[key]])
            return
        for (dst_fn, src_ap) in self.src[name](m):
            self.P.dma("pool", f"ws{s}", dst_fn(self.ring[:, s]), src_ap, writes=[self.slots[s]])
        if self.scratch is not None and key in self.uidx:
            self.wtok[key] = self.P.dma("sp", f"sw{s}", self.scratch[self.uidx[key]], self.ring[:, s], reads=[self.slots[s]])

    def get(self, name, m):
        if self.sched is None:
            self.rec.append((name, m))
            return self.ring[:, 0], self.slots[0]
        assert self.sched[self.i] == (name, m), (self.i, self.sched[self.i], name, m)
        while self.issued < min(len(self.sched), self.i + RING - 3):
            self._issue(self.issued)
            self.issued += 1
        s = self.i % RING
        self.i += 1
        return self.ring[:, s], self.slots[s]


def build(nc, es, dry, sched):
    P = Prog(nc, es, dry)

    def din(name, shape):
        return nc.dram_tensor(name, list(shape), F32, kind="ExternalInput").ap()

    def dout(name, shape):
        return nc.dram_tensor(name, list(shape), F32, kind="ExternalOutput").ap()

    if not dry:
        xp = din("xp", [128 + SEQ_CORE, D]); xs = din("xs", [128, D]); mem = din("mem", [256, D])
        ck = din("ck", [16, 128, 128]); cv = din("cv", [16, 128, 128]); spool = din("spool", [16, 15, 512])
        cmk = din("cmk", [16, 256, D]); cmv = din("cmv", [16, 256, D])
        w_in = din("w_in", [D, 1280]); w_pool = din("w_pool", [4, 128, 128]); w_out = din("w_out", [D, D])
        w_cq = din("w_cq", [D, D]); w_ck = din("w_ck", [D, D]); w_cv = din("w_cv", [D, D]); w_co = din("w_co", [D, D])
        w_up = din("w_up", [D, 4 * D]); w_down = din("w_down", [4 * D, D])
        gvec_d = din("gvec", [128, 40]); pscale_d = din("pscale", [128, 4])
        sinkp_d = din("sinkp", [128, 8]); sinks_d = din("sinks", [128, 8])
        biasg_d = din("biasg", [128, 8 * 256]); biasf_d = din("biasf", [128, 8 * 256]); biass_d = din("biass", [128, 2 * 160])
        invc_d = din("invc", [128, 64])
        yp = dout("yp", [SEQ_CORE, D]); ys = dout("ys", [128, D])
        wkp = dout("wkp", [128, 128]); wvp = dout("wvp", [128, 128]); poolp = dout("poolp", [15, 512])
        memk_o = dout("memk", [256, D]); memv_o = dout("memv", [256, D])
        wks = dout("wks", [16, 128, 128]); wvs = dout("wvs", [16, 128, 128]); pools = dout("pools", [16, 15, 512])

    def sb(name, shape, dt):
        return es.enter_context(nc.sbuf_tensor("sb_" + name, list(shape), dt))

    xTs = [sb(f"xT{i}", [128, 8, NT_P], F32) for i in range(2)]; B_xTs = [Buf(f"xT{i}") for i in range(2)]
    xT, B_xT = xTs[0], B_xTs[0]
    hT = sb("hT", [128, 8, NT_P], BF16); B_hT = Buf("hT")
    rstd = sb("rstd", [128, NT_P], F32); B_rstd = Buf("rstd")
    aoT = sb("aoT", [128, 8, NT_P], BF16); B_aoT = Buf("aoT")
    qT = sb("qT", [128, 4, NT_P], BF16); B_qT = Buf("qT")
    kT = sb("kT", [128, 128 + NT_P], BF16); B_kT = Buf("kT")
    vT = sb("vT", [128, NT_P], BF16); B_vT = Buf("vT")
    vtok = sb("vtok", [128, 5, 128], BF16); B_vtok = Buf("vtok")
    kv32 = sb("kv32", [128, 2, 128], F32); B_kv32 = Buf("kv32")
    dT = sb("dT", [128, 4, NT_P], BF16); B_dT = Buf("dT")
    pexp = sb("pexp", [128, 8, 256], BF16); B_pexpH = [Buf("pexp0"), Buf("pexp1")]; B_pexp = B_pexpH[0]
    pT = sb("pT", [128, 8, 2, 128], BF16); B_pTH = [Buf("pT0"), Buf("pT1")]; B_pT = B_pTH[0]
    Dg = sb("Dg", [128, 8, 128], BF16); B_DgH = [Buf("Dg0"), Buf("Dg1")]; B_Dg = B_DgH[0]
    pTc = sb("pTc", [128, 4, 2, 128], BF16); B_pTc = Buf("pTc")
    bias = sb("bias", [128, 8, 256], F32); B_bias = Buf("bias")
    biass = sb("biass", [128, 2, 160], F32); B_biass = Buf("biass")
    memkT = sb("memkT", [128, 8, 256], BF16); B_memkT = Buf("memkT")
    memv = sb("memv", [128, 2, D], BF16); B_memv = Buf("memv")
    ring = sb("ring", [128, RING, 8, 128], BF16)
    xin = [sb(f"xin{i}", [128, D], F32) for i in range(2)]; B_xin = [Buf(f"xin{i}") for i in range(2)]
    yst1 = sb("yst1", [128, D], F32); yst, B_yst = [None, yst1], [None, Buf("yst1")]
    ident = sb("ident", [128, 128], BF16); identf = sb("identf", [128, 128], F32); B_const = Buf("const")
    ones = sb("ones", [128, 128], BF16)
    gvec = sb("gvec", [128, 5, 8], F32); pscale = sb("pscale", [128, 4], F32)
    sinkp = sb("sinkp", [128, 8], F32); sinks = sb("sinks", [128, 8], F32)
    invc = sb("invc", [128, 4, 16], F32)
    wpool = sb("wpool", [128, 4, 128], BF16); B_wpool = Buf("wpool")
    st = sb("st", [128, 64], F32); B_stH = [Buf("st0"), Buf("st1")]; B_st = B_stH[0]
    relu_t = [sb(f"relu{i}", [128, NT_P], BF16) for i in range(2)]; B_relu = [Buf(f"relu{i}") for i in range(2)]
    ost = sb("ost", [128, 512], F32); B_ost = Buf("ost")
    carryU = sb("carryU", [128, 4, 16], F32); B_cU = Buf("carryU")
    sqb = [sb(f"sq{i}", [128, NT_P], BF16) for i in range(2)]; B_sq = [Buf(f"sq{i}") for i in range(2)]
    rstd2 = sb("rstd2", [128, NT_P], F32); B_rstd2 = Buf("rstd2")

    R2 = 32 * NT_P * 2
    XO = R2 + 28672
    AR = XO + 10240
    arena = sb("arena", [128, AR // 2], BF16)

    def av(off, nbytes, dt, pat=None, **kw):
        v = arena[:, off // 2:(off + nbytes) // 2]
        if dt is F32:
            v = v.bitcast(F32)
        if pat:
            v = v.rearrange(pat, **kw)
        return v

    hidT = av(0, 32 * NT_P * 2, BF16, "p (k t) -> p k t", k=32); B_hid = Buf("hidT")
    yT = av(0, 8 * NT_P * 4, F32, "p (k t) -> p k t", k=8); B_yT = Buf("yT")
    WU = 16 + NT_P
    WH = 16 + 256
    U = av(R2, 4 * WU * 4, F32, "p (g t) -> p g t", g=4); B_U = Buf("U")
    SA = av(R2 + 4 * WU * 4, 4 * WH * 4, F32, "p (g t) -> p g t", g=4); B_SA = Buf("SA")
    SB = av(R2 + 4 * WU * 4 + 4 * WH * 4, 4 * WH * 4, F32, "p (g t) -> p g t", g=4); B_SB = Buf("SB")
    o_sb = R2 + 4 * WU * 4 + 8 * WH * 4
    sbias = av(o_sb, 8 * 256 * 4, F32, "p (u t) -> p u t", u=8); B_sbiasH = [Buf("sbias0"), Buf("sbias1")]; B_sbias = B_sbiasH[0]
    assert o_sb + 8192 <= XO
    kcT = av(XO, 16 * 128 * 2, BF16, "p (b t) -> p b t", b=16); B_kcT = Buf("kcT")
    vc = av(XO + 4096, 16 * 128 * 2, BF16, "p (b t) -> p b t", b=16); B_vc = Buf("vc")
    qs2 = av(XO + 8192, 16 * 32 * 2, BF16, "p (b t) -> p b t", b=16); B_qs2 = Buf("qs2")
    vnq = av(XO + 9216, 4 * 128 * 2, BF16, "p (i t) -> p i t", i=4); B_vnq = Buf("vnq")
    Kb = [av(R2 + i * 4096, 4096, BF16, "p (m t) -> p m t", m=2) for i in range(2)]; B_Kb = [Buf(f"Kb{i}") for i in range(2)]
    KbT = [av(R2 + 8192 + i * 4096, 4096, BF16, "p (c t) -> p c t", c=8) for i in range(2)]; B_KbT = [Buf(f"KbT{i}") for i in range(2)]
    Vb = [av(R2 + 16384 + i * 4096, 4096, BF16, "p (m t) -> p m t", m=2) for i in range(2)]; B_Vb = [Buf(f"Vb{i}") for i in range(2)]
    qpad = [av(R2 + 24576 + i * 2048, 2048, BF16, "p (c t) -> p c t", c=8) for i in range(2)]; B_qpad = [Buf(f"qpad{i}") for i in range(2)]
    memst = av(0, 8192, F32, "p (m t) -> p m t", m=2); B_memst = Buf("memst")
    mkst = av(8192, 8192, F32, "p (m t) -> p m t", m=2); B_mkst = Buf("mkst")
    alias(B_hid, B_yT)
    gX = [B_U, B_SA, B_SB] + B_sbiasH
    gY = B_Kb + B_KbT + B_Vb + B_qpad
    gZ = [B_memst, B_mkst]
    for ga, gb in ((gX, gY), ([B_hid, B_yT], gZ)):
        for a in ga:
            for b in gb:
                a.al.append(b)
                b.al.append(a)

    ps = [es.enter_context(nc.psum_tensor(f"ps{i}", [128, 512], F32)) for i in range(8)]
    B_ps = [Buf(f"ps{i}", excl=True) for i in range(8)]
    dctr = [0]

    def dbank():
        i = dctr[0] % 3
        dctr[0] += 1
        return ps[i], B_ps[i]
    PS_S = [3, 4]
    PS_T = [5, 6]
    PS_O = 7
    sctr = [0]
    tctr = [0]

    def sbank():
        i = PS_S[sctr[0] % 2]; sctr[0] += 1
        return ps[i], B_ps[i]

    def tbank():
        i = PS_T[tctr[0] % 2]; tctr[0] += 1
        return ps[i], B_ps[i]

    def scratch_fn(n):
        return nc.dram_tensor("wscratch", [n, 128, 8, 128], BF16, kind="Internal").ap()
    W = WStream(P, ring, sched, scratch_fn)
    if not dry:
        def std_src(wap):
            v = wap.rearrange("(k p) (m c) -> p m k c", p=128, c=128)
            return lambda m: [((lambda s: s), v[:, m])]
        W.src["ck"] = std_src(w_ck); W.src["cv"] = std_src(w_cv)
        W.src["cq"] = std_src(w_cq); W.src["co"] = std_src(w_co); W.src["up"] = std_src(w_up)
        vin_q = w_in[:, 0:512].rearrange("(k p) (kv g d) -> p g k kv d", p=128, kv=2, g=4, d=64)
        vin_r = w_in[:, 512:1280].rearrange("(k p) (m c) -> p m k c", p=128, c=128)

        def in_src(m):
            if m < 4:
                return [((lambda s: s[:, :, 0:64]), vin_q[:, m, :, 0, :]),
                        ((lambda s: s[:, :, 64:128]), vin_q[:, m, :, 1, :])]
            return [((lambda s: s), vin_r[:, m - 4])]
        W.src["in"] = in_src
        vo_a = w_out[0:512, :].rearrange("(kv g d) (m c) -> kv d m g c", kv=2, g=4, d=64, c=128)
        vo_p = w_out[512:1024, :].rearrange("(k p) (m c) -> p m k c", p=128, c=128)

        def out_src(m):
            return [((lambda s: s[0:64, 0:4, :]), vo_a[0, :, m]),
                    ((lambda s: s[64:128, 0:4, :]), vo_a[1, :, m]),
                    ((lambda s: s[:, 4:8, :]), vo_p[:, m])]
        W.src["out"] = out_src
        vdn = w_down.rearrange("(q k p) (m c) -> p m q k c", p=128, k=8, c=128)
        W.src["down"] = lambda mq: [((lambda s: s), vdn[:, mq // 4, mq % 4])]

    if not dry:
        P.op("pool", lambda e: e.memset(identf[:], 0.0), writes=[B_const])
        P.op("pool", lambda e: e.iota(identf[:], pattern=[[1, 128]], base=0, channel_multiplier=-1,
                                      allow_small_or_imprecise_dtypes=True), writes=[B_const])
        P.op("dve", lambda e: e.tensor_single_scalar(out=ident[:], in_=identf[:], scalar=0.0, op=ALU.is_equal),
             reads=[B_const], writes=[B_const])
        P.op("dve", lambda e: e.tensor_single_scalar(out=identf[:], in_=identf[:], scalar=0.0, op=ALU.is_equal),
             writes=[B_const])
        P.op("dve", lambda e: e.memset(ones[:], 1.0), writes=[B_const])
        for (dst, src) in ((gvec[:].rearrange("p a b -> p (a b)"), gvec_d), (pscale[:], pscale_d), (sinkp[:], sinkp_d),
                           (sinks[:], sinks_d), (invc[:].rearrange("p a b -> p (a b)"), invc_d),
                           (biass[:].rearrange("p a b -> p (a b)"), biass_d)):
            P.dma("sp", "cst", dst, src[:, :], writes=[B_const])
        P.dma("sp", "biasld", bias[:].rearrange("p a b -> p (a b)"), biasf_d[:, :], writes=[B_bias])
        P.dma("pool", "wpool", wpool[:], w_pool.rearrange("g c e -> c g e"), writes=[B_wpool])

    CONST = [B_const]

    class _Stop(Exception):
        pass

    def finish():
        for key, val in P.cnt.items():
            if key not in ("pe", "act", "dve", "pool"):
                P._wait("sp", (key, val))
        for e_ in ("pe", "act", "dve", "pool"):
            if P.cnt.get(e_, 0):
                P._wait("sp", (e_, P.cnt[e_]))
        return P, W
    if STAGE == -1:
        return finish()

    def run(gen):
        if gen is not None:
            for _ in gen:
                pass

    def advance(gen, n=1, until=None):
        if gen is None:
            return
        if until is not None:
            for v in gen:
                if v == until:
                    return
            return
        for _ in range(n):
            try:
                next(gen)
            except StopIteration:
                return

    def g_load_x(src_rows, ntiles, X):
        dst, dstB = xTs[X], B_xTs[X]
        for j in range(min(2, ntiles)):
            P.dma("sp", "x", xin[j % 2][:], src_rows(j), writes=[B_xin[j % 2]])
        for j in range(ntiles):
            xb, Bx = xin[j % 2], B_xin[j % 2]
            for hf in range(2):
                pb, Bp = dbank()
                pv = pb[:].rearrange("p (c t) -> p c t", c=4)
                P.mm([(lambda e, c=c, pv=pv, xb=xb, hf=hf: e.transpose(out=pv[:, c, :], in_=xb[:, (hf * 4 + c) * 128:(hf * 4 + c + 1) * 128],
                                                                       identity=identf[:])) for c in range(4)],
                     reads=[Bx] + CONST, writes=[Bp])
                P.op("act" if hf == 0 else "dve",
                     (lambda e, pv=pv, hf=hf, j=j: e.activation(out=dst[:, hf * 4:hf * 4 + 4, j * 128:(j + 1) * 128], in_=pv, func=AF.Copy))
                     if hf == 0 else
                     (lambda e, pv=pv, hf=hf, j=j: e.tensor_copy(out=dst[:, hf * 4:hf * 4 + 4, j * 128:(j + 1) * 128], in_=pv)),
                     reads=[Bp], writes=[dstB])
                if hf == 1 and j + 2 < ntiles:
                    P.dma("sp", "x", xb[:], src_rows(j + 2), writes=[Bx])
                yield

    def norm(src, Bsrc, gi, NT, dst, Bdst):
        P.op("act", lambda e: e.activation(out=hT[:, :, :NT], in_=src[:, :, :NT], func=AF.Square),
             reads=[Bsrc], writes=[B_hT])
        pb, Bp = dbank()
        P.mm([(lambda e, k=k: e.matmul(pb[:, :NT], lhsT=ones[:], rhs=hT[:, k, :NT], start=(k == 0), stop=(k == 7)))
              for k in range(8)], reads=[B_hT] + CONST, writes=[Bp])
        P.op("act", lambda e: e.activation(out=rstd[:, :NT], in_=pb[:, :NT], func=AF.Ln, scale=1.0 / D, bias=EPS),
             reads=[Bp], writes=[B_rstd])
        P.op("act", lambda e: e.activation(out=rstd[:, :NT], in_=rstd[:, :NT], func=AF.Exp, scale=-0.5),
             reads=[B_rstd], writes=[B_rstd])
        for k in range(8):
            P.op("dve", lambda e, k=k: e.scalar_tensor_tensor(out=dst[:, k, :NT], in0=src[:, k, :NT], scalar=gvec[:, gi, k:k + 1],
                                                              in1=rstd[:, :NT], op0=ALU.mult, op1=ALU.mult),
                 reads=[Bsrc, B_rstd] + CONST, writes=[Bdst])

    def g_dense(wname, units, NT, rhs_fn, Brhs, evac, kgroups=1):
        for m in units:
            pb, Bp = dbank()
            fns, Bs = [], []
            for q in range(kgroups):
                slot, Bslot = W.get(wname, m * kgroups + q if kgroups > 1 else m)
                Bs.append(Bslot)
                for k in range(8):
                    fns.append(lambda e, slot=slot, k=k, q=q, pb=pb: e.matmul(
                        pb[:, :NT], lhsT=slot[:, k, :], rhs=rhs_fn(q * 8 + k),
                        start=(q == 0 and k == 0), stop=(q == kgroups - 1 and k == 7)))
            P.mm(fns, reads=Bs + [Brhs], writes=[Bp])
            evac(m, pb, Bp)
            for _ in range(kgroups):
                yield

    def dense(*a, **kw):
        run(g_dense(*a, **kw))

    def resid_evac(NT, X, prep=None, scale2=False):
        xt, Bxt = xTs[X], B_xTs[X]

        def f(m, pb, Bp):
            if scale2:
                P.op("dve", lambda e: e.tensor_tensor(out=pb[:, :NT], in0=pb[:, :NT], in1=rstd2[:, :NT], op=ALU.mult),
                     reads=[Bp, B_rstd2], writes=[Bp])
            P.op("dve", lambda e: e.tensor_tensor(out=xt[:, m, :NT], in0=pb[:, :NT], in1=xt[:, m, :NT], op=ALU.add),
                 reads=[Bp], writes=[Bxt])
            if prep is not None:
                prep[0](m)
        return f

    def make_prep(X, NT, gidx, exp_scale, rdst, Brdst):
        xt, Bxt = xTs[X], B_xTs[X]
        sp_, Bsp = ps[PS_O], B_ps[PS_O]
        pend = []

        def emit_mm(k):
            P.mm([lambda e, k=k: e.matmul(sp_[:, :NT], lhsT=ones[:], rhs=sqb[k % 2][:, :NT], start=(k == 0), stop=(k == 7))],
                 reads=[B_sq[k % 2]] + CONST, writes=[Bsp])

        def after(m):
            while len(pend) >= 2:
                emit_mm(pend.pop(0))
            P.op("act", lambda e: e.activation(out=hT[:, m, :NT], in_=xt[:, m, :NT], func=AF.Copy, scale=gvec[:, gidx, m:m + 1]),
                 reads=[Bxt] + CONST, writes=[B_hT])
            P.op("act", lambda e: e.activation(out=sqb[m % 2][:, :NT], in_=xt[:, m, :NT], func=AF.Square),
                 reads=[Bxt], writes=[B_sq[m % 2]])
            pend.append(m)

        def flush():
            while pend:
                emit_mm(pend.pop(0))
            P.op("act", lambda e: e.activation(out=rdst[:, :NT], in_=sp_[:, :NT], func=AF.Ln, scale=1.0 / D, bias=EPS),
                 reads=[Bsp], writes=[Brdst])
            P.op("act", lambda e: e.activation(out=rdst[:, :NT], in_=rdst[:, :NT], func=AF.Exp, scale=exp_scale),
                 reads=[Brdst], writes=[Brdst])
        return after, flush

    def diag_T(u0, nu, nkc, p_src, Bp_src, BDg, dst4, dst_fn, Bdst, kw):
        items = [(u, kc) for u in range(u0, u0 + nu) for kc in range(nkc)]
        full = all(w == 128 for w in kw)
        for bi, i0 in enumerate(range(0, len(items), 4)):
            chunk = items[i0:i0 + 4]
            pb, Bp = tbank()
            pv = pb[:].rearrange("p (s t) -> p s t", s=4)
            P.mm([(lambda e, s=s, u=u, kc=kc, pv=pv: e.matmul(pv[0:kw[kc], s, :], lhsT=p_src[:, u, kc * 128:kc * 128 + kw[kc]],
                                                              rhs=Dg[:, u, :], start=True, stop=True))
                  for s, (u, kc) in enumerate(chunk)], reads=list(Bp_src) + list(BDg), writes=[Bp])
            if full:
                uu = chunk[0][0]
                if bi % 2 == 0:
                    P.op("act", lambda e, pb=pb, uu=uu: e.activation(out=dst4(uu), in_=pb[:, 0:512], func=AF.Copy), reads=[Bp], writes=list(Bdst))
                else:
                    P.op("dve", lambda e, pb=pb, uu=uu: e.tensor_copy(out=dst4(uu), in_=pb[:, 0:512]), reads=[Bp], writes=list(Bdst))
            else:
                for s, (u, kc) in enumerate(chunk):
                    P.op("act" if bi % 2 == 0 else "dve",
                         (lambda e, s=s, u=u, kc=kc, pv=pv: e.activation(out=dst_fn(u, kc), in_=pv[0:kw[kc], s, :], func=AF.Copy))
                         if bi % 2 == 0 else
                         (lambda e, s=s, u=u, kc=kc, pv=pv: e.tensor_copy(out=dst_fn(u, kc), in_=pv[0:kw[kc], s, :])),
                         reads=[Bp], writes=list(Bdst))
            yield

    def make_Dg(u0, nu, Bst, BDg):
        P.op("dve", lambda e: e.tensor_tensor(out=Dg[:, u0:u0 + nu, :], in0=ident[:].unsqueeze(1).to_broadcast([128, nu, 128]),
                                              in1=st[:, 56 + u0:56 + u0 + nu].unsqueeze(2).to_broadcast([128, nu, 128]), op=ALU.mult),
             reads=list(Bst) + CONST, writes=list(BDg))

    def win_softmax(u0, nu, width, sink_ap, hs):
        Bsb = [B_sbiasH[h] for h in hs]; Bst = [B_stH[h] for h in hs]
        Bpe = [B_pexpH[h] for h in hs]; BDg = [B_DgH[h] for h in hs]
        c = lambda base: slice(base + u0, base + u0 + nu)
        P.op("dve", lambda e: e.tensor_reduce(out=st[:, c(0)], in_=sbias[:, u0:u0 + nu, 0:width], axis=AX.X, op=ALU.max),
             reads=Bsb, writes=Bst)
        P.op("dve", lambda e: e.tensor_tensor(out=st[:, c(8)], in0=st[:, c(0)], in1=sink_ap, op=ALU.max),
             reads=Bst + CONST, writes=Bst)
        P.op("dve", lambda e: e.tensor_scalar(out=st[:, c(16)], in0=st[:, c(8)], scalar1=-1.0, scalar2=None, op0=ALU.mult),
             reads=Bst, writes=Bst)
        P.op("dve", lambda e: e.tensor_tensor(out=st[:, c(24)], in0=sink_ap, in1=st[:, c(16)], op=ALU.add),
             reads=Bst + CONST, writes=Bst)
        for u in range(u0, u0 + nu):
            P.op("act", lambda e, u=u: e.activation(out=pexp[:, u, 0:width], in_=sbias[:, u, 0:width], func=AF.Exp,
                                                    bias=st[:, 16 + u:17 + u], scale=1.0, accum_out=st[:, 32 + u:33 + u]),
                 reads=Bsb + Bst, writes=Bpe + Bst)
        P.op("act", lambda e: e.activation(out=st[:, c(40)], in_=st[:, c(24)], func=AF.Exp),
             reads=Bst, writes=Bst)
        yield
        P.op("dve", lambda e: e.tensor_tensor(out=st[:, c(48)], in0=st[:, c(32)], in1=st[:, c(40)], op=ALU.add),
             reads=Bst, writes=Bst)
        P.op("dve", lambda e: e.reciprocal(out=st[:, c(56)], in_=st[:, c(48)]), reads=Bst, writes=Bst)
        make_Dg(u0, nu, Bst, BDg)

    def cross_softmax(score_banks):
        for hp, (pb, Bp) in enumerate(score_banks):
            pv = pb[:].rearrange("p (h t) -> p h t", h=2)
            P.op("dve", lambda e, pv=pv, hp=hp: e.tensor_reduce(out=st[:, 2 * hp:2 * hp + 2], in_=pv, axis=AX.X, op=ALU.max),
                 reads=[Bp], writes=[B_st])
        P.op("dve", lambda e: e.tensor_scalar(out=st[:, 16:20], in0=st[:, 0:4], scalar1=-1.0 / 16.0, scalar2=None, op0=ALU.mult),
             reads=[B_st], writes=[B_st])
        for hp, (pb, Bp) in enumerate(score_banks):
            pv = pb[:].rearrange("p (h t) -> p h t", h=2)
            for hh in range(2):
                h = 2 * hp + hh
                P.op("act", lambda e, pv=pv, hh=hh, h=h: e.activation(out=pexp[:, h, :], in_=pv[:, hh, :], func=AF.Exp,
                                                                      bias=st[:, 16 + h:17 + h], scale=1.0 / 16.0,
                                                                      accum_out=st[:, 32 + h:33 + h]),
                     reads=[Bp, B_st], writes=[B_pexp, B_st])
        P.op("dve", lambda e: e.reciprocal(out=st[:, 56:60], in_=st[:, 32:36]), reads=[B_st], writes=[B_st])
        make_Dg(0, 4, [B_st], [B_Dg])

    def out_tok_major(srcs, Bsrcs, ncols_each, dst_dma):
        pb, Bp = dbank()
        pv = pb[:].rearrange("p (c t) -> p c t", c=4)
        n = len(srcs)
        P.mm([(lambda e, i=i: e.transpose(out=pv[:, i, :], in_=srcs[i], identity=identf[:])) for i in range(n)],
             reads=list(Bsrcs) + CONST, writes=[Bp])
        P.op("dve", lambda e: e.tensor_copy(out=ost[:, 0:n * 128], in_=pb[:, 0:n * 128]), reads=[Bp], writes=[B_ost])
        dst_dma()

    def mem_setup(gen):
        mx, Bmx = xTs[1], B_xTs[1]
        for t in range(2):
            P.dma("sp", "memld", memst[:, t, :], mem[t * 128:(t + 1) * 128, :], writes=[B_memst])
        for t in range(2):
            for hf in range(2):
                pb, Bp = dbank()
                pv = pb[:].rearrange("p (c t) -> p c t", c=4)
                P.mm([(lambda e, c=c, pv=pv, t=t, hf=hf: e.transpose(out=pv[:, c, :], in_=memst[:, t, (hf * 4 + c) * 128:(hf * 4 + c + 1) * 128],
                                                                     identity=identf[:])) for c in range(4)],
                     reads=[B_memst] + CONST, writes=[Bp])
                P.op("act", lambda e, pv=pv, hf=hf, t=t: e.activation(out=mx[:, hf * 4:hf * 4 + 4, t * 128:(t + 1) * 128], in_=pv, func=AF.Copy),
                     reads=[Bp], writes=[Bmx])
        norm(mx, Bmx, 2, 256, hT, B_hT)
        for (wn, is_k) in (("ck", True), ("cv", False)):
            for m in range(8):
                slot, Bslot = W.get(wn, m)
                if is_k:
                    pb, Bp = dbank()
                    P.mm([(lambda e, k=k, slot=slot, pb=pb: e.matmul(pb[:, :256], lhsT=slot[:, k, :], rhs=hT[:, k, :256],
                                                                    start=(k == 0), stop=(k == 7))) for k in range(8)],
                         reads=[Bslot, B_hT], writes=[Bp])
                    P.op("act", lambda e, m=m, pb=pb: e.activation(out=memkT[:, m, :], in_=pb[:, :256], func=AF.Copy),
                         reads=[Bp], writes=[B_memkT])
                pb, Bp = dbank()
                fns = []
                for t in range(2):
                    for k in range(8):
                        fns.append(lambda e, k=k, slot=slot, pb=pb, t=t: e.matmul(
                            pb[:, t * 128:(t + 1) * 128], lhsT=hT[:, k, t * 128:(t + 1) * 128], rhs=slot[:, k, :],
                            start=(k == 0), stop=(k == 7)))
                P.mm(fns, reads=[Bslot, B_hT], writes=[Bp])
                pv2 = pb[:, 0:256].rearrange("p (t c) -> p t c", t=2)
                P.op("dve", lambda e, pv2=pv2, m=m: e.tensor_copy(out=mkst[:, :, m * 128:(m + 1) * 128], in_=pv2),
                     reads=[Bp], writes=[B_mkst])
                if not is_k:
                    P.op("act", lambda e, pv2=pv2, m=m: e.activation(out=memv[:, :, m * 128:(m + 1) * 128], in_=pv2, func=AF.Copy),
                         reads=[Bp], writes=[B_memv])
                advance(gen, 3)
            for t in range(2):
                P.dma("pool", "memout", (memk_o if is_k else memv_o)[t * 128:(t + 1) * 128, :], mkst[:, t, :], reads=[B_mkst])

    def geom(kind):
        sample, halo = (kind == "S"), (kind == "H")
        NT = 128 if (sample or halo) else NT_P
        return sample, halo, NT, NT // 128

    def early(kind, gi, X):
        sample, halo, NT, ntl = geom(kind)
        xt, Bxt = xTs[X], B_xTs[X]
        if sample:
            yield from g_load_x(lambda j: xs[:, :], 1, X)
        elif halo:
            yield from g_load_x(lambda j: xp[0:128, :], 1, X)
        else:
            yield from g_load_x(lambda j: xp[128 + gi * NT_P + j * 128: 128 + gi * NT_P + (j + 1) * 128, :], ntl, X)
        yield "L"
        prep1 = make_prep(X, NT, 0, -0.5, rstd, B_rstd)
        for m in range(8):
            prep1[0](m)
            if m % 2 == 1:
                yield
        prep1[1]()
        yield
        last = (kind == "P" and gi == NG_P - 1)
        want32 = last or sample
        Us = U[:, :, 0:384].rearrange("p g (b c) -> p g b c", b=16)

        if sample:
            P.op("dve", lambda e: e.memset(U[:, :, 0:384], 0.0), writes=[B_U])
            for hb in range(2):
                P.dma("sp", "spld", xin[hb][0:120, 0:512], spool[hb * 8:(hb + 1) * 8].rearrange("b r f -> (b r) f"), writes=[B_xin[hb]])
                pb, Bp = dbank()
                pv = pb[:].rearrange("p (c t) -> p c t", c=4)
                P.mm([(lambda e, c=c, pv=pv, hb=hb: e.transpose(out=pv[:, c, 0:120], in_=xin[hb][0:120, c * 128:(c + 1) * 128],
                                                                identity=identf[0:120, 0:120])) for c in range(4)],
                     reads=[B_xin[hb]] + CONST, writes=[Bp])
                for c in range(4):
                    P.op("dve", lambda e, c=c, pv=pv, hb=hb: e.tensor_copy(
                        out=Us[:, c, hb * 8:(hb + 1) * 8, 1:16], in_=pv[:, c, 0:120].rearrange("p (b r) -> p b r", b=8)),
                        reads=[Bp], writes=[B_U])
            yield

        def in_evac(m, pb, Bp):
            rd = [Bp, B_rstd]
            if m < 4:
                P.op("dve", lambda e: e.tensor_tensor(out=qT[:, m, :NT], in0=pb[:, :NT], in1=rstd[:, :NT], op=ALU.mult), reads=rd, writes=[B_qT])
            elif m == 4 or m == 5:
                dst_, Bd_ = (kT[:, 128:128 + NT], B_kT) if m == 4 else (vT[:, :NT], B_vT)
                P.op("dve", lambda e: e.tensor_tensor(out=dst_, in0=pb[:, :NT], in1=rstd[:, :NT], op=ALU.mult), reads=rd, writes=[Bd_])
                if want32:
                    P.op("dve", lambda e: e.tensor_tensor(out=kv32[:, m - 4, :], in0=pb[:, NT - 128:NT], in1=rstd[:, NT - 128:NT], op=ALU.mult),
                         reads=rd, writes=[B_kv32])
            else:
                g = m - 6
                if sample:
                    P.op("dve", lambda e: e.tensor_tensor(out=Us[:, g, :, 16:24], in0=pb[:, 0:128].rearrange("p (b t) -> p b t", b=16),
                                                          in1=rstd[:, 0:128].rearrange("p (b t) -> p b t", b=16), op=ALU.mult),
                         reads=rd, writes=[B_U])
                else:
                    P.op("dve", lambda e: e.tensor_tensor(out=U[:, g, 16:16 + NT], in0=pb[:, :NT], in1=rstd[:, :NT], op=ALU.mult), reads=rd, writes=[B_U])
        yield from g_dense("in", list(range(4, 10)) if halo else list(range(10)), NT, lambda k: hT[:, k, :NT], B_hT, in_evac)
        yield "P1"

        if sample:
            for hb in range(2):
                P.dma("pool", "ckld", vc[:, hb * 8:(hb + 1) * 8, :], cv[hb * 8:(hb + 1) * 8].rearrange("b s f -> s b f"), writes=[B_vc])
            kst = pexp[:].rearrange("p u t -> p (u t)").rearrange("p (b f) -> p b f", b=16)
            for hb in range(2):
                P.dma("pool", "ckld", kst[:, hb * 8:(hb + 1) * 8, :], ck[hb * 8:(hb + 1) * 8].rearrange("b s f -> s b f"), writes=[B_pexpH[hb]])
            for hb in range(2):
                pb, Bp = tbank()
                pv = pb[:].bitcast(BF16).rearrange("p (b t) -> p b t", b=8)
                P.mm([(lambda e, b=b, pv=pv, hb=hb: e.transpose(out=pv[:, b, :], in_=kst[:, hb * 8 + b, :], identity=ident[:])) for b in range(8)],
                     reads=[B_pexpH[hb]] + CONST, writes=[Bp])
                P.op("dve", lambda e, pv=pv, hb=hb: e.tensor_copy(out=kcT[:, hb * 8:(hb + 1) * 8, :], in_=pv), reads=[Bp], writes=[B_kcT])
            yield

        for j0 in range(0, ntl, 4):
            pb, Bp = tbank()
            pv = pb[:].bitcast(BF16)[:, 0:512].rearrange("p (j t) -> p j t", j=4)
            P.mm([(lambda e, j=j, pv=pv: e.transpose(out=pv[:, j - j0, :], in_=vT[:, j * 128:(j + 1) * 128], identity=ident[:]))
                  for j in range(j0, min(ntl, j0 + 4))], reads=[B_vT] + CONST, writes=[Bp])
            nj = min(ntl, j0 + 4) - j0
            P.op("dve", lambda e, pv=pv, j0=j0, nj=nj: e.tensor_copy(out=vtok[:, 1 + j0:1 + j0 + nj, :], in_=pv[:, 0:nj, :]),
                 reads=[Bp], writes=[B_vtok])
        yield

        def carry():
            P.op("dve", lambda e: e.tensor_copy(out=kT[:, 0:128], in_=kT[:, NT:NT + 128]), reads=[B_kT], writes=[B_kT])
            P.op("dve", lambda e: e.tensor_copy(out=vtok[:, 0, :], in_=vtok[:, ntl, :]), reads=[B_vtok], writes=[B_vtok])

        if halo:
            carry()
            P.op("dve", lambda e: e.tensor_copy(out=carryU[:], in_=U[:, :, NT:NT + 16]), reads=[B_U], writes=[B_cU])
            return

        if want32:
            if last:
                def dd():
                    P.dma("pool", "kvout", wkp[:, :], ost[:, 0:128], reads=[B_ost])
                    P.dma("pool", "kvout", wvp[:, :], ost[:, 128:256], reads=[B_ost])
            else:
                def dd():
                    for t in range(8):
                        P.dma("pool", "kvout", wks[:, 120 + t, :], ost[t:128:8, 0:128], reads=[B_ost])
                        P.dma("pool", "kvout", wvs[:, 120 + t, :], ost[t:128:8, 128:256], reads=[B_ost])
                    P.dma("sp", "d2d_k", wks[:, 0:120, :], ck[:, 8:128, :])
                    P.dma("sp", "d2d_v", wvs[:, 0:120, :], cv[:, 8:128, :])
            out_tok_major([kv32[:, 0, :], kv32[:, 1, :]], [B_kv32], 128, dd)
            yield

        if not sample:
            P.op("dve", lambda e: e.tensor_copy(out=U[:, :, 0:16], in_=carryU[:]), reads=[B_cU], writes=[B_U])
        for hh in range(2):
            if sample:
                Wd, c0 = 192, hh * 192
            else:
                Wd, c0 = 16 + NT // 2, hh * (NT // 2)
            Uh = U[:, :, c0:c0 + Wd]
            P.op("dve", lambda e, Uh=Uh, Wd=Wd: e.tensor_tensor(out=SA[:, :, 1:Wd], in0=Uh[:, :, 1:Wd], in1=Uh[:, :, 0:Wd - 1], op=ALU.add),
                 reads=[B_U], writes=[B_SA])
            P.op("dve", lambda e, Wd=Wd: e.tensor_tensor(out=SB[:, 1:4, 3:Wd], in0=SA[:, 1:4, 3:Wd], in1=SA[:, 1:4, 1:Wd - 2], op=ALU.add),
                 reads=[B_SA], writes=[B_SB])
            yield
            P.op("dve", lambda e, Wd=Wd: e.tensor_tensor(out=SA[:, 2:4, 7:Wd], in0=SB[:, 2:4, 7:Wd], in1=SB[:, 2:4, 3:Wd - 4], op=ALU.add),
                 reads=[B_SB], writes=[B_SA])
            P.op("dve", lambda e, Wd=Wd: e.tensor_tensor(out=SB[:, 3, 15:Wd], in0=SA[:, 3, 15:Wd], in1=SA[:, 3, 7:Wd - 8], op=ALU.add),
                 reads=[B_SA], writes=[B_SB])
            yield
            for g in range(4):
                S_, BS_ = (SA, B_SA) if g % 2 == 0 else (SB, B_SB)
                if sample:
                    sv = S_[:, g, 0:192].rearrange("p (b c) -> p b c", b=8)[:, :, 16:24]
                    uv = Uh[:, g, :].rearrange("p (b c) -> p b c", b=8)[:, :, 16:24]
                    dv = dT[:, g, hh * 64:(hh + 1) * 64].rearrange("p (b t) -> p b t", b=8)
                else:
                    sv, uv, dv = S_[:, g, 16:Wd], Uh[:, g, 16:Wd], dT[:, g, c0:c0 + NT // 2]
                P.op("dve", lambda e, sv=sv, uv=uv, dv=dv, g=g: e.scalar_tensor_tensor(out=dv, in0=sv, scalar=1.0 / (2 << g), in1=uv,
                                                                                   op0=ALU.mult, op1=ALU.subtract),
                     reads=[BS_, B_U], writes=[B_dT])
                if kind == "P" and gi == 0 and hh == 0:
                    P.op("dve", lambda e, S_=S_, g=g: e.tensor_tensor(out=st[:, 0:16], in0=S_[:, g, 16:32], in1=invc[:, g, :], op=ALU.mult),
                         reads=[BS_] + CONST, writes=B_stH)
                    P.op("dve", lambda e, g=g: e.tensor_tensor(out=dT[:, g, 0:16], in0=st[:, 0:16], in1=U[:, g, 16:32], op=ALU.subtract),
                         reads=B_stH + [B_U], writes=[B_dT])
            yield
        if last:
            def dd2():
                P.dma("pool", "poolout", poolp[:, :], ost[113:128, :], reads=[B_ost])
            out_tok_major([U[:, g, 16 + NT - 128:16 + NT] for g in range(4)], [B_U], 128, dd2)
        if sample:
            for g in range(4):
                P.op("dve", lambda e, g=g: e.tensor_copy(out=SA[:, g, 0:128].rearrange("p (b t) -> p b t", b=16), in_=Us[:, g, :, 16:24]),
                     reads=[B_U, B_dT], writes=[B_SA])

            def dd3():
                for t in range(8):
                    P.dma("pool", "poolout", pools[:, 7 + t, :], ost[t:128:8, :], reads=[B_ost])
                P.dma("sp", "d2d_p", pools[:, 0:7, :], spool[:, 8:15, :])
            out_tok_major([SA[:, g, 0:128] for g in range(4)], [B_SA], 128, dd3)
        else:
            P.op("dve", lambda e: e.tensor_copy(out=carryU[:], in_=U[:, :, NT:NT + 16]), reads=[B_U], writes=[B_cU])
        yield
        for g in range(4):
            pb, Bp = dbank()
            P.mm([lambda e, g=g, pb=pb: e.matmul(pb[:, :NT], lhsT=wpool[:, g, :], rhs=dT[:, g, :NT], start=True, stop=True)],
                 reads=[B_wpool, B_dT], writes=[Bp])
            P.op("act", lambda e, g=g, pb=pb: e.activation(out=aoT[:, 4 + g, :NT], in_=pb[:, :NT], func=AF.Copy, scale=pscale[:, g:g + 1]),
                 reads=[Bp] + CONST, writes=[B_aoT])
            yield

        if not sample:
            for j in range(ntl):
                if gi == 0 and j == 1:
                    P.dma("sp", "biasld", bias[:].rearrange("p a b -> p (a b)"), biasg_d[:, :], writes=[B_bias])
                for gp in range(2):
                    bk = [sbank(), sbank()]
                    fns = []
                    for g in (2 * gp, 2 * gp + 1):
                        for kv in range(2):
                            fns.append(lambda e, kv=kv, g=g, j=j, bk=bk: e.matmul(
                                bk[kv][0][:, (g % 2) * 256:(g % 2 + 1) * 256], lhsT=qT[kv * 64:(kv + 1) * 64, g, j * 128:(j + 1) * 128],
                                rhs=kT[kv * 64:(kv + 1) * 64, j * 128:j * 128 + 256], start=True, stop=True))
                    P.mm(fns, reads=[B_qT, B_kT], writes=[bk[0][1], bk[1][1]])
                    for kv in range(2):
                        u0 = 4 * gp + kv
                        P.op("dve", lambda e, kv=kv, u0=u0, bk=bk: e.scalar_tensor_tensor(
                            out=sbias[:, u0:u0 + 3:2, :], in0=bk[kv][0][:].rearrange("p (g t) -> p g t", g=2), scalar=0.125,
                            in1=bias[:, u0:u0 + 3:2, :], op0=ALU.mult, op1=ALU.add),
                            reads=[bk[kv][1], B_bias], writes=[B_sbiasH[gp]])
                    yield
                sm = [win_softmax(4 * gp, 4, 256, sinkp[:, 4 * gp:4 * gp + 4], [gp]) for gp in range(2)]
                next(sm[0])
                yield
                next(sm[1])
                yield
                yield
                run(sm[0])
                yield
                run(sm[1])
                yield
                for gp in range(2):
                    yield from diag_T(4 * gp, 4, 2, pexp, [B_pexpH[gp]], [B_DgH[gp]],
                                      lambda u0: pT[:, u0:u0 + 2, :, :].rearrange("p u k t -> p (u k t)"), None, [B_pTH[gp]], [128, 128])
                yield
                po, Bpo = ps[PS_O], B_ps[PS_O]
                pov = po[:].rearrange("p (g t) -> p g t", g=4)
                fns = []
                for g in range(4):
                    for kv in range(2):
                        for kc in range(2):
                            fns.append(lambda e, g=g, kv=kv, kc=kc, j=j: e.matmul(
                                pov[kv * 64:(kv + 1) * 64, g, :], lhsT=vtok[:, j + kc, kv * 64:(kv + 1) * 64], rhs=pT[:, 2 * g + kv, kc, :],
                                start=(kc == 0), stop=(kc == 1)))
                P.mm(fns, reads=[B_vtok] + B_pTH, writes=[Bpo])
                P.op("act", lambda e, j=j: e.activation(out=aoT[:, 0:4, j * 128:(j + 1) * 128], in_=pov, func=AF.Copy),
                     reads=[Bpo], writes=[B_aoT])
                yield
            carry()
        else:
            P.op("dve", lambda e: e.tensor_copy(out=qs2[:].rearrange("p b (g t) -> p b g t", g=4),
                                                in_=qT[:, :, 0:128].rearrange("p g (b t) -> p b g t", b=16)), reads=[B_qT], writes=[B_qs2])
            pb, Bp = dbank()
            pvb = pb[:].bitcast(BF16)
            P.mm([(lambda e, i=i, pvb=pvb: e.transpose(out=pvb[0:32, i * 128:(i + 1) * 128], in_=vT[:, i * 32:(i + 1) * 32], identity=ident[:]))
                  for i in range(4)], reads=[B_vT] + CONST, writes=[Bp])
            P.op("dve", lambda e, pvb=pvb: e.tensor_copy(out=vnq[0:32, :, :], in_=pvb[0:32, 0:512].rearrange("p (i t) -> p i t", i=4)),
                 reads=[Bp], writes=[B_vnq])
            yield
            for i in range(4):
                bk = [sbank(), sbank()]
                fns = []
                for kv in range(2):
                    pvk = bk[kv][0]
                    for jq in range(4):
                        b = 4 * i + jq
                        fns.append(lambda e, kv=kv, jq=jq, b=b, pvk=pvk: e.matmul(
                            pvk[32 * jq:32 * jq + 32, 0:128], lhsT=qs2[kv * 64:(kv + 1) * 64, b, :], rhs=kcT[kv * 64:(kv + 1) * 64, b, :],
                            start=True, stop=True, tile_position=(kv * 64, 32 * jq)))
                    fns.append(lambda e, kv=kv, i=i, pvk=pvk: e.matmul(
                        pvk[:, 128:160], lhsT=qs2[kv * 64:(kv + 1) * 64, 4 * i:4 * i + 4, :].rearrange("p b t -> p (b t)"),
                        rhs=kT[kv * 64:(kv + 1) * 64, 128 + 32 * i:128 + 32 * i + 32], start=True, stop=True))
                P.mm(fns, reads=[B_qs2, B_kcT, B_kT], writes=[bk[0][1], bk[1][1]])
                for kv in range(2):
                    P.op("dve", lambda e, kv=kv, i=i, bk=bk: e.scalar_tensor_tensor(
                        out=sbias[:, 2 * i + kv, 0:160], in0=bk[kv][0][:, 0:160], scalar=0.125,
                        in1=biass[:, kv, :], op0=ALU.mult, op1=ALU.add),
                        reads=[bk[kv][1]] + CONST, writes=[B_sbiasH[i // 2]])
                yield
            smx = win_softmax(0, 8, 160, sinks[:, 0:8], [0, 1])
            next(smx)
            yield
            yield
            run(smx)
            yield
            yield from diag_T(0, 8, 2, pexp, B_pexpH, B_DgH, None, lambda u, kc: pT[0:(128 if kc == 0 else 32), u, kc, :], B_pTH, [128, 32])
            for i in range(4):
                pb, Bp = dbank()
                fns = []
                for kv in range(2):
                    u = 2 * i + kv
                    for jq in range(4):
                        b = 4 * i + jq
                        fns.append(lambda e, kv=kv, jq=jq, b=b, u=u, pb=pb: e.matmul(
                            pb[kv * 64:(kv + 1) * 64, 32 * jq:32 * jq + 32], lhsT=vc[:, b, kv * 64:(kv + 1) * 64], rhs=pT[:, u, 0, 32 * jq:32 * jq + 32],
                            start=(jq == 0), stop=False, skip_group_check=True))
                for kv in range(2):
                    u = 2 * i + kv
                    fns.append(lambda e, kv=kv, u=u, i=i, pb=pb: e.matmul(
                        pb[kv * 64:(kv + 1) * 64, 0:128], lhsT=vnq[0:32, i, kv * 64:(kv + 1) * 64], rhs=pT[0:32, u, 1, :],
                        start=False, stop=True, skip_group_check=True))
                P.mm(fns, reads=[B_vc, B_vnq] + B_pTH, writes=[Bp])
                P.op("act", lambda e, pb=pb, i=i: e.activation(
                    out=aoT[:, 0:4, 32 * i:32 * i + 32].rearrange("p g (j t) -> p j g t", j=4),
                    in_=pb[:, 0:128].rearrange("p (j g t) -> p j g t", j=4, g=4), func=AF.Copy),
                    reads=[Bp], writes=[B_aoT])
                yield

    def late_pre(kind, gi, X, gen, tgen=None):
        sample, halo, NT, ntl = geom(kind)
        xt, Bxt = xTs[X], B_xTs[X]
        loaded = [gen is None]
        xfree = [tgen is None]

        def step_tail(n):
            for _ in range(n):
                if tgen is not None and next(tgen, "END") == "XDONE":
                    xfree[0] = True

        def step_load():
            if not loaded[0] and xfree[0]:
                if next(gen, "L") == "L":
                    loaded[0] = True
        p1done = [gen is None]

        def step_p1(n):
            for _ in range(n):
                if not p1done[0]:
                    if next(gen, "P1") == "P1":
                        p1done[0] = True
        prep2 = make_prep(X, NT, 1, -0.5, rstd, B_rstd)
        for _ in g_dense("out", list(range(8)), NT, lambda k: aoT[:, k, :NT], B_aoT, resid_evac(NT, X, prep=prep2)):
            step_load()
            step_tail(2)
        prep2[1]()
        qcT, B_qcT = aoT, B_aoT
        ocT, B_ocT = hidT, B_hid

        def cq_evac(m, pb, Bp):
            P.op("dve", lambda e: e.tensor_tensor(out=qcT[:, m, :NT], in0=pb[:, :NT], in1=rstd[:, :NT], op=ALU.mult),
                 reads=[Bp, B_rstd], writes=[B_qcT])
        for _ in g_dense("cq", list(range(8)), NT, lambda k: hT[:, k, :NT], B_hT, cq_evac):
            step_load()
            step_tail(2)
        while tgen is not None and not xfree[0]:
            step_tail(1)
        while not loaded[0]:
            step_load()
        run(tgen)

        if not sample:
            for j in range(ntl):
                banks = [sbank(), sbank()]
                for hp, (pb, Bp) in enumerate(banks):
                    pv = pb[:].rearrange("p (h t) -> p h t", h=2)
                    fns = []
                    for hh in range(2):
                        h = 2 * hp + hh
                        for dc in range(2):
                            fns.append(lambda e, pv=pv, hh=hh, h=h, dc=dc, j=j: e.matmul(
                                pv[:, hh, :], lhsT=qcT[:, 2 * h + dc, j * 128:(j + 1) * 128], rhs=memkT[:, 2 * h + dc, :],
                                start=(dc == 0), stop=(dc == 1)))
                    P.mm(fns, reads=[B_qcT, B_memkT], writes=[Bp])
                cross_softmax(banks)
                step_p1(2)
                run(diag_T(0, 4, 2, pexp, [B_pexp], [B_Dg], lambda h0: pTc[:, h0:h0 + 2, :, :].rearrange("p u k t -> p (u k t)"), None, [B_pTc], [128, 128]))
                for half in range(2):
                    pb, Bp = dbank()
                    pv = pb[:].rearrange("p (c t) -> p c t", c=4)
                    fns = []
                    for cc in range(4):
                        c = half * 4 + cc
                        h = c // 2
                        for mc in range(2):
                            fns.append(lambda e, pv=pv, cc=cc, c=c, h=h, mc=mc: e.matmul(
                                pv[:, cc, :], lhsT=memv[:, mc, c * 128:(c + 1) * 128], rhs=pTc[:, h, mc, :], start=(mc == 0), stop=(mc == 1)))
                    P.mm(fns, reads=[B_memv, B_pTc], writes=[Bp])
                    P.op("act" if half == 0 else "dve",
                         (lambda e, pv=pv, half=half, j=j: e.activation(out=ocT[:, half * 4:half * 4 + 4, j * 128:(j + 1) * 128], in_=pv, func=AF.Copy))
                         if half == 0 else
                         (lambda e, pv=pv, half=half, j=j: e.tensor_copy(out=ocT[:, half * 4:half * 4 + 4, j * 128:(j + 1) * 128], in_=pv)),
                         reads=[Bp], writes=[B_ocT])
                step_p1(2)
        else:
            banks = [sbank(), sbank()]
            for i in range(2):
                P.op("dve", lambda e, i=i: e.memset(qpad[i][:], 0.0), writes=[B_qpad[i]])

            def xslot(c0):
                return xTs[0][:, :, c0:c0 + 128].bitcast(BF16)
            KX = [xslot(128), xslot(256)]
            VX = [xslot(384)]
            B_KX = [Buf("KX0"), Buf("KX1")]
            B_VX = [Buf("VX0")]
            for bb_ in B_KX + B_VX:
                bb_.al.append(B_xTs[0])
                B_xTs[0].al.append(bb_)
            NK, NV = 4, 3

            def kslot(b):
                i = b % NK
                return (("a", Kb[i], B_Kb[i]) if i < 2 else ("x", KX[i - 2], B_KX[i - 2]))

            def vslot(b):
                i = b % NV
                return (("a", Vb[i], B_Vb[i]) if i < 2 else ("x", VX[i - 2], B_VX[i - 2]))

            def ld(slot, src):
                kind_, ap_, B_ = slot
                if kind_ == "a":
                    P.dma("pool", "c", ap_[:], src.rearrange("(m p) f -> p m f", p=128), writes=[B_])
                else:
                    for m_ in range(2):
                        P.dma("pool", "c", ap_[:, m_ * 4:(m_ + 1) * 4, :],
                              src[m_ * 128:(m_ + 1) * 128, :].rearrange("p (q j) -> p q j", j=256), writes=[B_])

            def tile_of(slot, mt, c):
                kind_, ap_, B_ = slot
                if kind_ == "a":
                    return ap_[:, mt, c * 128:(c + 1) * 128]
                return ap_[:, mt * 4 + c // 2, (c % 2) * 128:(c % 2 + 1) * 128]

            for b in range(3):
                ld(kslot(b), cmk[b])
            for b in range(3):
                ld(vslot(b), cmv[b])
            def Tstage(b):
                s2 = b % 2
                ks = kslot(b)
                if b + 3 < 16:
                    ld(kslot(b + 3), cmk[b + 3])
                for mt in range(2):
                    pb, Bp = tbank()
                    pv = pb[:].bitcast(BF16).rearrange("p (c t) -> p c t", c=8)
                    P.mm([(lambda e, c=c, pv=pv, mt=mt, ks=ks: e.transpose(out=pv[:, c, :], in_=tile_of(ks, mt, c), identity=ident[:]))
                          for c in range(8)], reads=[ks[2]] + CONST, writes=[Bp])
                    P.op("act" if mt == 0 else "dve",
                         (lambda e, pv=pv, mt=mt, s2=s2: e.activation(out=KbT[s2][:, :, mt * 128:(mt + 1) * 128], in_=pv, func=AF.Copy))
                         if mt == 0 else
                         (lambda e, pv=pv, mt=mt, s2=s2: e.tensor_copy(out=KbT[s2][:, :, mt * 128:(mt + 1) * 128], in_=pv)),
                         reads=[Bp], writes=[B_KbT[s2]])
                if b >= 2:
                    P.op("dve", lambda e, s2=s2, b=b: e.memset(qpad[s2][:, :, (b - 2) * 8:(b - 1) * 8], 0.0), writes=[B_qpad[s2]])
                P.op("dve", lambda e, s2=s2, b=b: e.tensor_copy(out=qpad[s2][:, :, b * 8:(b + 1) * 8], in_=qcT[:, :, b * 8:(b + 1) * 8]),
                     reads=[B_qcT], writes=[B_qpad[s2]])

            def Sstage(b):
                s2 = b % 2
                for hp, (pb, Bp) in enumerate(banks):
                    pv = pb[:].rearrange("p (h t) -> p h t", h=2)
                    fns = []
                    for hh in range(2):
                        h = 2 * hp + hh
                        for dc in range(2):
                            fns.append(lambda e, pv=pv, hh=hh, h=h, dc=dc, s2=s2, b=b: e.matmul(
                                pv[:, hh, :], lhsT=qpad[s2][:, 2 * h + dc, :], rhs=KbT[s2][:, 2 * h + dc, :],
                                start=(b == 0 and hh == 0 and dc == 0), stop=(b == 15 and dc == 1), skip_group_check=True))
                    P.mm(fns, reads=[B_qpad[s2], B_KbT[s2]], writes=[Bp])

            Tstage(0)
            for b in range(16):
                if b + 1 < 16:
                    Tstage(b + 1)
                Sstage(b)
            cross_softmax(banks)
            run(diag_T(0, 4, 2, pexp, [B_pexp], [B_Dg], lambda h0: pTc[:, h0:h0 + 2, :, :].rearrange("p u k t -> p (u k t)"), None, [B_pTc], [128, 128]))
            pbs = [dbank(), dbank()]
            for b in range(16):
                vs = vslot(b)
                fns = []
                for c in range(8):
                    pv = pbs[c // 4][0][:].rearrange("p (c t) -> p c t", c=4)
                    h = c // 2
                    for mc in range(2):
                        fns.append(lambda e, pv=pv, c=c, h=h, mc=mc, vs=vs, b=b: e.matmul(
                            pv[:, c % 4, b * 8:(b + 1) * 8], lhsT=tile_of(vs, mc, c), rhs=pTc[:, h, mc, b * 8:(b + 1) * 8],
                            start=(mc == 0), stop=(mc == 1), skip_group_check=True))
                P.mm(fns, reads=[vs[2], B_pTc], writes=[pbs[0][1], pbs[1][1]])
                if b + 3 < 16:
                    ld(vslot(b + 3), cmv[b + 3])
            for half in range(2):
                pv = pbs[half][0][:].rearrange("p (c t) -> p c t", c=4)
                P.op("act" if half == 0 else "dve",
                     (lambda e, pv=pv, half=half: e.activation(out=ocT[:, half * 4:half * 4 + 4, 0:128], in_=pv, func=AF.Copy))
                     if half == 0 else
                     (lambda e, pv=pv, half=half: e.tensor_copy(out=ocT[:, half * 4:half * 4 + 4, 0:128], in_=pv)),
                     reads=[pbs[half][1]], writes=[B_ocT])
        while not p1done[0]:
            step_p1(1)
        prep3 = make_prep(X, NT, 3, -1.0, rstd2, B_rstd2)
        dense("co", list(range(8)), NT, lambda k: ocT[:, k, :NT], B_ocT, resid_evac(NT, X, prep=prep3))
        prep3[1]()

    def ffn(kind, gi, X, gen):
        sample, halo, NT, ntl = geom(kind)
        xt, Bxt = xTs[X], B_xTs[X]
        uctr = [0]

        def up_evac(m, pb, Bp):
            r, Br = relu_t[uctr[0] % 2], B_relu[uctr[0] % 2]
            uctr[0] += 1
            P.op("act", lambda e: e.activation(out=r[:, :NT], in_=pb[:, :NT], func=AF.Relu), reads=[Bp], writes=[Br])
            P.op("pool", lambda e: e.tensor_tensor(out=hidT[:, m, :NT], in0=r[:, :NT], in1=r[:, :NT], op=ALU.mult), reads=[Br], writes=[B_hid])
        for _ in g_dense("up", list(range(32)), NT, lambda k: hT[:, k, :NT], B_hT, up_evac):
            advance(gen, 1)
        for _ in g_dense("down", list(range(8)), NT, lambda k: hidT[:, k, :NT], B_hid, resid_evac(NT, X, scale2=True), kgroups=4):
            advance(gen, 1)

    def g_tail(kind, gi, X):
        sample, halo, NT, ntl = geom(kind)
        xt, Bxt = xTs[X], B_xTs[X]
        sq = hidT[:, 16:24, :]
        P.op("act", lambda e: e.activation(out=sq[:, :, :NT], in_=xt[:, :, :NT], func=AF.Square), reads=[Bxt], writes=[B_hid])
        yield
        pb0, Bp0 = dbank()
        P.mm([(lambda e, k=k: e.matmul(pb0[:, :NT], lhsT=ones[:], rhs=sq[:, k, :NT], start=(k == 0), stop=(k == 7)))
              for k in range(8)], reads=[B_hid] + CONST, writes=[Bp0])
        P.op("act", lambda e: e.activation(out=rstd2[:, :NT], in_=pb0[:, :NT], func=AF.Ln, scale=1.0 / D, bias=EPS),
             reads=[Bp0], writes=[B_rstd2])
        P.op("act", lambda e: e.activation(out=rstd2[:, :NT], in_=rstd2[:, :NT], func=AF.Exp, scale=-0.5),
             reads=[B_rstd2], writes=[B_rstd2])
        yield
        for k in range(8):
            P.op("dve", lambda e, k=k: e.scalar_tensor_tensor(out=yT[:, k, :NT], in0=xt[:, k, :NT], scalar=gvec[:, 4, k:k + 1],
                                                              in1=rstd2[:, :NT], op0=ALU.mult, op1=ALU.mult),
                 reads=[Bxt, B_rstd2] + CONST, writes=[B_yT])
            if k % 2 == 1 and k < 7:
                yield
        yield "XDONE"
        for j in range(ntl):
            ys_, Bys = yst[1], B_yst[1]
            for hf in range(2):
                pb, Bp = dbank()
                pv = pb[:].rearrange("p (c t) -> p c t", c=4)
                P.mm([(lambda e, c=c, pv=pv, hf=hf, j=j: e.transpose(out=pv[:, c, :], in_=yT[:, hf * 4 + c, j * 128:(j + 1) * 128], identity=identf[:]))
                      for c in range(4)], reads=[B_yT] + CONST, writes=[Bp])
                P.op("act" if hf == 0 else "dve",
                     (lambda e, pb=pb, hf=hf, ys_=ys_: e.activation(out=ys_[:, hf * 512:(hf + 1) * 512], in_=pb[:, :], func=AF.Copy))
                     if hf == 0 else
                     (lambda e, pb=pb, hf=hf, ys_=ys_: e.tensor_copy(out=ys_[:, hf * 512:(hf + 1) * 512], in_=pb[:, :])),
                     reads=[Bp], writes=[Bys])
                yield
            if sample:
                P.dma("pool", "y", ys[:, :], ys_[:], reads=[Bys])
            else:
                r0 = gi * NT_P + j * 128
                P.dma("pool", "y", yp[r0:r0 + 128, :], ys_[:], reads=[Bys])

    order = [("P", g) for g in range(NG_P)] + [("S", 0)]
    order = order[:max(0, min(len(order), STAGE))] if STAGE < 50 else order
    run(early("H", 0, 0))
    gen0 = early(order[0][0], order[0][1], 0) if order else None
    advance(gen0, until="P1")
    mem_setup(gen0)
    run(gen0)
    tgen = None
    for idx, (kind, gi) in enumerate(order):
        X = idx % 2
        nxt = order[idx + 1] if idx + 1 < len(order) else None
        gen = early(nxt[0], nxt[1], (idx + 1) % 2) if nxt else None
        late_pre(kind, gi, X, gen, tgen)
        ffn(kind, gi, X, gen)
        run(gen)
        tgen = g_tail(kind, gi, X)
        if not TAIL_OVERLAP:
            run(tgen)
            tgen = None
    run(tgen)

    return finish()


_CACHE = {}


def _build_nc():
    if "nc" in _CACHE:
        return _CACHE["nc"]
    nc0 = bass.Bass("TRN2", target_bir_lowering=False)
    with ExitStack() as es0:
        _, W0 = build_sched(nc0, es0)
    sched = W0.rec
    nc = bass.Bass("TRN2", target_bir_lowering=False)
    with ExitStack() as es:
        P, W = build(nc, es, False, sched)
        assert W.i == len(sched), (W.i, len(sched))
        block = es.enter_context(nc.Block())
        P.flush(block)
    _CACHE["nc"] = nc
    return nc


def build_sched(nc0, es0):
    return build(nc0, es0, False, None)


def _tables(half):
    slopes = 2.0 ** (-(np.arange(8) + 1.0))
    q = np.arange(128)[:, None]
    c = np.arange(256)[None, :]
    dist = q - c + 128
    valid = (dist >= 0) & (dist <= 128)
    biasg = np.empty((128, 8, 256), np.float32)
    for g in range(4):
        for kv in range(2):
            h = kv * 4 + g
            biasg[:, 2 * g + kv, :] = np.where(valid, -slopes[h] * dist, -1e30)
    biasf = biasg.copy()
    if half == 0:
        biasf[:, :, 0:128] = -1e30
    biass = np.full((128, 2, 160), -1e30, np.float32)
    for j in range(4):
        for g in range(4):
            for t in range(8):
                r = j * 32 + g * 8 + t
                for kv in range(2):
                    h = kv * 4 + g
                    cc = np.arange(128)
                    d = t + 128 - cc
                    biass[r, kv, 0:128] = np.where(cc >= t, -slopes[h] * d, -1e30)
                    for tp in range(t + 1):
                        biass[r, kv, 128 + j * 8 + tp] = -slopes[h] * (t - tp)
    invc = np.empty((128, 4, 16), np.float32)
    for g in range(4):
        w = 2 << g
        for p in range(16):
            invc[:, g, p] = 1.0 / (min(p + 1, w) if half == 0 else w)
    return biasg.reshape(128, -1), biasf.reshape(128, -1), biass.reshape(128, -1), invc.reshape(128, -1)


def _prep(x_prompt, x_sample, cache_win_k, cache_win_v, state_pool, cache_mem_k, cache_mem_v,
          mem_prompt, g_mix, w_in, attn_sinks, w_pool, pool_scale, w_out, g_cross, g_mem,
          w_cq, w_ck, w_cv, w_co, g_ffn, w_up, w_down, g_final):
    f = lambda a: np.ascontiguousarray(np.asarray(a, dtype=np.float32))
    x_prompt, x_sample = f(x_prompt), f(x_sample)
    shared = dict(w_in=f(w_in)[0], w_pool=f(w_pool)[0], w_out=f(w_out)[0], w_cq=f(w_cq)[0], w_ck=f(w_ck)[0],
                  w_cv=f(w_cv)[0], w_co=f(w_co)[0], w_up=f(w_up)[0], w_down=f(w_down)[0])
    gs = np.stack([f(g_mix)[0], f(g_cross)[0], f(g_mem)[0], f(g_ffn)[0], f(g_final)], 0)
    shared["gvec"] = np.ascontiguousarray(gs.reshape(5, 8, 128).transpose(2, 0, 1).reshape(128, 40))
    shared["pscale"] = np.ascontiguousarray(f(pool_scale)[0].reshape(4, 128).T)
    sk = f(attn_sinks)[0]
    sinkp = np.empty((128, 8), np.float32)
    for g in range(4):
        for kv in range(2):
            sinkp[:, 2 * g + kv] = sk[kv * 4 + g]
    shared["sinkp"] = sinkp
    sinks = np.empty((128, 8), np.float32)
    for r in range(128):
        g = (r % 32) // 8
        for i in range(4):
            sinks[r, 2 * i] = sk[g]
            sinks[r, 2 * i + 1] = sk[4 + g]
    shared["sinks"] = sinks
    ckf, cvf, spf = f(cache_win_k)[0], f(cache_win_v)[0], f(state_pool)[0]
    cmkf, cmvf, memf = f(cache_mem_k)[0], f(cache_mem_v)[0], f(mem_prompt)
    in_maps = []
    for c in range(NCORES):
        b, half = c // 2, c % 2
        s0 = half * SEQ_CORE
        xp = np.zeros((128 + SEQ_CORE, D), np.float32)
        xp[128:] = x_prompt[b, s0:s0 + SEQ_CORE]
        if half == 1:
            xp[:128] = x_prompt[b, s0 - 128:s0]
        biasg, biasf, biass, invc = _tables(half)
        sl = slice(16 * c, 16 * c + 16)
        m = dict(shared)
        m.update(xp=xp, xs=np.ascontiguousarray(x_sample[sl].reshape(128, D)), mem=np.ascontiguousarray(memf[b]),
                 ck=np.ascontiguousarray(ckf[sl].reshape(16, 128, 128)), cv=np.ascontiguousarray(cvf[sl].reshape(16, 128, 128)),
                 spool=np.ascontiguousarray(spf[sl]), cmk=np.ascontiguousarray(cmkf[sl].reshape(16, 256, D)),
                 cmv=np.ascontiguousarray(cmvf[sl].reshape(16, 256, D)),
                 biasg=biasg, biasf=biasf, biass=biass, invc=invc)
        in_maps.append(m)
    return in_maps


def kernel(**inputs):
    in_maps = _prep(**inputs)
    nc = _build_nc()
    res = run_bass_kernel_spmd(nc, in_maps, core_ids=list(range(NCORES))).results
    return _assemble(res)


def _assemble(res):
    B, S = 4, 4096
    y_prompt = np.empty((B, S, D), np.float32)
    y_sample = np.empty((128, 8, D), np.float32)
    wk_p = np.empty((1, B, 128, 2, 64), np.float32); wv_p = np.empty_like(wk_p)
    pool_p = np.empty((1, B, 15, 512), np.float32)
    mk_p = np.empty((1, B, 256, 4, 256), np.float32); mv_p = np.empty_like(mk_p)
    wk_s = np.empty((1, 128, 128, 2, 64), np.float32); wv_s = np.empty_like(wk_s)
    pool_s = np.empty((1, 128, 15, 512), np.float32)
    for c in range(NCORES):
        r = res[c]
        b, half = c // 2, c % 2
        y_prompt[b, half * SEQ_CORE:(half + 1) * SEQ_CORE] = r["yp"]
        sl = slice(16 * c, 16 * c + 16)
        y_sample[sl] = r["ys"].reshape(16, 8, D)
        if half == 1:
            wk_p[0, b] = r["wkp"].reshape(128, 2, 64)
            wv_p[0, b] = r["wvp"].reshape(128, 2, 64)
            pool_p[0, b] = r["poolp"]
        else:
            mk_p[0, b] = r["memk"].reshape(256, 4, 256)
            mv_p[0, b] = r["memv"].reshape(256, 4, 256)
        wk_s[0, sl] = r["wks"].reshape(16, 128, 2, 64)
        wv_s[0, sl] = r["wvs"].reshape(16, 128, 2, 64)
        pool_s[0, sl] = r["pools"]
    return (y_prompt, y_sample, wk_p, wv_p, pool_p, mk_p, mv_p, wk_s, wv_s, pool_s)
```

```python
import numpy as np
from contextlib import ExitStack
import concourse.bass as bass
import concourse.mybir as mybir
from concourse.bass_utils import run_bass_kernel_spmd

F32 = mybir.dt.float32
BF16 = mybir.dt.bfloat16
ALU = mybir.AluOpType
AF = mybir.ActivationFunctionType
AX = mybir.AxisListType

NCORES = 8
STAGE = 99
TAIL_OVERLAP = True
D = 1024
SEQ_CORE = 2048
NT_P = 512
NG_P = SEQ_CORE // NT_P
RING = 11
EPS = 1e-5


class Buf:
    __slots__ = ("name", "w", "r", "al", "excl")

    def __init__(self, name, excl=False):
        self.name = name
        self.w = None
        self.r = {}
        self.al = []
        self.excl = excl


def alias(*bufs):
    for a in bufs:
        for b in bufs:
            if a is not b and b not in a.al:
                a.al.append(b)


class Prog:
    def __init__(self, nc, es, dry):
        self.nc, self.es, self.dry = nc, es, dry
        self.q = {e: [] for e in ("pe", "act", "dve", "pool", "sp")}
        self.cnt, self.sems = {}, {}
        self.waited = {e: {} for e in self.q}

    def sem(self, key):
        if key not in self.sems:
            self.sems[key] = None if self.dry else self.es.enter_context(self.nc.semaphore(key))
            self.cnt[key] = 0

    def _wait(self, eng, tok):
        if tok is None:
            return
        key, val = tok
        if self.waited[eng].get(key, 0) >= val:
            return
        self.waited[eng][key] = val
        self.q[eng].append(("w", key, val))

    def _deps(self, eng, reads, writes, extra):
        for b in reads:
            self._wait(eng, b.w)
            if b.excl:
                for k, v in b.r.items():
                    if k != eng:
                        self._wait(eng, (k, v))
        for b in writes:
            for bb in [b] + b.al:
                self._wait(eng, bb.w)
                for k, v in bb.r.items():
                    self._wait(eng, (k, v))
        for t in extra:
            self._wait(eng, t)

    def _commit(self, tok, reads, writes):
        k, v = tok
        for b in reads:
            b.r[k] = max(b.r.get(k, 0), v)
        for b in writes:
            b.w = tok
            b.r = {}

    def op(self, eng, fn, reads=(), writes=(), extra=()):
        self._deps(eng, reads, writes, extra)
        self.sem(eng)
        self.cnt[eng] += 1
        tok = (eng, self.cnt[eng])
        self.q[eng].append(("i", fn, eng, 1))
        self._commit(tok, reads, writes)
        return tok

    def mm(self, fns, reads=(), writes=(), extra=()):
        self._deps("pe", reads, writes, extra)
        for f in fns[:-1]:
            self.q["pe"].append(("i", f, None, 0))
        self.sem("pe")
        self.cnt["pe"] += 1
        tok = ("pe", self.cnt["pe"])
        self.q["pe"].append(("i", fns[-1], "pe", 1))
        self._commit(tok, reads, writes)
        return tok

    def dma(self, qeng, semkey, out, in_, reads=(), writes=(), extra=()):
        if writes:
            semkey = "dw" + qeng[0] + "_" + writes[0].name
        elif reads:
            semkey = "dr" + qeng[0] + "_" + reads[0].name
        for b in reads:
            self._wait(qeng, b.w)
        for b in writes:
            for bb in [b] + b.al:
                if not (bb.w is not None and bb.w[0] == semkey):
                    self._wait(qeng, bb.w)
                for k, v in bb.r.items():
                    self._wait(qeng, (k, v))
        for t in extra:
            self._wait(qeng, t)
        self.sem(semkey)
        self.cnt[semkey] += 16
        tok = (semkey, self.cnt[semkey])
        self.q[qeng].append(("i", (lambda e, o=out, i=in_: e.dma_start(out=o, in_=i)), semkey, 16))
        self._commit(tok, reads, writes)
        return tok

    def flush(self, block):
        def run(name):
            def f(e):
                for it in self.q[name]:
                    if it[0] == "w":
                        e.wait_ge(self.sems[it[1]], it[2])
                    else:
                        ins = it[1](e)
                        if it[3]:
                            ins.then_inc(self.sems[it[2]], it[3])
            return f
        block.tensor(run("pe"))
        block.scalar(run("act"))
        block.vector(run("dve"))
        block.gpsimd(run("pool"))
        block.sync(run("sp"))


class WStream:
    def __init__(self, P, ring_ap, sched, scratch_fn=None):
        self.P, self.ring = P, ring_ap
        self.sched = sched
        self.rec = []
        self.i = 0
        self.issued = 0
        self.slots = [Buf(f"ws{i}") for i in range(RING)]
        self.src = {}
        self.uidx, self.wtok = {}, {}
        self.scratch = None
        if sched is not None:
            cnt = {}
            for k in sched:
                cnt[k] = cnt.get(k, 0) + 1
            for k in sched:
                if cnt[k] > 1 and k not in self.uidx:
                    self.uidx[k] = len(self.uidx)
            if scratch_fn is not None and self.uidx:
                self.scratch = scratch_fn(len(self.uidx))

    def _issue(self, j):
        key = self.sched[j]
        name, m = key
        s = j % RING
        if self.scratch is not None and key in self.wtok:
            self.P.dma("sp", f"ws{s}", self.ring[:, s], self.scratch[self.uidx[key]], writes=[self.slots[s]],
                       extra=[self.wtok[key]])
            return
        for (dst_fn, src_ap) in self.src[name](m):
            self.P.dma("pool", f"ws{s}", dst_fn(self.ring[:, s]), src_ap, writes=[self.slots[s]])
        if self.scratch is not None and key in self.uidx:
            self.wtok[key] = self.P.dma("sp", f"sw{s}", self.scratch[self.uidx[key]], self.ring[:, s], reads=[self.slots[s]])

    def get(self, name, m):
        if self.sched is None:
            self.rec.append((name, m))
            return self.ring[:, 0], self.slots[0]
        assert self.sched[self.i] == (name, m), (self.i, self.sched[self.i], name, m)
        while self.issued < min(len(self.sched), self.i + RING - 3):
            self._issue(self.issued)
            self.issued += 1
        s = self.i % RING
        self.i += 1
        return self.ring[:, s], self.slots[s]


def build(nc, es, dry, sched):
    P = Prog(nc, es, dry)

    def din(name, shape):
        return nc.dram_tensor(name, list(shape), F32, kind="ExternalInput").ap()

    def dout(name, shape):
        return nc.dram_tensor(name, list(shape), F32, kind="ExternalOutput").ap()

    if not dry:
        xp = din("xp", [128 + SEQ_CORE, D]); xs = din("xs", [128, D]); mem = din("mem", [256, D])
        ck = din("ck", [16, 128, 128]); cv = din("cv", [16, 128, 128]); spool = din("spool", [16, 15, 512])
        cmk = din("cmk", [16, 256, D]); cmv = din("cmv", [16, 256, D])
        w_in = din("w_in", [D, 1280]); w_pool = din("w_pool", [4, 128, 128]); w_out = din("w_out", [D, D])
        w_cq = din("w_cq", [D, D]); w_ck = din("w_ck", [D, D]); w_cv = din("w_cv", [D, D]); w_co = din("w_co", [D, D])
        w_up = din("w_up", [D, 4 * D]); w_down = din("w_down", [4 * D, D])
        gvec_d = din("gvec", [128, 40]); pscale_d = din("pscale", [128, 4])
        sinkp_d = din("sinkp", [128, 8]); sinks_d = din("sinks", [128, 8])
        biasg_d = din("biasg", [128, 8 * 256]); biasf_d = din("biasf", [128, 8 * 256]); biass_d = din("biass", [128, 2 * 160])
        invc_d = din("invc", [128, 64])
        yp = dout("yp", [SEQ_CORE, D]); ys = dout("ys", [128, D])
        wkp = dout("wkp", [128, 128]); wvp = dout("wvp", [128, 128]); poolp = dout("poolp", [15, 512])
        memk_o = dout("memk", [256, D]); memv_o = dout("memv", [256, D])
        wks = dout("wks", [16, 128, 128]); wvs = dout("wvs", [16, 128, 128]); pools = dout("pools", [16, 15, 512])

    def sb(name, shape, dt):
        return es.enter_context(nc.sbuf_tensor("sb_" + name, list(shape), dt))

    xTs = [sb(f"xT{i}", [128, 8, NT_P], F32) for i in range(2)]; B_xTs = [Buf(f"xT{i}") for i in range(2)]
    xT, B_xT = xTs[0], B_xTs[0]
    hT = sb("hT", [128, 8, NT_P], BF16); B_hT = Buf("hT")
    rstd = sb("rstd", [128, NT_P], F32); B_rstd = Buf("rstd")
    aoT = sb("aoT", [128, 8, NT_P], BF16); B_aoT = Buf("aoT")
    qT = sb("qT", [128, 4, NT_P], BF16); B_qT = Buf("qT")
    kT = sb("kT", [128, 128 + NT_P], BF16); B_kT = Buf("kT")
    vT = sb("vT", [128, NT_P], BF16); B_vT = Buf("vT")
    vtok = sb("vtok", [128, 5, 128], BF16); B_vtok = Buf("vtok")
    kv32 = sb("kv32", [128, 2, 128], F32); B_kv32 = Buf("kv32")
    dT = sb("dT", [128, 4, NT_P], BF16); B_dT = Buf("dT")
    pexp = sb("pexp", [128, 8, 256], BF16); B_pexpH = [Buf("pexp0"), Buf("pexp1")]; B_pexp = B_pexpH[0]
    pT = sb("pT", [128, 8, 2, 128], BF16); B_pTH = [Buf("pT0"), Buf("pT1")]; B_pT = B_pTH[0]
    Dg = sb("Dg", [128, 8, 128], BF16); B_DgH = [Buf("Dg0"), Buf("Dg1")]; B_Dg = B_DgH[0]
    pTc = sb("pTc", [128, 4, 2, 128], BF16); B_pTc = Buf("pTc")
    bias = sb("bias", [128, 8, 256], F32); B_bias = Buf("bias")
    biass = sb("biass", [128, 2, 160], F32); B_biass = Buf("biass")
    memkT = sb("memkT", [128, 8, 256], BF16); B_memkT = Buf("memkT")
    memv = sb("memv", [128, 2, D], BF16); B_memv = Buf("memv")
    ring = sb("ring", [128, RING, 8, 128], BF16)
    xin = [sb(f"xin{i}", [128, D], F32) for i in range(2)]; B_xin = [Buf(f"xin{i}") for i in range(2)]
    yst1 = sb("yst1", [128, D], F32); yst, B_yst = [None, yst1], [None, Buf("yst1")]
    ident = sb("ident", [128, 128], BF16); identf = sb("identf", [128, 128], F32); B_const = Buf("const")
    ones = sb("ones", [128, 128], BF16)
    gvec = sb("gvec", [128, 5, 8], F32); pscale = sb("pscale", [128, 4], F32)
    sinkp = sb("sinkp", [128, 8], F32); sinks = sb("sinks", [128, 8], F32)
    invc = sb("invc", [128, 4, 16], F32)
    wpool = sb("wpool", [128, 4, 128], BF16); B_wpool = Buf("wpool")
    st = sb("st", [128, 64], F32); B_stH = [Buf("st0"), Buf("st1")]; B_st = B_stH[0]
    relu_t = [sb(f"relu{i}", [128, NT_P], BF16) for i in range(2)]; B_relu = [Buf(f"relu{i}") for i in range(2)]
    ost = sb("ost", [128, 512], F32); B_ost = Buf("ost")
    carryU = sb("carryU", [128, 4, 16], F32); B_cU = Buf("carryU")
    sqb = [sb(f"sq{i}", [128, NT_P], BF16) for i in range(2)]; B_sq = [Buf(f"sq{i}") for i in range(2)]
    rstd2 = sb("rstd2", [128, NT_P], F32); B_rstd2 = Buf("rstd2")

    R2 = 32 * NT_P * 2
    XO = R2 + 28672
    AR = XO + 10240
    arena = sb("arena", [128, AR // 2], BF16)

    def av(off, nbytes, dt, pat=None, **kw):
        v = arena[:, off // 2:(off + nbytes) // 2]
        if dt is F32:
            v = v.bitcast(F32)
        if pat:
            v = v.rearrange(pat, **kw)
        return v

    hidT = av(0, 32 * NT_P * 2, BF16, "p (k t) -> p k t", k=32); B_hid = Buf("hidT")
    yT = av(0, 8 * NT_P * 4, F32, "p (k t) -> p k t", k=8); B_yT = Buf("yT")
    WU = 16 + NT_P
    WH = 16 + 256
    U = av(R2, 4 * WU * 4, F32, "p (g t) -> p g t", g=4); B_U = Buf("U")
    SA = av(R2 + 4 * WU * 4, 4 * WH * 4, F32, "p (g t) -> p g t", g=4); B_SA = Buf("SA")
    SB = av(R2 + 4 * WU * 4 + 4 * WH * 4, 4 * WH * 4, F32, "p (g t) -> p g t", g=4); B_SB = Buf("SB")
    o_sb = R2 + 4 * WU * 4 + 8 * WH * 4
    sbias = av(o_sb, 8 * 256 * 4, F32, "p (u t) -> p u t", u=8); B_sbiasH = [Buf("sbias0"), Buf("sbias1")]; B_sbias = B_sbiasH[0]
    assert o_sb + 8192 <= XO
    kcT = av(XO, 16 * 128 * 2, BF16, "p (b t) -> p b t", b=16); B_kcT = Buf("kcT")
    vc = av(XO + 4096, 16 * 128 * 2, BF16, "p (b t) -> p b t", b=16); B_vc = Buf("vc")
    qs2 = av(XO + 8192, 16 * 32 * 2, BF16, "p (b t) -> p b t", b=16); B_qs2 = Buf("qs2")
    vnq = av(XO + 9216, 4 * 128 * 2, BF16, "p (i t) -> p i t", i=4); B_vnq = Buf("vnq")
    Kb = [av(R2 + i * 4096, 4096, BF16, "p (m t) -> p m t", m=2) for i in range(2)]; B_Kb = [Buf(f"Kb{i}") for i in range(2)]
    KbT = [av(R2 + 8192 + i * 4096, 4096, BF16, "p (c t) -> p c t", c=8) for i in range(2)]; B_KbT = [Buf(f"KbT{i}") for i in range(2)]
    Vb = [av(R2 + 16384 + i * 4096, 4096, BF16, "p (m t) -> p m t", m=2) for i in range(2)]; B_Vb = [Buf(f"Vb{i}") for i in range(2)]
    qpad = [av(R2 + 24576 + i * 2048, 2048, BF16, "p (c t) -> p c t", c=8) for i in range(2)]; B_qpad = [Buf(f"qpad{i}") for i in range(2)]
    memst = av(0, 8192, F32, "p (m t) -> p m t", m=2); B_memst = Buf("memst")
    mkst = av(8192, 8192, F32, "p (m t) -> p m t", m=2); B_mkst = Buf("mkst")
    alias(B_hid, B_yT)
    gX = [B_U, B_SA, B_SB] + B_sbiasH
    gY = B_Kb + B_KbT + B_Vb + B_qpad
    gZ = [B_memst, B_mkst]
    for ga, gb in ((gX, gY), ([B_hid, B_yT], gZ)):
        for a in ga:
            for b in gb:
                a.al.append(b)
                b.al.append(a)

    ps = [es.enter_context(nc.psum_tensor(f"ps{i}", [128, 512], F32)) for i in range(8)]
    B_ps = [Buf(f"ps{i}", excl=True) for i in range(8)]
    dctr = [0]

    def dbank():
        i = dctr[0] % 3
        dctr[0] += 1
        return ps[i], B_ps[i]
    PS_S = [3, 4]
    PS_T = [5, 6]
    PS_O = 7
    sctr = [0]
    tctr = [0]

    def sbank():
        i = PS_S[sctr[0] % 2]; sctr[0] += 1
        return ps[i], B_ps[i]

    def tbank():
        i = PS_T[tctr[0] % 2]; tctr[0] += 1
        return ps[i], B_ps[i]

    def scratch_fn(n):
        return nc.dram_tensor("wscratch", [n, 128, 8, 128], BF16, kind="Internal").ap()
    W = WStream(P, ring, sched, scratch_fn)
    if not dry:
        def std_src(wap):
            v = wap.rearrange("(k p) (m c) -> p m k c", p=128, c=128)
            return lambda m: [((lambda s: s), v[:, m])]
        W.src["ck"] = std_src(w_ck); W.src["cv"] = std_src(w_cv)
        W.src["cq"] = std_src(w_cq); W.src["co"] = std_src(w_co); W.src["up"] = std_src(w_up)
        vin_q = w_in[:, 0:512].rearrange("(k p) (kv g d) -> p g k kv d", p=128, kv=2, g=4, d=64)
        vin_r = w_in[:, 512:1280].rearrange("(k p) (m c) -> p m k c", p=128, c=128)

        def in_src(m):
            if m < 4:
                return [((lambda s: s[:, :, 0:64]), vin_q[:, m, :, 0, :]),
                        ((lambda s: s[:, :, 64:128]), vin_q[:, m, :, 1, :])]
            return [((lambda s: s), vin_r[:, m - 4])]
        W.src["in"] = in_src
        vo_a = w_out[0:512, :].rearrange("(kv g d) (m c) -> kv d m g c", kv=2, g=4, d=64, c=128)
        vo_p = w_out[512:1024, :].rearrange("(k p) (m c) -> p m k c", p=128, c=128)

        def out_src(m):
            return [((lambda s: s[0:64, 0:4, :]), vo_a[0, :, m]),
                    ((lambda s: s[64:128, 0:4, :]), vo_a[1, :, m]),
                    ((lambda s: s[:, 4:8, :]), vo_p[:, m])]
        W.src["out"] = out_src
        vdn = w_down.rearrange("(q k p) (m c) -> p m q k c", p=128, k=8, c=128)
        W.src["down"] = lambda mq: [((lambda s: s), vdn[:, mq // 4, mq % 4])]

    if not dry:
        P.op("pool", lambda e: e.memset(identf[:], 0.0), writes=[B_const])
        P.op("pool", lambda e: e.iota(identf[:], pattern=[[1, 128]], base=0, channel_multiplier=-1,
                                      allow_small_or_imprecise_dtypes=True), writes=[B_const])
        P.op("dve", lambda e: e.tensor_single_scalar(out=ident[:], in_=identf[:], scalar=0.0, op=ALU.is_equal),
             reads=[B_const], writes=[B_const])
        P.op("dve", lambda e: e.tensor_single_scalar(out=identf[:], in_=identf[:], scalar=0.0, op=ALU.is_equal),
             writes=[B_const])
        P.op("dve", lambda e: e.memset(ones[:], 1.0), writes=[B_const])
        for (dst, src) in ((gvec[:].rearrange("p a b -> p (a b)"), gvec_d), (pscale[:], pscale_d), (sinkp[:], sinkp_d),
                           (sinks[:], sinks_d), (invc[:].rearrange("p a b -> p (a b)"), invc_d),
                           (biass[:].rearrange("p a b -> p (a b)"), biass_d)):
            P.dma("sp", "cst", dst, src[:, :], writes=[B_const])
        P.dma("sp", "biasld", bias[:].rearrange("p a b -> p (a b)"), biasf_d[:, :], writes=[B_bias])
        P.dma("pool", "wpool", wpool[:], w_pool.rearrange("g c e -> c g e"), writes=[B_wpool])

    CONST = [B_const]

    class _Stop(Exception):
        pass

    def finish():
        for key, val in P.cnt.items():
            if key not in ("pe", "act", "dve", "pool"):
                P._wait("sp", (key, val))
        for e_ in ("pe", "act", "dve", "pool"):
            if P.cnt.get(e_, 0):
                P._wait("sp", (e_, P.cnt[e_]))
        return P, W
    if STAGE == -1:
        return finish()

    def run(gen):
        if gen is not None:
            for _ in gen:
                pass

    def advance(gen, n=1, until=None):
        if gen is None:
            return
        if until is not None:
            for v in gen:
                if v == until:
                    return
            return
        for _ in range(n):
            try:
                next(gen)
            except StopIteration:
                return

    def g_load_x(src_rows, ntiles, X):
        dst, dstB = xTs[X], B_xTs[X]
        for j in range(min(2, ntiles)):
            P.dma("sp", "x", xin[j % 2][:], src_rows(j), writes=[B_xin[j % 2]])
        for j in range(ntiles):
            xb, Bx = xin[j % 2], B_xin[j % 2]
            for hf in range(2):
                pb, Bp = dbank()
                pv = pb[:].rearrange("p (c t) -> p c t", c=4)
                P.mm([(lambda e, c=c, pv=pv, xb=xb, hf=hf: e.transpose(out=pv[:, c, :], in_=xb[:, (hf * 4 + c) * 128:(hf * 4 + c + 1) * 128],
                                                                       identity=identf[:])) for c in range(4)],
                     reads=[Bx] + CONST, writes=[Bp])
                P.op("act" if hf == 0 else "dve",
                     (lambda e, pv=pv, hf=hf, j=j: e.activation(out=dst[:, hf * 4:hf * 4 + 4, j * 128:(j + 1) * 128], in_=pv, func=AF.Copy))
                     if hf == 0 else
                     (lambda e, pv=pv, hf=hf, j=j: e.tensor_copy(out=dst[:, hf * 4:hf * 4 + 4, j * 128:(j + 1) * 128], in_=pv)),
                     reads=[Bp], writes=[dstB])
                if hf == 1 and j + 2 < ntiles:
                    P.dma("sp", "x", xb[:], src_rows(j + 2), writes=[Bx])
                yield

    def norm(src, Bsrc, gi, NT, dst, Bdst):
        P.op("act", lambda e: e.activation(out=hT[:, :, :NT], in_=src[:, :, :NT], func=AF.Square),
             reads=[Bsrc], writes=[B_hT])
        pb, Bp = dbank()
        P.mm([(lambda e, k=k: e.matmul(pb[:, :NT], lhsT=ones[:], rhs=hT[:, k, :NT], start=(k == 0), stop=(k == 7)))
              for k in range(8)], reads=[B_hT] + CONST, writes=[Bp])
        P.op("act", lambda e: e.activation(out=rstd[:, :NT], in_=pb[:, :NT], func=AF.Ln, scale=1.0 / D, bias=EPS),
             reads=[Bp], writes=[B_rstd])
        P.op("act", lambda e: e.activation(out=rstd[:, :NT], in_=rstd[:, :NT], func=AF.Exp, scale=-0.5),
             reads=[B_rstd], writes=[B_rstd])
        for k in range(8):
            P.op("dve", lambda e, k=k: e.scalar_tensor_tensor(out=dst[:, k, :NT], in0=src[:, k, :NT], scalar=gvec[:, gi, k:k + 1],
                                                              in1=rstd[:, :NT], op0=ALU.mult, op1=ALU.mult),
                 reads=[Bsrc, B_rstd] + CONST, writes=[Bdst])

    def g_dense(wname, units, NT, rhs_fn, Brhs, evac, kgroups=1):
        for m in units:
            pb, Bp = dbank()
            fns, Bs = [], []
            for q in range(kgroups):
                slot, Bslot = W.get(wname, m * kgroups + q if kgroups > 1 else m)
                Bs.append(Bslot)
                for k in range(8):
                    fns.append(lambda e, slot=slot, k=k, q=q, pb=pb: e.matmul(
                        pb[:, :NT], lhsT=slot[:, k, :], rhs=rhs_fn(q * 8 + k),
                        start=(q == 0 and k == 0), stop=(q == kgroups - 1 and k == 7)))
            P.mm(fns, reads=Bs + [Brhs], writes=[Bp])
            evac(m, pb, Bp)
            for _ in range(kgroups):
                yield

    def dense(*a, **kw):
        run(g_dense(*a, **kw))

    def resid_evac(NT, X, prep=None, scale2=False):
        xt, Bxt = xTs[X], B_xTs[X]

        def f(m, pb, Bp):
            if scale2:
                P.op("dve", lambda e: e.tensor_tensor(out=pb[:, :NT], in0=pb[:, :NT], in1=rstd2[:, :NT], op=ALU.mult),
                     reads=[Bp, B_rstd2], writes=[Bp])
            P.op("dve", lambda e: e.tensor_tensor(out=xt[:, m, :NT], in0=pb[:, :NT], in1=xt[:, m, :NT], op=ALU.add),
                 reads=[Bp], writes=[Bxt])
            if prep is not None:
                prep[0](m)
        return f

    def make_prep(X, NT, gidx, exp_scale, rdst, Brdst):
        xt, Bxt = xTs[X], B_xTs[X]
        sp_, Bsp = ps[PS_O], B_ps[PS_O]
        pend = []

        def emit_mm(k):
            P.mm([lambda e, k=k: e.matmul(sp_[:, :NT], lhsT=ones[:], rhs=sqb[k % 2][:, :NT], start=(k == 0), stop=(k == 7))],
                 reads=[B_sq[k % 2]] + CONST, writes=[Bsp])

        def after(m):
            while len(pend) >= 2:
                emit_mm(pend.pop(0))
            P.op("act", lambda e: e.activation(out=hT[:, m, :NT], in_=xt[:, m, :NT], func=AF.Copy, scale=gvec[:, gidx, m:m + 1]),
                 reads=[Bxt] + CONST, writes=[B_hT])
            P.op("act", lambda e: e.activation(out=sqb[m % 2][:, :NT], in_=xt[:, m, :NT], func=AF.Square),
                 reads=[Bxt], writes=[B_sq[m % 2]])
            pend.append(m)

        def flush():
            while pend:
                emit_mm(pend.pop(0))
            P.op("act", lambda e: e.activation(out=rdst[:, :NT], in_=sp_[:, :NT], func=AF.Ln, scale=1.0 / D, bias=EPS),
                 reads=[Bsp], writes=[Brdst])
            P.op("act", lambda e: e.activation(out=rdst[:, :NT], in_=rdst[:, :NT], func=AF.Exp, scale=exp_scale),
                 reads=[Brdst], writes=[Brdst])
        return after, flush

    def diag_T(u0, nu, nkc, p_src, Bp_src, BDg, dst4, dst_fn, Bdst, kw):
        items = [(u, kc) for u in range(u0, u0 + nu) for kc in range(nkc)]
        full = all(w == 128 for w in kw)
        for bi, i0 in enumerate(range(0, len(items), 4)):
            chunk = items[i0:i0 + 4]
            pb, Bp = tbank()
            pv = pb[:].rearrange("p (s t) -> p s t", s=4)
            P.mm([(lambda e, s=s, u=u, kc=kc, pv=pv: e.matmul(pv[0:kw[kc], s, :], lhsT=p_src[:, u, kc * 128:kc * 128 + kw[kc]],
                                                              rhs=Dg[:, u, :], start=True, stop=True))
                  for s, (u, kc) in enumerate(chunk)], reads=list(Bp_src) + list(BDg), writes=[Bp])
            if full:
                uu = chunk[0][0]
                if bi % 2 == 0:
                    P.op("act", lambda e, pb=pb, uu=uu: e.activation(out=dst4(uu), in_=pb[:, 0:512], func=AF.Copy), reads=[Bp], writes=list(Bdst))
                else:
                    P.op("dve", lambda e, pb=pb, uu=uu: e.tensor_copy(out=dst4(uu), in_=pb[:, 0:512]), reads=[Bp], writes=list(Bdst))
            else:
                for s, (u, kc) in enumerate(chunk):
                    P.op("act" if bi % 2 == 0 else "dve",
                         (lambda e, s=s, u=u, kc=kc, pv=pv: e.activation(out=dst_fn(u, kc), in_=pv[0:kw[kc], s, :], func=AF.Copy))
                         if bi % 2 == 0 else
                         (lambda e, s=s, u=u, kc=kc, pv=pv: e.tensor_copy(out=dst_fn(u, kc), in_=pv[0:kw[kc], s, :])),
                         reads=[Bp], writes=list(Bdst))
            yield

    def make_Dg(u0, nu, Bst, BDg):
        P.op("dve", lambda e: e.tensor_tensor(out=Dg[:, u0:u0 + nu, :], in0=ident[:].unsqueeze(1).to_broadcast([128, nu, 128]),
                                              in1=st[:, 56 + u0:56 + u0 + nu].unsqueeze(2).to_broadcast([128, nu, 128]), op=ALU.mult),
             reads=list(Bst) + CONST, writes=list(BDg))

    def win_softmax(u0, nu, width, sink_ap, hs):
        Bsb = [B_sbiasH[h] for h in hs]; Bst = [B_stH[h] for h in hs]
        Bpe = [B_pexpH[h] for h in hs]; BDg = [B_DgH[h] for h in hs]
        c = lambda base: slice(base + u0, base + u0 + nu)
        P.op("dve", lambda e: e.tensor_reduce(out=st[:, c(0)], in_=sbias[:, u0:u0 + nu, 0:width], axis=AX.X, op=ALU.max),
             reads=Bsb, writes=Bst)
        P.op("dve", lambda e: e.tensor_tensor(out=st[:, c(8)], in0=st[:, c(0)], in1=sink_ap, op=ALU.max),
             reads=Bst + CONST, writes=Bst)
        P.op("dve", lambda e: e.tensor_scalar(out=st[:, c(16)], in0=st[:, c(8)], scalar1=-1.0, scalar2=None, op0=ALU.mult),
             reads=Bst, writes=Bst)
        P.op("dve", lambda e: e.tensor_tensor(out=st[:, c(24)], in0=sink_ap, in1=st[:, c(16)], op=ALU.add),
             reads=Bst + CONST, writes=Bst)
        for u in range(u0, u0 + nu):
            P.op("act", lambda e, u=u: e.activation(out=pexp[:, u, 0:width], in_=sbias[:, u, 0:width], func=AF.Exp,
                                                    bias=st[:, 16 + u:17 + u], scale=1.0, accum_out=st[:, 32 + u:33 + u]),
                 reads=Bsb + Bst, writes=Bpe + Bst)
        P.op("act", lambda e: e.activation(out=st[:, c(40)], in_=st[:, c(24)], func=AF.Exp),
             reads=Bst, writes=Bst)
        yield
        P.op("dve", lambda e: e.tensor_tensor(out=st[:, c(48)], in0=st[:, c(32)], in1=st[:, c(40)], op=ALU.add),
             reads=Bst, writes=Bst)
        P.op("dve", lambda e: e.reciprocal(out=st[:, c(56)], in_=st[:, c(48)]), reads=Bst, writes=Bst)
        make_Dg(u0, nu, Bst, BDg)

    def cross_softmax(score_banks):
        for hp, (pb, Bp) in enumerate(score_banks):
            pv = pb[:].rearrange("p (h t) -> p h t", h=2)
            P.op("dve", lambda e, pv=pv, hp=hp: e.tensor_reduce(out=st[:, 2 * hp:2 * hp + 2], in_=pv, axis=AX.X, op=ALU.max),
                 reads=[Bp], writes=[B_st])
        P.op("dve", lambda e: e.tensor_scalar(out=st[:, 16:20], in0=st[:, 0:4], scalar1=-1.0 / 16.0, scalar2=None, op0=ALU.mult),
             reads=[B_st], writes=[B_st])
        for hp, (pb, Bp) in enumerate(score_banks):
            pv = pb[:].rearrange("p (h t) -> p h t", h=2)
            for hh in range(2):
                h = 2 * hp + hh
                P.op("act", lambda e, pv=pv, hh=hh, h=h: e.activation(out=pexp[:, h, :], in_=pv[:, hh, :], func=AF.Exp,
                                                                      bias=st[:, 16 + h:17 + h], scale=1.0 / 16.0,
                                                                      accum_out=st[:, 32 + h:33 + h]),
                     reads=[Bp, B_st], writes=[B_pexp, B_st])
        P.op("dve", lambda e: e.reciprocal(out=st[:, 56:60], in_=st[:, 32:36]), reads=[B_st], writes=[B_st])
        make_Dg(0, 4, [B_st], [B_Dg])

    def out_tok_major(srcs, Bsrcs, ncols_each, dst_dma):
        pb, Bp = dbank()
        pv = pb[:].rearrange("p (c t) -> p c t", c=4)
        n = len(srcs)
        P.mm([(lambda e, i=i: e.transpose(out=pv[:, i, :], in_=srcs[i], identity=identf[:])) for i in range(n)],
             reads=list(Bsrcs) + CONST, writes=[Bp])
        P.op("dve", lambda e: e.tensor_copy(out=ost[:, 0:n * 128], in_=pb[:, 0:n * 128]), reads=[Bp], writes=[B_ost])
        dst_dma()

    def mem_setup(gen):
        mx, Bmx = xTs[1], B_xTs[1]
        for t in range(2):
            P.dma("sp", "memld", memst[:, t, :], mem[t * 128:(t + 1) * 128, :], writes=[B_memst])
        for t in range(2):
            for hf in range(2):
                pb, Bp = dbank()
                pv = pb[:].rearrange("p (c t) -> p c t", c=4)
                P.mm([(lambda e, c=c, pv=pv, t=t, hf=hf: e.transpose(out=pv[:, c, :], in_=memst[:, t, (hf * 4 + c) * 128:(hf * 4 + c + 1) * 128],
                                                                     identity=identf[:])) for c in range(4)],
                     reads=[B_memst] + CONST, writes=[Bp])
                P.op("act", lambda e, pv=pv, hf=hf, t=t: e.activation(out=mx[:, hf * 4:hf * 4 + 4, t * 128:(t + 1) * 128], in_=pv, func=AF.Copy),
                     reads=[Bp], writes=[Bmx])
        norm(mx, Bmx, 2, 256, hT, B_hT)
        for (wn, is_k) in (("ck", True), ("cv", False)):
            for m in range(8):
                slot, Bslot = W.get(wn, m)
                if is_k:
                    pb, Bp = dbank()
                    P.mm([(lambda e, k=k, slot=slot, pb=pb: e.matmul(pb[:, :256], lhsT=slot[:, k, :], rhs=hT[:, k, :256],
                                                                    start=(k == 0), stop=(k == 7))) for k in range(8)],
                         reads=[Bslot, B_hT], writes=[Bp])
                    P.op("act", lambda e, m=m, pb=pb: e.activation(out=memkT[:, m, :], in_=pb[:, :256], func=AF.Copy),
                         reads=[Bp], writes=[B_memkT])
                pb, Bp = dbank()
                fns = []
                for t in range(2):
                    for k in range(8):
                        fns.append(lambda e, k=k, slot=slot, pb=pb, t=t: e.matmul(
                            pb[:, t * 128:(t + 1) * 128], lhsT=hT[:, k, t * 128:(t + 1) * 128], rhs=slot[:, k, :],
                            start=(k == 0), stop=(k == 7)))
                P.mm(fns, reads=[Bslot, B_hT], writes=[Bp])
                pv2 = pb[:, 0:256].rearrange("p (t c) -> p t c", t=2)
                P.op("dve", lambda e, pv2=pv2, m=m: e.tensor_copy(out=mkst[:, :, m * 128:(m + 1) * 128], in_=pv2),
                     reads=[Bp], writes=[B_mkst])
                if not is_k:
                    P.op("act", lambda e, pv2=pv2, m=m: e.activation(out=memv[:, :, m * 128:(m + 1) * 128], in_=pv2, func=AF.Copy),
                         reads=[Bp], writes=[B_memv])
                advance(gen, 3)
            for t in range(2):
                P.dma("pool", "memout", (memk_o if is_k else memv_o)[t * 128:(t + 1) * 128, :], mkst[:, t, :], reads=[B_mkst])

    def geom(kind):
        sample, halo = (kind == "S"), (kind == "H")
        NT = 128 if (sample or halo) else NT_P
        return sample, halo, NT, NT // 128

    def early(kind, gi, X):
        sample, halo, NT, ntl = geom(kind)
        xt, Bxt = xTs[X], B_xTs[X]
        if sample:
            yield from g_load_x(lambda j: xs[:, :], 1, X)
        elif halo:
            yield from g_load_x(lambda j: xp[0:128, :], 1, X)
        else:
            yield from g_load_x(lambda j: xp[128 + gi * NT_P + j * 128: 128 + gi * NT_P + (j + 1) * 128, :], ntl, X)
        yield "L"
        prep1 = make_prep(X, NT, 0, -0.5, rstd, B_rstd)
        for m in range(8):
            prep1[0](m)
            if m % 2 == 1:
                yield
        prep1[1]()
        yield
        last = (kind == "P" and gi == NG_P - 1)
        want32 = last or sample
        Us = U[:, :, 0:384].rearrange("p g (b c) -> p g b c", b=16)

        if sample:
            P.op("dve", lambda e: e.memset(U[:, :, 0:384], 0.0), writes=[B_U])
            for hb in range(2):
                P.dma("sp", "spld", xin[hb][0:120, 0:512], spool[hb * 8:(hb + 1) * 8].rearrange("b r f -> (b r) f"), writes=[B_xin[hb]])
                pb, Bp = dbank()
                pv = pb[:].rearrange("p (c t) -> p c t", c=4)
                P.mm([(lambda e, c=c, pv=pv, hb=hb: e.transpose(out=pv[:, c, 0:120], in_=xin[hb][0:120, c * 128:(c + 1) * 128],
                                                                identity=identf[0:120, 0:120])) for c in range(4)],
                     reads=[B_xin[hb]] + CONST, writes=[Bp])
                for c in range(4):
                    P.op("dve", lambda e, c=c, pv=pv, hb=hb: e.tensor_copy(
                        out=Us[:, c, hb * 8:(hb + 1) * 8, 1:16], in_=pv[:, c, 0:120].rearrange("p (b r) -> p b r", b=8)),
                        reads=[Bp], writes=[B_U])
            yield

        def in_evac(m, pb, Bp):
            rd = [Bp, B_rstd]
            if m < 4:
                P.op("dve", lambda e: e.tensor_tensor(out=qT[:, m, :NT], in0=pb[:, :NT], in1=rstd[:, :NT], op=ALU.mult), reads=rd, writes=[B_qT])
            elif m == 4 or m == 5:
                dst_, Bd_ = (kT[:, 128:128 + NT], B_kT) if m == 4 else (vT[:, :NT], B_vT)
                P.op("dve", lambda e: e.tensor_tensor(out=dst_, in0=pb[:, :NT], in1=rstd[:, :NT], op=ALU.mult), reads=rd, writes=[Bd_])
                if want32:
                    P.op("dve", lambda e: e.tensor_tensor(out=kv32[:, m - 4, :], in0=pb[:, NT - 128:NT], in1=rstd[:, NT - 128:NT], op=ALU.mult),
                         reads=rd, writes=[B_kv32])
            else:
                g = m - 6
                if sample:
                    P.op("dve", lambda e: e.tensor_tensor(out=Us[:, g, :, 16:24], in0=pb[:, 0:128].rearrange("p (b t) -> p b t", b=16),
                                                          in1=rstd[:, 0:128].rearrange("p (b t) -> p b t", b=16), op=ALU.mult),
                         reads=rd, writes=[B_U])
                else:
                    P.op("dve", lambda e: e.tensor_tensor(out=U[:, g, 16:16 + NT], in0=pb[:, :NT], in1=rstd[:, :NT], op=ALU.mult), reads=rd, writes=[B_U])
        yield from g_dense("in", list(range(4, 10)) if halo else list(range(10)), NT, lambda k: hT[:, k, :NT], B_hT, in_evac)
        yield "P1"

        if sample:
            for hb in range(2):
                P.dma("pool", "ckld", vc[:, hb * 8:(hb + 1) * 8, :], cv[hb * 8:(hb + 1) * 8].rearrange("b s f -> s b f"), writes=[B_vc])
            kst = pexp[:].rearrange("p u t -> p (u t)").rearrange("p (b f) -> p b f", b=16)
            for hb in range(2):
                P.dma("pool", "ckld", kst[:, hb * 8:(hb + 1) * 8, :], ck[hb * 8:(hb + 1) * 8].rearrange("b s f -> s b f"), writes=[B_pexpH[hb]])
            for hb in range(2):
                pb, Bp = tbank()
                pv = pb[:].bitcast(BF16).rearrange("p (b t) -> p b t", b=8)
                P.mm([(lambda e, b=b, pv=pv, hb=hb: e.transpose(out=pv[:, b, :], in_=kst[:, hb * 8 + b, :], identity=ident[:])) for b in range(8)],
                     reads=[B_pexpH[hb]] + CONST, writes=[Bp])
                P.op("dve", lambda e, pv=pv, hb=hb: e.tensor_copy(out=kcT[:, hb * 8:(hb + 1) * 8, :], in_=pv), reads=[Bp], writes=[B_kcT])
            yield

        for j0 in range(0, ntl, 4):
            pb, Bp = tbank()
            pv = pb[:].bitcast(BF16)[:, 0:512].rearrange("p (j t) -> p j t", j=4)
            P.mm([(lambda e, j=j, pv=pv: e.transpose(out=pv[:, j - j0, :], in_=vT[:, j * 128:(j + 1) * 128], identity=ident[:]))
                  for j in range(j0, min(ntl, j0 + 4))], reads=[B_vT] + CONST, writes=[Bp])
            nj = min(ntl, j0 + 4) - j0
            P.op("dve", lambda e, pv=pv, j0=j0, nj=nj: e.tensor_copy(out=vtok[:, 1 + j0:1 + j0 + nj, :], in_=pv[:, 0:nj, :]),
                 reads=[Bp], writes=[B_vtok])
        yield

        def carry():
            P.op("dve", lambda e: e.tensor_copy(out=kT[:, 0:128], in_=kT[:, NT:NT + 128]), reads=[B_kT], writes=[B_kT])
            P.op("dve", lambda e: e.tensor_copy(out=vtok[:, 0, :], in_=vtok[:, ntl, :]), reads=[B_vtok], writes=[B_vtok])

        if halo:
            carry()
            P.op("dve", lambda e: e.tensor_copy(out=carryU[:], in_=U[:, :, NT:NT + 16]), reads=[B_U], writes=[B_cU])
            return

        if want32:
            if last:
                def dd():
                    P.dma("pool", "kvout", wkp[:, :], ost[:, 0:128], reads=[B_ost])
                    P.dma("pool", "kvout", wvp[:, :], ost[:, 128:256], reads=[B_ost])
            else:
                def dd():
                    for t in range(8):
                        P.dma("pool", "kvout", wks[:, 120 + t, :], ost[t:128:8, 0:128], reads=[B_ost])
                        P.dma("pool", "kvout", wvs[:, 120 + t, :], ost[t:128:8, 128:256], reads=[B_ost])
                    P.dma("sp", "d2d_k", wks[:, 0:120, :], ck[:, 8:128, :])
                    P.dma("sp", "d2d_v", wvs[:, 0:120, :], cv[:, 8:128, :])
            out_tok_major([kv32[:, 0, :], kv32[:, 1, :]], [B_kv32], 128, dd)
            yield

        if not sample:
            P.op("dve", lambda e: e.tensor_copy(out=U[:, :, 0:16], in_=carryU[:]), reads=[B_cU], writes=[B_U])
        for hh in range(2):
            if sample:
                Wd, c0 = 192, hh * 192
            else:
                Wd, c0 = 16 + NT // 2, hh * (NT // 2)
            Uh = U[:, :, c0:c0 + Wd]
            P.op("dve", lambda e, Uh=Uh, Wd=Wd: e.tensor_tensor(out=SA[:, :, 1:Wd], in0=Uh[:, :, 1:Wd], in1=Uh[:, :, 0:Wd - 1], op=ALU.add),
                 reads=[B_U], writes=[B_SA])
            P.op("dve", lambda e, Wd=Wd: e.tensor_tensor(out=SB[:, 1:4, 3:Wd], in0=SA[:, 1:4, 3:Wd], in1=SA[:, 1:4, 1:Wd - 2], op=ALU.add),
                 reads=[B_SA], writes=[B_SB])
            yield
            P.op("dve", lambda e, Wd=Wd: e.tensor_tensor(out=SA[:, 2:4, 7:Wd], in0=SB[:, 2:4, 7:Wd], in1=SB[:, 2:4, 3:Wd - 4], op=ALU.add),
                 reads=[B_SB], writes=[B_SA])
            P.op("dve", lambda e, Wd=Wd: e.tensor_tensor(out=SB[:, 3, 15:Wd], in0=SA[:, 3, 15:Wd], in1=SA[:, 3, 7:Wd - 8], op=ALU.add),
                 reads=[B_SA], writes=[B_SB])
            yield
            for g in range(4):
                S_, BS_ = (SA, B_SA) if g % 2 == 0 else (SB, B_SB)
                if sample:
                    sv = S_[:, g, 0:192].rearrange("p (b c) -> p b c", b=8)[:, :, 16:24]
                    uv = Uh[:, g, :].rearrange("p (b c) -> p b c", b=8)[:, :, 16:24]
                    dv = dT[:, g, hh * 64:(hh + 1) * 64].rearrange("p (b t) -> p b t", b=8)
                else:
                    sv, uv, dv = S_[:, g, 16:Wd], Uh[:, g, 16:Wd], dT[:, g, c0:c0 + NT // 2]
                P.op("dve", lambda e, sv=sv, uv=uv, dv=dv, g=g: e.scalar_tensor_tensor(out=dv, in0=sv, scalar=1.0 / (2 << g), in1=uv,
                                                                                   op0=ALU.mult, op1=ALU.subtract),
                     reads=[BS_, B_U], writes=[B_dT])
                if kind == "P" and gi == 0 and hh == 0:
                    P.op("dve", lambda e, S_=S_, g=g: e.tensor_tensor(out=st[:, 0:16], in0=S_[:, g, 16:32], in1=invc[:, g, :], op=ALU.mult),
                         reads=[BS_] + CONST, writes=B_stH)
                    P.op("dve", lambda e, g=g: e.tensor_tensor(out=dT[:, g, 0:16], in0=st[:, 0:16], in1=U[:, g, 16:32], op=ALU.subtract),
                         reads=B_stH + [B_U], writes=[B_dT])
            yield
        if last:
            def dd2():
                P.dma("pool", "poolout", poolp[:, :], ost[113:128, :], reads=[B_ost])
            out_tok_major([U[:, g, 16 + NT - 128:16 + NT] for g in range(4)], [B_U], 128, dd2)
        if sample:
            for g in range(4):
                P.op("dve", lambda e, g=g: e.tensor_copy(out=SA[:, g, 0:128].rearrange("p (b t) -> p b t", b=16), in_=Us[:, g, :, 16:24]),
                     reads=[B_U, B_dT], writes=[B_SA])

            def dd3():
                for t in range(8):
                    P.dma("pool", "poolout", pools[:, 7 + t, :], ost[t:128:8, :], reads=[B_ost])
                P.dma("sp", "d2d_p", pools[:, 0:7, :], spool[:, 8:15, :])
            out_tok_major([SA[:, g, 0:128] for g in range(4)], [B_SA], 128, dd3)
        else:
            P.op("dve", lambda e: e.tensor_copy(out=carryU[:], in_=U[:, :, NT:NT + 16]), reads=[B_U], writes=[B_cU])
        yield
        for g in range(4):
            pb, Bp = dbank()
            P.mm([lambda e, g=g, pb=pb: e.matmul(pb[:, :NT], lhsT=wpool[:, g, :], rhs=dT[:, g, :NT], start=True, stop=True)],
                 reads=[B_wpool, B_dT], writes=[Bp])
            P.op("act", lambda e, g=g, pb=pb: e.activation(out=aoT[:, 4 + g, :NT], in_=pb[:, :NT], func=AF.Copy, scale=pscale[:, g:g + 1]),
                 reads=[Bp] + CONST, writes=[B_aoT])
            yield

        if not sample:
            for j in range(ntl):
                if gi == 0 and j == 1:
                    P.dma("sp", "biasld", bias[:].rearrange("p a b -> p (a b)"), biasg_d[:, :], writes=[B_bias])
                for gp in range(2):
                    bk = [sbank(), sbank()]
                    fns = []
                    for g in (2 * gp, 2 * gp + 1):
                        for kv in range(2):
                            fns.append(lambda e, kv=kv, g=g, j=j, bk=bk: e.matmul(
                                bk[kv][0][:, (g % 2) * 256:(g % 2 + 1) * 256], lhsT=qT[kv * 64:(kv + 1) * 64, g, j * 128:(j + 1) * 128],
                                rhs=kT[kv * 64:(kv + 1) * 64, j * 128:j * 128 + 256], start=True, stop=True))
                    P.mm(fns, reads=[B_qT, B_kT], writes=[bk[0][1], bk[1][1]])
                    for kv in range(2):
                        u0 = 4 * gp + kv
                        P.op("dve", lambda e, kv=kv, u0=u0, bk=bk: e.scalar_tensor_tensor(
                            out=sbias[:, u0:u0 + 3:2, :], in0=bk[kv][0][:].rearrange("p (g t) -> p g t", g=2), scalar=0.125,
                            in1=bias[:, u0:u0 + 3:2, :], op0=ALU.mult, op1=ALU.add),
                            reads=[bk[kv][1], B_bias], writes=[B_sbiasH[gp]])
                    yield
                sm = [win_softmax(4 * gp, 4, 256, sinkp[:, 4 * gp:4 * gp + 4], [gp]) for gp in range(2)]
                next(sm[0])
                yield
                next(sm[1])
                yield
                yield
                run(sm[0])
                yield
                run(sm[1])
                yield
                for gp in range(2):
                    yield from diag_T(4 * gp, 4, 2, pexp, [B_pexpH[gp]], [B_DgH[gp]],
                                      lambda u0: pT[:, u0:u0 + 2, :, :].rearrange("p u k t -> p (u k t)"), None, [B_pTH[gp]], [128, 128])
                yield
                po, Bpo = ps[PS_O], B_ps[PS_O]
                pov = po[:].rearrange("p (g t) -> p g t", g=4)
                fns = []
                for g in range(4):
                    for kv in range(2):
                        for kc in range(2):
                            fns.append(lambda e, g=g, kv=kv, kc=kc, j=j: e.matmul(
                                pov[kv * 64:(kv + 1) * 64, g, :], lhsT=vtok[:, j + kc, kv * 64:(kv + 1) * 64], rhs=pT[:, 2 * g + kv, kc, :],
                                start=(kc == 0), stop=(kc == 1)))
                P.mm(fns, reads=[B_vtok] + B_pTH, writes=[Bpo])
                P.op("act", lambda e, j=j: e.activation(out=aoT[:, 0:4, j * 128:(j + 1) * 128], in_=pov, func=AF.Copy),
                     reads=[Bpo], writes=[B_aoT])
                yield
            carry()
        else:
            P.op("dve", lambda e: e.tensor_copy(out=qs2[:].rearrange("p b (g t) -> p b g t", g=4),
                                                in_=qT[:, :, 0:128].rearrange("p g (b t) -> p b g t", b=16)), reads=[B_qT], writes=[B_qs2])
            pb, Bp = dbank()
            pvb = pb[:].bitcast(BF16)
            P.mm([(lambda e, i=i, pvb=pvb: e.transpose(out=pvb[0:32, i * 128:(i + 1) * 128], in_=vT[:, i * 32:(i + 1) * 32], identity=ident[:]))
                  for i in range(4)], reads=[B_vT] + CONST, writes=[Bp])
            P.op("dve", lambda e, pvb=pvb: e.tensor_copy(out=vnq[0:32, :, :], in_=pvb[0:32, 0:512].rearrange("p (i t) -> p i t", i=4)),
                 reads=[Bp], writes=[B_vnq])
            yield
            for i in range(4):
                bk = [sbank(), sbank()]
                fns = []
                for kv in range(2):
                    pvk = bk[kv][0]
                    for jq in range(4):
                        b = 4 * i + jq
                        fns.append(lambda e, kv=kv, jq=jq, b=b, pvk=pvk: e.matmul(
                            pvk[32 * jq:32 * jq + 32, 0:128], lhsT=qs2[kv * 64:(kv + 1) * 64, b, :], rhs=kcT[kv * 64:(kv + 1) * 64, b, :],
                            start=True, stop=True, tile_position=(kv * 64, 32 * jq)))
                    fns.append(lambda e, kv=kv, i=i, pvk=pvk: e.matmul(
                        pvk[:, 128:160], lhsT=qs2[kv * 64:(kv + 1) * 64, 4 * i:4 * i + 4, :].rearrange("p b t -> p (b t)"),
                        rhs=kT[kv * 64:(kv + 1) * 64, 128 + 32 * i:128 + 32 * i + 32], start=True, stop=True))
                P.mm(fns, reads=[B_qs2, B_kcT, B_kT], writes=[bk[0][1], bk[1][1]])
                for kv in range(2):
                    P.op("dve", lambda e, kv=kv, i=i, bk=bk: e.scalar_tensor_tensor(
                        out=sbias[:, 2 * i + kv, 0:160], in0=bk[kv][0][:, 0:160], scalar=0.125,
                        in1=biass[:, kv, :], op0=ALU.mult, op1=ALU.add),
                        reads=[bk[kv][1]] + CONST, writes=[B_sbiasH[i // 2]])
                yield
            smx = win_softmax(0, 8, 160, sinks[:, 0:8], [0, 1])
            next(smx)
            yield
            yield
            run(smx)
            yield
            yield from diag_T(0, 8, 2, pexp, B_pexpH, B_DgH, None, lambda u, kc: pT[0:(128 if kc == 0 else 32), u, kc, :], B_pTH, [128, 32])
            for i in range(4):
                pb, Bp = dbank()
                fns = []
                for kv in range(2):
                    u = 2 * i + kv
                    for jq in range(4):
                        b = 4 * i + jq
                        fns.append(lambda e, kv=kv, jq=jq, b=b, u=u, pb=pb: e.matmul(
                            pb[kv * 64:(kv + 1) * 64, 32 * jq:32 * jq + 32], lhsT=vc[:, b, kv * 64:(kv + 1) * 64], rhs=pT[:, u, 0, 32 * jq:32 * jq + 32],
                            start=(jq == 0), stop=False, skip_group_check=True))
                for kv in range(2):
                    u = 2 * i + kv
                    fns.append(lambda e, kv=kv, u=u, i=i, pb=pb: e.matmul(
                        pb[kv * 64:(kv + 1) * 64, 0:128], lhsT=vnq[0:32, i, kv * 64:(kv + 1) * 64], rhs=pT[0:32, u, 1, :],
                        start=False, stop=True, skip_group_check=True))
                P.mm(fns, reads=[B_vc, B_vnq] + B_pTH, writes=[Bp])
                P.op("act", lambda e, pb=pb, i=i: e.activation(
                    out=aoT[:, 0:4, 32 * i:32 * i + 32].rearrange("p g (j t) -> p j g t", j=4),
                    in_=pb[:, 0:128].rearrange("p (j g t) -> p j g t", j=4, g=4), func=AF.Copy),
                    reads=[Bp], writes=[B_aoT])
                yield

    def late_pre(kind, gi, X, gen, tgen=None):
        sample, halo, NT, ntl = geom(kind)
        xt, Bxt = xTs[X], B_xTs[X]
        loaded = [gen is None]
        xfree = [tgen is None]

        def step_tail(n):
            for _ in range(n):
                if tgen is not None and next(tgen, "END") == "XDONE":
                    xfree[0] = True

        def step_load():
            if not loaded[0] and xfree[0]:
                if next(gen, "L") == "L":
                    loaded[0] = True
        p1done = [gen is None]

        def step_p1(n):
            for _ in range(n):
                if not p1done[0]:
                    if next(gen, "P1") == "P1":
                        p1done[0] = True
        prep2 = make_prep(X, NT, 1, -0.5, rstd, B_rstd)
        for _ in g_dense("out", list(range(8)), NT, lambda k: aoT[:, k, :NT], B_aoT, resid_evac(NT, X, prep=prep2)):
            step_load()
            step_tail(2)
        prep2[1]()
        qcT, B_qcT = aoT, B_aoT
        ocT, B_ocT = hidT, B_hid

        def cq_evac(m, pb, Bp):
            P.op("dve", lambda e: e.tensor_tensor(out=qcT[:, m, :NT], in0=pb[:, :NT], in1=rstd[:, :NT], op=ALU.mult),
                 reads=[Bp, B_rstd], writes=[B_qcT])
        for _ in g_dense("cq", list(range(8)), NT, lambda k: hT[:, k, :NT], B_hT, cq_evac):
            step_load()
            step_tail(2)
        while tgen is not None and not xfree[0]:
            step_tail(1)
        while not loaded[0]:
            step_load()
        run(tgen)

        if not sample:
            for j in range(ntl):
                banks = [sbank(), sbank()]
                for hp, (pb, Bp) in enumerate(banks):
                    pv = pb[:].rearrange("p (h t) -> p h t", h=2)
                    fns = []
                    for hh in range(2):
                        h = 2 * hp + hh
                        for dc in range(2):
                            fns.append(lambda e, pv=pv, hh=hh, h=h, dc=dc, j=j: e.matmul(
                                pv[:, hh, :], lhsT=qcT[:, 2 * h + dc, j * 128:(j + 1) * 128], rhs=memkT[:, 2 * h + dc, :],
                                start=(dc == 0), stop=(dc == 1)))
                    P.mm(fns, reads=[B_qcT, B_memkT], writes=[Bp])
                cross_softmax(banks)
                step_p1(2)
                run(diag_T(0, 4, 2, pexp, [B_pexp], [B_Dg], lambda h0: pTc[:, h0:h0 + 2, :, :].rearrange("p u k t -> p (u k t)"), None, [B_pTc], [128, 128]))
                for half in range(2):
                    pb, Bp = dbank()
                    pv = pb[:].rearrange("p (c t) -> p c t", c=4)
                    fns = []
                    for cc in range(4):
                        c = half * 4 + cc
                        h = c // 2
                        for mc in range(2):
                            fns.append(lambda e, pv=pv, cc=cc, c=c, h=h, mc=mc: e.matmul(
                                pv[:, cc, :], lhsT=memv[:, mc, c * 128:(c + 1) * 128], rhs=pTc[:, h, mc, :], start=(mc == 0), stop=(mc == 1)))
                    P.mm(fns, reads=[B_memv, B_pTc], writes=[Bp])
                    P.op("act" if half == 0 else "dve",
                         (lambda e, pv=pv, half=half, j=j: e.activation(out=ocT[:, half * 4:half * 4 + 4, j * 128:(j + 1) * 128], in_=pv, func=AF.Copy))
                         if half == 0 else
                         (lambda e, pv=pv, half=half, j=j: e.tensor_copy(out=ocT[:, half * 4:half * 4 + 4, j * 128:(j + 1) * 128], in_=pv)),
                         reads=[Bp], writes=[B_ocT])
                step_p1(2)
        else:
            banks = [sbank(), sbank()]
            for i in range(2):
                P.op("dve", lambda e, i=i: e.memset(qpad[i][:], 0.0), writes=[B_qpad[i]])

            def xslot(c0):
                return xTs[0][:, :, c0:c0 + 128].bitcast(BF16)
            KX = [xslot(128), xslot(256)]
            VX = [xslot(384)]
            B_KX = [Buf("KX0"), Buf("KX1")]
            B_VX = [Buf("VX0")]
            for bb_ in B_KX + B_VX:
                bb_.al.append(B_xTs[0])
                B_xTs[0].al.append(bb_)
            NK, NV = 4, 3

            def kslot(b):
                i = b % NK
                return (("a", Kb[i], B_Kb[i]) if i < 2 else ("x", KX[i - 2], B_KX[i - 2]))

            def vslot(b):
                i = b % (NV + NK)
                if i < NV:
                    return (("a", Vb[i], B_Vb[i]) if i < 2 else ("x", VX[i - 2], B_VX[i - 2]))
                i -= NV
                return (("a", Kb[i], B_Kb[i]) if i < 2 else ("x", KX[i - 2], B_KX[i - 2]))

            def ld(slot, src):
                kind_, ap_, B_ = slot
                if kind_ == "a":
                    P.dma("pool", "c", ap_[:], src.rearrange("(m p) f -> p m f", p=128), writes=[B_])
                else:
                    for m_ in range(2):
                        P.dma("pool", "c", ap_[:, m_ * 4:(m_ + 1) * 4, :],
                              src[m_ * 128:(m_ + 1) * 128, :].rearrange("p (q j) -> p q j", j=256), writes=[B_])

            def tile_of(slot, mt, c):
                kind_, ap_, B_ = slot
                if kind_ == "a":
                    return ap_[:, mt, c * 128:(c + 1) * 128]
                return ap_[:, mt * 4 + c // 2, (c % 2) * 128:(c % 2 + 1) * 128]

            for b in range(3):
                ld(kslot(b), cmk[b])
            for b in range(3):
                ld(vslot(b), cmv[b])
            def Tstage(b):
                s2 = b % 2
                ks = kslot(b)
                if b + 3 < 16:
                    ld(kslot(b + 3), cmk[b + 3])
                for mt in range(2):
                    pb, Bp = tbank()
                    pv = pb[:].bitcast(BF16).rearrange("p (c t) -> p c t", c=8)
                    P.mm([(lambda e, c=c, pv=pv, mt=mt, ks=ks: e.transpose(out=pv[:, c, :], in_=tile_of(ks, mt, c), identity=ident[:]))
                          for c in range(8)], reads=[ks[2]] + CONST, writes=[Bp])
                    P.op("act" if mt == 0 else "dve",
                         (lambda e, pv=pv, mt=mt, s2=s2: e.activation(out=KbT[s2][:, :, mt * 128:(mt + 1) * 128], in_=pv, func=AF.Copy))
                         if mt == 0 else
                         (lambda e, pv=pv, mt=mt, s2=s2: e.tensor_copy(out=KbT[s2][:, :, mt * 128:(mt + 1) * 128], in_=pv)),
                         reads=[Bp], writes=[B_KbT[s2]])
                if b >= 2:
                    P.op("dve", lambda e, s2=s2, b=b: e.memset(qpad[s2][:, :, (b - 2) * 8:(b - 1) * 8], 0.0), writes=[B_qpad[s2]])
                P.op("dve", lambda e, s2=s2, b=b: e.tensor_copy(out=qpad[s2][:, :, b * 8:(b + 1) * 8], in_=qcT[:, :, b * 8:(b + 1) * 8]),
                     reads=[B_qcT], writes=[B_qpad[s2]])

            def Sstage(b):
                s2 = b % 2
                for hp, (pb, Bp) in enumerate(banks):
                    pv = pb[:].rearrange("p (h t) -> p h t", h=2)
                    fns = []
                    for hh in range(2):
                        h = 2 * hp + hh
                        for dc in range(2):
                            fns.append(lambda e, pv=pv, hh=hh, h=h, dc=dc, s2=s2, b=b: e.matmul(
                                pv[:, hh, :], lhsT=qpad[s2][:, 2 * h + dc, :], rhs=KbT[s2][:, 2 * h + dc, :],
                                start=(b == 0 and hh == 0 and dc == 0), stop=(b == 15 and dc == 1), skip_group_check=True))
                    P.mm(fns, reads=[B_qpad[s2], B_KbT[s2]], writes=[Bp])

            Tstage(0)
            for b in range(16):
                if b + 1 < 16:
                    Tstage(b + 1)
                Sstage(b)
            for b in range(NV, NV + NK):
                ld(vslot(b), cmv[b])
            cross_softmax(banks)
            run(diag_T(0, 4, 2, pexp, [B_pexp], [B_Dg], lambda h0: pTc[:, h0:h0 + 2, :, :].rearrange("p u k t -> p (u k t)"), None, [B_pTc], [128, 128]))
            pbs = [dbank(), dbank()]
            for b in range(16):
                vs = vslot(b)
                fns = []
                for c in range(8):
                    pv = pbs[c // 4][0][:].rearrange("p (c t) -> p c t", c=4)
                    h = c // 2
                    for mc in range(2):
                        fns.append(lambda e, pv=pv, c=c, h=h, mc=mc, vs=vs, b=b: e.matmul(
                            pv[:, c % 4, b * 8:(b + 1) * 8], lhsT=tile_of(vs, mc, c), rhs=pTc[:, h, mc, b * 8:(b + 1) * 8],
                            start=(mc == 0), stop=(mc == 1), skip_group_check=True))
                P.mm(fns, reads=[vs[2], B_pTc], writes=[pbs[0][1], pbs[1][1]])
                if b + NV + NK < 16:
                    ld(vslot(b + NV + NK), cmv[b + NV + NK])
            for half in range(2):
                pv = pbs[half][0][:].rearrange("p (c t) -> p c t", c=4)
                P.op("act" if half == 0 else "dve",
                     (lambda e, pv=pv, half=half: e.activation(out=ocT[:, half * 4:half * 4 + 4, 0:128], in_=pv, func=AF.Copy))
                     if half == 0 else
                     (lambda e, pv=pv, half=half: e.tensor_copy(out=ocT[:, half * 4:half * 4 + 4, 0:128], in_=pv)),
                     reads=[pbs[half][1]], writes=[B_ocT])
        while not p1done[0]:
            step_p1(1)
        prep3 = make_prep(X, NT, 3, -1.0, rstd2, B_rstd2)
        dense("co", list(range(8)), NT, lambda k: ocT[:, k, :NT], B_ocT, resid_evac(NT, X, prep=prep3))
        prep3[1]()

    def ffn(kind, gi, X, gen):
        sample, halo, NT, ntl = geom(kind)
        xt, Bxt = xTs[X], B_xTs[X]
        uctr = [0]

        def up_evac(m, pb, Bp):
            r, Br = relu_t[uctr[0] % 2], B_relu[uctr[0] % 2]
            uctr[0] += 1
            P.op("act", lambda e: e.activation(out=r[:, :NT], in_=pb[:, :NT], func=AF.Relu), reads=[Bp], writes=[Br])
            P.op("pool", lambda e: e.tensor_tensor(out=hidT[:, m, :NT], in0=r[:, :NT], in1=r[:, :NT], op=ALU.mult), reads=[Br], writes=[B_hid])
        for _ in g_dense("up", list(range(32)), NT, lambda k: hT[:, k, :NT], B_hT, up_evac):
            advance(gen, 1)
        for _ in g_dense("down", list(range(8)), NT, lambda k: hidT[:, k, :NT], B_hid, resid_evac(NT, X, scale2=True), kgroups=4):
            advance(gen, 1)

    def g_tail(kind, gi, X):
        sample, halo, NT, ntl = geom(kind)
        xt, Bxt = xTs[X], B_xTs[X]
        sq = hidT[:, 16:24, :]
        for k in range(8):
            P.op("act", lambda e, k=k: e.activation(out=sq[:, k, :NT], in_=xt[:, k, :NT], func=AF.Square), reads=[Bxt], writes=[B_hid])
            if k % 4 == 3:
                yield
        pb0, Bp0 = dbank()
        P.mm([(lambda e, k=k: e.matmul(pb0[:, :NT], lhsT=ones[:], rhs=sq[:, k, :NT], start=(k == 0), stop=(k == 7)))
              for k in range(8)], reads=[B_hid] + CONST, writes=[Bp0])
        P.op("act", lambda e: e.activation(out=rstd2[:, :NT], in_=pb0[:, :NT], func=AF.Ln, scale=1.0 / D, bias=EPS),
             reads=[Bp0], writes=[B_rstd2])
        P.op("act", lambda e: e.activation(out=rstd2[:, :NT], in_=rstd2[:, :NT], func=AF.Exp, scale=-0.5),
             reads=[B_rstd2], writes=[B_rstd2])
        yield
        for k in range(8):
            P.op("dve", lambda e, k=k: e.scalar_tensor_tensor(out=yT[:, k, :NT], in0=xt[:, k, :NT], scalar=gvec[:, 4, k:k + 1],
                                                              in1=rstd2[:, :NT], op0=ALU.mult, op1=ALU.mult),
                 reads=[Bxt, B_rstd2] + CONST, writes=[B_yT])
            if k % 2 == 1 and k < 7:
                yield
        yield "XDONE"
        for j in range(ntl):
            ys_, Bys = yst[1], B_yst[1]
            for hf in range(2):
                pb, Bp = dbank()
                pv = pb[:].rearrange("p (c t) -> p c t", c=4)
                P.mm([(lambda e, c=c, pv=pv, hf=hf, j=j: e.transpose(out=pv[:, c, :], in_=yT[:, hf * 4 + c, j * 128:(j + 1) * 128], identity=identf[:]))
                      for c in range(4)], reads=[B_yT] + CONST, writes=[Bp])
                P.op("act" if hf == 0 else "dve",
                     (lambda e, pb=pb, hf=hf, ys_=ys_: e.activation(out=ys_[:, hf * 512:(hf + 1) * 512], in_=pb[:, :], func=AF.Copy))
                     if hf == 0 else
                     (lambda e, pb=pb, hf=hf, ys_=ys_: e.tensor_copy(out=ys_[:, hf * 512:(hf + 1) * 512], in_=pb[:, :])),
                     reads=[Bp], writes=[Bys])
                yield
            if sample:
                P.dma("pool", "y", ys[:, :], ys_[:], reads=[Bys])
            else:
                r0 = gi * NT_P + j * 128
                P.dma("pool", "y", yp[r0:r0 + 128, :], ys_[:], reads=[Bys])

    order = [("P", g) for g in range(NG_P)] + [("S", 0)]
    order = order[:max(0, min(len(order), STAGE))] if STAGE < 50 else order
    run(early("H", 0, 0))
    gen0 = early(order[0][0], order[0][1], 0) if order else None
    advance(gen0, until="P1")
    mem_setup(gen0)
    run(gen0)
    tgen = None
    for idx, (kind, gi) in enumerate(order):
        X = idx % 2
        nxt = order[idx + 1] if idx + 1 < len(order) else None
        gen = early(nxt[0], nxt[1], (idx + 1) % 2) if nxt else None
        late_pre(kind, gi, X, gen, tgen)
        ffn(kind, gi, X, gen)
        run(gen)
        tgen = g_tail(kind, gi, X)
        if not TAIL_OVERLAP:
            run(tgen)
            tgen = None
    run(tgen)

    return finish()


_CACHE = {}


def _build_nc():
    if "nc" in _CACHE:
        return _CACHE["nc"]
    nc0 = bass.Bass("TRN2", target_bir_lowering=False)
    with ExitStack() as es0:
        _, W0 = build_sched(nc0, es0)
    sched = W0.rec
    nc = bass.Bass("TRN2", target_bir_lowering=False)
    with ExitStack() as es:
        P, W = build(nc, es, False, sched)
        assert W.i == len(sched), (W.i, len(sched))
        block = es.enter_context(nc.Block())
        P.flush(block)
    _CACHE["nc"] = nc
    return nc


def build_sched(nc0, es0):
    return build(nc0, es0, False, None)


def _tables(half):
    slopes = 2.0 ** (-(np.arange(8) + 1.0))
    q = np.arange(128)[:, None]
    c = np.arange(256)[None, :]
    dist = q - c + 128
    valid = (dist >= 0) & (dist <= 128)
    biasg = np.empty((128, 8, 256), np.float32)
    for g in range(4):
        for kv in range(2):
            h = kv * 4 + g
            biasg[:, 2 * g + kv, :] = np.where(valid, -slopes[h] * dist, -1e30)
    biasf = biasg.copy()
    if half == 0:
        biasf[:, :, 0:128] = -1e30
    biass = np.full((128, 2, 160), -1e30, np.float32)
    for j in range(4):
        for g in range(4):
            for t in range(8):
                r = j * 32 + g * 8 + t
                for kv in range(2):
                    h = kv * 4 + g
                    cc = np.arange(128)
                    d = t + 128 - cc
                    biass[r, kv, 0:128] = np.where(cc >= t, -slopes[h] * d, -1e30)
                    for tp in range(t + 1):
                        biass[r, kv, 128 + j * 8 + tp] = -slopes[h] * (t - tp)
    invc = np.empty((128, 4, 16), np.float32)
    for g in range(4):
        w = 2 << g
        for p in range(16):
            invc[:, g, p] = 1.0 / (min(p + 1, w) if half == 0 else w)
    return biasg.reshape(128, -1), biasf.reshape(128, -1), biass.reshape(128, -1), invc.reshape(128, -1)


def _prep(x_prompt, x_sample, cache_win_k, cache_win_v, state_pool, cache_mem_k, cache_mem_v,
          mem_prompt, g_mix, w_in, attn_sinks, w_pool, pool_scale, w_out, g_cross, g_mem,
          w_cq, w_ck, w_cv, w_co, g_ffn, w_up, w_down, g_final):
    f = lambda a: np.ascontiguousarray(np.asarray(a, dtype=np.float32))
    x_prompt, x_sample = f(x_prompt), f(x_sample)
    shared = dict(w_in=f(w_in)[0], w_pool=f(w_pool)[0], w_out=f(w_out)[0], w_cq=f(w_cq)[0], w_ck=f(w_ck)[0],
                  w_cv=f(w_cv)[0], w_co=f(w_co)[0], w_up=f(w_up)[0], w_down=f(w_down)[0])
    gs = np.stack([f(g_mix)[0], f(g_cross)[0], f(g_mem)[0], f(g_ffn)[0], f(g_final)], 0)
    shared["gvec"] = np.ascontiguousarray(gs.reshape(5, 8, 128).transpose(2, 0, 1).reshape(128, 40))
    shared["pscale"] = np.ascontiguousarray(f(pool_scale)[0].reshape(4, 128).T)
    sk = f(attn_sinks)[0]
    sinkp = np.empty((128, 8), np.float32)
    for g in range(4):
        for kv in range(2):
            sinkp[:, 2 * g + kv] = sk[kv * 4 + g]
    shared["sinkp"] = sinkp
    sinks = np.empty((128, 8), np.float32)
    for r in range(128):
        g = (r % 32) // 8
        for i in range(4):
            sinks[r, 2 * i] = sk[g]
            sinks[r, 2 * i + 1] = sk[4 + g]
    shared["sinks"] = sinks
    ckf, cvf, spf = f(cache_win_k)[0], f(cache_win_v)[0], f(state_pool)[0]
    cmkf, cmvf, memf = f(cache_mem_k)[0], f(cache_mem_v)[0], f(mem_prompt)
    in_maps = []
    for c in range(NCORES):
        b, half = c // 2, c % 2
        s0 = half * SEQ_CORE
        xp = np.zeros((128 + SEQ_CORE, D), np.float32)
        xp[128:] = x_prompt[b, s0:s0 + SEQ_CORE]
        if half == 1:
            xp[:128] = x_prompt[b, s0 - 128:s0]
        biasg, biasf, biass, invc = _tables(half)
        sl = slice(16 * c, 16 * c + 16)
        m = dict(shared)
        m.update(xp=xp, xs=np.ascontiguousarray(x_sample[sl].reshape(128, D)), mem=np.ascontiguousarray(memf[b]),
                 ck=np.ascontiguousarray(ckf[sl].reshape(16, 128, 128)), cv=np.ascontiguousarray(cvf[sl].reshape(16, 128, 128)),
                 spool=np.ascontiguousarray(spf[sl]), cmk=np.ascontiguousarray(cmkf[sl].reshape(16, 256, D)),
                 cmv=np.ascontiguousarray(cmvf[sl].reshape(16, 256, D)),
                 biasg=biasg, biasf=biasf, biass=biass, invc=invc)
        in_maps.append(m)
    return in_maps


def kernel(**inputs):
    in_maps = _prep(**inputs)
    nc = _build_nc()
    res = run_bass_kernel_spmd(nc, in_maps, core_ids=list(range(NCORES))).results
    return _assemble(res)


def _assemble(res):
    B, S = 4, 4096
    y_prompt = np.empty((B, S, D), np.float32)
    y_sample = np.empty((128, 8, D), np.float32)
    wk_p = np.empty((1, B, 128, 2, 64), np.float32); wv_p = np.empty_like(wk_p)
    pool_p = np.empty((1, B, 15, 512), np.float32)
    mk_p = np.empty((1, B, 256, 4, 256), np.float32); mv_p = np.empty_like(mk_p)
    wk_s = np.empty((1, 128, 128, 2, 64), np.float32); wv_s = np.empty_like(wk_s)
    pool_s = np.empty((1, 128, 15, 512), np.float32)
    for c in range(NCORES):
        r = res[c]
        b, half = c // 2, c % 2
        y_prompt[b, half * SEQ_CORE:(half + 1) * SEQ_CORE] = r["yp"]
        sl = slice(16 * c, 16 * c + 16)
        y_sample[sl] = r["ys"].reshape(16, 8, D)
        if half == 1:
            wk_p[0, b] = r["wkp"].reshape(128, 2, 64)
            wv_p[0, b] = r["wvp"].reshape(128, 2, 64)
            pool_p[0, b] = r["poolp"]
        else:
            mk_p[0, b] = r["memk"].reshape(256, 4, 256)
            mv_p[0, b] = r["memv"].reshape(256, 4, 256)
        wk_s[0, sl] = r["wks"].reshape(16, 128, 2, 64)
        wv_s[0, sl] = r["wvs"].reshape(16, 128, 2, 64)
        pool_s[0, sl] = r["pools"]
    return (y_prompt, y_sample, wk_p, wv_p, pool_p, mk_p, mv_p, wk_s, wv_s, pool_s)
```

```python
import numpy as np
from contextlib import ExitStack
import concourse.bass as bass
import concourse.mybir as mybir
from concourse.bass_utils import run_bass_kernel_spmd

F32 = mybir.dt.float32
BF16 = mybir.dt.bfloat16
ALU = mybir.AluOpType
AF = mybir.ActivationFunctionType
AX = mybir.AxisListType

NCORES = 8
STAGE = 99
TAIL_OVERLAP = True
D = 1024
SEQ_CORE = 2048
NT_P = 512
NG_P = SEQ_CORE // NT_P
RING = 11
EPS = 1e-5


class Buf:
    __slots__ = ("name", "w", "r", "al", "excl")

    def __init__(self, name, excl=False):
        self.name = name
        self.w = None
        self.r = {}
        self.al = []
        self.excl = excl


def alias(*bufs):
    for a in bufs:
        for b in bufs:
            if a is not b and b not in a.al:
                a.al.append(b)


class Prog:
    def __init__(self, nc, es, dry):
        self.nc, self.es, self.dry = nc, es, dry
        self.q = {e: [] for e in ("pe", "act", "dve", "pool", "sp")}
        self.cnt, self.sems = {}, {}
        self.waited = {e: {} for e in self.q}

    def sem(self, key):
        if key not in self.sems:
            self.sems[key] = None if self.dry else self.es.enter_context(self.nc.semaphore(key))
            self.cnt[key] = 0

    def _wait(self, eng, tok):
        if tok is None:
            return
        key, val = tok
        if self.waited[eng].get(key, 0) >= val:
            return
        self.waited[eng][key] = val
        self.q[eng].append(("w", key, val))

    def _deps(self, eng, reads, writes, extra):
        for b in reads:
            self._wait(eng, b.w)
            if b.excl:
                for k, v in b.r.items():
                    if k != eng:
                        self._wait(eng, (k, v))
        for b in writes:
            for bb in [b] + b.al:
                self._wait(eng, bb.w)
                for k, v in bb.r.items():
                    self._wait(eng, (k, v))
        for t in extra:
            self._wait(eng, t)

    def _commit(self, tok, reads, writes):
        k, v = tok
        for b in reads:
            b.r[k] = max(b.r.get(k, 0), v)
        for b in writes:
            b.w = tok
            b.r = {}

    def op(self, eng, fn, reads=(), writes=(), extra=()):
        self._deps(eng, reads, writes, extra)
        self.sem(eng)
        self.cnt[eng] += 1
        tok = (eng, self.cnt[eng])
        self.q[eng].append(("i", fn, eng, 1))
        self._commit(tok, reads, writes)
        return tok

    def mm(self, fns, reads=(), writes=(), extra=()):
        self._deps("pe", reads, writes, extra)
        for f in fns[:-1]:
            self.q["pe"].append(("i", f, None, 0))
        self.sem("pe")
        self.cnt["pe"] += 1
        tok = ("pe", self.cnt["pe"])
        self.q["pe"].append(("i", fns[-1], "pe", 1))
        self._commit(tok, reads, writes)
        return tok

    def dma(self, qeng, semkey, out, in_, reads=(), writes=(), extra=()):
        if writes:
            semkey = "dw" + qeng[0] + "_" + writes[0].name
        elif reads:
            semkey = "dr" + qeng[0] + "_" + reads[0].name
        for b in reads:
            self._wait(qeng, b.w)
        for b in writes:
            for bb in [b] + b.al:
                if not (bb.w is not None and bb.w[0] == semkey):
                    self._wait(qeng, bb.w)
                for k, v in bb.r.items():
                    self._wait(qeng, (k, v))
        for t in extra:
            self._wait(qeng, t)
        self.sem(semkey)
        self.cnt[semkey] += 16
        tok = (semkey, self.cnt[semkey])
        self.q[qeng].append(("i", (lambda e, o=out, i=in_: e.dma_start(out=o, in_=i)), semkey, 16))
        self._commit(tok, reads, writes)
        return tok

    def flush(self, block):
        def run(name):
            def f(e):
                for it in self.q[name]:
                    if it[0] == "w":
                        e.wait_ge(self.sems[it[1]], it[2])
                    else:
                        ins = it[1](e)
                        if it[3]:
                            ins.then_inc(self.sems[it[2]], it[3])
            return f
        block.tensor(run("pe"))
        block.scalar(run("act"))
        block.vector(run("dve"))
        block.gpsimd(run("pool"))
        block.sync(run("sp"))


class WStream:
    def __init__(self, P, ring_ap, sched, scratch_fn=None):
        self.P, self.ring = P, ring_ap
        self.sched = sched
        self.rec = []
        self.i = 0
        self.issued = 0
        self.slots = [Buf(f"ws{i}") for i in range(RING)]
        self.src = {}
        self.uidx, self.wtok = {}, {}
        self.scratch = None
        if sched is not None:
            cnt = {}
            for k in sched:
                cnt[k] = cnt.get(k, 0) + 1
            for k in sched:
                if cnt[k] > 1 and k not in self.uidx:
                    self.uidx[k] = len(self.uidx)
            if scratch_fn is not None and self.uidx:
                self.scratch = scratch_fn(len(self.uidx))

    def _issue(self, j):
        key = self.sched[j]
        name, m = key
        s = j % RING
        if self.scratch is not None and key in self.wtok:
            self.P.dma("sp", f"ws{s}", self.ring[:, s], self.scratch[self.uidx[key]], writes=[self.slots[s]],
                       extra=[self.wtok[key]])
            return
        for (dst_fn, src_ap) in self.src[name](m):
            self.P.dma("pool", f"ws{s}", dst_fn(self.ring[:, s]), src_ap, writes=[self.slots[s]])
        if self.scratch is not None and key in self.uidx:
            self.wtok[key] = self.P.dma("sp", f"sw{s}", self.scratch[self.uidx[key]], self.ring[:, s], reads=[self.slots[s]])

    def get(self, name, m):
        if self.sched is None:
            self.rec.append((name, m))
            return self.ring[:, 0], self.slots[0]
        assert self.sched[self.i] == (name, m), (self.i, self.sched[self.i], name, m)
        while self.issued < min(len(self.sched), self.i + RING - 3):
            self._issue(self.issued)
            self.issued += 1
        s = self.i % RING
        self.i += 1
        return self.ring[:, s], self.slots[s]


def build(nc, es, dry, sched):
    P = Prog(nc, es, dry)

    def din(name, shape):
        return nc.dram_tensor(name, list(shape), F32, kind="ExternalInput").ap()

    def dout(name, shape):
        return nc.dram_tensor(name, list(shape), F32, kind="ExternalOutput").ap()

    if not dry:
        xp = din("xp", [128 + SEQ_CORE, D]); xs = din("xs", [128, D]); mem = din("mem", [256, D])
        ck = din("ck", [16, 128, 128]); cv = din("cv", [16, 128, 128]); spool = din("spool", [16, 15, 512])
        cmk = din("cmk", [16, 256, D]); cmv = din("cmv", [16, 256, D])
        w_in = din("w_in", [D, 1280]); w_pool = din("w_pool", [4, 128, 128]); w_out = din("w_out", [D, D])
        w_cq = din("w_cq", [D, D]); w_ck = din("w_ck", [D, D]); w_cv = din("w_cv", [D, D]); w_co = din("w_co", [D, D])
        w_up = din("w_up", [D, 4 * D]); w_down = din("w_down", [4 * D, D])
        gvec_d = din("gvec", [128, 40]); pscale_d = din("pscale", [128, 4])
        sinkp_d = din("sinkp", [128, 8]); sinks_d = din("sinks", [128, 8])
        biasg_d = din("biasg", [128, 8 * 256]); biasf_d = din("biasf", [128, 8 * 256]); biass_d = din("biass", [128, 2 * 160])
        invc_d = din("invc", [128, 64])
        yp = dout("yp", [SEQ_CORE, D]); ys = dout("ys", [128, D])
        wkp = dout("wkp", [128, 128]); wvp = dout("wvp", [128, 128]); poolp = dout("poolp", [15, 512])
        memk_o = dout("memk", [256, D]); memv_o = dout("memv", [256, D])
        wks = dout("wks", [16, 128, 128]); wvs = dout("wvs", [16, 128, 128]); pools = dout("pools", [16, 15, 512])

    def sb(name, shape, dt):
        return es.enter_context(nc.sbuf_tensor("sb_" + name, list(shape), dt))

    xTs = [sb(f"xT{i}", [128, 8, NT_P], F32) for i in range(2)]; B_xTs = [Buf(f"xT{i}") for i in range(2)]
    xT, B_xT = xTs[0], B_xTs[0]
    hT = sb("hT", [128, 8, NT_P], BF16); B_hT = Buf("hT")
    rstd = sb("rstd", [128, NT_P], F32); B_rstd = Buf("rstd")
    aoT = sb("aoT", [128, 8, NT_P], BF16); B_aoT = Buf("aoT")
    qT = sb("qT", [128, 4, NT_P], BF16); B_qT = Buf("qT")
    kT = sb("kT", [128, 128 + NT_P], BF16); B_kT = Buf("kT")
    vT = sb("vT", [128, NT_P], BF16); B_vT = Buf("vT")
    vtok = sb("vtok", [128, 5, 128], BF16); B_vtok = Buf("vtok")
    kv32 = sb("kv32", [128, 2, 128], F32); B_kv32 = Buf("kv32")
    dT = sb("dT", [128, 4, NT_P], BF16); B_dT = Buf("dT")
    pexp = sb("pexp", [128, 8, 256], BF16); B_pexpH = [Buf("pexp0"), Buf("pexp1")]; B_pexp = B_pexpH[0]
    pT = sb("pT", [128, 8, 2, 128], BF16); B_pTH = [Buf("pT0"), Buf("pT1")]; B_pT = B_pTH[0]
    Dg = sb("Dg", [128, 8, 128], BF16); B_DgH = [Buf("Dg0"), Buf("Dg1")]; B_Dg = B_DgH[0]
    pTc = sb("pTc", [128, 4, 2, 128], BF16); B_pTc = Buf("pTc")
    bias = sb("bias", [128, 8, 256], F32); B_bias = Buf("bias")
    biass = sb("biass", [128, 2, 160], F32); B_biass = Buf("biass")
    memkT = sb("memkT", [128, 8, 256], BF16); B_memkT = Buf("memkT")
    memv = sb("memv", [128, 2, D], BF16); B_memv = Buf("memv")
    ring = sb("ring", [128, RING, 8, 128], BF16)
    xin = [sb(f"xin{i}", [128, D], F32) for i in range(2)]; B_xin = [Buf(f"xin{i}") for i in range(2)]
    yst1 = sb("yst1", [128, D], F32); yst, B_yst = [None, yst1], [None, Buf("yst1")]
    ident = sb("ident", [128, 128], BF16); identf = sb("identf", [128, 128], F32); B_const = Buf("const")
    ones = sb("ones", [128, 128], BF16)
    gvec = sb("gvec", [128, 5, 8], F32); pscale = sb("pscale", [128, 4], F32)
    sinkp = sb("sinkp", [128, 8], F32); sinks = sb("sinks", [128, 8], F32)
    invc = sb("invc", [128, 4, 16], F32)
    wpool = sb("wpool", [128, 4, 128], BF16); B_wpool = Buf("wpool")
    st = sb("st", [128, 64], F32); B_stH = [Buf("st0"), Buf("st1")]; B_st = B_stH[0]
    relu_t = [sb(f"relu{i}", [128, NT_P], BF16) for i in range(2)]; B_relu = [Buf(f"relu{i}") for i in range(2)]
    ost = sb("ost", [128, 512], F32); B_ost = Buf("ost")
    carryU = sb("carryU", [128, 4, 16], F32); B_cU = Buf("carryU")
    sqb = [sb(f"sq{i}", [128, NT_P], BF16) for i in range(2)]; B_sq = [Buf(f"sq{i}") for i in range(2)]
    rstd2 = sb("rstd2", [128, NT_P], F32); B_rstd2 = Buf("rstd2")

    R2 = 32 * NT_P * 2
    XO = R2 + 28672
    AR = XO + 10240
    arena = sb("arena", [128, AR // 2], BF16)

    def av(off, nbytes, dt, pat=None, **kw):
        v = arena[:, off // 2:(off + nbytes) // 2]
        if dt is F32:
            v = v.bitcast(F32)
        if pat:
            v = v.rearrange(pat, **kw)
        return v

    hidT = av(0, 32 * NT_P * 2, BF16, "p (k t) -> p k t", k=32); B_hid = Buf("hidT")
    yT = av(0, 8 * NT_P * 4, F32, "p (k t) -> p k t", k=8); B_yT = Buf("yT")
    WU = 16 + NT_P
    WH = 16 + 256
    U = av(R2, 4 * WU * 4, F32, "p (g t) -> p g t", g=4); B_U = Buf("U")
    SA = av(R2 + 4 * WU * 4, 4 * WH * 4, F32, "p (g t) -> p g t", g=4); B_SA = Buf("SA")
    SB = av(R2 + 4 * WU * 4 + 4 * WH * 4, 4 * WH * 4, F32, "p (g t) -> p g t", g=4); B_SB = Buf("SB")
    o_sb = R2 + 4 * WU * 4 + 8 * WH * 4
    sbias = av(o_sb, 8 * 256 * 4, F32, "p (u t) -> p u t", u=8); B_sbiasH = [Buf("sbias0"), Buf("sbias1")]; B_sbias = B_sbiasH[0]
    assert o_sb + 8192 <= XO
    kcT = av(XO, 16 * 128 * 2, BF16, "p (b t) -> p b t", b=16); B_kcT = Buf("kcT")
    vc = av(XO + 4096, 16 * 128 * 2, BF16, "p (b t) -> p b t", b=16); B_vc = Buf("vc")
    qs2 = av(XO + 8192, 16 * 32 * 2, BF16, "p (b t) -> p b t", b=16); B_qs2 = Buf("qs2")
    vnq = av(XO + 9216, 4 * 128 * 2, BF16, "p (i t) -> p i t", i=4); B_vnq = Buf("vnq")
    Kb = [av(R2 + i * 4096, 4096, BF16, "p (m t) -> p m t", m=2) for i in range(2)]; B_Kb = [Buf(f"Kb{i}") for i in range(2)]
    KbT = [av(R2 + 8192 + i * 4096, 4096, BF16, "p (c t) -> p c t", c=8) for i in range(2)]; B_KbT = [Buf(f"KbT{i}") for i in range(2)]
    Vb = [av(R2 + 16384 + i * 4096, 4096, BF16, "p (m t) -> p m t", m=2) for i in range(2)]; B_Vb = [Buf(f"Vb{i}") for i in range(2)]
    qpad = [av(R2 + 24576 + i * 2048, 2048, BF16, "p (c t) -> p c t", c=8) for i in range(2)]; B_qpad = [Buf(f"qpad{i}") for i in range(2)]
    memst = av(0, 8192, F32, "p (m t) -> p m t", m=2); B_memst = Buf("memst")
    mkst = av(8192, 8192, F32, "p (m t) -> p m t", m=2); B_mkst = Buf("mkst")
    alias(B_hid, B_yT)
    gX = [B_U, B_SA, B_SB] + B_sbiasH
    gY = B_Kb + B_KbT + B_Vb + B_qpad
    gZ = [B_memst, B_mkst]
    for ga, gb in ((gX, gY), ([B_hid, B_yT], gZ)):
        for a in ga:
            for b in gb:
                a.al.append(b)
                b.al.append(a)

    ps = [es.enter_context(nc.psum_tensor(f"ps{i}", [128, 512], F32)) for i in range(8)]
    B_ps = [Buf(f"ps{i}", excl=True) for i in range(8)]
    dctr = [0]

    DB = [[0, 1, 2]]

    def dbank():
        i = DB[0][dctr[0] % len(DB[0])]
        dctr[0] += 1
        return ps[i], B_ps[i]
    PS_S = [3, 4]
    PS_T = [5, 6]
    PS_O = 7
    sctr = [0]
    tctr = [0]

    def sbank():
        i = PS_S[sctr[0] % 2]; sctr[0] += 1
        return ps[i], B_ps[i]

    def tbank():
        i = PS_T[tctr[0] % 2]; tctr[0] += 1
        return ps[i], B_ps[i]

    def scratch_fn(n):
        return nc.dram_tensor("wscratch", [n, 128, 8, 128], BF16, kind="Internal").ap()
    W = WStream(P, ring, sched, scratch_fn)
    if not dry:
        def std_src(wap):
            v = wap.rearrange("(k p) (m c) -> p m k c", p=128, c=128)
            return lambda m: [((lambda s: s), v[:, m])]
        W.src["ck"] = std_src(w_ck); W.src["cv"] = std_src(w_cv)
        W.src["cq"] = std_src(w_cq); W.src["co"] = std_src(w_co); W.src["up"] = std_src(w_up)
        vin_q = w_in[:, 0:512].rearrange("(k p) (kv g d) -> p g k kv d", p=128, kv=2, g=4, d=64)
        vin_r = w_in[:, 512:1280].rearrange("(k p) (m c) -> p m k c", p=128, c=128)

        def in_src(m):
            if m < 4:
                return [((lambda s: s[:, :, 0:64]), vin_q[:, m, :, 0, :]),
                        ((lambda s: s[:, :, 64:128]), vin_q[:, m, :, 1, :])]
            return [((lambda s: s), vin_r[:, m - 4])]
        W.src["in"] = in_src
        vo_a = w_out[0:512, :].rearrange("(kv g d) (m c) -> kv d m g c", kv=2, g=4, d=64, c=128)
        vo_p = w_out[512:1024, :].rearrange("(k p) (m c) -> p m k c", p=128, c=128)

        def out_src(m):
            return [((lambda s: s[0:64, 0:4, :]), vo_a[0, :, m]),
                    ((lambda s: s[64:128, 0:4, :]), vo_a[1, :, m]),
                    ((lambda s: s[:, 4:8, :]), vo_p[:, m])]
        W.src["out"] = out_src
        vdn = w_down.rearrange("(q k p) (m c) -> p m q k c", p=128, k=8, c=128)
        W.src["down"] = lambda mq: [((lambda s: s), vdn[:, mq // 4, mq % 4])]

    if not dry:
        P.op("pool", lambda e: e.memset(identf[:], 0.0), writes=[B_const])
        P.op("pool", lambda e: e.iota(identf[:], pattern=[[1, 128]], base=0, channel_multiplier=-1,
                                      allow_small_or_imprecise_dtypes=True), writes=[B_const])
        P.op("dve", lambda e: e.tensor_single_scalar(out=ident[:], in_=identf[:], scalar=0.0, op=ALU.is_equal),
             reads=[B_const], writes=[B_const])
        P.op("dve", lambda e: e.tensor_single_scalar(out=identf[:], in_=identf[:], scalar=0.0, op=ALU.is_equal),
             writes=[B_const])
        P.op("dve", lambda e: e.memset(ones[:], 1.0), writes=[B_const])
        for (dst, src) in ((gvec[:].rearrange("p a b -> p (a b)"), gvec_d), (pscale[:], pscale_d), (sinkp[:], sinkp_d),
                           (sinks[:], sinks_d), (invc[:].rearrange("p a b -> p (a b)"), invc_d),
                           (biass[:].rearrange("p a b -> p (a b)"), biass_d)):
            P.dma("sp", "cst", dst, src[:, :], writes=[B_const])
        P.dma("sp", "biasld", bias[:].rearrange("p a b -> p (a b)"), biasf_d[:, :], writes=[B_bias])
        P.dma("pool", "wpool", wpool[:], w_pool.rearrange("g c e -> c g e"), writes=[B_wpool])

    CONST = [B_const]

    class _Stop(Exception):
        pass

    def finish():
        for key, val in P.cnt.items():
            if key not in ("pe", "act", "dve", "pool"):
                P._wait("sp", (key, val))
        for e_ in ("pe", "act", "dve", "pool"):
            if P.cnt.get(e_, 0):
                P._wait("sp", (e_, P.cnt[e_]))
        return P, W
    if STAGE == -1:
        return finish()

    def run(gen):
        if gen is not None:
            for _ in gen:
                pass

    def advance(gen, n=1, until=None):
        if gen is None:
            return
        if until is not None:
            for v in gen:
                if v == until:
                    return
            return
        for _ in range(n):
            try:
                next(gen)
            except StopIteration:
                return

    def g_load_x(src_rows, ntiles, X):
        dst, dstB = xTs[X], B_xTs[X]
        for j in range(min(2, ntiles)):
            P.dma("sp", "x", xin[j % 2][:], src_rows(j), writes=[B_xin[j % 2]])
        for j in range(ntiles):
            xb, Bx = xin[j % 2], B_xin[j % 2]
            for hf in range(2):
                pb, Bp = dbank()
                pv = pb[:].rearrange("p (c t) -> p c t", c=4)
                P.mm([(lambda e, c=c, pv=pv, xb=xb, hf=hf: e.transpose(out=pv[:, c, :], in_=xb[:, (hf * 4 + c) * 128:(hf * 4 + c + 1) * 128],
                                                                       identity=identf[:])) for c in range(4)],
                     reads=[Bx] + CONST, writes=[Bp])
                P.op("act" if hf == 0 else "dve",
                     (lambda e, pv=pv, hf=hf, j=j: e.activation(out=dst[:, hf * 4:hf * 4 + 4, j * 128:(j + 1) * 128], in_=pv, func=AF.Copy))
                     if hf == 0 else
                     (lambda e, pv=pv, hf=hf, j=j: e.tensor_copy(out=dst[:, hf * 4:hf * 4 + 4, j * 128:(j + 1) * 128], in_=pv)),
                     reads=[Bp], writes=[dstB])
                if hf == 1 and j + 2 < ntiles:
                    P.dma("sp", "x", xb[:], src_rows(j + 2), writes=[Bx])
                yield

    def norm(src, Bsrc, gi, NT, dst, Bdst):
        P.op("act", lambda e: e.activation(out=hT[:, :, :NT], in_=src[:, :, :NT], func=AF.Square),
             reads=[Bsrc], writes=[B_hT])
        pb, Bp = dbank()
        P.mm([(lambda e, k=k: e.matmul(pb[:, :NT], lhsT=ones[:], rhs=hT[:, k, :NT], start=(k == 0), stop=(k == 7)))
              for k in range(8)], reads=[B_hT] + CONST, writes=[Bp])
        P.op("act", lambda e: e.activation(out=rstd[:, :NT], in_=pb[:, :NT], func=AF.Ln, scale=1.0 / D, bias=EPS),
             reads=[Bp], writes=[B_rstd])
        P.op("act", lambda e: e.activation(out=rstd[:, :NT], in_=rstd[:, :NT], func=AF.Exp, scale=-0.5),
             reads=[B_rstd], writes=[B_rstd])
        for k in range(8):
            P.op("dve", lambda e, k=k: e.scalar_tensor_tensor(out=dst[:, k, :NT], in0=src[:, k, :NT], scalar=gvec[:, gi, k:k + 1],
                                                              in1=rstd[:, :NT], op0=ALU.mult, op1=ALU.mult),
                 reads=[Bsrc, B_rstd] + CONST, writes=[Bdst])

    def g_dense(wname, units, NT, rhs_fn, Brhs, evac, kgroups=1):
        for m in units:
            pb, Bp = dbank()
            fns, Bs = [], []
            for q in range(kgroups):
                slot, Bslot = W.get(wname, m * kgroups + q if kgroups > 1 else m)
                Bs.append(Bslot)
                for k in range(8):
                    fns.append(lambda e, slot=slot, k=k, q=q, pb=pb: e.matmul(
                        pb[:, :NT], lhsT=slot[:, k, :], rhs=rhs_fn(q * 8 + k),
                        start=(q == 0 and k == 0), stop=(q == kgroups - 1 and k == 7)))
            P.mm(fns, reads=Bs + [Brhs], writes=[Bp])
            evac(m, pb, Bp)
            for _ in range(kgroups):
                yield

    def dense(*a, **kw):
        run(g_dense(*a, **kw))

    def resid_evac(NT, X, prep=None, scale2=False):
        xt, Bxt = xTs[X], B_xTs[X]

        def f(m, pb, Bp):
            if scale2:
                P.op("dve", lambda e: e.tensor_tensor(out=pb[:, :NT], in0=pb[:, :NT], in1=rstd2[:, :NT], op=ALU.mult),
                     reads=[Bp, B_rstd2], writes=[Bp])
            P.op("dve", lambda e: e.tensor_tensor(out=xt[:, m, :NT], in0=pb[:, :NT], in1=xt[:, m, :NT], op=ALU.add),
                 reads=[Bp], writes=[Bxt])
            if prep is not None:
                prep[0](m)
        return f

    def make_prep(X, NT, gidx, exp_scale, rdst, Brdst):
        xt, Bxt = xTs[X], B_xTs[X]
        sp_, Bsp = ps[PS_O], B_ps[PS_O]
        pend = []

        def emit_mm(k):
            P.mm([lambda e, k=k: e.matmul(sp_[:, :NT], lhsT=ones[:], rhs=sqb[k % 2][:, :NT], start=(k == 0), stop=(k == 7))],
                 reads=[B_sq[k % 2]] + CONST, writes=[Bsp])

        def after(m):
            while len(pend) >= 2:
                emit_mm(pend.pop(0))
            P.op("act", lambda e: e.activation(out=hT[:, m, :NT], in_=xt[:, m, :NT], func=AF.Copy, scale=gvec[:, gidx, m:m + 1]),
                 reads=[Bxt] + CONST, writes=[B_hT])
            P.op("act", lambda e: e.activation(out=sqb[m % 2][:, :NT], in_=xt[:, m, :NT], func=AF.Square),
                 reads=[Bxt], writes=[B_sq[m % 2]])
            pend.append(m)

        def flush():
            while pend:
                emit_mm(pend.pop(0))
            P.op("act", lambda e: e.activation(out=rdst[:, :NT], in_=sp_[:, :NT], func=AF.Ln, scale=1.0 / D, bias=EPS),
                 reads=[Bsp], writes=[Brdst])
            P.op("act", lambda e: e.activation(out=rdst[:, :NT], in_=rdst[:, :NT], func=AF.Exp, scale=exp_scale),
                 reads=[Brdst], writes=[Brdst])
        return after, flush

    def diag_T(u0, nu, nkc, p_src, Bp_src, BDg, dst4, dst_fn, Bdst, kw):
        items = [(u, kc) for u in range(u0, u0 + nu) for kc in range(nkc)]
        full = all(w == 128 for w in kw)
        for bi, i0 in enumerate(range(0, len(items), 4)):
            chunk = items[i0:i0 + 4]
            pb, Bp = tbank()
            pv = pb[:].rearrange("p (s t) -> p s t", s=4)
            P.mm([(lambda e, s=s, u=u, kc=kc, pv=pv: e.matmul(pv[0:kw[kc], s, :], lhsT=p_src[:, u, kc * 128:kc * 128 + kw[kc]],
                                                              rhs=Dg[:, u, :], start=True, stop=True))
                  for s, (u, kc) in enumerate(chunk)], reads=list(Bp_src) + list(BDg), writes=[Bp])
            if full:
                uu = chunk[0][0]
                if bi % 2 == 0:
                    P.op("act", lambda e, pb=pb, uu=uu: e.activation(out=dst4(uu), in_=pb[:, 0:512], func=AF.Copy), reads=[Bp], writes=list(Bdst))
                else:
                    P.op("dve", lambda e, pb=pb, uu=uu: e.tensor_copy(out=dst4(uu), in_=pb[:, 0:512]), reads=[Bp], writes=list(Bdst))
            else:
                for s, (u, kc) in enumerate(chunk):
                    P.op("act" if bi % 2 == 0 else "dve",
                         (lambda e, s=s, u=u, kc=kc, pv=pv: e.activation(out=dst_fn(u, kc), in_=pv[0:kw[kc], s, :], func=AF.Copy))
                         if bi % 2 == 0 else
                         (lambda e, s=s, u=u, kc=kc, pv=pv: e.tensor_copy(out=dst_fn(u, kc), in_=pv[0:kw[kc], s, :])),
                         reads=[Bp], writes=list(Bdst))
            yield

    def make_Dg(u0, nu, Bst, BDg):
        P.op("dve", lambda e: e.tensor_tensor(out=Dg[:, u0:u0 + nu, :], in0=ident[:].unsqueeze(1).to_broadcast([128, nu, 128]),
                                              in1=st[:, 56 + u0:56 + u0 + nu].unsqueeze(2).to_broadcast([128, nu, 128]), op=ALU.mult),
             reads=list(Bst) + CONST, writes=list(BDg))

    def win_softmax(u0, nu, width, sink_ap, hs):
        Bsb = [B_sbiasH[h] for h in hs]; Bst = [B_stH[h] for h in hs]
        Bpe = [B_pexpH[h] for h in hs]; BDg = [B_DgH[h] for h in hs]
        c = lambda base: slice(base + u0, base + u0 + nu)
        P.op("dve", lambda e: e.tensor_reduce(out=st[:, c(0)], in_=sbias[:, u0:u0 + nu, 0:width], axis=AX.X, op=ALU.max),
             reads=Bsb, writes=Bst)
        P.op("dve", lambda e: e.tensor_tensor(out=st[:, c(8)], in0=st[:, c(0)], in1=sink_ap, op=ALU.max),
             reads=Bst + CONST, writes=Bst)
        P.op("dve", lambda e: e.tensor_scalar(out=st[:, c(16)], in0=st[:, c(8)], scalar1=-1.0, scalar2=None, op0=ALU.mult),
             reads=Bst, writes=Bst)
        P.op("dve", lambda e: e.tensor_tensor(out=st[:, c(24)], in0=sink_ap, in1=st[:, c(16)], op=ALU.add),
             reads=Bst + CONST, writes=Bst)
        for u in range(u0, u0 + nu):
            P.op("act", lambda e, u=u: e.activation(out=pexp[:, u, 0:width], in_=sbias[:, u, 0:width], func=AF.Exp,
                                                    bias=st[:, 16 + u:17 + u], scale=1.0, accum_out=st[:, 32 + u:33 + u]),
                 reads=Bsb + Bst, writes=Bpe + Bst)
        P.op("act", lambda e: e.activation(out=st[:, c(40)], in_=st[:, c(24)], func=AF.Exp),
             reads=Bst, writes=Bst)
        yield
        P.op("dve", lambda e: e.tensor_tensor(out=st[:, c(48)], in0=st[:, c(32)], in1=st[:, c(40)], op=ALU.add),
             reads=Bst, writes=Bst)
        P.op("dve", lambda e: e.reciprocal(out=st[:, c(56)], in_=st[:, c(48)]), reads=Bst, writes=Bst)
        make_Dg(u0, nu, Bst, BDg)

    def cross_softmax(score_banks):
        for hp, (pb, Bp) in enumerate(score_banks):
            pv = pb[:].rearrange("p (h t) -> p h t", h=2)
            P.op("dve", lambda e, pv=pv, hp=hp: e.tensor_reduce(out=st[:, 2 * hp:2 * hp + 2], in_=pv, axis=AX.X, op=ALU.max),
                 reads=[Bp], writes=[B_st])
        P.op("dve", lambda e: e.tensor_scalar(out=st[:, 16:20], in0=st[:, 0:4], scalar1=-1.0 / 16.0, scalar2=None, op0=ALU.mult),
             reads=[B_st], writes=[B_st])
        for hp, (pb, Bp) in enumerate(score_banks):
            pv = pb[:].rearrange("p (h t) -> p h t", h=2)
            for hh in range(2):
                h = 2 * hp + hh
                P.op("act", lambda e, pv=pv, hh=hh, h=h: e.activation(out=pexp[:, h, :], in_=pv[:, hh, :], func=AF.Exp,
                                                                      bias=st[:, 16 + h:17 + h], scale=1.0 / 16.0,
                                                                      accum_out=st[:, 32 + h:33 + h]),
                     reads=[Bp, B_st], writes=[B_pexp, B_st])
        P.op("dve", lambda e: e.reciprocal(out=st[:, 56:60], in_=st[:, 32:36]), reads=[B_st], writes=[B_st])
        make_Dg(0, 4, [B_st], [B_Dg])

    def out_tok_major(srcs, Bsrcs, ncols_each, dst_dma):
        pb, Bp = dbank()
        pv = pb[:].rearrange("p (c t) -> p c t", c=4)
        n = len(srcs)
        P.mm([(lambda e, i=i: e.transpose(out=pv[:, i, :], in_=srcs[i], identity=identf[:])) for i in range(n)],
             reads=list(Bsrcs) + CONST, writes=[Bp])
        P.op("dve", lambda e: e.tensor_copy(out=ost[:, 0:n * 128], in_=pb[:, 0:n * 128]), reads=[Bp], writes=[B_ost])
        dst_dma()

    def mem_setup(gen):
        mx, Bmx = xTs[1], B_xTs[1]
        for t in range(2):
            P.dma("sp", "memld", memst[:, t, :], mem[t * 128:(t + 1) * 128, :], writes=[B_memst])
        for t in range(2):
            for hf in range(2):
                pb, Bp = dbank()
                pv = pb[:].rearrange("p (c t) -> p c t", c=4)
                P.mm([(lambda e, c=c, pv=pv, t=t, hf=hf: e.transpose(out=pv[:, c, :], in_=memst[:, t, (hf * 4 + c) * 128:(hf * 4 + c + 1) * 128],
                                                                     identity=identf[:])) for c in range(4)],
                     reads=[B_memst] + CONST, writes=[Bp])
                P.op("act", lambda e, pv=pv, hf=hf, t=t: e.activation(out=mx[:, hf * 4:hf * 4 + 4, t * 128:(t + 1) * 128], in_=pv, func=AF.Copy),
                     reads=[Bp], writes=[Bmx])
        norm(mx, Bmx, 2, 256, hT, B_hT)
        for (wn, is_k) in (("ck", True), ("cv", False)):
            for m in range(8):
                slot, Bslot = W.get(wn, m)
                if is_k:
                    pb, Bp = dbank()
                    P.mm([(lambda e, k=k, slot=slot, pb=pb: e.matmul(pb[:, :256], lhsT=slot[:, k, :], rhs=hT[:, k, :256],
                                                                    start=(k == 0), stop=(k == 7))) for k in range(8)],
                         reads=[Bslot, B_hT], writes=[Bp])
                    P.op("act", lambda e, m=m, pb=pb: e.activation(out=memkT[:, m, :], in_=pb[:, :256], func=AF.Copy),
                         reads=[Bp], writes=[B_memkT])
                pb, Bp = dbank()
                fns = []
                for t in range(2):
                    for k in range(8):
                        fns.append(lambda e, k=k, slot=slot, pb=pb, t=t: e.matmul(
                            pb[:, t * 128:(t + 1) * 128], lhsT=hT[:, k, t * 128:(t + 1) * 128], rhs=slot[:, k, :],
                            start=(k == 0), stop=(k == 7)))
                P.mm(fns, reads=[Bslot, B_hT], writes=[Bp])
                pv2 = pb[:, 0:256].rearrange("p (t c) -> p t c", t=2)
                P.op("dve", lambda e, pv2=pv2, m=m: e.tensor_copy(out=mkst[:, :, m * 128:(m + 1) * 128], in_=pv2),
                     reads=[Bp], writes=[B_mkst])
                if not is_k:
                    P.op("act", lambda e, pv2=pv2, m=m: e.activation(out=memv[:, :, m * 128:(m + 1) * 128], in_=pv2, func=AF.Copy),
                         reads=[Bp], writes=[B_memv])
                advance(gen, 3)
            for t in range(2):
                P.dma("pool", "memout", (memk_o if is_k else memv_o)[t * 128:(t + 1) * 128, :], mkst[:, t, :], reads=[B_mkst])

    def geom(kind):
        sample, halo = (kind == "S"), (kind == "H")
        NT = 128 if (sample or halo) else NT_P
        return sample, halo, NT, NT // 128

    def early(kind, gi, X):
        sample, halo, NT, ntl = geom(kind)
        xt, Bxt = xTs[X], B_xTs[X]
        if sample:
            yield from g_load_x(lambda j: xs[:, :], 1, X)
        elif halo:
            yield from g_load_x(lambda j: xp[0:128, :], 1, X)
        else:
            yield from g_load_x(lambda j: xp[128 + gi * NT_P + j * 128: 128 + gi * NT_P + (j + 1) * 128, :], ntl, X)
        yield "L"
        prep1 = make_prep(X, NT, 0, -0.5, rstd, B_rstd)
        for m in range(8):
            prep1[0](m)
            if m % 2 == 1:
                yield
        prep1[1]()
        yield
        last = (kind == "P" and gi == NG_P - 1)
        want32 = last or sample
        Us = U[:, :, 0:384].rearrange("p g (b c) -> p g b c", b=16)

        if sample:
            P.op("dve", lambda e: e.memset(U[:, :, 0:384], 0.0), writes=[B_U])
            for hb in range(2):
                P.dma("sp", "spld", xin[hb][0:120, 0:512], spool[hb * 8:(hb + 1) * 8].rearrange("b r f -> (b r) f"), writes=[B_xin[hb]])
                pb, Bp = dbank()
                pv = pb[:].rearrange("p (c t) -> p c t", c=4)
                P.mm([(lambda e, c=c, pv=pv, hb=hb: e.transpose(out=pv[:, c, 0:120], in_=xin[hb][0:120, c * 128:(c + 1) * 128],
                                                                identity=identf[0:120, 0:120])) for c in range(4)],
                     reads=[B_xin[hb]] + CONST, writes=[Bp])
                for c in range(4):
                    P.op("dve", lambda e, c=c, pv=pv, hb=hb: e.tensor_copy(
                        out=Us[:, c, hb * 8:(hb + 1) * 8, 1:16], in_=pv[:, c, 0:120].rearrange("p (b r) -> p b r", b=8)),
                        reads=[Bp], writes=[B_U])
            yield

        def in_evac(m, pb, Bp):
            rd = [Bp, B_rstd]
            if m < 4:
                P.op("dve", lambda e: e.tensor_tensor(out=qT[:, m, :NT], in0=pb[:, :NT], in1=rstd[:, :NT], op=ALU.mult), reads=rd, writes=[B_qT])
            elif m == 4 or m == 5:
                dst_, Bd_ = (kT[:, 128:128 + NT], B_kT) if m == 4 else (vT[:, :NT], B_vT)
                P.op("dve", lambda e: e.tensor_tensor(out=dst_, in0=pb[:, :NT], in1=rstd[:, :NT], op=ALU.mult), reads=rd, writes=[Bd_])
                if want32:
                    P.op("dve", lambda e: e.tensor_tensor(out=kv32[:, m - 4, :], in0=pb[:, NT - 128:NT], in1=rstd[:, NT - 128:NT], op=ALU.mult),
                         reads=rd, writes=[B_kv32])
            else:
                g = m - 6
                if sample:
                    P.op("dve", lambda e: e.tensor_tensor(out=Us[:, g, :, 16:24], in0=pb[:, 0:128].rearrange("p (b t) -> p b t", b=16),
                                                          in1=rstd[:, 0:128].rearrange("p (b t) -> p b t", b=16), op=ALU.mult),
                         reads=rd, writes=[B_U])
                else:
                    P.op("dve", lambda e: e.tensor_tensor(out=U[:, g, 16:16 + NT], in0=pb[:, :NT], in1=rstd[:, :NT], op=ALU.mult), reads=rd, writes=[B_U])
        yield from g_dense("in", list(range(4, 10)) if halo else list(range(10)), NT, lambda k: hT[:, k, :NT], B_hT, in_evac)
        yield "P1"

        if sample:
            for hb in range(2):
                P.dma("pool", "ckld", vc[:, hb * 8:(hb + 1) * 8, :], cv[hb * 8:(hb + 1) * 8].rearrange("b s f -> s b f"), writes=[B_vc])
            kst = pexp[:].rearrange("p u t -> p (u t)").rearrange("p (b f) -> p b f", b=16)
            for hb in range(2):
                P.dma("pool", "ckld", kst[:, hb * 8:(hb + 1) * 8, :], ck[hb * 8:(hb + 1) * 8].rearrange("b s f -> s b f"), writes=[B_pexpH[hb]])
            for hb in range(2):
                pb, Bp = tbank()
                pv = pb[:].bitcast(BF16).rearrange("p (b t) -> p b t", b=8)
                P.mm([(lambda e, b=b, pv=pv, hb=hb: e.transpose(out=pv[:, b, :], in_=kst[:, hb * 8 + b, :], identity=ident[:])) for b in range(8)],
                     reads=[B_pexpH[hb]] + CONST, writes=[Bp])
                P.op("dve", lambda e, pv=pv, hb=hb: e.tensor_copy(out=kcT[:, hb * 8:(hb + 1) * 8, :], in_=pv), reads=[Bp], writes=[B_kcT])
            yield

        for j0 in range(0, ntl, 4):
            pb, Bp = tbank()
            pv = pb[:].bitcast(BF16)[:, 0:512].rearrange("p (j t) -> p j t", j=4)
            P.mm([(lambda e, j=j, pv=pv: e.transpose(out=pv[:, j - j0, :], in_=vT[:, j * 128:(j + 1) * 128], identity=ident[:]))
                  for j in range(j0, min(ntl, j0 + 4))], reads=[B_vT] + CONST, writes=[Bp])
            nj = min(ntl, j0 + 4) - j0
            P.op("dve", lambda e, pv=pv, j0=j0, nj=nj: e.tensor_copy(out=vtok[:, 1 + j0:1 + j0 + nj, :], in_=pv[:, 0:nj, :]),
                 reads=[Bp], writes=[B_vtok])
        yield

        def carry():
            P.op("dve", lambda e: e.tensor_copy(out=kT[:, 0:128], in_=kT[:, NT:NT + 128]), reads=[B_kT], writes=[B_kT])
            P.op("dve", lambda e: e.tensor_copy(out=vtok[:, 0, :], in_=vtok[:, ntl, :]), reads=[B_vtok], writes=[B_vtok])

        if halo:
            carry()
            P.op("dve", lambda e: e.tensor_copy(out=carryU[:], in_=U[:, :, NT:NT + 16]), reads=[B_U], writes=[B_cU])
            return

        if want32:
            if last:
                def dd():
                    P.dma("pool", "kvout", wkp[:, :], ost[:, 0:128], reads=[B_ost])
                    P.dma("pool", "kvout", wvp[:, :], ost[:, 128:256], reads=[B_ost])
            else:
                def dd():
                    for t in range(8):
                        P.dma("pool", "kvout", wks[:, 120 + t, :], ost[t:128:8, 0:128], reads=[B_ost])
                        P.dma("pool", "kvout", wvs[:, 120 + t, :], ost[t:128:8, 128:256], reads=[B_ost])
                    P.dma("sp", "d2d_k", wks[:, 0:120, :], ck[:, 8:128, :])
                    P.dma("sp", "d2d_v", wvs[:, 0:120, :], cv[:, 8:128, :])
            out_tok_major([kv32[:, 0, :], kv32[:, 1, :]], [B_kv32], 128, dd)
            yield

        if not sample:
            P.op("dve", lambda e: e.tensor_copy(out=U[:, :, 0:16], in_=carryU[:]), reads=[B_cU], writes=[B_U])
        for hh in range(2):
            if sample:
                Wd, c0 = 192, hh * 192
            else:
                Wd, c0 = 16 + NT // 2, hh * (NT // 2)
            Uh = U[:, :, c0:c0 + Wd]
            P.op("dve", lambda e, Uh=Uh, Wd=Wd: e.tensor_tensor(out=SA[:, :, 1:Wd], in0=Uh[:, :, 1:Wd], in1=Uh[:, :, 0:Wd - 1], op=ALU.add),
                 reads=[B_U], writes=[B_SA])
            P.op("dve", lambda e, Wd=Wd: e.tensor_tensor(out=SB[:, 1:4, 3:Wd], in0=SA[:, 1:4, 3:Wd], in1=SA[:, 1:4, 1:Wd - 2], op=ALU.add),
                 reads=[B_SA], writes=[B_SB])
            yield
            P.op("dve", lambda e, Wd=Wd: e.tensor_tensor(out=SA[:, 2:4, 7:Wd], in0=SB[:, 2:4, 7:Wd], in1=SB[:, 2:4, 3:Wd - 4], op=ALU.add),
                 reads=[B_SB], writes=[B_SA])
            P.op("dve", lambda e, Wd=Wd: e.tensor_tensor(out=SB[:, 3, 15:Wd], in0=SA[:, 3, 15:Wd], in1=SA[:, 3, 7:Wd - 8], op=ALU.add),
                 reads=[B_SA], writes=[B_SB])
            yield
            for g in range(4):
                S_, BS_ = (SA, B_SA) if g % 2 == 0 else (SB, B_SB)
                if sample:
                    sv = S_[:, g, 0:192].rearrange("p (b c) -> p b c", b=8)[:, :, 16:24]
                    uv = Uh[:, g, :].rearrange("p (b c) -> p b c", b=8)[:, :, 16:24]
                    dv = dT[:, g, hh * 64:(hh + 1) * 64].rearrange("p (b t) -> p b t", b=8)
                else:
                    sv, uv, dv = S_[:, g, 16:Wd], Uh[:, g, 16:Wd], dT[:, g, c0:c0 + NT // 2]
                P.op("dve", lambda e, sv=sv, uv=uv, dv=dv, g=g: e.scalar_tensor_tensor(out=dv, in0=sv, scalar=1.0 / (2 << g), in1=uv,
                                                                                   op0=ALU.mult, op1=ALU.subtract),
                     reads=[BS_, B_U], writes=[B_dT])
                if kind == "P" and gi == 0 and hh == 0:
                    P.op("dve", lambda e, S_=S_, g=g: e.tensor_tensor(out=st[:, 0:16], in0=S_[:, g, 16:32], in1=invc[:, g, :], op=ALU.mult),
                         reads=[BS_] + CONST, writes=B_stH)
                    P.op("dve", lambda e, g=g: e.tensor_tensor(out=dT[:, g, 0:16], in0=st[:, 0:16], in1=U[:, g, 16:32], op=ALU.subtract),
                         reads=B_stH + [B_U], writes=[B_dT])
            yield
        if last:
            def dd2():
                P.dma("pool", "poolout", poolp[:, :], ost[113:128, :], reads=[B_ost])
            out_tok_major([U[:, g, 16 + NT - 128:16 + NT] for g in range(4)], [B_U], 128, dd2)
        if sample:
            for g in range(4):
                P.op("dve", lambda e, g=g: e.tensor_copy(out=SA[:, g, 0:128].rearrange("p (b t) -> p b t", b=16), in_=Us[:, g, :, 16:24]),
                     reads=[B_U, B_dT], writes=[B_SA])

            def dd3():
                for t in range(8):
                    P.dma("pool", "poolout", pools[:, 7 + t, :], ost[t:128:8, :], reads=[B_ost])
                P.dma("sp", "d2d_p", pools[:, 0:7, :], spool[:, 8:15, :])
            out_tok_major([SA[:, g, 0:128] for g in range(4)], [B_SA], 128, dd3)
        else:
            P.op("dve", lambda e: e.tensor_copy(out=carryU[:], in_=U[:, :, NT:NT + 16]), reads=[B_U], writes=[B_cU])
        yield
        for g in range(4):
            pb, Bp = dbank()
            P.mm([lambda e, g=g, pb=pb: e.matmul(pb[:, :NT], lhsT=wpool[:, g, :], rhs=dT[:, g, :NT], start=True, stop=True)],
                 reads=[B_wpool, B_dT], writes=[Bp])
            P.op("act", lambda e, g=g, pb=pb: e.activation(out=aoT[:, 4 + g, :NT], in_=pb[:, :NT], func=AF.Copy, scale=pscale[:, g:g + 1]),
                 reads=[Bp] + CONST, writes=[B_aoT])
            yield

        if not sample:
            for j in range(ntl):
                if gi == 0 and j == 1:
                    P.dma("sp", "biasld", bias[:].rearrange("p a b -> p (a b)"), biasg_d[:, :], writes=[B_bias])
                for gp in range(2):
                    bk = [sbank(), sbank()]
                    fns = []
                    for g in (2 * gp, 2 * gp + 1):
                        for kv in range(2):
                            fns.append(lambda e, kv=kv, g=g, j=j, bk=bk: e.matmul(
                                bk[kv][0][:, (g % 2) * 256:(g % 2 + 1) * 256], lhsT=qT[kv * 64:(kv + 1) * 64, g, j * 128:(j + 1) * 128],
                                rhs=kT[kv * 64:(kv + 1) * 64, j * 128:j * 128 + 256], start=True, stop=True))
                    P.mm(fns, reads=[B_qT, B_kT], writes=[bk[0][1], bk[1][1]])
                    for kv in range(2):
                        u0 = 4 * gp + kv
                        P.op("dve", lambda e, kv=kv, u0=u0, bk=bk: e.scalar_tensor_tensor(
                            out=sbias[:, u0:u0 + 3:2, :], in0=bk[kv][0][:].rearrange("p (g t) -> p g t", g=2), scalar=0.125,
                            in1=bias[:, u0:u0 + 3:2, :], op0=ALU.mult, op1=ALU.add),
                            reads=[bk[kv][1], B_bias], writes=[B_sbiasH[gp]])
                    yield
                sm = [win_softmax(4 * gp, 4, 256, sinkp[:, 4 * gp:4 * gp + 4], [gp]) for gp in range(2)]
                next(sm[0])
                yield
                next(sm[1])
                yield
                yield
                run(sm[0])
                yield
                run(sm[1])
                yield
                for gp in range(2):
                    yield from diag_T(4 * gp, 4, 2, pexp, [B_pexpH[gp]], [B_DgH[gp]],
                                      lambda u0: pT[:, u0:u0 + 2, :, :].rearrange("p u k t -> p (u k t)"), None, [B_pTH[gp]], [128, 128])
                yield
                po, Bpo = ps[PS_O], B_ps[PS_O]
                pov = po[:].rearrange("p (g t) -> p g t", g=4)
                fns = []
                for g in range(4):
                    for kv in range(2):
                        for kc in range(2):
                            fns.append(lambda e, g=g, kv=kv, kc=kc, j=j: e.matmul(
                                pov[kv * 64:(kv + 1) * 64, g, :], lhsT=vtok[:, j + kc, kv * 64:(kv + 1) * 64], rhs=pT[:, 2 * g + kv, kc, :],
                                start=(kc == 0), stop=(kc == 1)))
                P.mm(fns, reads=[B_vtok] + B_pTH, writes=[Bpo])
                P.op("act", lambda e, j=j: e.activation(out=aoT[:, 0:4, j * 128:(j + 1) * 128], in_=pov, func=AF.Copy),
                     reads=[Bpo], writes=[B_aoT])
                yield
            carry()
        else:
            P.op("dve", lambda e: e.tensor_copy(out=qs2[:].rearrange("p b (g t) -> p b g t", g=4),
                                                in_=qT[:, :, 0:128].rearrange("p g (b t) -> p b g t", b=16)), reads=[B_qT], writes=[B_qs2])
            pb, Bp = dbank()
            pvb = pb[:].bitcast(BF16)
            P.mm([(lambda e, i=i, pvb=pvb: e.transpose(out=pvb[0:32, i * 128:(i + 1) * 128], in_=vT[:, i * 32:(i + 1) * 32], identity=ident[:]))
                  for i in range(4)], reads=[B_vT] + CONST, writes=[Bp])
            P.op("dve", lambda e, pvb=pvb: e.tensor_copy(out=vnq[0:32, :, :], in_=pvb[0:32, 0:512].rearrange("p (i t) -> p i t", i=4)),
                 reads=[Bp], writes=[B_vnq])
            yield
            for i in range(4):
                bk = [sbank(), sbank()]
                fns = []
                for kv in range(2):
                    pvk = bk[kv][0]
                    for jq in range(4):
                        b = 4 * i + jq
                        fns.append(lambda e, kv=kv, jq=jq, b=b, pvk=pvk: e.matmul(
                            pvk[32 * jq:32 * jq + 32, 0:128], lhsT=qs2[kv * 64:(kv + 1) * 64, b, :], rhs=kcT[kv * 64:(kv + 1) * 64, b, :],
                            start=True, stop=True, tile_position=(kv * 64, 32 * jq)))
                    fns.append(lambda e, kv=kv, i=i, pvk=pvk: e.matmul(
                        pvk[:, 128:160], lhsT=qs2[kv * 64:(kv + 1) * 64, 4 * i:4 * i + 4, :].rearrange("p b t -> p (b t)"),
                        rhs=kT[kv * 64:(kv + 1) * 64, 128 + 32 * i:128 + 32 * i + 32], start=True, stop=True))
                P.mm(fns, reads=[B_qs2, B_kcT, B_kT], writes=[bk[0][1], bk[1][1]])
                for kv in range(2):
                    P.op("dve", lambda e, kv=kv, i=i, bk=bk: e.scalar_tensor_tensor(
                        out=sbias[:, 2 * i + kv, 0:160], in0=bk[kv][0][:, 0:160], scalar=0.125,
                        in1=biass[:, kv, :], op0=ALU.mult, op1=ALU.add),
                        reads=[bk[kv][1]] + CONST, writes=[B_sbiasH[i // 2]])
                yield
            smx = win_softmax(0, 8, 160, sinks[:, 0:8], [0, 1])
            next(smx)
            yield
            yield
            run(smx)
            yield
            yield from diag_T(0, 8, 2, pexp, B_pexpH, B_DgH, None, lambda u, kc: pT[0:(128 if kc == 0 else 32), u, kc, :], B_pTH, [128, 32])
            for i in range(4):
                pb, Bp = dbank()
                fns = []
                for kv in range(2):
                    u = 2 * i + kv
                    for jq in range(4):
                        b = 4 * i + jq
                        fns.append(lambda e, kv=kv, jq=jq, b=b, u=u, pb=pb: e.matmul(
                            pb[kv * 64:(kv + 1) * 64, 32 * jq:32 * jq + 32], lhsT=vc[:, b, kv * 64:(kv + 1) * 64], rhs=pT[:, u, 0, 32 * jq:32 * jq + 32],
                            start=(jq == 0), stop=False, skip_group_check=True))
                for kv in range(2):
                    u = 2 * i + kv
                    fns.append(lambda e, kv=kv, u=u, i=i, pb=pb: e.matmul(
                        pb[kv * 64:(kv + 1) * 64, 0:128], lhsT=vnq[0:32, i, kv * 64:(kv + 1) * 64], rhs=pT[0:32, u, 1, :],
                        start=False, stop=True, skip_group_check=True))
                P.mm(fns, reads=[B_vc, B_vnq] + B_pTH, writes=[Bp])
                P.op("act", lambda e, pb=pb, i=i: e.activation(
                    out=aoT[:, 0:4, 32 * i:32 * i + 32].rearrange("p g (j t) -> p j g t", j=4),
                    in_=pb[:, 0:128].rearrange("p (j g t) -> p j g t", j=4, g=4), func=AF.Copy),
                    reads=[Bp], writes=[B_aoT])
                yield

    def late_pre(kind, gi, X, gen, tgen=None):
        sample, halo, NT, ntl = geom(kind)
        xt, Bxt = xTs[X], B_xTs[X]
        loaded = [gen is None]
        xfree = [tgen is None]

        def step_tail(n):
            for _ in range(n):
                if tgen is not None and next(tgen, "END") == "XDONE":
                    xfree[0] = True

        def step_load():
            if not loaded[0] and xfree[0]:
                if next(gen, "L") == "L":
                    loaded[0] = True
        p1done = [gen is None]

        def step_p1(n):
            for _ in range(n):
                if not p1done[0]:
                    if next(gen, "P1") == "P1":
                        p1done[0] = True
        DB[0] = [0, 1, 2, 3, 4, 5, 6]
        prep2 = make_prep(X, NT, 1, -0.5, rstd, B_rstd)
        for _ in g_dense("out", list(range(8)), NT, lambda k: aoT[:, k, :NT], B_aoT, resid_evac(NT, X, prep=prep2)):
            step_load()
            step_tail(2)
        prep2[1]()
        qcT, B_qcT = aoT, B_aoT
        ocT, B_ocT = hidT, B_hid

        def cq_evac(m, pb, Bp):
            P.op("dve", lambda e: e.tensor_tensor(out=qcT[:, m, :NT], in0=pb[:, :NT], in1=rstd[:, :NT], op=ALU.mult),
                 reads=[Bp, B_rstd], writes=[B_qcT])
        for _ in g_dense("cq", list(range(8)), NT, lambda k: hT[:, k, :NT], B_hT, cq_evac):
            step_load()
            step_tail(2)
        while tgen is not None and not xfree[0]:
            step_tail(1)
        while not loaded[0]:
            step_load()
        run(tgen)
        DB[0] = [0, 1, 2]

        if not sample:
            for j in range(ntl):
                banks = [sbank(), sbank()]
                for hp, (pb, Bp) in enumerate(banks):
                    pv = pb[:].rearrange("p (h t) -> p h t", h=2)
                    fns = []
                    for hh in range(2):
                        h = 2 * hp + hh
                        for dc in range(2):
                            fns.append(lambda e, pv=pv, hh=hh, h=h, dc=dc, j=j: e.matmul(
                                pv[:, hh, :], lhsT=qcT[:, 2 * h + dc, j * 128:(j + 1) * 128], rhs=memkT[:, 2 * h + dc, :],
                                start=(dc == 0), stop=(dc == 1)))
                    P.mm(fns, reads=[B_qcT, B_memkT], writes=[Bp])
                cross_softmax(banks)
                step_p1(2)
                run(diag_T(0, 4, 2, pexp, [B_pexp], [B_Dg], lambda h0: pTc[:, h0:h0 + 2, :, :].rearrange("p u k t -> p (u k t)"), None, [B_pTc], [128, 128]))
                for half in range(2):
                    pb, Bp = dbank()
                    pv = pb[:].rearrange("p (c t) -> p c t", c=4)
                    fns = []
                    for cc in range(4):
                        c = half * 4 + cc
                        h = c // 2
                        for mc in range(2):
                            fns.append(lambda e, pv=pv, cc=cc, c=c, h=h, mc=mc: e.matmul(
                                pv[:, cc, :], lhsT=memv[:, mc, c * 128:(c + 1) * 128], rhs=pTc[:, h, mc, :], start=(mc == 0), stop=(mc == 1)))
                    P.mm(fns, reads=[B_memv, B_pTc], writes=[Bp])
                    P.op("act" if half == 0 else "dve",
                         (lambda e, pv=pv, half=half, j=j: e.activation(out=ocT[:, half * 4:half * 4 + 4, j * 128:(j + 1) * 128], in_=pv, func=AF.Copy))
                         if half == 0 else
                         (lambda e, pv=pv, half=half, j=j: e.tensor_copy(out=ocT[:, half * 4:half * 4 + 4, j * 128:(j + 1) * 128], in_=pv)),
                         reads=[Bp], writes=[B_ocT])
                step_p1(2)
        else:
            banks = [sbank(), sbank()]
            for i in range(2):
                P.op("dve", lambda e, i=i: e.memset(qpad[i][:], 0.0), writes=[B_qpad[i]])

            def xslot(c0):
                return xTs[0][:, :, c0:c0 + 128].bitcast(BF16)
            KX = [xslot(128), xslot(256)]
            VX = [xslot(384)]
            B_KX = [Buf("KX0"), Buf("KX1")]
            B_VX = [Buf("VX0")]
            for bb_ in B_KX + B_VX:
                bb_.al.append(B_xTs[0])
                B_xTs[0].al.append(bb_)
            NK, NV = 4, 3

            def kslot(b):
                i = b % NK
                return (("a", Kb[i], B_Kb[i]) if i < 2 else ("x", KX[i - 2], B_KX[i - 2]))

            def vslot(b):
                i = b % (NV + NK)
                if i < NV:
                    return (("a", Vb[i], B_Vb[i]) if i < 2 else ("x", VX[i - 2], B_VX[i - 2]))
                i -= NV
                return (("a", Kb[i], B_Kb[i]) if i < 2 else ("x", KX[i - 2], B_KX[i - 2]))

            def ld(slot, src):
                kind_, ap_, B_ = slot
                if kind_ == "a":
                    P.dma("pool", "c", ap_[:], src.rearrange("(m p) f -> p m f", p=128), writes=[B_])
                else:
                    for m_ in range(2):
                        P.dma("pool", "c", ap_[:, m_ * 4:(m_ + 1) * 4, :],
                              src[m_ * 128:(m_ + 1) * 128, :].rearrange("p (q j) -> p q j", j=256), writes=[B_])

            def tile_of(slot, mt, c):
                kind_, ap_, B_ = slot
                if kind_ == "a":
                    return ap_[:, mt, c * 128:(c + 1) * 128]
                return ap_[:, mt * 4 + c // 2, (c % 2) * 128:(c % 2 + 1) * 128]

            for b in range(3):
                ld(kslot(b), cmk[b])
            for b in range(3):
                ld(vslot(b), cmv[b])
            def Tstage(b):
                s2 = b % 2
                ks = kslot(b)
                if b + 3 < 16:
                    ld(kslot(b + 3), cmk[b + 3])
                for mt in range(2):
                    pb, Bp = tbank()
                    pv = pb[:].bitcast(BF16).rearrange("p (c t) -> p c t", c=8)
                    P.mm([(lambda e, c=c, pv=pv, mt=mt, ks=ks: e.transpose(out=pv[:, c, :], in_=tile_of(ks, mt, c), identity=ident[:]))
                          for c in range(8)], reads=[ks[2]] + CONST, writes=[Bp])
                    P.op("act" if mt == 0 else "dve",
                         (lambda e, pv=pv, mt=mt, s2=s2: e.activation(out=KbT[s2][:, :, mt * 128:(mt + 1) * 128], in_=pv, func=AF.Copy))
                         if mt == 0 else
                         (lambda e, pv=pv, mt=mt, s2=s2: e.tensor_copy(out=KbT[s2][:, :, mt * 128:(mt + 1) * 128], in_=pv)),
                         reads=[Bp], writes=[B_KbT[s2]])
                if b >= 2:
                    P.op("dve", lambda e, s2=s2, b=b: e.memset(qpad[s2][:, :, (b - 2) * 8:(b - 1) * 8], 0.0), writes=[B_qpad[s2]])
                P.op("dve", lambda e, s2=s2, b=b: e.tensor_copy(out=qpad[s2][:, :, b * 8:(b + 1) * 8], in_=qcT[:, :, b * 8:(b + 1) * 8]),
                     reads=[B_qcT], writes=[B_qpad[s2]])

            def Sstage(b):
                s2 = b % 2
                for hp, (pb, Bp) in enumerate(banks):
                    pv = pb[:].rearrange("p (h t) -> p h t", h=2)
                    fns = []
                    for hh in range(2):
                        h = 2 * hp + hh
                        for dc in range(2):
                            fns.append(lambda e, pv=pv, hh=hh, h=h, dc=dc, s2=s2, b=b: e.matmul(
                                pv[:, hh, :], lhsT=qpad[s2][:, 2 * h + dc, :], rhs=KbT[s2][:, 2 * h + dc, :],
                                start=(b == 0 and hh == 0 and dc == 0), stop=(b == 15 and dc == 1), skip_group_check=True))
                    P.mm(fns, reads=[B_qpad[s2], B_KbT[s2]], writes=[Bp])

            Tstage(0)
            for b in range(16):
                if b + 1 < 16:
                    Tstage(b + 1)
                Sstage(b)
            for b in range(NV, NV + NK):
                ld(vslot(b), cmv[b])
            cross_softmax(banks)
            run(diag_T(0, 4, 2, pexp, [B_pexp], [B_Dg], lambda h0: pTc[:, h0:h0 + 2, :, :].rearrange("p u k t -> p (u k t)"), None, [B_pTc], [128, 128]))
            pbs = [dbank(), dbank()]
            for b in range(16):
                vs = vslot(b)
                fns = []
                for c in range(8):
                    pv = pbs[c // 4][0][:].rearrange("p (c t) -> p c t", c=4)
                    h = c // 2
                    for mc in range(2):
                        fns.append(lambda e, pv=pv, c=c, h=h, mc=mc, vs=vs, b=b: e.matmul(
                            pv[:, c % 4, b * 8:(b + 1) * 8], lhsT=tile_of(vs, mc, c), rhs=pTc[:, h, mc, b * 8:(b + 1) * 8],
                            start=(mc == 0), stop=(mc == 1), skip_group_check=True))
                P.mm(fns, reads=[vs[2], B_pTc], writes=[pbs[0][1], pbs[1][1]])
                if b + NV + NK < 16:
                    ld(vslot(b + NV + NK), cmv[b + NV + NK])
            for half in range(2):
                pv = pbs[half][0][:].rearrange("p (c t) -> p c t", c=4)
                P.op("act" if half == 0 else "dve",
                     (lambda e, pv=pv, half=half: e.activation(out=ocT[:, half * 4:half * 4 + 4, 0:128], in_=pv, func=AF.Copy))
                     if half == 0 else
                     (lambda e, pv=pv, half=half: e.tensor_copy(out=ocT[:, half * 4:half * 4 + 4, 0:128], in_=pv)),
                     reads=[pbs[half][1]], writes=[B_ocT])
        while not p1done[0]:
            step_p1(1)
        prep3 = make_prep(X, NT, 3, -1.0, rstd2, B_rstd2)
        DB[0] = [0, 1, 2, 3, 4, 5, 6]
        dense("co", list(range(8)), NT, lambda k: ocT[:, k, :NT], B_ocT, resid_evac(NT, X, prep=prep3))
        prep3[1]()
        DB[0] = [0, 1, 2]

    def ffn(kind, gi, X, gen):
        sample, halo, NT, ntl = geom(kind)
        xt, Bxt = xTs[X], B_xTs[X]
        uctr = [0]

        def up_evac(m, pb, Bp):
            r, Br = relu_t[uctr[0] % 2], B_relu[uctr[0] % 2]
            uctr[0] += 1
            P.op("act", lambda e: e.activation(out=r[:, :NT], in_=pb[:, :NT], func=AF.Relu), reads=[Bp], writes=[Br])
            P.op("pool", lambda e: e.tensor_tensor(out=hidT[:, m, :NT], in0=r[:, :NT], in1=r[:, :NT], op=ALU.mult), reads=[Br], writes=[B_hid])
        for _ in g_dense("up", list(range(32)), NT, lambda k: hT[:, k, :NT], B_hT, up_evac):
            advance(gen, 1)
        for _ in g_dense("down", list(range(8)), NT, lambda k: hidT[:, k, :NT], B_hid, resid_evac(NT, X, scale2=True), kgroups=4):
            advance(gen, 1)

    def g_tail(kind, gi, X):
        sample, halo, NT, ntl = geom(kind)
        xt, Bxt = xTs[X], B_xTs[X]
        sq = hidT[:, 16:24, :]
        for k in range(8):
            P.op("act", lambda e, k=k: e.activation(out=sq[:, k, :NT], in_=xt[:, k, :NT], func=AF.Square), reads=[Bxt], writes=[B_hid])
            if k % 4 == 3:
                yield
        pb0, Bp0 = dbank()
        P.mm([(lambda e, k=k: e.matmul(pb0[:, :NT], lhsT=ones[:], rhs=sq[:, k, :NT], start=(k == 0), stop=(k == 7)))
              for k in range(8)], reads=[B_hid] + CONST, writes=[Bp0])
        P.op("act", lambda e: e.activation(out=rstd2[:, :NT], in_=pb0[:, :NT], func=AF.Ln, scale=1.0 / D, bias=EPS),
             reads=[Bp0], writes=[B_rstd2])
        P.op("act", lambda e: e.activation(out=rstd2[:, :NT], in_=rstd2[:, :NT], func=AF.Exp, scale=-0.5),
             reads=[B_rstd2], writes=[B_rstd2])
        yield
        for k in range(8):
            P.op("dve", lambda e, k=k: e.scalar_tensor_tensor(out=yT[:, k, :NT], in0=xt[:, k, :NT], scalar=gvec[:, 4, k:k + 1],
                                                              in1=rstd2[:, :NT], op0=ALU.mult, op1=ALU.mult),
                 reads=[Bxt, B_rstd2] + CONST, writes=[B_yT])
            if k % 2 == 1 and k < 7:
                yield
        yield "XDONE"
        for j in range(ntl):
            ys_, Bys = yst[1], B_yst[1]
            for hf in range(2):
                pb, Bp = dbank()
                pv = pb[:].rearrange("p (c t) -> p c t", c=4)
                P.mm([(lambda e, c=c, pv=pv, hf=hf, j=j: e.transpose(out=pv[:, c, :], in_=yT[:, hf * 4 + c, j * 128:(j + 1) * 128], identity=identf[:]))
                      for c in range(4)], reads=[B_yT] + CONST, writes=[Bp])
                P.op("act" if hf == 0 else "dve",
                     (lambda e, pb=pb, hf=hf, ys_=ys_: e.activation(out=ys_[:, hf * 512:(hf + 1) * 512], in_=pb[:, :], func=AF.Copy))
                     if hf == 0 else
                     (lambda e, pb=pb, hf=hf, ys_=ys_: e.tensor_copy(out=ys_[:, hf * 512:(hf + 1) * 512], in_=pb[:, :])),
                     reads=[Bp], writes=[Bys])
                yield
            if sample:
                P.dma("pool", "y", ys[:, :], ys_[:], reads=[Bys])
            else:
                r0 = gi * NT_P + j * 128
                P.dma("pool", "y", yp[r0:r0 + 128, :], ys_[:], reads=[Bys])

    order = [("P", g) for g in range(NG_P)] + [("S", 0)]
    order = order[:max(0, min(len(order), STAGE))] if STAGE < 50 else order
    run(early("H", 0, 0))
    gen0 = early(order[0][0], order[0][1], 0) if order else None
    advance(gen0, until="P1")
    mem_setup(gen0)
    run(gen0)
    tgen = None
    for idx, (kind, gi) in enumerate(order):
        X = idx % 2
        nxt = order[idx + 1] if idx + 1 < len(order) else None
        gen = early(nxt[0], nxt[1], (idx + 1) % 2) if nxt else None
        late_pre(kind, gi, X, gen, tgen)
        ffn(kind, gi, X, gen)
        run(gen)
        tgen = g_tail(kind, gi, X)
        if not TAIL_OVERLAP:
            run(tgen)
            tgen = None
    run(tgen)

    return finish()


_CACHE = {}


def _build_nc():
    if "nc" in _CACHE:
        return _CACHE["nc"]
    nc0 = bass.Bass("TRN2", target_bir_lowering=False)
    with ExitStack() as es0:
        _, W0 = build_sched(nc0, es0)
    sched = W0.rec
    nc = bass.Bass("TRN2", target_bir_lowering=False)
    with ExitStack() as es:
        P, W = build(nc, es, False, sched)
        assert W.i == len(sched), (W.i, len(sched))
        block = es.enter_context(nc.Block())
        P.flush(block)
    _CACHE["nc"] = nc
    return nc


def build_sched(nc0, es0):
    return build(nc0, es0, False, None)


def _tables(half):
    slopes = 2.0 ** (-(np.arange(8) + 1.0))
    q = np.arange(128)[:, None]
    c = np.arange(256)[None, :]
    dist = q - c + 128
    valid = (dist >= 0) & (dist <= 128)
    biasg = np.empty((128, 8, 256), np.float32)
    for g in range(4):
        for kv in range(2):
            h = kv * 4 + g
            biasg[:, 2 * g + kv, :] = np.where(valid, -slopes[h] * dist, -1e30)
    biasf = biasg.copy()
    if half == 0:
        biasf[:, :, 0:128] = -1e30
    biass = np.full((128, 2, 160), -1e30, np.float32)
    for j in range(4):
        for g in range(4):
            for t in range(8):
                r = j * 32 + g * 8 + t
                for kv in range(2):
                    h = kv * 4 + g
                    cc = np.arange(128)
                    d = t + 128 - cc
                    biass[r, kv, 0:128] = np.where(cc >= t, -slopes[h] * d, -1e30)
                    for tp in range(t + 1):
                        biass[r, kv, 128 + j * 8 + tp] = -slopes[h] * (t - tp)
    invc = np.empty((128, 4, 16), np.float32)
    for g in range(4):
        w = 2 << g
        for p in range(16):
            invc[:, g, p] = 1.0 / (min(p + 1, w) if half == 0 else w)
    return biasg.reshape(128, -1), biasf.reshape(128, -1), biass.reshape(128, -1), invc.reshape(128, -1)


def _prep(x_prompt, x_sample, cache_win_k, cache_win_v, state_pool, cache_mem_k, cache_mem_v,
          mem_prompt, g_mix, w_in, attn_sinks, w_pool, pool_scale, w_out, g_cross, g_mem,
          w_cq, w_ck, w_cv, w_co, g_ffn, w_up, w_down, g_final):
    f = lambda a: np.ascontiguousarray(np.asarray(a, dtype=np.float32))
    x_prompt, x_sample = f(x_prompt), f(x_sample)
    shared = dict(w_in=f(w_in)[0], w_pool=f(w_pool)[0], w_out=f(w_out)[0], w_cq=f(w_cq)[0], w_ck=f(w_ck)[0],
                  w_cv=f(w_cv)[0], w_co=f(w_co)[0], w_up=f(w_up)[0], w_down=f(w_down)[0])
    gs = np.stack([f(g_mix)[0], f(g_cross)[0], f(g_mem)[0], f(g_ffn)[0], f(g_final)], 0)
    shared["gvec"] = np.ascontiguousarray(gs.reshape(5, 8, 128).transpose(2, 0, 1).reshape(128, 40))
    shared["pscale"] = np.ascontiguousarray(f(pool_scale)[0].reshape(4, 128).T)
    sk = f(attn_sinks)[0]
    sinkp = np.empty((128, 8), np.float32)
    for g in range(4):
        for kv in range(2):
            sinkp[:, 2 * g + kv] = sk[kv * 4 + g]
    shared["sinkp"] = sinkp
    sinks = np.empty((128, 8), np.float32)
    for r in range(128):
        g = (r % 32) // 8
        for i in range(4):
            sinks[r, 2 * i] = sk[g]
            sinks[r, 2 * i + 1] = sk[4 + g]
    shared["sinks"] = sinks
    ckf, cvf, spf = f(cache_win_k)[0], f(cache_win_v)[0], f(state_pool)[0]
    cmkf, cmvf, memf = f(cache_mem_k)[0], f(cache_mem_v)[0], f(mem_prompt)
    in_maps = []
    for c in range(NCORES):
        b, half = c // 2, c % 2
        s0 = half * SEQ_CORE
        xp = np.zeros((128 + SEQ_CORE, D), np.float32)
        xp[128:] = x_prompt[b, s0:s0 + SEQ_CORE]
        if half == 1:
            xp[:128] = x_prompt[b, s0 - 128:s0]
        biasg, biasf, biass, invc = _tables(half)
        sl = slice(16 * c, 16 * c + 16)
        m = dict(shared)
        m.update(xp=xp, xs=np.ascontiguousarray(x_sample[sl].reshape(128, D)), mem=np.ascontiguousarray(memf[b]),
                 ck=np.ascontiguousarray(ckf[sl].reshape(16, 128, 128)), cv=np.ascontiguousarray(cvf[sl].reshape(16, 128, 128)),
                 spool=np.ascontiguousarray(spf[sl]), cmk=np.ascontiguousarray(cmkf[sl].reshape(16, 256, D)),
                 cmv=np.ascontiguousarray(cmvf[sl].reshape(16, 256, D)),
                 biasg=biasg, biasf=biasf, biass=biass, invc=invc)
        in_maps.append(m)
    return in_maps


def kernel(**inputs):
    in_maps = _prep(**inputs)
    nc = _build_nc()
    res = run_bass_kernel_spmd(nc, in_maps, core_ids=list(range(NCORES))).results
    return _assemble(res)


def _assemble(res):
    B, S = 4, 4096
    y_prompt = np.empty((B, S, D), np.float32)
    y_sample = np.empty((128, 8, D), np.float32)
    wk_p = np.empty((1, B, 128, 2, 64), np.float32); wv_p = np.empty_like(wk_p)
    pool_p = np.empty((1, B, 15, 512), np.float32)
    mk_p = np.empty((1, B, 256, 4, 256), np.float32); mv_p = np.empty_like(mk_p)
    wk_s = np.empty((1, 128, 128, 2, 64), np.float32); wv_s = np.empty_like(wk_s)
    pool_s = np.empty((1, 128, 15, 512), np.float32)
    for c in range(NCORES):
        r = res[c]
        b, half = c // 2, c % 2
        y_prompt[b, half * SEQ_CORE:(half + 1) * SEQ_CORE] = r["yp"]
        sl = slice(16 * c, 16 * c + 16)
        y_sample[sl] = r["ys"].reshape(16, 8, D)
        if half == 1:
            wk_p[0, b] = r["wkp"].reshape(128, 2, 64)
            wv_p[0, b] = r["wvp"].reshape(128, 2, 64)
            pool_p[0, b] = r["poolp"]
        else:
            mk_p[0, b] = r["memk"].reshape(256, 4, 256)
            mv_p[0, b] = r["memv"].reshape(256, 4, 256)
        wk_s[0, sl] = r["wks"].reshape(16, 128, 2, 64)
        wv_s[0, sl] = r["wvs"].reshape(16, 128, 2, 64)
        pool_s[0, sl] = r["pools"]
    return (y_prompt, y_sample, wk_p, wv_p, pool_p, mk_p, mv_p, wk_s, wv_s, pool_s)
```

```python
import numpy as np
from contextlib import ExitStack
import concourse.bass as bass
import concourse.mybir as mybir
from concourse.bass_utils import run_bass_kernel_spmd

F32 = mybir.dt.float32
BF16 = mybir.dt.bfloat16
ALU = mybir.AluOpType
AF = mybir.ActivationFunctionType
AX = mybir.AxisListType

NCORES = 8
STAGE = 99
TAIL_OVERLAP = True
D = 1024
SEQ_CORE = 2048
NT_P = 512
NG_P = SEQ_CORE // NT_P
RING = 11
EPS = 1e-5


class Buf:
    __slots__ = ("name", "w", "r", "al", "excl")

    def __init__(self, name, excl=False):
        self.name = name
        self.w = None
        self.r = {}
        self.al = []
        self.excl = excl


def alias(*bufs):
    for a in bufs:
        for b in bufs:
            if a is not b and b not in a.al:
                a.al.append(b)


class Prog:
    def __init__(self, nc, es, dry):
        self.nc, self.es, self.dry = nc, es, dry
        self.q = {e: [] for e in ("pe", "act", "dve", "pool", "sp")}
        self.cnt, self.sems = {}, {}
        self.waited = {e: {} for e in self.q}

    def sem(self, key):
        if key not in self.sems:
            self.sems[key] = None if self.dry else self.es.enter_context(self.nc.semaphore(key))
            self.cnt[key] = 0

    def _wait(self, eng, tok):
        if tok is None:
            return
        key, val = tok
        if self.waited[eng].get(key, 0) >= val:
            return
        self.waited[eng][key] = val
        self.q[eng].append(("w", key, val))

    def _deps(self, eng, reads, writes, extra):
        for b in reads:
            self._wait(eng, b.w)
            if b.excl:
                for k, v in b.r.items():
                    if k != eng:
                        self._wait(eng, (k, v))
        for b in writes:
            for bb in [b] + b.al:
                self._wait(eng, bb.w)
                for k, v in bb.r.items():
                    self._wait(eng, (k, v))
        for t in extra:
            self._wait(eng, t)

    def _commit(self, tok, reads, writes):
        k, v = tok
        for b in reads:
            b.r[k] = max(b.r.get(k, 0), v)
        for b in writes:
            b.w = tok
            b.r = {}

    def op(self, eng, fn, reads=(), writes=(), extra=()):
        self._deps(eng, reads, writes, extra)
        self.sem(eng)
        self.cnt[eng] += 1
        tok = (eng, self.cnt[eng])
        self.q[eng].append(("i", fn, eng, 1))
        self._commit(tok, reads, writes)
        return tok

    def mm(self, fns, reads=(), writes=(), extra=()):
        self._deps("pe", reads, writes, extra)
        for f in fns[:-1]:
            self.q["pe"].append(("i", f, None, 0))
        self.sem("pe")
        self.cnt["pe"] += 1
        tok = ("pe", self.cnt["pe"])
        self.q["pe"].append(("i", fns[-1], "pe", 1))
        self._commit(tok, reads, writes)
        return tok

    def dma(self, qeng, semkey, out, in_, reads=(), writes=(), extra=()):
        if writes:
            semkey = "dw" + qeng[0] + "_" + writes[0].name
        elif reads:
            semkey = "dr" + qeng[0] + "_" + reads[0].name
        for b in reads:
            self._wait(qeng, b.w)
        for b in writes:
            for bb in [b] + b.al:
                if not (bb.w is not None and bb.w[0] == semkey):
                    self._wait(qeng, bb.w)
                for k, v in bb.r.items():
                    self._wait(qeng, (k, v))
        for t in extra:
            self._wait(qeng, t)
        self.sem(semkey)
        self.cnt[semkey] += 16
        tok = (semkey, self.cnt[semkey])
        self.q[qeng].append(("i", (lambda e, o=out, i=in_: e.dma_start(out=o, in_=i)), semkey, 16))
        self._commit(tok, reads, writes)
        return tok

    def flush(self, block):
        def run(name):
            def f(e):
                for it in self.q[name]:
                    if it[0] == "w":
                        e.wait_ge(self.sems[it[1]], it[2])
                    else:
                        ins = it[1](e)
                        if it[3]:
                            ins.then_inc(self.sems[it[2]], it[3])
            return f
        block.tensor(run("pe"))
        block.scalar(run("act"))
        block.vector(run("dve"))
        block.gpsimd(run("pool"))
        block.sync(run("sp"))


class WStream:
    def __init__(self, P, ring_ap, sched, scratch_fn=None):
        self.P, self.ring = P, ring_ap
        self.sched = sched
        self.rec = []
        self.i = 0
        self.issued = 0
        self.slots = [Buf(f"ws{i}") for i in range(RING)]
        self.src = {}
        self.uidx, self.wtok = {}, {}
        self.scratch = None
        if sched is not None:
            cnt = {}
            for k in sched:
                cnt[k] = cnt.get(k, 0) + 1
            for k in sched:
                if cnt[k] > 1 and k not in self.uidx:
                    self.uidx[k] = len(self.uidx)
            if scratch_fn is not None and self.uidx:
                self.scratch = scratch_fn(len(self.uidx))

    def _issue(self, j):
        key = self.sched[j]
        name, m = key
        s = j % RING
        if self.scratch is not None and key in self.wtok:
            self.P.dma("sp", f"ws{s}", self.ring[:, s], self.scratch[self.uidx[key]], writes=[self.slots[s]],
                       extra=[self.wtok[key]])
            return
        for (dst_fn, src_ap) in self.src[name](m):
            self.P.dma("pool", f"ws{s}", dst_fn(self.ring[:, s]), src_ap, writes=[self.slots[s]])
        if self.scratch is not None and key in self.uidx:
            self.wtok[key] = self.P.dma("sp", f"sw{s}", self.scratch[self.uidx[key]], self.ring[:, s], reads=[self.slots[s]])

    def get(self, name, m):
        if self.sched is None:
            self.rec.append((name, m))
            return self.ring[:, 0], self.slots[0]
        assert self.sched[self.i] == (name, m), (self.i, self.sched[self.i], name, m)
        while self.issued < min(len(self.sched), self.i + RING - 3):
            self._issue(self.issued)
            self.issued += 1
        s = self.i % RING
        self.i += 1
        return self.ring[:, s], self.slots[s]


def build(nc, es, dry, sched):
    P = Prog(nc, es, dry)

    def din(name, shape):
        return nc.dram_tensor(name, list(shape), F32, kind="ExternalInput").ap()

    def dout(name, shape):
        return nc.dram_tensor(name, list(shape), F32, kind="ExternalOutput").ap()

    if not dry:
        xp = din("xp", [128 + SEQ_CORE, D]); xs = din("xs", [128, D]); mem = din("mem", [256, D])
        ck = din("ck", [16, 128, 128]); cv = din("cv", [16, 128, 128]); spool = din("spool", [16, 15, 512])
        cmk = din("cmk", [16, 256, D]); cmv = din("cmv", [16, 256, D])
        w_in = din("w_in", [D, 1280]); w_pool = din("w_pool", [4, 128, 128]); w_out = din("w_out", [D, D])
        w_cq = din("w_cq", [D, D]); w_ck = din("w_ck", [D, D]); w_cv = din("w_cv", [D, D]); w_co = din("w_co", [D, D])
        w_up = din("w_up", [D, 4 * D]); w_down = din("w_down", [4 * D, D])
        gvec_d = din("gvec", [128, 40]); pscale_d = din("pscale", [128, 4])
        sinkp_d = din("sinkp", [128, 8]); sinks_d = din("sinks", [128, 8])
        biasg_d = din("biasg", [128, 8 * 256]); biasf_d = din("biasf", [128, 8 * 256]); biass_d = din("biass", [128, 2 * 160])
        invc_d = din("invc", [128, 64])
        yp = dout("yp", [SEQ_CORE, D]); ys = dout("ys", [128, D])
        wkp = dout("wkp", [128, 128]); wvp = dout("wvp", [128, 128]); poolp = dout("poolp", [15, 512])
        memk_o = dout("memk", [256, D]); memv_o = dout("memv", [256, D])
        wks = dout("wks", [16, 128, 128]); wvs = dout("wvs", [16, 128, 128]); pools = dout("pools", [16, 15, 512])

    def sb(name, shape, dt):
        return es.enter_context(nc.sbuf_tensor("sb_" + name, list(shape), dt))

    xTs = [sb(f"xT{i}", [128, 8, NT_P], F32) for i in range(2)]; B_xTs = [Buf(f"xT{i}") for i in range(2)]
    xT, B_xT = xTs[0], B_xTs[0]
    hT = sb("hT", [128, 8, NT_P], BF16); B_hT = Buf("hT")
    rstd = sb("rstd", [128, NT_P], F32); B_rstd = Buf("rstd")
    aoT = sb("aoT", [128, 8, NT_P], BF16); B_aoT = Buf("aoT")
    qT = sb("qT", [128, 4, NT_P], BF16); B_qT = Buf("qT")
    kT = sb("kT", [128, 128 + NT_P], BF16); B_kT = Buf("kT")
    vT = sb("vT", [128, NT_P], BF16); B_vT = Buf("vT")
    vtok = sb("vtok", [128, 5, 128], BF16); B_vtok = Buf("vtok")
    kv32 = sb("kv32", [128, 2, 128], F32); B_kv32 = Buf("kv32")
    dT = sb("dT", [128, 4, NT_P], BF16); B_dT = Buf("dT")
    pexp = sb("pexp", [128, 8, 256], BF16); B_pexpH = [Buf("pexp0"), Buf("pexp1")]; B_pexp = B_pexpH[0]
    pT = sb("pT", [128, 8, 2, 128], BF16); B_pTH = [Buf("pT0"), Buf("pT1")]; B_pT = B_pTH[0]
    Dg = sb("Dg", [128, 8, 128], BF16); B_DgH = [Buf("Dg0"), Buf("Dg1")]; B_Dg = B_DgH[0]
    pTc = sb("pTc", [128, 4, 2, 128], BF16); B_pTc = Buf("pTc")
    bias = sb("bias", [128, 8, 256], F32); B_bias = Buf("bias")
    biass = sb("biass", [128, 2, 160], F32); B_biass = Buf("biass")
    memkT = sb("memkT", [128, 8, 256], BF16); B_memkT = Buf("memkT")
    memv = sb("memv", [128, 2, D], BF16); B_memv = Buf("memv")
    ring = sb("ring", [128, RING, 8, 128], BF16)
    xin = [sb(f"xin{i}", [128, D], F32) for i in range(2)]; B_xin = [Buf(f"xin{i}") for i in range(2)]
    yst1 = sb("yst1", [128, D], F32); yst, B_yst = [None, yst1], [None, Buf("yst1")]
    ident = sb("ident", [128, 128], BF16); identf = sb("identf", [128, 128], F32); B_const = Buf("const")
    ones = sb("ones", [128, 128], BF16)
    gvec = sb("gvec", [128, 5, 8], F32); pscale = sb("pscale", [128, 4], F32)
    sinkp = sb("sinkp", [128, 8], F32); sinks = sb("sinks", [128, 8], F32)
    invc = sb("invc", [128, 4, 16], F32)
    wpool = sb("wpool", [128, 4, 128], BF16); B_wpool = Buf("wpool")
    st = sb("st", [128, 64], F32); B_stH = [Buf("st0"), Buf("st1")]; B_st = B_stH[0]
    relu_t = [sb(f"relu{i}", [128, NT_P], BF16) for i in range(2)]; B_relu = [Buf(f"relu{i}") for i in range(2)]
    ost = sb("ost", [128, 512], F32); B_ost = Buf("ost")
    carryU = sb("carryU", [128, 4, 16], F32); B_cU = Buf("carryU")
    sqb = [sb(f"sq{i}", [128, NT_P], BF16) for i in range(2)]; B_sq = [Buf(f"sq{i}") for i in range(2)]
    rstd2 = sb("rstd2", [128, NT_P], F32); B_rstd2 = Buf("rstd2")

    R2 = 32 * NT_P * 2
    XO = R2 + 28672
    AR = XO + 10240
    arena = sb("arena", [128, AR // 2], BF16)

    def av(off, nbytes, dt, pat=None, **kw):
        v = arena[:, off // 2:(off + nbytes) // 2]
        if dt is F32:
            v = v.bitcast(F32)
        if pat:
            v = v.rearrange(pat, **kw)
        return v

    hidT = av(0, 32 * NT_P * 2, BF16, "p (k t) -> p k t", k=32); B_hid = Buf("hidT")
    yT = av(0, 8 * NT_P * 4, F32, "p (k t) -> p k t", k=8); B_yT = Buf("yT")
    WU = 16 + NT_P
    WH = 16 + 256
    U = av(R2, 4 * WU * 4, F32, "p (g t) -> p g t", g=4); B_U = Buf("U")
    SA = av(R2 + 4 * WU * 4, 4 * WH * 4, F32, "p (g t) -> p g t", g=4); B_SA = Buf("SA")
    SB = av(R2 + 4 * WU * 4 + 4 * WH * 4, 4 * WH * 4, F32, "p (g t) -> p g t", g=4); B_SB = Buf("SB")
    o_sb = R2 + 4 * WU * 4 + 8 * WH * 4
    sbias = av(o_sb, 8 * 256 * 4, F32, "p (u t) -> p u t", u=8); B_sbiasH = [Buf("sbias0"), Buf("sbias1")]; B_sbias = B_sbiasH[0]
    assert o_sb + 8192 <= XO
    kcT = av(XO, 16 * 128 * 2, BF16, "p (b t) -> p b t", b=16); B_kcT = Buf("kcT")
    vc = av(XO + 4096, 16 * 128 * 2, BF16, "p (b t) -> p b t", b=16); B_vc = Buf("vc")
    qs2 = av(XO + 8192, 16 * 32 * 2, BF16, "p (b t) -> p b t", b=16); B_qs2 = Buf("qs2")
    vnq = av(XO + 9216, 4 * 128 * 2, BF16, "p (i t) -> p i t", i=4); B_vnq = Buf("vnq")
    Kb = [av(R2 + i * 4096, 4096, BF16, "p (m t) -> p m t", m=2) for i in range(2)]; B_Kb = [Buf(f"Kb{i}") for i in range(2)]
    KbT = [av(R2 + 8192 + i * 4096, 4096, BF16, "p (c t) -> p c t", c=8) for i in range(2)]; B_KbT = [Buf(f"KbT{i}") for i in range(2)]
    Vb = [av(R2 + 16384 + i * 4096, 4096, BF16, "p (m t) -> p m t", m=2) for i in range(2)]; B_Vb = [Buf(f"Vb{i}") for i in range(2)]
    qpad = [av(R2 + 24576 + i * 2048, 2048, BF16, "p (c t) -> p c t", c=8) for i in range(2)]; B_qpad = [Buf(f"qpad{i}") for i in range(2)]
    memst = av(0, 8192, F32, "p (m t) -> p m t", m=2); B_memst = Buf("memst")
    mkst = av(8192, 8192, F32, "p (m t) -> p m t", m=2); B_mkst = Buf("mkst")
    alias(B_hid, B_yT)
    gX = [B_U, B_SA, B_SB] + B_sbiasH
    gY = B_Kb + B_KbT + B_Vb + B_qpad
    gZ = [B_memst, B_mkst]
    for ga, gb in ((gX, gY), ([B_hid, B_yT], gZ)):
        for a in ga:
            for b in gb:
                a.al.append(b)
                b.al.append(a)

    ps = [es.enter_context(nc.psum_tensor(f"ps{i}", [128, 512], F32)) for i in range(8)]
    B_ps = [Buf(f"ps{i}", excl=True) for i in range(8)]
    dctr = [0]

    def dbank():
        i = dctr[0] % 3
        dctr[0] += 1
        return ps[i], B_ps[i]
    PS_S = [3, 4]
    PS_T = [5, 6]
    PS_O = 7
    sctr = [0]
    tctr = [0]

    def sbank():
        i = PS_S[sctr[0] % 2]; sctr[0] += 1
        return ps[i], B_ps[i]

    def tbank():
        i = PS_T[tctr[0] % 2]; tctr[0] += 1
        return ps[i], B_ps[i]

    def scratch_fn(n):
        return nc.dram_tensor("wscratch", [n, 128, 8, 128], BF16, kind="Internal").ap()
    W = WStream(P, ring, sched, scratch_fn)
    if not dry:
        def std_src(wap):
            v = wap.rearrange("(k p) (m c) -> p m k c", p=128, c=128)
            return lambda m: [((lambda s: s), v[:, m])]
        W.src["ck"] = std_src(w_ck); W.src["cv"] = std_src(w_cv)
        W.src["cq"] = std_src(w_cq); W.src["co"] = std_src(w_co); W.src["up"] = std_src(w_up)
        vin_q = w_in[:, 0:512].rearrange("(k p) (kv g d) -> p g k kv d", p=128, kv=2, g=4, d=64)
        vin_r = w_in[:, 512:1280].rearrange("(k p) (m c) -> p m k c", p=128, c=128)

        def in_src(m):
            if m < 4:
                return [((lambda s: s[:, :, 0:64]), vin_q[:, m, :, 0, :]),
                        ((lambda s: s[:, :, 64:128]), vin_q[:, m, :, 1, :])]
            return [((lambda s: s), vin_r[:, m - 4])]
        W.src["in"] = in_src
        vo_a = w_out[0:512, :].rearrange("(kv g d) (m c) -> kv d m g c", kv=2, g=4, d=64, c=128)
        vo_p = w_out[512:1024, :].rearrange("(k p) (m c) -> p m k c", p=128, c=128)

        def out_src(m):
            return [((lambda s: s[0:64, 0:4, :]), vo_a[0, :, m]),
                    ((lambda s: s[64:128, 0:4, :]), vo_a[1, :, m]),
                    ((lambda s: s[:, 4:8, :]), vo_p[:, m])]
        W.src["out"] = out_src
        vdn = w_down.rearrange("(q k p) (m c) -> p m q k c", p=128, k=8, c=128)
        W.src["down"] = lambda mq: [((lambda s: s), vdn[:, mq // 4, mq % 4])]

    if not dry:
        P.op("pool", lambda e: e.memset(identf[:], 0.0), writes=[B_const])
        P.op("pool", lambda e: e.iota(identf[:], pattern=[[1, 128]], base=0, channel_multiplier=-1,
                                      allow_small_or_imprecise_dtypes=True), writes=[B_const])
        P.op("dve", lambda e: e.tensor_single_scalar(out=ident[:], in_=identf[:], scalar=0.0, op=ALU.is_equal),
             reads=[B_const], writes=[B_const])
        P.op("dve", lambda e: e.tensor_single_scalar(out=identf[:], in_=identf[:], scalar=0.0, op=ALU.is_equal),
             writes=[B_const])
        P.op("dve", lambda e: e.memset(ones[:], 1.0), writes=[B_const])
        for (dst, src) in ((gvec[:].rearrange("p a b -> p (a b)"), gvec_d), (pscale[:], pscale_d), (sinkp[:], sinkp_d),
                           (sinks[:], sinks_d), (invc[:].rearrange("p a b -> p (a b)"), invc_d),
                           (biass[:].rearrange("p a b -> p (a b)"), biass_d)):
            P.dma("sp", "cst", dst, src[:, :], writes=[B_const])
        P.dma("sp", "biasld", bias[:].rearrange("p a b -> p (a b)"), biasf_d[:, :], writes=[B_bias])
        P.dma("pool", "wpool", wpool[:], w_pool.rearrange("g c e -> c g e"), writes=[B_wpool])

    CONST = [B_const]

    class _Stop(Exception):
        pass

    def finish():
        for key, val in P.cnt.items():
            if key not in ("pe", "act", "dve", "pool"):
                P._wait("sp", (key, val))
        for e_ in ("pe", "act", "dve", "pool"):
            if P.cnt.get(e_, 0):
                P._wait("sp", (e_, P.cnt[e_]))
        return P, W
    if STAGE == -1:
        return finish()

    def run(gen):
        if gen is not None:
            for _ in gen:
                pass

    def advance(gen, n=1, until=None):
        if gen is None:
            return
        if until is not None:
            for v in gen:
                if v == until:
                    return
            return
        for _ in range(n):
            try:
                next(gen)
            except StopIteration:
                return

    def g_load_x(src_rows, ntiles, X):
        dst, dstB = xTs[X], B_xTs[X]
        for j in range(min(2, ntiles)):
            P.dma("sp", "x", xin[j % 2][:], src_rows(j), writes=[B_xin[j % 2]])
        for j in range(ntiles):
            xb, Bx = xin[j % 2], B_xin[j % 2]
            for hf in range(2):
                pb, Bp = dbank()
                pv = pb[:].rearrange("p (c t) -> p c t", c=4)
                P.mm([(lambda e, c=c, pv=pv, xb=xb, hf=hf: e.transpose(out=pv[:, c, :], in_=xb[:, (hf * 4 + c) * 128:(hf * 4 + c + 1) * 128],
                                                                       identity=identf[:])) for c in range(4)],
                     reads=[Bx] + CONST, writes=[Bp])
                P.op("act" if hf == 0 else "dve",
                     (lambda e, pv=pv, hf=hf, j=j: e.activation(out=dst[:, hf * 4:hf * 4 + 4, j * 128:(j + 1) * 128], in_=pv, func=AF.Copy))
                     if hf == 0 else
                     (lambda e, pv=pv, hf=hf, j=j: e.tensor_copy(out=dst[:, hf * 4:hf * 4 + 4, j * 128:(j + 1) * 128], in_=pv)),
                     reads=[Bp], writes=[dstB])
                if hf == 1 and j + 2 < ntiles:
                    P.dma("sp", "x", xb[:], src_rows(j + 2), writes=[Bx])
                yield

    def norm(src, Bsrc, gi, NT, dst, Bdst):
        P.op("act", lambda e: e.activation(out=hT[:, :, :NT], in_=src[:, :, :NT], func=AF.Square),
             reads=[Bsrc], writes=[B_hT])
        pb, Bp = dbank()
        P.mm([(lambda e, k=k: e.matmul(pb[:, :NT], lhsT=ones[:], rhs=hT[:, k, :NT], start=(k == 0), stop=(k == 7)))
              for k in range(8)], reads=[B_hT] + CONST, writes=[Bp])
        P.op("act", lambda e: e.activation(out=rstd[:, :NT], in_=pb[:, :NT], func=AF.Ln, scale=1.0 / D, bias=EPS),
             reads=[Bp], writes=[B_rstd])
        P.op("act", lambda e: e.activation(out=rstd[:, :NT], in_=rstd[:, :NT], func=AF.Exp, scale=-0.5),
             reads=[B_rstd], writes=[B_rstd])
        for k in range(8):
            P.op("dve", lambda e, k=k: e.scalar_tensor_tensor(out=dst[:, k, :NT], in0=src[:, k, :NT], scalar=gvec[:, gi, k:k + 1],
                                                              in1=rstd[:, :NT], op0=ALU.mult, op1=ALU.mult),
                 reads=[Bsrc, B_rstd] + CONST, writes=[Bdst])

    def g_dense(wname, units, NT, rhs_fn, Brhs, evac, kgroups=1):
        for m in units:
            pb, Bp = dbank()
            fns, Bs = [], []
            for q in range(kgroups):
                slot, Bslot = W.get(wname, m * kgroups + q if kgroups > 1 else m)
                Bs.append(Bslot)
                for k in range(8):
                    fns.append(lambda e, slot=slot, k=k, q=q, pb=pb: e.matmul(
                        pb[:, :NT], lhsT=slot[:, k, :], rhs=rhs_fn(q * 8 + k),
                        start=(q == 0 and k == 0), stop=(q == kgroups - 1 and k == 7)))
            P.mm(fns, reads=Bs + [Brhs], writes=[Bp])
            evac(m, pb, Bp)
            for _ in range(kgroups):
                yield

    def dense(*a, **kw):
        run(g_dense(*a, **kw))

    def resid_evac(NT, X, prep=None, scale2=False):
        xt, Bxt = xTs[X], B_xTs[X]

        def f(m, pb, Bp):
            if scale2:
                P.op("dve", lambda e: e.tensor_tensor(out=pb[:, :NT], in0=pb[:, :NT], in1=rstd2[:, :NT], op=ALU.mult),
                     reads=[Bp, B_rstd2], writes=[Bp])
            P.op("dve", lambda e: e.tensor_tensor(out=xt[:, m, :NT], in0=pb[:, :NT], in1=xt[:, m, :NT], op=ALU.add),
                 reads=[Bp], writes=[Bxt])
            if prep is not None:
                prep[0](m)
        return f

    def make_prep(X, NT, gidx, exp_scale, rdst, Brdst):
        xt, Bxt = xTs[X], B_xTs[X]
        sp_, Bsp = ps[PS_O], B_ps[PS_O]
        pend = []

        def emit_mm(k):
            P.mm([lambda e, k=k: e.matmul(sp_[:, :NT], lhsT=ones[:], rhs=sqb[k % 2][:, :NT], start=(k == 0), stop=(k == 7))],
                 reads=[B_sq[k % 2]] + CONST, writes=[Bsp])

        def after(m):
            while len(pend) >= 2:
                emit_mm(pend.pop(0))
            P.op("act", lambda e: e.activation(out=hT[:, m, :NT], in_=xt[:, m, :NT], func=AF.Copy, scale=gvec[:, gidx, m:m + 1]),
                 reads=[Bxt] + CONST, writes=[B_hT])
            P.op("act", lambda e: e.activation(out=sqb[m % 2][:, :NT], in_=xt[:, m, :NT], func=AF.Square),
                 reads=[Bxt], writes=[B_sq[m % 2]])
            pend.append(m)

        def flush():
            while pend:
                emit_mm(pend.pop(0))
            P.op("act", lambda e: e.activation(out=rdst[:, :NT], in_=sp_[:, :NT], func=AF.Ln, scale=1.0 / D, bias=EPS),
                 reads=[Bsp], writes=[Brdst])
            P.op("act", lambda e: e.activation(out=rdst[:, :NT], in_=rdst[:, :NT], func=AF.Exp, scale=exp_scale),
                 reads=[Brdst], writes=[Brdst])
        return after, flush

    def diag_T(u0, nu, nkc, p_src, Bp_src, BDg, dst4, dst_fn, Bdst, kw):
        items = [(u, kc) for u in range(u0, u0 + nu) for kc in range(nkc)]
        full = all(w == 128 for w in kw)
        for bi, i0 in enumerate(range(0, len(items), 4)):
            chunk = items[i0:i0 + 4]
            pb, Bp = tbank()
            pv = pb[:].rearrange("p (s t) -> p s t", s=4)
            P.mm([(lambda e, s=s, u=u, kc=kc, pv=pv: e.matmul(pv[0:kw[kc], s, :], lhsT=p_src[:, u, kc * 128:kc * 128 + kw[kc]],
                                                              rhs=Dg[:, u, :], start=True, stop=True))
                  for s, (u, kc) in enumerate(chunk)], reads=list(Bp_src) + list(BDg), writes=[Bp])
            if full:
                uu = chunk[0][0]
                if bi % 2 == 0:
                    P.op("act", lambda e, pb=pb, uu=uu: e.activation(out=dst4(uu), in_=pb[:, 0:512], func=AF.Copy), reads=[Bp], writes=list(Bdst))
                else:
                    P.op("dve", lambda e, pb=pb, uu=uu: e.tensor_copy(out=dst4(uu), in_=pb[:, 0:512]), reads=[Bp], writes=list(Bdst))
            else:
                for s, (u, kc) in enumerate(chunk):
                    P.op("act" if bi % 2 == 0 else "dve",
                         (lambda e, s=s, u=u, kc=kc, pv=pv: e.activation(out=dst_fn(u, kc), in_=pv[0:kw[kc], s, :], func=AF.Copy))
                         if bi % 2 == 0 else
                         (lambda e, s=s, u=u, kc=kc, pv=pv: e.tensor_copy(out=dst_fn(u, kc), in_=pv[0:kw[kc], s, :])),
                         reads=[Bp], writes=list(Bdst))
            yield

    def make_Dg(u0, nu, Bst, BDg):
        P.op("dve", lambda e: e.tensor_tensor(out=Dg[:, u0:u0 + nu, :], in0=ident[:].unsqueeze(1).to_broadcast([128, nu, 128]),
                                              in1=st[:, 56 + u0:56 + u0 + nu].unsqueeze(2).to_broadcast([128, nu, 128]), op=ALU.mult),
             reads=list(Bst) + CONST, writes=list(BDg))

    def win_softmax(u0, nu, width, sink_ap, hs):
        Bsb = [B_sbiasH[h] for h in hs]; Bst = [B_stH[h] for h in hs]
        Bpe = [B_pexpH[h] for h in hs]; BDg = [B_DgH[h] for h in hs]
        c = lambda base: slice(base + u0, base + u0 + nu)
        P.op("dve", lambda e: e.tensor_reduce(out=st[:, c(0)], in_=sbias[:, u0:u0 + nu, 0:width], axis=AX.X, op=ALU.max),
             reads=Bsb, writes=Bst)
        P.op("dve", lambda e: e.tensor_tensor(out=st[:, c(8)], in0=st[:, c(0)], in1=sink_ap, op=ALU.max),
             reads=Bst + CONST, writes=Bst)
        P.op("dve", lambda e: e.tensor_scalar(out=st[:, c(16)], in0=st[:, c(8)], scalar1=-1.0, scalar2=None, op0=ALU.mult),
             reads=Bst, writes=Bst)
        P.op("dve", lambda e: e.tensor_tensor(out=st[:, c(24)], in0=sink_ap, in1=st[:, c(16)], op=ALU.add),
             reads=Bst + CONST, writes=Bst)
        for u in range(u0, u0 + nu):
            P.op("act", lambda e, u=u: e.activation(out=pexp[:, u, 0:width], in_=sbias[:, u, 0:width], func=AF.Exp,
                                                    bias=st[:, 16 + u:17 + u], scale=1.0, accum_out=st[:, 32 + u:33 + u]),
                 reads=Bsb + Bst, writes=Bpe + Bst)
        P.op("act", lambda e: e.activation(out=st[:, c(40)], in_=st[:, c(24)], func=AF.Exp),
             reads=Bst, writes=Bst)
        yield
        P.op("dve", lambda e: e.tensor_tensor(out=st[:, c(48)], in0=st[:, c(32)], in1=st[:, c(40)], op=ALU.add),
             reads=Bst, writes=Bst)
        P.op("dve", lambda e: e.reciprocal(out=st[:, c(56)], in_=st[:, c(48)]), reads=Bst, writes=Bst)
        make_Dg(u0, nu, Bst, BDg)

    def cross_softmax(score_banks):
        for hp, (pb, Bp) in enumerate(score_banks):
            pv = pb[:].rearrange("p (h t) -> p h t", h=2)
            P.op("dve", lambda e, pv=pv, hp=hp: e.tensor_reduce(out=st[:, 2 * hp:2 * hp + 2], in_=pv, axis=AX.X, op=ALU.max),
                 reads=[Bp], writes=[B_st])
        P.op("dve", lambda e: e.tensor_scalar(out=st[:, 16:20], in0=st[:, 0:4], scalar1=-1.0 / 16.0, scalar2=None, op0=ALU.mult),
             reads=[B_st], writes=[B_st])
        for hp, (pb, Bp) in enumerate(score_banks):
            pv = pb[:].rearrange("p (h t) -> p h t", h=2)
            for hh in range(2):
                h = 2 * hp + hh
                P.op("act", lambda e, pv=pv, hh=hh, h=h: e.activation(out=pexp[:, h, :], in_=pv[:, hh, :], func=AF.Exp,
                                                                      bias=st[:, 16 + h:17 + h], scale=1.0 / 16.0,
                                                                      accum_out=st[:, 32 + h:33 + h]),
                     reads=[Bp, B_st], writes=[B_pexp, B_st])
        P.op("dve", lambda e: e.reciprocal(out=st[:, 56:60], in_=st[:, 32:36]), reads=[B_st], writes=[B_st])
        make_Dg(0, 4, [B_st], [B_Dg])

    def out_tok_major(srcs, Bsrcs, ncols_each, dst_dma):
        pb, Bp = dbank()
        pv = pb[:].rearrange("p (c t) -> p c t", c=4)
        n = len(srcs)
        P.mm([(lambda e, i=i: e.transpose(out=pv[:, i, :], in_=srcs[i], identity=identf[:])) for i in range(n)],
             reads=list(Bsrcs) + CONST, writes=[Bp])
        P.op("dve", lambda e: e.tensor_copy(out=ost[:, 0:n * 128], in_=pb[:, 0:n * 128]), reads=[Bp], writes=[B_ost])
        dst_dma()

    def mem_setup(gen):
        mx, Bmx = xTs[1], B_xTs[1]
        for t in range(2):
            P.dma("sp", "memld", memst[:, t, :], mem[t * 128:(t + 1) * 128, :], writes=[B_memst])
        for t in range(2):
            for hf in range(2):
                pb, Bp = dbank()
                pv = pb[:].rearrange("p (c t) -> p c t", c=4)
                P.mm([(lambda e, c=c, pv=pv, t=t, hf=hf: e.transpose(out=pv[:, c, :], in_=memst[:, t, (hf * 4 + c) * 128:(hf * 4 + c + 1) * 128],
                                                                     identity=identf[:])) for c in range(4)],
                     reads=[B_memst] + CONST, writes=[Bp])
                P.op("act", lambda e, pv=pv, hf=hf, t=t: e.activation(out=mx[:, hf * 4:hf * 4 + 4, t * 128:(t + 1) * 128], in_=pv, func=AF.Copy),
                     reads=[Bp], writes=[Bmx])
        norm(mx, Bmx, 2, 256, hT, B_hT)
        for (wn, is_k) in (("ck", True), ("cv", False)):
            for m in range(8):
                slot, Bslot = W.get(wn, m)
                if is_k:
                    pb, Bp = dbank()
                    P.mm([(lambda e, k=k, slot=slot, pb=pb: e.matmul(pb[:, :256], lhsT=slot[:, k, :], rhs=hT[:, k, :256],
                                                                    start=(k == 0), stop=(k == 7))) for k in range(8)],
                         reads=[Bslot, B_hT], writes=[Bp])
                    P.op("act", lambda e, m=m, pb=pb: e.activation(out=memkT[:, m, :], in_=pb[:, :256], func=AF.Copy),
                         reads=[Bp], writes=[B_memkT])
                pb, Bp = dbank()
                fns = []
                for t in range(2):
                    for k in range(8):
                        fns.append(lambda e, k=k, slot=slot, pb=pb, t=t: e.matmul(
                            pb[:, t * 128:(t + 1) * 128], lhsT=hT[:, k, t * 128:(t + 1) * 128], rhs=slot[:, k, :],
                            start=(k == 0), stop=(k == 7)))
                P.mm(fns, reads=[Bslot, B_hT], writes=[Bp])
                pv2 = pb[:, 0:256].rearrange("p (t c) -> p t c", t=2)
                P.op("dve", lambda e, pv2=pv2, m=m: e.tensor_copy(out=mkst[:, :, m * 128:(m + 1) * 128], in_=pv2),
                     reads=[Bp], writes=[B_mkst])
                if not is_k:
                    P.op("act", lambda e, pv2=pv2, m=m: e.activation(out=memv[:, :, m * 128:(m + 1) * 128], in_=pv2, func=AF.Copy),
                         reads=[Bp], writes=[B_memv])
                advance(gen, 3)
            for t in range(2):
                P.dma("pool", "memout", (memk_o if is_k else memv_o)[t * 128:(t + 1) * 128, :], mkst[:, t, :], reads=[B_mkst])

    def geom(kind):
        sample, halo = (kind == "S"), (kind == "H")
        NT = 128 if (sample or halo) else NT_P
        return sample, halo, NT, NT // 128

    def early(kind, gi, X):
        sample, halo, NT, ntl = geom(kind)
        xt, Bxt = xTs[X], B_xTs[X]
        if sample:
            yield from g_load_x(lambda j: xs[:, :], 1, X)
        elif halo:
            yield from g_load_x(lambda j: xp[0:128, :], 1, X)
        else:
            yield from g_load_x(lambda j: xp[128 + gi * NT_P + j * 128: 128 + gi * NT_P + (j + 1) * 128, :], ntl, X)
        yield "L"
        prep1 = make_prep(X, NT, 0, -0.5, rstd, B_rstd)
        for m in range(8):
            prep1[0](m)
            if m % 2 == 1:
                yield
        prep1[1]()
        yield
        last = (kind == "P" and gi == NG_P - 1)
        want32 = last or sample
        Us = U[:, :, 0:384].rearrange("p g (b c) -> p g b c", b=16)

        if sample:
            P.op("dve", lambda e: e.memset(U[:, :, 0:384], 0.0), writes=[B_U])
            for hb in range(2):
                P.dma("sp", "spld", xin[hb][0:120, 0:512], spool[hb * 8:(hb + 1) * 8].rearrange("b r f -> (b r) f"), writes=[B_xin[hb]])
                pb, Bp = dbank()
                pv = pb[:].rearrange("p (c t) -> p c t", c=4)
                P.mm([(lambda e, c=c, pv=pv, hb=hb: e.transpose(out=pv[:, c, 0:120], in_=xin[hb][0:120, c * 128:(c + 1) * 128],
                                                                identity=identf[0:120, 0:120])) for c in range(4)],
                     reads=[B_xin[hb]] + CONST, writes=[Bp])
                for c in range(4):
                    P.op("dve", lambda e, c=c, pv=pv, hb=hb: e.tensor_copy(
                        out=Us[:, c, hb * 8:(hb + 1) * 8, 1:16], in_=pv[:, c, 0:120].rearrange("p (b r) -> p b r", b=8)),
                        reads=[Bp], writes=[B_U])
            yield

        def in_evac(m, pb, Bp):
            rd = [Bp, B_rstd]
            if m < 4:
                P.op("dve", lambda e: e.tensor_tensor(out=qT[:, m, :NT], in0=pb[:, :NT], in1=rstd[:, :NT], op=ALU.mult), reads=rd, writes=[B_qT])
            elif m == 4 or m == 5:
                dst_, Bd_ = (kT[:, 128:128 + NT], B_kT) if m == 4 else (vT[:, :NT], B_vT)
                P.op("dve", lambda e: e.tensor_tensor(out=dst_, in0=pb[:, :NT], in1=rstd[:, :NT], op=ALU.mult), reads=rd, writes=[Bd_])
                if want32:
                    P.op("dve", lambda e: e.tensor_tensor(out=kv32[:, m - 4, :], in0=pb[:, NT - 128:NT], in1=rstd[:, NT - 128:NT], op=ALU.mult),
                         reads=rd, writes=[B_kv32])
            else:
                g = m - 6
                if sample:
                    P.op("dve", lambda e: e.tensor_tensor(out=Us[:, g, :, 16:24], in0=pb[:, 0:128].rearrange("p (b t) -> p b t", b=16),
                                                          in1=rstd[:, 0:128].rearrange("p (b t) -> p b t", b=16), op=ALU.mult),
                         reads=rd, writes=[B_U])
                else:
                    P.op("dve", lambda e: e.tensor_tensor(out=U[:, g, 16:16 + NT], in0=pb[:, :NT], in1=rstd[:, :NT], op=ALU.mult), reads=rd, writes=[B_U])
        yield from g_dense("in", list(range(4, 10)) if halo else list(range(10)), NT, lambda k: hT[:, k, :NT], B_hT, in_evac)
        yield "P1"

        if sample:
            for hb in range(2):
                P.dma("pool", "ckld", vc[:, hb * 8:(hb + 1) * 8, :], cv[hb * 8:(hb + 1) * 8].rearrange("b s f -> s b f"), writes=[B_vc])
            kst = pexp[:].rearrange("p u t -> p (u t)").rearrange("p (b f) -> p b f", b=16)
            for hb in range(2):
                P.dma("pool", "ckld", kst[:, hb * 8:(hb + 1) * 8, :], ck[hb * 8:(hb + 1) * 8].rearrange("b s f -> s b f"), writes=[B_pexpH[hb]])
            for hb in range(2):
                pb, Bp = tbank()
                pv = pb[:].bitcast(BF16).rearrange("p (b t) -> p b t", b=8)
                P.mm([(lambda e, b=b, pv=pv, hb=hb: e.transpose(out=pv[:, b, :], in_=kst[:, hb * 8 + b, :], identity=ident[:])) for b in range(8)],
                     reads=[B_pexpH[hb]] + CONST, writes=[Bp])
                P.op("dve", lambda e, pv=pv, hb=hb: e.tensor_copy(out=kcT[:, hb * 8:(hb + 1) * 8, :], in_=pv), reads=[Bp], writes=[B_kcT])
            yield

        for j0 in range(0, ntl, 4):
            pb, Bp = tbank()
            pv = pb[:].bitcast(BF16)[:, 0:512].rearrange("p (j t) -> p j t", j=4)
            P.mm([(lambda e, j=j, pv=pv: e.transpose(out=pv[:, j - j0, :], in_=vT[:, j * 128:(j + 1) * 128], identity=ident[:]))
                  for j in range(j0, min(ntl, j0 + 4))], reads=[B_vT] + CONST, writes=[Bp])
            nj = min(ntl, j0 + 4) - j0
            P.op("dve", lambda e, pv=pv, j0=j0, nj=nj: e.tensor_copy(out=vtok[:, 1 + j0:1 + j0 + nj, :], in_=pv[:, 0:nj, :]),
                 reads=[Bp], writes=[B_vtok])
        yield

        def carry():
            P.op("dve", lambda e: e.tensor_copy(out=kT[:, 0:128], in_=kT[:, NT:NT + 128]), reads=[B_kT], writes=[B_kT])
            P.op("dve", lambda e: e.tensor_copy(out=vtok[:, 0, :], in_=vtok[:, ntl, :]), reads=[B_vtok], writes=[B_vtok])

        if halo:
            carry()
            P.op("dve", lambda e: e.tensor_copy(out=carryU[:], in_=U[:, :, NT:NT + 16]), reads=[B_U], writes=[B_cU])
            return

        if want32:
            if last:
                def dd():
                    P.dma("pool", "kvout", wkp[:, :], ost[:, 0:128], reads=[B_ost])
                    P.dma("pool", "kvout", wvp[:, :], ost[:, 128:256], reads=[B_ost])
            else:
                def dd():
                    for t in range(8):
                        P.dma("pool", "kvout", wks[:, 120 + t, :], ost[t:128:8, 0:128], reads=[B_ost])
                        P.dma("pool", "kvout", wvs[:, 120 + t, :], ost[t:128:8, 128:256], reads=[B_ost])
                    P.dma("sp", "d2d_k", wks[:, 0:120, :], ck[:, 8:128, :])
                    P.dma("sp", "d2d_v", wvs[:, 0:120, :], cv[:, 8:128, :])
            out_tok_major([kv32[:, 0, :], kv32[:, 1, :]], [B_kv32], 128, dd)
            yield

        if not sample:
            P.op("dve", lambda e: e.tensor_copy(out=U[:, :, 0:16], in_=carryU[:]), reads=[B_cU], writes=[B_U])
        for hh in range(2):
            if sample:
                Wd, c0 = 192, hh * 192
            else:
                Wd, c0 = 16 + NT // 2, hh * (NT // 2)
            Uh = U[:, :, c0:c0 + Wd]
            P.op("dve", lambda e, Uh=Uh, Wd=Wd: e.tensor_tensor(out=SA[:, :, 1:Wd], in0=Uh[:, :, 1:Wd], in1=Uh[:, :, 0:Wd - 1], op=ALU.add),
                 reads=[B_U], writes=[B_SA])
            P.op("dve", lambda e, Wd=Wd: e.tensor_tensor(out=SB[:, 1:4, 3:Wd], in0=SA[:, 1:4, 3:Wd], in1=SA[:, 1:4, 1:Wd - 2], op=ALU.add),
                 reads=[B_SA], writes=[B_SB])
            yield
            P.op("dve", lambda e, Wd=Wd: e.tensor_tensor(out=SA[:, 2:4, 7:Wd], in0=SB[:, 2:4, 7:Wd], in1=SB[:, 2:4, 3:Wd - 4], op=ALU.add),
                 reads=[B_SB], writes=[B_SA])
            P.op("dve", lambda e, Wd=Wd: e.tensor_tensor(out=SB[:, 3, 15:Wd], in0=SA[:, 3, 15:Wd], in1=SA[:, 3, 7:Wd - 8], op=ALU.add),
                 reads=[B_SA], writes=[B_SB])
            yield
            for g in range(4):
                S_, BS_ = (SA, B_SA) if g % 2 == 0 else (SB, B_SB)
                if sample:
                    sv = S_[:, g, 0:192].rearrange("p (b c) -> p b c", b=8)[:, :, 16:24]
                    uv = Uh[:, g, :].rearrange("p (b c) -> p b c", b=8)[:, :, 16:24]
                    dv = dT[:, g, hh * 64:(hh + 1) * 64].rearrange("p (b t) -> p b t", b=8)
                else:
                    sv, uv, dv = S_[:, g, 16:Wd], Uh[:, g, 16:Wd], dT[:, g, c0:c0 + NT // 2]
                P.op("dve", lambda e, sv=sv, uv=uv, dv=dv, g=g: e.scalar_tensor_tensor(out=dv, in0=sv, scalar=1.0 / (2 << g), in1=uv,
                                                                                   op0=ALU.mult, op1=ALU.subtract),
                     reads=[BS_, B_U], writes=[B_dT])
                if kind == "P" and gi == 0 and hh == 0:
                    P.op("dve", lambda e, S_=S_, g=g: e.tensor_tensor(out=st[:, 0:16], in0=S_[:, g, 16:32], in1=invc[:, g, :], op=ALU.mult),
                         reads=[BS_] + CONST, writes=B_stH)
                    P.op("dve", lambda e, g=g: e.tensor_tensor(out=dT[:, g, 0:16], in0=st[:, 0:16], in1=U[:, g, 16:32], op=ALU.subtract),
                         reads=B_stH + [B_U], writes=[B_dT])
            yield
        if last:
            def dd2():
                P.dma("pool", "poolout", poolp[:, :], ost[113:128, :], reads=[B_ost])
            out_tok_major([U[:, g, 16 + NT - 128:16 + NT] for g in range(4)], [B_U], 128, dd2)
        if sample:
            for g in range(4):
                P.op("dve", lambda e, g=g: e.tensor_copy(out=SA[:, g, 0:128].rearrange("p (b t) -> p b t", b=16), in_=Us[:, g, :, 16:24]),
                     reads=[B_U, B_dT], writes=[B_SA])

            def dd3():
                for t in range(8):
                    P.dma("pool", "poolout", pools[:, 7 + t, :], ost[t:128:8, :], reads=[B_ost])
                P.dma("sp", "d2d_p", pools[:, 0:7, :], spool[:, 8:15, :])
            out_tok_major([SA[:, g, 0:128] for g in range(4)], [B_SA], 128, dd3)
        else:
            P.op("dve", lambda e: e.tensor_copy(out=carryU[:], in_=U[:, :, NT:NT + 16]), reads=[B_U], writes=[B_cU])
        yield
        for g in range(4):
            pb, Bp = dbank()
            P.mm([lambda e, g=g, pb=pb: e.matmul(pb[:, :NT], lhsT=wpool[:, g, :], rhs=dT[:, g, :NT], start=True, stop=True)],
                 reads=[B_wpool, B_dT], writes=[Bp])
            P.op("act", lambda e, g=g, pb=pb: e.activation(out=aoT[:, 4 + g, :NT], in_=pb[:, :NT], func=AF.Copy, scale=pscale[:, g:g + 1]),
                 reads=[Bp] + CONST, writes=[B_aoT])
            yield

        if not sample:
            for j in range(ntl):
                if gi == 0 and j == 1:
                    P.dma("sp", "biasld", bias[:].rearrange("p a b -> p (a b)"), biasg_d[:, :], writes=[B_bias])
                for gp in range(2):
                    bk = [sbank(), sbank()]
                    fns = []
                    for g in (2 * gp, 2 * gp + 1):
                        for kv in range(2):
                            fns.append(lambda e, kv=kv, g=g, j=j, bk=bk: e.matmul(
                                bk[kv][0][:, (g % 2) * 256:(g % 2 + 1) * 256], lhsT=qT[kv * 64:(kv + 1) * 64, g, j * 128:(j + 1) * 128],
                                rhs=kT[kv * 64:(kv + 1) * 64, j * 128:j * 128 + 256], start=True, stop=True))
                    P.mm(fns, reads=[B_qT, B_kT], writes=[bk[0][1], bk[1][1]])
                    for kv in range(2):
                        u0 = 4 * gp + kv
                        P.op("dve", lambda e, kv=kv, u0=u0, bk=bk: e.scalar_tensor_tensor(
                            out=sbias[:, u0:u0 + 3:2, :], in0=bk[kv][0][:].rearrange("p (g t) -> p g t", g=2), scalar=0.125,
                            in1=bias[:, u0:u0 + 3:2, :], op0=ALU.mult, op1=ALU.add),
                            reads=[bk[kv][1], B_bias], writes=[B_sbiasH[gp]])
                    yield
                sm = [win_softmax(4 * gp, 4, 256, sinkp[:, 4 * gp:4 * gp + 4], [gp]) for gp in range(2)]
                next(sm[0])
                yield
                next(sm[1])
                yield
                yield
                run(sm[0])
                yield
                run(sm[1])
                yield
                for gp in range(2):
                    yield from diag_T(4 * gp, 4, 2, pexp, [B_pexpH[gp]], [B_DgH[gp]],
                                      lambda u0: pT[:, u0:u0 + 2, :, :].rearrange("p u k t -> p (u k t)"), None, [B_pTH[gp]], [128, 128])
                yield
                po, Bpo = ps[PS_O], B_ps[PS_O]
                pov = po[:].rearrange("p (g t) -> p g t", g=4)
                fns = []
                for g in range(4):
                    for kv in range(2):
                        for kc in range(2):
                            fns.append(lambda e, g=g, kv=kv, kc=kc, j=j: e.matmul(
                                pov[kv * 64:(kv + 1) * 64, g, :], lhsT=vtok[:, j + kc, kv * 64:(kv + 1) * 64], rhs=pT[:, 2 * g + kv, kc, :],
                                start=(kc == 0), stop=(kc == 1)))
                P.mm(fns, reads=[B_vtok] + B_pTH, writes=[Bpo])
                P.op("act", lambda e, j=j: e.activation(out=aoT[:, 0:4, j * 128:(j + 1) * 128], in_=pov, func=AF.Copy),
                     reads=[Bpo], writes=[B_aoT])
                yield
            carry()
        else:
            P.op("dve", lambda e: e.tensor_copy(out=qs2[:].rearrange("p b (g t) -> p b g t", g=4),
                                                in_=qT[:, :, 0:128].rearrange("p g (b t) -> p b g t", b=16)), reads=[B_qT], writes=[B_qs2])
            pb, Bp = dbank()
            pvb = pb[:].bitcast(BF16)
            P.mm([(lambda e, i=i, pvb=pvb: e.transpose(out=pvb[0:32, i * 128:(i + 1) * 128], in_=vT[:, i * 32:(i + 1) * 32], identity=ident[:]))
                  for i in range(4)], reads=[B_vT] + CONST, writes=[Bp])
            P.op("dve", lambda e, pvb=pvb: e.tensor_copy(out=vnq[0:32, :, :], in_=pvb[0:32, 0:512].rearrange("p (i t) -> p i t", i=4)),
                 reads=[Bp], writes=[B_vnq])
            yield
            for i in range(4):
                bk = [sbank(), sbank()]
                fns = []
                for kv in range(2):
                    pvk = bk[kv][0]
                    for jq in range(4):
                        b = 4 * i + jq
                        fns.append(lambda e, kv=kv, jq=jq, b=b, pvk=pvk: e.matmul(
                            pvk[32 * jq:32 * jq + 32, 0:128], lhsT=qs2[kv * 64:(kv + 1) * 64, b, :], rhs=kcT[kv * 64:(kv + 1) * 64, b, :],
                            start=True, stop=True, tile_position=(kv * 64, 32 * jq)))
                    fns.append(lambda e, kv=kv, i=i, pvk=pvk: e.matmul(
                        pvk[:, 128:160], lhsT=qs2[kv * 64:(kv + 1) * 64, 4 * i:4 * i + 4, :].rearrange("p b t -> p (b t)"),
                        rhs=kT[kv * 64:(kv + 1) * 64, 128 + 32 * i:128 + 32 * i + 32], start=True, stop=True))
                P.mm(fns, reads=[B_qs2, B_kcT, B_kT], writes=[bk[0][1], bk[1][1]])
                for kv in range(2):
                    P.op("dve", lambda e, kv=kv, i=i, bk=bk: e.scalar_tensor_tensor(
                        out=sbias[:, 2 * i + kv, 0:160], in0=bk[kv][0][:, 0:160], scalar=0.125,
                        in1=biass[:, kv, :], op0=ALU.mult, op1=ALU.add),
                        reads=[bk[kv][1]] + CONST, writes=[B_sbiasH[i // 2]])
                yield
            smx = win_softmax(0, 8, 160, sinks[:, 0:8], [0, 1])
            next(smx)
            yield
            yield
            run(smx)
            yield
            yield from diag_T(0, 8, 2, pexp, B_pexpH, B_DgH, None, lambda u, kc: pT[0:(128 if kc == 0 else 32), u, kc, :], B_pTH, [128, 32])
            for i in range(4):
                pb, Bp = dbank()
                fns = []
                for kv in range(2):
                    u = 2 * i + kv
                    for jq in range(4):
                        b = 4 * i + jq
                        fns.append(lambda e, kv=kv, jq=jq, b=b, u=u, pb=pb: e.matmul(
                            pb[kv * 64:(kv + 1) * 64, 32 * jq:32 * jq + 32], lhsT=vc[:, b, kv * 64:(kv + 1) * 64], rhs=pT[:, u, 0, 32 * jq:32 * jq + 32],
                            start=(jq == 0), stop=False, skip_group_check=True))
                for kv in range(2):
                    u = 2 * i + kv
                    fns.append(lambda e, kv=kv, u=u, i=i, pb=pb: e.matmul(
                        pb[kv * 64:(kv + 1) * 64, 0:128], lhsT=vnq[0:32, i, kv * 64:(kv + 1) * 64], rhs=pT[0:32, u, 1, :],
                        start=False, stop=True, skip_group_check=True))
                P.mm(fns, reads=[B_vc, B_vnq] + B_pTH, writes=[Bp])
                P.op("act", lambda e, pb=pb, i=i: e.activation(
                    out=aoT[:, 0:4, 32 * i:32 * i + 32].rearrange("p g (j t) -> p j g t", j=4),
                    in_=pb[:, 0:128].rearrange("p (j g t) -> p j g t", j=4, g=4), func=AF.Copy),
                    reads=[Bp], writes=[B_aoT])
                yield

    def late_pre(kind, gi, X, gen, tgen=None):
        sample, halo, NT, ntl = geom(kind)
        xt, Bxt = xTs[X], B_xTs[X]
        loaded = [gen is None]
        xfree = [tgen is None]

        def step_tail(n):
            for _ in range(n):
                if tgen is not None and next(tgen, "END") == "XDONE":
                    xfree[0] = True

        def step_load():
            if not loaded[0] and xfree[0]:
                if next(gen, "L") == "L":
                    loaded[0] = True
        p1done = [gen is None]

        def step_p1(n):
            for _ in range(n):
                if not p1done[0]:
                    if next(gen, "P1") == "P1":
                        p1done[0] = True
        if sample:
            def xslot(c0):
                return xTs[0][:, :, c0:c0 + 128].bitcast(BF16)
            KX = [xslot(128), xslot(256)]
            VX = [xslot(384)]
            B_KX = [Buf("KX0"), Buf("KX1")]
            B_VX = [Buf("VX0")]
            for bb_ in B_KX + B_VX:
                bb_.al.append(B_xTs[0])
                B_xTs[0].al.append(bb_)
            NK, NV = 4, 3

            def kslot(b):
                i = b % NK
                return (("a", Kb[i], B_Kb[i]) if i < 2 else ("x", KX[i - 2], B_KX[i - 2]))

            def vslot(b):
                i = b % (NV + NK)
                if i < NV:
                    return (("a", Vb[i], B_Vb[i]) if i < 2 else ("x", VX[i - 2], B_VX[i - 2]))
                i -= NV
                return (("a", Kb[i], B_Kb[i]) if i < 2 else ("x", KX[i - 2], B_KX[i - 2]))

            def ld(slot, src):
                kind_, ap_, B_ = slot
                if kind_ == "a":
                    P.dma("pool", "c", ap_[:], src.rearrange("(m p) f -> p m f", p=128), writes=[B_])
                else:
                    for m_ in range(2):
                        P.dma("pool", "c", ap_[:, m_ * 4:(m_ + 1) * 4, :],
                              src[m_ * 128:(m_ + 1) * 128, :].rearrange("p (q j) -> p q j", j=256), writes=[B_])

            def tile_of(slot, mt, c):
                kind_, ap_, B_ = slot
                if kind_ == "a":
                    return ap_[:, mt, c * 128:(c + 1) * 128]
                return ap_[:, mt * 4 + c // 2, (c % 2) * 128:(c % 2 + 1) * 128]

            for b in range(3):
                ld(kslot(b), cmk[b])
            for b in range(3):
                ld(vslot(b), cmv[b])

        prep2 = make_prep(X, NT, 1, -0.5, rstd, B_rstd)
        for _ in g_dense("out", list(range(8)), NT, lambda k: aoT[:, k, :NT], B_aoT, resid_evac(NT, X, prep=prep2)):
            step_load()
            step_tail(2)
        prep2[1]()
        qcT, B_qcT = aoT, B_aoT
        ocT, B_ocT = hidT, B_hid

        def cq_evac(m, pb, Bp):
            P.op("dve", lambda e: e.tensor_tensor(out=qcT[:, m, :NT], in0=pb[:, :NT], in1=rstd[:, :NT], op=ALU.mult),
                 reads=[Bp, B_rstd], writes=[B_qcT])
        for _ in g_dense("cq", list(range(8)), NT, lambda k: hT[:, k, :NT], B_hT, cq_evac):
            step_load()
            step_tail(2)
        while tgen is not None and not xfree[0]:
            step_tail(1)
        while not loaded[0]:
            step_load()
        run(tgen)

        if not sample:
            for j in range(ntl):
                banks = [sbank(), sbank()]
                for hp, (pb, Bp) in enumerate(banks):
                    pv = pb[:].rearrange("p (h t) -> p h t", h=2)
                    fns = []
                    for hh in range(2):
                        h = 2 * hp + hh
                        for dc in range(2):
                            fns.append(lambda e, pv=pv, hh=hh, h=h, dc=dc, j=j: e.matmul(
                                pv[:, hh, :], lhsT=qcT[:, 2 * h + dc, j * 128:(j + 1) * 128], rhs=memkT[:, 2 * h + dc, :],
                                start=(dc == 0), stop=(dc == 1)))
                    P.mm(fns, reads=[B_qcT, B_memkT], writes=[Bp])
                cross_softmax(banks)
                step_p1(2)
                run(diag_T(0, 4, 2, pexp, [B_pexp], [B_Dg], lambda h0: pTc[:, h0:h0 + 2, :, :].rearrange("p u k t -> p (u k t)"), None, [B_pTc], [128, 128]))
                for half in range(2):
                    pb, Bp = dbank()
                    pv = pb[:].rearrange("p (c t) -> p c t", c=4)
                    fns = []
                    for cc in range(4):
                        c = half * 4 + cc
                        h = c // 2
                        for mc in range(2):
                            fns.append(lambda e, pv=pv, cc=cc, c=c, h=h, mc=mc: e.matmul(
                                pv[:, cc, :], lhsT=memv[:, mc, c * 128:(c + 1) * 128], rhs=pTc[:, h, mc, :], start=(mc == 0), stop=(mc == 1)))
                    P.mm(fns, reads=[B_memv, B_pTc], writes=[Bp])
                    P.op("act" if half == 0 else "dve",
                         (lambda e, pv=pv, half=half, j=j: e.activation(out=ocT[:, half * 4:half * 4 + 4, j * 128:(j + 1) * 128], in_=pv, func=AF.Copy))
                         if half == 0 else
                         (lambda e, pv=pv, half=half, j=j: e.tensor_copy(out=ocT[:, half * 4:half * 4 + 4, j * 128:(j + 1) * 128], in_=pv)),
                         reads=[Bp], writes=[B_ocT])
                step_p1(2)
        else:
            banks = [sbank(), sbank()]
            for i in range(2):
                P.op("dve", lambda e, i=i: e.memset(qpad[i][:], 0.0), writes=[B_qpad[i]])

            def Tstage(b):
                s2 = b % 2
                ks = kslot(b)
                if b + 3 < 16:
                    ld(kslot(b + 3), cmk[b + 3])
                for mt in range(2):
                    pb, Bp = tbank()
                    pv = pb[:].bitcast(BF16).rearrange("p (c t) -> p c t", c=8)
                    P.mm([(lambda e, c=c, pv=pv, mt=mt, ks=ks: e.transpose(out=pv[:, c, :], in_=tile_of(ks, mt, c), identity=ident[:]))
                          for c in range(8)], reads=[ks[2]] + CONST, writes=[Bp])
                    P.op("act" if mt == 0 else "dve",
                         (lambda e, pv=pv, mt=mt, s2=s2: e.activation(out=KbT[s2][:, :, mt * 128:(mt + 1) * 128], in_=pv, func=AF.Copy))
                         if mt == 0 else
                         (lambda e, pv=pv, mt=mt, s2=s2: e.tensor_copy(out=KbT[s2][:, :, mt * 128:(mt + 1) * 128], in_=pv)),
                         reads=[Bp], writes=[B_KbT[s2]])
                if b >= 2:
                    P.op("dve", lambda e, s2=s2, b=b: e.memset(qpad[s2][:, :, (b - 2) * 8:(b - 1) * 8], 0.0), writes=[B_qpad[s2]])
                P.op("dve", lambda e, s2=s2, b=b: e.tensor_copy(out=qpad[s2][:, :, b * 8:(b + 1) * 8], in_=qcT[:, :, b * 8:(b + 1) * 8]),
                     reads=[B_qcT], writes=[B_qpad[s2]])

            def Sstage(b):
                s2 = b % 2
                for hp, (pb, Bp) in enumerate(banks):
                    pv = pb[:].rearrange("p (h t) -> p h t", h=2)
                    fns = []
                    for hh in range(2):
                        h = 2 * hp + hh
                        for dc in range(2):
                            fns.append(lambda e, pv=pv, hh=hh, h=h, dc=dc, s2=s2, b=b: e.matmul(
                                pv[:, hh, :], lhsT=qpad[s2][:, 2 * h + dc, :], rhs=KbT[s2][:, 2 * h + dc, :],
                                start=(b == 0 and hh == 0 and dc == 0), stop=(b == 15 and dc == 1), skip_group_check=True))
                    P.mm(fns, reads=[B_qpad[s2], B_KbT[s2]], writes=[Bp])

            Tstage(0)
            for b in range(16):
                if b + 1 < 16:
                    Tstage(b + 1)
                Sstage(b)
            for b in range(NV, NV + NK):
                ld(vslot(b), cmv[b])
            cross_softmax(banks)
            run(diag_T(0, 4, 2, pexp, [B_pexp], [B_Dg], lambda h0: pTc[:, h0:h0 + 2, :, :].rearrange("p u k t -> p (u k t)"), None, [B_pTc], [128, 128]))
            pbs = [dbank(), dbank()]
            for b in range(16):
                vs = vslot(b)
                fns = []
                for c in range(8):
                    pv = pbs[c // 4][0][:].rearrange("p (c t) -> p c t", c=4)
                    h = c // 2
                    for mc in range(2):
                        fns.append(lambda e, pv=pv, c=c, h=h, mc=mc, vs=vs, b=b: e.matmul(
                            pv[:, c % 4, b * 8:(b + 1) * 8], lhsT=tile_of(vs, mc, c), rhs=pTc[:, h, mc, b * 8:(b + 1) * 8],
                            start=(mc == 0), stop=(mc == 1), skip_group_check=True))
                P.mm(fns, reads=[vs[2], B_pTc], writes=[pbs[0][1], pbs[1][1]])
                if b + NV + NK < 16:
                    ld(vslot(b + NV + NK), cmv[b + NV + NK])
            for half in range(2):
                pv = pbs[half][0][:].rearrange("p (c t) -> p c t", c=4)
                P.op("act" if half == 0 else "dve",
                     (lambda e, pv=pv, half=half: e.activation(out=ocT[:, half * 4:half * 4 + 4, 0:128], in_=pv, func=AF.Copy))
                     if half == 0 else
                     (lambda e, pv=pv, half=half: e.tensor_copy(out=ocT[:, half * 4:half * 4 + 4, 0:128], in_=pv)),
                     reads=[pbs[half][1]], writes=[B_ocT])
        while not p1done[0]:
            step_p1(1)
        prep3 = make_prep(X, NT, 3, -1.0, rstd2, B_rstd2)
        dense("co", list(range(8)), NT, lambda k: ocT[:, k, :NT], B_ocT, resid_evac(NT, X, prep=prep3))
        prep3[1]()

    def ffn(kind, gi, X, gen):
        sample, halo, NT, ntl = geom(kind)
        xt, Bxt = xTs[X], B_xTs[X]
        uctr = [0]

        def up_evac(m, pb, Bp):
            r, Br = relu_t[uctr[0] % 2], B_relu[uctr[0] % 2]
            uctr[0] += 1
            P.op("act", lambda e: e.activation(out=r[:, :NT], in_=pb[:, :NT], func=AF.Relu), reads=[Bp], writes=[Br])
            P.op("pool", lambda e: e.tensor_tensor(out=hidT[:, m, :NT], in0=r[:, :NT], in1=r[:, :NT], op=ALU.mult), reads=[Br], writes=[B_hid])
        for _ in g_dense("up", list(range(32)), NT, lambda k: hT[:, k, :NT], B_hT, up_evac):
            advance(gen, 1)
        for _ in g_dense("down", list(range(8)), NT, lambda k: hidT[:, k, :NT], B_hid, resid_evac(NT, X, scale2=True), kgroups=4):
            advance(gen, 1)

    def g_tail(kind, gi, X):
        sample, halo, NT, ntl = geom(kind)
        xt, Bxt = xTs[X], B_xTs[X]
        sq = hidT[:, 16:24, :]
        for k in range(8):
            P.op("act", lambda e, k=k: e.activation(out=sq[:, k, :NT], in_=xt[:, k, :NT], func=AF.Square), reads=[Bxt], writes=[B_hid])
            if k % 4 == 3:
                yield
        pb0, Bp0 = dbank()
        P.mm([(lambda e, k=k: e.matmul(pb0[:, :NT], lhsT=ones[:], rhs=sq[:, k, :NT], start=(k == 0), stop=(k == 7)))
              for k in range(8)], reads=[B_hid] + CONST, writes=[Bp0])
        P.op("act", lambda e: e.activation(out=rstd2[:, :NT], in_=pb0[:, :NT], func=AF.Ln, scale=1.0 / D, bias=EPS),
             reads=[Bp0], writes=[B_rstd2])
        P.op("act", lambda e: e.activation(out=rstd2[:, :NT], in_=rstd2[:, :NT], func=AF.Exp, scale=-0.5),
             reads=[B_rstd2], writes=[B_rstd2])
        yield
        for k in range(8):
            P.op("dve", lambda e, k=k: e.scalar_tensor_tensor(out=yT[:, k, :NT], in0=xt[:, k, :NT], scalar=gvec[:, 4, k:k + 1],
                                                              in1=rstd2[:, :NT], op0=ALU.mult, op1=ALU.mult),
                 reads=[Bxt, B_rstd2] + CONST, writes=[B_yT])
            if k % 2 == 1 and k < 7:
                yield
        yield "XDONE"
        for j in range(ntl):
            ys_, Bys = yst[1], B_yst[1]
            for hf in range(2):
                pb, Bp = dbank()
                pv = pb[:].rearrange("p (c t) -> p c t", c=4)
                P.mm([(lambda e, c=c, pv=pv, hf=hf, j=j: e.transpose(out=pv[:, c, :], in_=yT[:, hf * 4 + c, j * 128:(j + 1) * 128], identity=identf[:]))
                      for c in range(4)], reads=[B_yT] + CONST, writes=[Bp])
                P.op("act" if hf == 0 else "dve",
                     (lambda e, pb=pb, hf=hf, ys_=ys_: e.activation(out=ys_[:, hf * 512:(hf + 1) * 512], in_=pb[:, :], func=AF.Copy))
                     if hf == 0 else
                     (lambda e, pb=pb, hf=hf, ys_=ys_: e.tensor_copy(out=ys_[:, hf * 512:(hf + 1) * 512], in_=pb[:, :])),
                     reads=[Bp], writes=[Bys])
                yield
            if sample:
                P.dma("pool", "y", ys[:, :], ys_[:], reads=[Bys])
            else:
                r0 = gi * NT_P + j * 128
                P.dma("pool", "y", yp[r0:r0 + 128, :], ys_[:], reads=[Bys])

    order = [("P", g) for g in range(NG_P)] + [("S", 0)]
    order = order[:max(0, min(len(order), STAGE))] if STAGE < 50 else order
    run(early("H", 0, 0))
    gen0 = early(order[0][0], order[0][1], 0) if order else None
    advance(gen0, until="P1")
    mem_setup(gen0)
    run(gen0)
    tgen = None
    for idx, (kind, gi) in enumerate(order):
        X = idx % 2
        nxt = order[idx + 1] if idx + 1 < len(order) else None
        gen = early(nxt[0], nxt[1], (idx + 1) % 2) if nxt else None
        late_pre(kind, gi, X, gen, tgen)
        ffn(kind, gi, X, gen)
        run(gen)
        tgen = g_tail(kind, gi, X)
        if not TAIL_OVERLAP:
            run(tgen)
            tgen = None
    run(tgen)

    return finish()


_CACHE = {}


def _build_nc():
    if "nc" in _CACHE:
        return _CACHE["nc"]
    nc0 = bass.Bass("TRN2", target_bir_lowering=False)
    with ExitStack() as es0:
        _, W0 = build_sched(nc0, es0)
    sched = W0.rec
    nc = bass.Bass("TRN2", target_bir_lowering=False)
    with ExitStack() as es:
        P, W = build(nc, es, False, sched)
        assert W.i == len(sched), (W.i, len(sched))
        block = es.enter_context(nc.Block())
        P.flush(block)
    _CACHE["nc"] = nc
    return nc


def build_sched(nc0, es0):
    return build(nc0, es0, False, None)


def _tables(half):
    slopes = 2.0 ** (-(np.arange(8) + 1.0))
    q = np.arange(128)[:, None]
    c = np.arange(256)[None, :]
    dist = q - c + 128
    valid = (dist >= 0) & (dist <= 128)
    biasg = np.empty((128, 8, 256), np.float32)
    for g in range(4):
        for kv in range(2):
            h = kv * 4 + g
            biasg[:, 2 * g + kv, :] = np.where(valid, -slopes[h] * dist, -1e30)
    biasf = biasg.copy()
    if half == 0:
        biasf[:, :, 0:128] = -1e30
    biass = np.full((128, 2, 160), -1e30, np.float32)
    for j in range(4):
        for g in range(4):
            for t in range(8):
                r = j * 32 + g * 8 + t
                for kv in range(2):
                    h = kv * 4 + g
                    cc = np.arange(128)
                    d = t + 128 - cc
                    biass[r, kv, 0:128] = np.where(cc >= t, -slopes[h] * d, -1e30)
                    for tp in range(t + 1):
                        biass[r, kv, 128 + j * 8 + tp] = -slopes[h] * (t - tp)
    invc = np.empty((128, 4, 16), np.float32)
    for g in range(4):
        w = 2 << g
        for p in range(16):
            invc[:, g, p] = 1.0 / (min(p + 1, w) if half == 0 else w)
    return biasg.reshape(128, -1), biasf.reshape(128, -1), biass.reshape(128, -1), invc.reshape(128, -1)


def _prep(x_prompt, x_sample, cache_win_k, cache_win_v, state_pool, cache_mem_k, cache_mem_v,
          mem_prompt, g_mix, w_in, attn_sinks, w_pool, pool_scale, w_out, g_cross, g_mem,
          w_cq, w_ck, w_cv, w_co, g_ffn, w_up, w_down, g_final):
    f = lambda a: np.ascontiguousarray(np.asarray(a, dtype=np.float32))
    x_prompt, x_sample = f(x_prompt), f(x_sample)
    shared = dict(w_in=f(w_in)[0], w_pool=f(w_pool)[0], w_out=f(w_out)[0], w_cq=f(w_cq)[0], w_ck=f(w_ck)[0],
                  w_cv=f(w_cv)[0], w_co=f(w_co)[0], w_up=f(w_up)[0], w_down=f(w_down)[0])
    gs = np.stack([f(g_mix)[0], f(g_cross)[0], f(g_mem)[0], f(g_ffn)[0], f(g_final)], 0)
    shared["gvec"] = np.ascontiguousarray(gs.reshape(5, 8, 128).transpose(2, 0, 1).reshape(128, 40))
    shared["pscale"] = np.ascontiguousarray(f(pool_scale)[0].reshape(4, 128).T)
    sk = f(attn_sinks)[0]
    sinkp = np.empty((128, 8), np.float32)
    for g in range(4):
        for kv in range(2):
            sinkp[:, 2 * g + kv] = sk[kv * 4 + g]
    shared["sinkp"] = sinkp
    sinks = np.empty((128, 8), np.float32)
    for r in range(128):
        g = (r % 32) // 8
        for i in range(4):
            sinks[r, 2 * i] = sk[g]
            sinks[r, 2 * i + 1] = sk[4 + g]
    shared["sinks"] = sinks
    ckf, cvf, spf = f(cache_win_k)[0], f(cache_win_v)[0], f(state_pool)[0]
    cmkf, cmvf, memf = f(cache_mem_k)[0], f(cache_mem_v)[0], f(mem_prompt)
    in_maps = []
    for c in range(NCORES):
        b, half = c // 2, c % 2
        s0 = half * SEQ_CORE
        xp = np.zeros((128 + SEQ_CORE, D), np.float32)
        xp[128:] = x_prompt[b, s0:s0 + SEQ_CORE]
        if half == 1:
            xp[:128] = x_prompt[b, s0 - 128:s0]
        biasg, biasf, biass, invc = _tables(half)
        sl = slice(16 * c, 16 * c + 16)
        m = dict(shared)
        m.update(xp=xp, xs=np.ascontiguousarray(x_sample[sl].reshape(128, D)), mem=np.ascontiguousarray(memf[b]),
                 ck=np.ascontiguousarray(ckf[sl].reshape(16, 128, 128)), cv=np.ascontiguousarray(cvf[sl].reshape(16, 128, 128)),
                 spool=np.ascontiguousarray(spf[sl]), cmk=np.ascontiguousarray(cmkf[sl].reshape(16, 256, D)),
                 cmv=np.ascontiguousarray(cmvf[sl].reshape(16, 256, D)),
                 biasg=biasg, biasf=biasf, biass=biass, invc=invc)
        in_maps.append(m)
    return in_maps


def kernel(**inputs):
    in_maps = _prep(**inputs)
    nc = _build_nc()
    res = run_bass_kernel_spmd(nc, in_maps, core_ids=list(range(NCORES))).results
    return _assemble(res)


def _assemble(res):
    B, S = 4, 4096
    y_prompt = np.empty((B, S, D), np.float32)
    y_sample = np.empty((128, 8, D), np.float32)
    wk_p = np.empty((1, B, 128, 2, 64), np.float32); wv_p = np.empty_like(wk_p)
    pool_p = np.empty((1, B, 15, 512), np.float32)
    mk_p = np.empty((1, B, 256, 4, 256), np.float32); mv_p = np.empty_like(mk_p)
    wk_s = np.empty((1, 128, 128, 2, 64), np.float32); wv_s = np.empty_like(wk_s)
    pool_s = np.empty((1, 128, 15, 512), np.float32)
    for c in range(NCORES):
        r = res[c]
        b, half = c // 2, c % 2
        y_prompt[b, half * SEQ_CORE:(half + 1) * SEQ_CORE] = r["yp"]
        sl = slice(16 * c, 16 * c + 16)
        y_sample[sl] = r["ys"].reshape(16, 8, D)
        if half == 1:
            wk_p[0, b] = r["wkp"].reshape(128, 2, 64)
            wv_p[0, b] = r["wvp"].reshape(128, 2, 64)
            pool_p[0, b] = r["poolp"]
        else:
            mk_p[0, b] = r["memk"].reshape(256, 4, 256)
            mv_p[0, b] = r["memv"].reshape(256, 4, 256)
        wk_s[0, sl] = r["wks"].reshape(16, 128, 2, 64)
        wv_s[0, sl] = r["wvs"].reshape(16, 128, 2, 64)
        pool_s[0, sl] = r["pools"]
    return (y_prompt, y_sample, wk_p, wv_p, pool_p, mk_p, mv_p, wk_s, wv_s, pool_s)
```
